# Optimizing a Trainium2 kernel written in Bass

```python
import math
import jax
import jax.numpy as jnp
from jax import lax
import numpy as np

D_MODEL = 2048
BATCH = 8
SEQ = 2048
DEPTH = 2

GRID_W = 64
CTX_LEN = 256
EPS = 1e-6
N_SUB = 3
N_MOD = 3 * N_SUB
D_FF = 5632
MIX_WIDTH = D_MODEL
S5_WIDTH = D_MODEL // 2
S5_GROUP = 16
S5_GROUPS = S5_WIDTH // S5_GROUP
S5_STATE = 64
NA_HEADS = 8
NA_HEAD_DIM = 128
NA_WIDTH = NA_HEADS * NA_HEAD_DIM
NA_KH_MAX = 8
NA_KW = 16
EVEN_IN = S5_WIDTH + 3 * NA_WIDTH
HY_WIDTH = D_MODEL // 2
HY_ORDER = 2
HY_SHORT = 3
HY_BANDS = 16
HY_EMB = 1 + 2 * HY_BANDS
HY_HIDDEN = 64
HY_N_MID = 2
HY_DECAY_TARGET = 1e-2
HY_FAST_PCT = 0.3
HY_SLOW_PCT = 1.5
HY_FILTER_STD = 0.007
SSD_INNER = D_MODEL // 2
SSD_HEAD_DIM = 64
SSD_HEADS = SSD_INNER // SSD_HEAD_DIM
SSD_GROUPS = 4
SSD_STATE = 128
SSD_CONV = 3
SSD_CHUNK = 128
SSD_BC = SSD_GROUPS * SSD_STATE
SSD_XBC = SSD_INNER + 2 * SSD_BC
ODD_IN = 3 * HY_WIDTH + SSD_INNER + SSD_XBC + 2 * SSD_HEADS
F32 = jnp.float32

kernel_name = 'hybrid_s5_natten_hyena_ssd_prefix_dit'


def _rmsnorm(x, g):
    xf = x.astype(F32)
    y = xf * lax.rsqrt(jnp.mean(xf * xf, axis=-1, keepdims=True) + EPS)
    return (y * g.astype(F32)).astype(x.dtype)


def _pre(h, g, m, i):
    return _rmsnorm(h, g) * (1 + m[:, 3 * i + 1]) + m[:, 3 * i]


def _swiglu(h, wg, wu, wd):
    return (jax.nn.silu(h @ wg) * (h @ wu)) @ wd


def _dwconv(u, w, b):
    k = w.shape[0]
    y = lax.conv_general_dilated(u, w[:, None, :].astype(u.dtype), (1,), [(k // 2, k // 2)],
                                 dimension_numbers=('NWC', 'WIO', 'NWC'),
                                 feature_group_count=u.shape[-1])
    return y + b.astype(u.dtype)


def _lin_combine(left, right):
    a_l, b_l = left
    a_r, b_r = right
    return a_l * a_r, a_r * b_l + b_r


def _s5_discretize(a_re, a_im, log_dt, b_re, b_im):
    lam = lax.complex(a_re.astype(F32), a_im.astype(F32))
    dt = jnp.exp(log_dt.astype(F32))[:, None]
    a_bar = jnp.exp(lam * dt)
    b_bar = ((a_bar - 1.0) / lam)[..., None] * lax.complex(b_re.astype(F32), b_im.astype(F32))
    return a_bar, b_bar


def _s5_scan(u, a_bar, b_bar, s0, reverse):
    if reverse:
        u = jnp.flip(u, 1)
    bu = jnp.einsum('gph,blgh->blgp', b_bar, u.astype(jnp.complex64))
    if s0 is not None:
        bu = bu.at[:, 0].add(a_bar * s0)
    a = jnp.broadcast_to(a_bar, (1,) + bu.shape[1:])
    _, s = lax.associative_scan(_lin_combine, (a, bu), axis=1)
    final = s[:, -1]
    if reverse:
        s = jnp.flip(s, 1)
    return s, final


def _s5_readout(c_mat, s):
    return jnp.einsum('ghp,blgp->blgh', c_mat, s).real


def _s5_mixer(uc, ul, a_re, a_im, log_dt, b_re, b_im, c_re, c_im, d, glu_w, glu_b, ctx_out):
    def grp(u):
        return u.astype(F32).reshape(u.shape[0], u.shape[1], S5_GROUPS, S5_GROUP)
    uc, ul = grp(uc), grp(ul)
    d = d.astype(F32)
    yl = d * ul
    yc = d * uc if ctx_out else None
    for k, rev in ((0, False), (1, True)):
        a_bar, b_bar = _s5_discretize(a_re[k], a_im[k], log_dt[k], b_re[k], b_im[k])
        c_mat = lax.complex(c_re[k].astype(F32), c_im[k].astype(F32))
        sc, fin = _s5_scan(uc, a_bar, b_bar, None, rev)
        sl, _ = _s5_scan(ul, a_bar, b_bar, fin, rev)
        yl = yl + _s5_readout(c_mat, sl)
        if ctx_out:
            yc = yc + _s5_readout(c_mat, sc)

    def glu(y):
        y = jax.nn.gelu(y.reshape(y.shape[0], y.shape[1], S5_WIDTH))
        return y * jax.nn.sigmoid(y @ glu_w.astype(F32) + glu_b.astype(F32))
    return (glu(yc) if ctx_out else None), glu(yl)


def _split_qkv(p):
    qkv = p[..., S5_WIDTH:].reshape(p.shape[0], p.shape[1], 3, NA_HEADS, NA_HEAD_DIM)
    return qkv[:, :, 0], qkv[:, :, 1], qkv[:, :, 2]


def _na_latent(q, k, v, kc, vc, rpb):
    bsz, length, heads, hd = q.shape
    rows = length // GRID_W
    kh = min(NA_KH_MAX, rows)
    scale = hd ** -0.5
    qg = q.reshape(bsz, rows, GRID_W, heads, hd)
    kg = k.reshape(bsz, rows, GRID_W, heads, hd)
    vg = v.reshape(bsz, rows, GRID_W, heads, hd)
    col = jnp.arange(GRID_W)
    cs = jnp.clip(col - NA_KW // 2, 0, GRID_W - NA_KW)
    col_mask = (col[None, :] >= cs[:, None]) & (col[None, :] < cs[:, None] + NA_KW)
    dc_idx = jnp.clip(col[None, :] - col[:, None] + NA_KW - 1, 0, 2 * NA_KW - 2)
    rpb_c = rpb.astype(F32)[:, :, dc_idx]
    n_loc = kh * GRID_W

    def row_block(r):
        rs = jnp.clip(r - kh // 2, 0, rows - kh)
        qr = lax.dynamic_index_in_dim(qg, r, axis=1, keepdims=False)
        kr = lax.dynamic_slice_in_dim(kg, rs, kh, axis=1)
        vr = lax.dynamic_slice_in_dim(vg, rs, kh, axis=1)
        dr_idx = rs + jnp.arange(kh) - r + NA_KH_MAX - 1
        bias = jnp.transpose(rpb_c[:, dr_idx], (0, 2, 1, 3))
        s_loc = jnp.einsum('bqhd,bjkhd->bhqjk', qr, kr).astype(F32) * scale + bias[None]
        s_loc = jnp.where(col_mask[None, None, :, None, :], s_loc, -jnp.inf)
        s_ctx = jnp.einsum('bqhd,bchd->bhqc', qr, kc).astype(F32) * scale
        p = jax.nn.softmax(jnp.concatenate([s_loc.reshape(bsz, heads, GRID_W, n_loc), s_ctx], -1), axis=-1)
        p_loc = p[..., :n_loc].reshape(bsz, heads, GRID_W, kh, GRID_W).astype(v.dtype)
        p_ctx = p[..., n_loc:].astype(v.dtype)
        return (jnp.einsum('bhqjk,bjkhd->bqhd', p_loc, vr)
                + jnp.einsum('bhqc,bchd->bqhd', p_ctx, vc))

    out = lax.map(row_block, jnp.arange(rows))
    return jnp.transpose(out, (1, 0, 2, 3, 4)).reshape(bsz, length, heads * hd)


def _attend_dense(q, k, v):
    s = jnp.einsum('bqhd,bkhd->bhqk', q, k).astype(F32) * q.shape[-1] ** -0.5
    p = jax.nn.softmax(s, axis=-1).astype(v.dtype)
    return jnp.einsum('bhqk,bkhd->bqhd', p, v).reshape(q.shape[0], q.shape[1], -1)


def _even_mixer(hc, hl, w_in, w_out, s5_params, rpb, ctx_out):
    pc = hc @ w_in
    pl = hl @ w_in
    s5_c, s5_l = _s5_mixer(pc[..., :S5_WIDTH], pl[..., :S5_WIDTH], *s5_params, ctx_out)
    qc, kc, vc = _split_qkv(pc)
    ql, kl, vl = _split_qkv(pl)
    yl = jnp.concatenate([s5_l, _na_latent(ql, kl, vl, kc, vc, rpb)], axis=-1) @ w_out
    if not ctx_out:
        return None, yl
    yc = jnp.concatenate([s5_c, _attend_dense(qc, kc, vc)], axis=-1) @ w_out
    return yc, yl


def _hyena_filters(length, w_in, b_in, w_mid, b_mid, w_out, freq):
    t = jnp.linspace(0.0, 1.0, length, dtype=F32)[:, None]
    w = 2.0 * math.pi * jnp.arange(length, dtype=F32)[:, None] / length
    f = jnp.linspace(1e-4, HY_BANDS - 1, HY_BANDS, dtype=F32)[None, :]
    z = jnp.concatenate([t, jnp.cos(f * w), -jnp.sin(f * w)], axis=-1)
    freq = freq.astype(F32)
    h = jnp.sin(freq * (z @ w_in.astype(F32) + b_in.astype(F32)))
    for i in range(HY_N_MID):
        h = jnp.sin(freq * (h @ w_mid[i].astype(F32) + b_mid[i].astype(F32)))
    h = (h @ w_out.astype(F32)).reshape(length, HY_ORDER, 2, HY_WIDTH)
    max_decay = math.log(HY_DECAY_TARGET) / HY_FAST_PCT
    min_decay = math.log(HY_DECAY_TARGET) / HY_SLOW_PCT
    deltas = jnp.abs(jnp.linspace(min_decay, max_decay, HY_WIDTH, dtype=F32))
    decay = jnp.exp(-t * deltas[None, :])
    return h * decay[:, None, None, :]


def _bidir_fftconv(u, h_fwd, h_bwd, bias):
    length, ch = h_fwd.shape
    n = 2 * length
    k = jnp.concatenate([h_fwd, jnp.zeros((1, ch), F32), jnp.flip(h_bwd[1:], 0)], axis=0)
    uf = jnp.fft.rfft(u.astype(F32), n=n, axis=1)
    kf = jnp.fft.rfft(k, n=n, axis=0)
    y = jnp.fft.irfft(uf * kf[None], n=n, axis=1)[:, :length]
    return y + u.astype(F32) * bias.astype(F32)


def _hyena(p, short_w, short_b, w_in, b_in, w_mid, b_mid, w_out, freq, fbias):
    x1, x2, v = jnp.split(_dwconv(p, short_w, short_b).astype(F32), 3, axis=-1)
    h = _hyena_filters(p.shape[1], w_in, b_in, w_mid, b_mid, w_out, freq)
    z = x1 * _bidir_fftconv(v, h[:, 0, 0], h[:, 0, 1], fbias[0])
    return x2 * _bidir_fftconv(z, h[:, 1, 0], h[:, 1, 1], fbias[1])


def _ssd_chunked(x, dt, a, b, c, s0, with_output):
    bsz, length = x.shape[:2]
    q = min(SSD_CHUNK, length)
    nc = length // q
    e = SSD_HEADS // SSD_GROUPS
    xdt = (x * dt[..., None]).reshape(bsz, nc, q, SSD_GROUPS, e, SSD_HEAD_DIM)
    cum = jnp.cumsum((dt * a).reshape(bsz, nc, q, SSD_GROUPS, e), axis=2)
    bq = b.reshape(bsz, nc, q, SSD_GROUPS, SSD_STATE)
    xdt_w = xdt * jnp.exp(cum[:, :, -1:] - cum)[..., None]
    states = jnp.einsum('bcsgn,bcsgep->bcgepn', bq, xdt_w)
    chunk_decay = jnp.exp(cum[:, :, -1])
    if s0 is None:
        s0 = jnp.zeros((bsz, SSD_GROUPS, e, SSD_HEAD_DIM, SSD_STATE), F32)

    def step(s, inp):
        st, dec = inp
        return s * dec[..., None, None] + st, s
    s_final, s_prev = lax.scan(step, s0, (jnp.moveaxis(states, 1, 0), jnp.moveaxis(chunk_decay, 1, 0)))
    if not with_output:
        return None, s_final
    cq = c.reshape(bsz, nc, q, SSD_GROUPS, SSD_STATE)
    seg = cum[:, :, :, None] - cum[:, :, None, :]
    tri = jnp.tril(jnp.ones((q, q), bool))[None, None, :, :, None, None]
    lmat = jnp.exp(jnp.where(tri, seg, -jnp.inf))
    cb = jnp.einsum('bclgn,bcsgn->bclsg', cq, bq)
    y_diag = jnp.einsum('bclsge,bcsgep->bclgep', cb[..., None] * lmat, xdt)
    y_off = jnp.einsum('bclgn,cbgepn->bclgep', cq, s_prev) * jnp.exp(cum)[..., None]
    return (y_diag + y_off).reshape(bsz, length, SSD_HEADS, SSD_HEAD_DIM), s_final


def _ssd_bidir(xbc, dt_raw, dt_bias, a_log, s0, with_output):
    bsz, length = xbc.shape[:2]
    xbc = xbc.astype(F32)
    xs = xbc[..., :SSD_INNER].reshape(bsz, length, SSD_HEADS, SSD_HEAD_DIM)
    bm = xbc[..., SSD_INNER:SSD_INNER + SSD_BC].reshape(bsz, length, SSD_GROUPS, SSD_STATE)
    cm = xbc[..., SSD_INNER + SSD_BC:].reshape(bsz, length, SSD_GROUPS, SSD_STATE)
    ys, finals = [], []
    for k in range(2):
        dt = jax.nn.softplus(dt_raw[..., k * SSD_HEADS:(k + 1) * SSD_HEADS].astype(F32) + dt_bias[k].astype(F32))
        a = -jnp.exp(a_log[k].astype(F32))
        seq = (xs, dt, bm, cm)
        if k == 1:
            seq = tuple(jnp.flip(s, 1) for s in seq)
        y, fin = _ssd_chunked(seq[0], seq[1], a, seq[2], seq[3], None if s0 is None else s0[k], with_output)
        if with_output:
            ys.append(jnp.flip(y, 1) if k == 1 else y)
        finals.append(fin)
    return ys, xs, finals


def _odd_mixer(hc, hl, w_in, w_out, hy_params, conv_w, conv_b, dt_bias, a_log, d_skip, norm_g, ctx_out):
    o_z = 3 * HY_WIDTH
    o_xbc = o_z + SSD_INNER
    o_dt = o_xbc + SSD_XBC
    pc = hc @ w_in
    pl = hl @ w_in
    xbc_c = jax.nn.silu(_dwconv(pc[..., o_xbc:o_dt], conv_w, conv_b))
    xbc_l = jax.nn.silu(_dwconv(pl[..., o_xbc:o_dt], conv_w, conv_b))
    ys_c, xs_c, fin_c = _ssd_bidir(xbc_c, pc[..., o_dt:], dt_bias, a_log, None, ctx_out)
    ys_l, xs_l, _ = _ssd_bidir(xbc_l, pl[..., o_dt:], dt_bias, a_log, fin_c, True)

    def ssd_out(p, ys, xs):
        y = ys[0] + ys[1] + d_skip.astype(F32)[:, None] * xs
        y = y.reshape(xs.shape[0], xs.shape[1], SSD_INNER) * jax.nn.silu(p[..., o_z:o_xbc].astype(F32))
        return _rmsnorm(y, norm_g)

    yl = jnp.concatenate([_hyena(pl[..., :o_z], *hy_params), ssd_out(pl, ys_l, xs_l)], axis=-1) @ w_out
    if not ctx_out:
        return None, yl
    yc = jnp.concatenate([_hyena(pc[..., :o_z], *hy_params), ssd_out(pc, ys_c, xs_c)], axis=-1) @ w_out
    return yc, yl


def setup_inputs(seed: int = 0) -> dict:
    key = jax.random.key(seed)
    keys = jax.random.split(key, 48)
    it = iter(range(48))

    def nrm(shape, std):
        return std * jax.random.normal(keys[next(it)], shape, F32)

    def unif(shape, lo, hi):
        return jax.random.uniform(keys[next(it)], shape, F32, lo, hi)

    ne, no = (DEPTH + 1) // 2, DEPTH // 2
    dt0 = jnp.exp(unif((no, 2, SSD_HEADS), math.log(1e-3), math.log(1e-1)))
    return {
        'x': nrm((BATCH, SEQ, D_MODEL), 1.0),
        'c': nrm((BATCH, D_MODEL), 1.0),
        'ctx': nrm((BATCH, CTX_LEN, D_MODEL), 1.0),
        'c_ctx': nrm((D_MODEL,), 1.0),
        'mod_w': nrm((DEPTH, D_MODEL, N_MOD * D_MODEL), 0.5 * D_MODEL ** -0.5),
        'mod_b': nrm((DEPTH, N_MOD * D_MODEL), 0.02),
        'norm_g': 1.0 + nrm((DEPTH, N_SUB, D_MODEL), 0.02),
        'ffn_wg': nrm((DEPTH, 2, D_MODEL, D_FF), D_MODEL ** -0.5),
        'ffn_wu': nrm((DEPTH, 2, D_MODEL, D_FF), D_MODEL ** -0.5),
        'ffn_wd': nrm((DEPTH, 2, D_FF, D_MODEL), D_FF ** -0.5),
        'final_g': 1.0 + nrm((D_MODEL,), 0.02),
        'ev_w_in': nrm((ne, D_MODEL, EVEN_IN), D_MODEL ** -0.5),
        'ev_w_out': nrm((ne, MIX_WIDTH, D_MODEL), MIX_WIDTH ** -0.5),
        's5_a_re': -0.5 + nrm((ne, 2, S5_GROUPS, S5_STATE), 0.01),
        's5_a_im': math.pi * jnp.arange(S5_STATE, dtype=F32) + nrm((ne, 2, S5_GROUPS, S5_STATE), 0.01),
        's5_log_dt': unif((ne, 2, S5_GROUPS), math.log(1e-3), math.log(1e-1)),
        's5_b_re': nrm((ne, 2, S5_GROUPS, S5_STATE, S5_GROUP), (2 * S5_GROUP) ** -0.5),
        's5_b_im': nrm((ne, 2, S5_GROUPS, S5_STATE, S5_GROUP), (2 * S5_GROUP) ** -0.5),
        's5_c_re': nrm((ne, 2, S5_GROUPS, S5_GROUP, S5_STATE), S5_STATE ** -0.5),
        's5_c_im': nrm((ne, 2, S5_GROUPS, S5_GROUP, S5_STATE), S5_STATE ** -0.5),
        's5_d': nrm((ne, S5_GROUPS, S5_GROUP), 1.0),
        's5_glu_w': nrm((ne, S5_WIDTH, S5_WIDTH), S5_WIDTH ** -0.5),
        's5_glu_b': nrm((ne, S5_WIDTH), 0.02),
        'na_rpb': nrm((ne, NA_HEADS, 2 * NA_KH_MAX - 1, 2 * NA_KW - 1), 0.1),
        'od_w_in': nrm((no, D_MODEL, ODD_IN), D_MODEL ** -0.5),
        'od_w_out': nrm((no, MIX_WIDTH, D_MODEL), MIX_WIDTH ** -0.5),
        'hy_short_w': nrm((no, HY_SHORT, 3 * HY_WIDTH), HY_SHORT ** -0.5),
        'hy_short_b': nrm((no, 3 * HY_WIDTH), 0.02),
        'hy_w_in': nrm((no, HY_EMB, HY_HIDDEN), HY_EMB ** -0.5),
        'hy_b_in': nrm((no, HY_HIDDEN), 0.1),
        'hy_w_mid': nrm((no, HY_N_MID, HY_HIDDEN, HY_HIDDEN), HY_HIDDEN ** -0.5),
        'hy_b_mid': nrm((no, HY_N_MID, HY_HIDDEN), 0.1),
        'hy_w_out': nrm((no, HY_HIDDEN, HY_ORDER * 2 * HY_WIDTH), HY_FILTER_STD),
        'hy_freq': 1.0 + nrm((no, HY_HIDDEN), 0.02),
        'hy_fbias': nrm((no, HY_ORDER, HY_WIDTH), 0.5),
        'ssd_conv_w': nrm((no, SSD_CONV, SSD_XBC), SSD_CONV ** -0.5),
        'ssd_conv_b': nrm((no, SSD_XBC), 0.02),
        'ssd_dt_bias': dt0 + jnp.log(-jnp.expm1(-dt0)),
        'ssd_a_log': jnp.log(unif((no, 2, SSD_HEADS), 1.0, 16.0)),
        'ssd_d': 1.0 + nrm((no, SSD_HEADS), 0.1),
        'ssd_norm_g': 1.0 + nrm((no, SSD_INNER), 0.02),
    }


def reference(x, c, ctx, c_ctx, mod_w, mod_b, norm_g, ffn_wg, ffn_wu, ffn_wd, final_g,
              ev_w_in, ev_w_out, s5_a_re, s5_a_im, s5_log_dt, s5_b_re, s5_b_im, s5_c_re, s5_c_im,
              s5_d, s5_glu_w, s5_glu_b, na_rpb,
              od_w_in, od_w_out, hy_short_w, hy_short_b, hy_w_in, hy_b_in, hy_w_mid, hy_b_mid,
              hy_w_out, hy_freq, hy_fbias,
              ssd_conv_w, ssd_conv_b, ssd_dt_bias, ssd_a_log, ssd_d, ssd_norm_g):
    bsz = c.shape[0]
    xl, xc = x, ctx
    for layer in range(DEPTH):
        last = layer == DEPTH - 1
        j = layer // 2
        ml = (jax.nn.silu(c) @ mod_w[layer] + mod_b[layer]).reshape(bsz, N_MOD, 1, D_MODEL)
        mc = (jax.nn.silu(c_ctx)[None] @ mod_w[layer] + mod_b[layer]).reshape(1, N_MOD, 1, D_MODEL)
        ffn_a = (ffn_wg[layer, 0], ffn_wu[layer, 0], ffn_wd[layer, 0])
        ffn_b = (ffn_wg[layer, 1], ffn_wu[layer, 1], ffn_wd[layer, 1])
        xl = xl + 0.5 * ml[:, 2] * _swiglu(_pre(xl, norm_g[layer, 0], ml, 0), *ffn_a)
        xc = xc + 0.5 * mc[:, 2] * _swiglu(_pre(xc, norm_g[layer, 0], mc, 0), *ffn_a)
        hl = _pre(xl, norm_g[layer, 1], ml, 1)
        hc = _pre(xc, norm_g[layer, 1], mc, 1)
        if layer % 2 == 0:
            s5_params = (s5_a_re[j], s5_a_im[j], s5_log_dt[j], s5_b_re[j], s5_b_im[j],
                         s5_c_re[j], s5_c_im[j], s5_d[j], s5_glu_w[j], s5_glu_b[j])
            yc, yl = _even_mixer(hc, hl, ev_w_in[j], ev_w_out[j], s5_params, na_rpb[j], not last)
        else:
            hy_params = (hy_short_w[j], hy_short_b[j], hy_w_in[j], hy_b_in[j], hy_w_mid[j],
                         hy_b_mid[j], hy_w_out[j], hy_freq[j], hy_fbias[j])
            yc, yl = _odd_mixer(hc, hl, od_w_in[j], od_w_out[j], hy_params, ssd_conv_w[j], ssd_conv_b[j],
                                ssd_dt_bias[j], ssd_a_log[j], ssd_d[j], ssd_norm_g[j], not last)
        xl = xl + ml[:, 5] * yl
        xl = xl + 0.5 * ml[:, 8] * _swiglu(_pre(xl, norm_g[layer, 2], ml, 2), *ffn_b)
        if not last:
            xc = xc + mc[:, 5] * yc
            xc = xc + 0.5 * mc[:, 8] * _swiglu(_pre(xc, norm_g[layer, 2], mc, 2), *ffn_b)
    return _rmsnorm(xl, final_g)
```

```python
import math
import numpy as np
import ml_dtypes
import concourse.bass as bass
import concourse.mybir as mybir
from concourse.bass_utils import run_bass_kernel_spmd

F32 = mybir.dt.float32
BF16 = mybir.dt.bfloat16
ALU = mybir.AluOpType
AF = mybir.ActivationFunctionType
AX = mybir.AxisListType

PE, DVE, ACT, POOL, SP = 0, 1, 2, 3, 4
NENG = 5
NDSEM = 12

D = 2048
T = 2304
LC = 256
L = 2048
DFF = 5632
NFT = DFF // 128
EPS = 1e-6
TB = [(0, 256), (256, 512), (768, 512), (1280, 512), (1792, 512)]
ARENA_BYTES = 196608
MAGIC = 12582912.0


class Reg:
    __slots__ = ('w', 'rs', 'excl')

    def __init__(self, excl=False):
        self.w = None
        self.rs = []
        self.excl = excl


class Op:
    __slots__ = ('eng', 'fn', 'deps', 'signal', 'sig', 'clock', 'dma', 'dsem', 'dval', 'waits')

    def __init__(self, eng, fn, dma):
        self.eng = eng
        self.fn = fn
        self.dma = dma
        self.deps = []
        self.signal = False
        self.sig = 0
        self.clock = None
        self.dsem = -1
        self.dval = 0
        self.waits = None


class Sched:
    def __init__(self, nc):
        self.nc = nc
        self.engs = [nc.tensor, nc.vector, nc.scalar, nc.gpsimd, nc.sync]
        self.esem = [nc.alloc_semaphore('es%d' % i) for i in range(NENG)]
        self.dsem = [[nc.alloc_semaphore('ds%d_%d' % (q, i)) for i in range(NDSEM)] for q in range(NENG)]
        self.ncomp = NENG + NENG * NDSEM
        self.pending = []
        self.sigcnt = [0] * NENG
        self.dcnt = [0] * NENG
        self.dlast = [[None] * NDSEM for _ in range(NENG)]
        self.clock = [[0] * self.ncomp for _ in range(NENG)]
        self.nops = 0
        self.nwaits = 0

    def op(self, eng, fn, reads=(), writes=(), dma=False):
        o = Op(eng, fn, dma)
        deps = o.deps
        for r in reads:
            if r.w is not None:
                deps.append((r.w, True))
            if r.excl:
                for x in r.rs:
                    if x.eng != eng:
                        deps.append((x, True))
            if dma:
                r.rs.append(o)
            else:
                rs = r.rs
                for i in range(len(rs)):
                    if (not rs[i].dma) and rs[i].eng == eng:
                        rs[i] = o
                        break
                else:
                    rs.append(o)
        for r in writes:
            if r.w is not None:
                deps.append((r.w, False))
            for x in r.rs:
                if x is not o:
                    deps.append((x, False))
            r.w = o
            r.rs = []
        self.pending.append(o)
        return o

    def flush(self, barrier=True):
        ops = self.pending
        self.pending = []
        for o in ops:
            for (d, raw) in o.deps:
                if d.dma:
                    continue
                if d.eng == o.eng and not o.dma and d.eng == PE:
                    continue
                d.signal = True
        if barrier:
            last = {}
            for o in ops:
                if not o.dma:
                    last[o.eng] = o
            for o in last.values():
                o.signal = True
        ncomp = self.ncomp
        for o in ops:
            e = o.eng
            ck = self.clock[e]
            waits = []
            if o.dma:
                i = self.dcnt[e]
                self.dcnt[e] += 1
                slot = i % NDSEM
                prev = self.dlast[e][slot]
                if prev is not None:
                    o.deps.append((prev, True))
                o.dsem = slot
                o.dval = 16 * (i // NDSEM + 1)
                self.dlast[e][slot] = o
            for (d, raw) in o.deps:
                if d.dma:
                    comp = NENG + d.eng * NDSEM + d.dsem
                    val = d.dval
                    sem = self.dsem[d.eng][d.dsem]
                else:
                    if d.eng == e and not o.dma and e == PE:
                        continue
                    if not d.signal:
                        continue
                    comp = d.eng
                    val = d.sig
                    sem = self.esem[d.eng]
                if ck[comp] >= val:
                    continue
                waits.append((sem, val))
                dc = d.clock
                if dc is not None:
                    for k in range(ncomp):
                        if dc[k] > ck[k]:
                            ck[k] = dc[k]
                if ck[comp] < val:
                    ck[comp] = val
            o.waits = waits
            if o.dma:
                o.clock = list(ck)
            elif o.signal:
                self.sigcnt[e] += 1
                o.sig = self.sigcnt[e]
                o.clock = list(ck)
            o.deps = None
        for o in ops:
            eng = self.engs[o.eng]
            for (sem, val) in o.waits:
                eng.wait_ge(sem, val)
                self.nwaits += 1
            ins = o.fn(eng)
            self.nops += 1
            if o.dma:
                ins.then_inc(self.dsem[o.eng][o.dsem], 16)
            elif o.signal:
                ins.then_inc(self.esem[o.eng], 1)
            o.fn = None
            o.waits = None
        if barrier:
            self.barrier()

    def barrier(self):
        for e in range(NENG):
            eng = self.engs[e]
            ck = self.clock[e]
            for f in range(NENG):
                if ck[f] < self.sigcnt[f]:
                    eng.wait_ge(self.esem[f], self.sigcnt[f])
                    ck[f] = self.sigcnt[f]
            for q in range(NENG):
                for s in range(NDSEM):
                    d = self.dlast[q][s]
                    if d is not None:
                        comp = NENG + q * NDSEM + s
                        if ck[comp] < d.dval:
                            eng.wait_ge(self.dsem[q][s], d.dval)
                            ck[comp] = d.dval


class Tl:
    __slots__ = ('ap', 'r')

    def __init__(self, ap, r=None):
        self.ap = ap
        self.r = r if r is not None else Reg()

    def __getitem__(self, k):
        return self.ap[k]


class Ctx:
    def __init__(self, dbg_in=(), dbg_out=()):
        self.nc = nc = bass.Bass("TRN2", target_bir_lowering=False)
        self.S = Sched(nc)
        self.dbg_in = set(dbg_in)
        self.dbg_out = set(dbg_out)
        self.arena = nc.alloc_sbuf_tensor('arena', [128, ARENA_BYTES // 4], F32)
        self.aoff = 0
        self.psum = [Tl(nc.alloc_psum_tensor('ps%d' % i, [128, 512], F32)[:], Reg(excl=True)) for i in range(8)]
        self.inputs = {}
        self.outputs = {}
        self.n_p = 0

    def dram(self, name, shape, dtype=F32, kind=None):
        if kind is None:
            kind = 'Internal'
            if name in self.dbg_in:
                kind = 'ExternalInput'
            elif name in self.dbg_out:
                kind = 'ExternalOutput'
        t = self.nc.dram_tensor(name, list(shape), dtype, kind=kind).ap()
        if kind == 'ExternalInput':
            self.inputs[name] = (tuple(shape), dtype)
        elif kind == 'ExternalOutput':
            self.outputs[name] = (tuple(shape), dtype)
        return t

    def sb(self, free_shape, dtype=F32):
        esz = 4 if dtype == F32 else 2
        n = 1
        for v in free_shape:
            n *= v
        nbytes = (n * esz + 31) // 32 * 32
        assert self.aoff + nbytes <= ARENA_BYTES, ('arena overflow', self.aoff, nbytes)
        a = self.arena[:, self.aoff // 4:(self.aoff + nbytes) // 4]
        self.aoff += nbytes
        if dtype != F32:
            a = a.bitcast(dtype)
        a = a[:, 0:n]
        if len(free_shape) == 2:
            a = a.rearrange('p (a b) -> p a b', b=free_shape[1])
        elif len(free_shape) == 3:
            a = a.rearrange('p (a b c) -> p a b c', b=free_shape[1], c=free_shape[2])
        return Tl(a)

    def persist(self, free_shape, dtype=F32):
        self.n_p += 1
        t = self.nc.alloc_sbuf_tensor('pp%d' % self.n_p, [128] + list(free_shape), dtype)
        return Tl(t[:])

    def stage_end(self):
        self.S.flush(barrier=True)
        self.aoff = 0

    def dma(self, out, in_, R, W, q=SP):
        self.S.op(q, lambda e: e.dma_start(out=out, in_=in_), reads=R, writes=W, dma=True)

    def mm(self, out, lhsT, rhs, start, stop, R, W):
        self.S.op(PE, lambda e: e.matmul(out, lhsT=lhsT, rhs=rhs, start=start, stop=stop), reads=R, writes=W)

    def act(self, out, in_, func, R, W, scale=1.0, bias=0.0, accum_out=None):
        if accum_out is None:
            self.S.op(ACT, lambda e: e.activation(out=out, in_=in_, func=func, bias=bias, scale=scale), reads=R, writes=W)
        else:
            self.S.op(ACT, lambda e: e.activation(out=out, in_=in_, func=func, bias=bias, scale=scale, accum_out=accum_out), reads=R, writes=W)

    def tt(self, out, in0, in1, op, R, W, eng=DVE):
        self.S.op(eng, lambda e: e.tensor_tensor(out=out, in0=in0, in1=in1, op=op), reads=R, writes=W)

    def ts(self, out, in0, s1, s2, op0, op1, R, W, eng=DVE):
        if s2 is None:
            self.S.op(eng, lambda e: e.tensor_scalar(out=out, in0=in0, scalar1=s1, scalar2=None, op0=op0), reads=R, writes=W)
        else:
            self.S.op(eng, lambda e: e.tensor_scalar(out=out, in0=in0, scalar1=s1, scalar2=s2, op0=op0, op1=op1), reads=R, writes=W)

    def stt(self, out, in0, scalar, in1, op0, op1, R, W, eng=DVE):
        self.S.op(eng, lambda e: e.scalar_tensor_tensor(out=out, in0=in0, scalar=scalar, in1=in1, op0=op0, op1=op1), reads=R, writes=W)

    def copy(self, out, in_, R, W, eng=DVE):
        self.S.op(eng, lambda e: e.tensor_copy(out=out, in_=in_), reads=R, writes=W)

    def memset(self, out, val, W, eng=DVE):
        self.S.op(eng, lambda e: e.memset(out, val), writes=W)

    def recip(self, out, in_, R, W):
        self.S.op(DVE, lambda e: e.reciprocal(out=out, in_=in_), reads=R, writes=W)

    def transpose(self, out, in_, ident, R, W):
        self.S.op(PE, lambda e: e.transpose(out, in_, ident), reads=R, writes=W)


def st_consts(c):
    c.ones_bf = c.persist([128], BF16)
    c.memset(c.ones_bf.ap, 1.0, [c.ones_bf.r])
    c.ident_f = c.persist([128], F32)
    c.memset(c.ident_f.ap, 0.0, [c.ident_f.r], eng=POOL)
    idf = c.ident_f
    c.S.op(POOL, lambda e: e.affine_select(out=idf.ap, in_=idf.ap, pattern=[[-1, 128]], compare_op=ALU.not_equal,
                                           fill=1.0, base=0, channel_multiplier=1), reads=[idf.r], writes=[idf.r])
    c.ident_b = c.persist([128], BF16)
    c.copy(c.ident_b.ap, c.ident_f.ap, [c.ident_f.r], [c.ident_b.r])
    c.eps_t = c.persist([1], F32)
    c.memset(c.eps_t.ap, EPS, [c.eps_t.r])


def st_mod(c, layer):
    M = c.persist([2, 9, 16], F32)
    A = c.persist([2, 3, 16], F32)
    G = c.persist([2, 3, 16], F32)
    c.M[layer], c.A[layer], c.G[layer] = M, A, G
    sc = c.sb([16, 2], F32)
    sg = c.sb([16, 2], F32)
    c.dma(sc.ap, c.d_cc, [], [sc.r])
    c.act(sg.ap, sc.ap, AF.Sigmoid, [sc.r], [sg.r])
    c.tt(sc.ap, sc.ap, sg.ap, ALU.mult, [sc.r, sg.r], [sc.r])
    mb = c.sb([144], F32)
    c.dma(mb.ap, c.d_modb[layer], [], [mb.r])
    ng = c.sb([3, 16], F32)
    c.dma(ng.ap, c.d_normg[layer], [], [ng.r])
    NB = 3
    wbuf = [c.sb([16, 512], F32) for _ in range(NB)]
    ps = c.psum[0]
    wsrc = c.d_modw[layer].rearrange('(kt p) f -> p kt f', p=128)
    for blk in range(36):
        wb = wbuf[blk % NB]
        c.dma(wb.ap, wsrc[:, :, blk * 512:(blk + 1) * 512], [], [wb.r])
        for j in range(4):
            ft = blk * 4 + j
            for kt in range(16):
                c.mm(ps.ap[:, 2 * ft:2 * ft + 2], wb.ap[:, kt, j * 128:(j + 1) * 128], sc.ap[:, kt, :],
                     kt == 0, kt == 15, [wb.r, sc.r], [ps.r])
    for g in range(2):
        src = ps.ap[:, 0:288].rearrange('p (f g) -> p g f', g=2)[:, g, :]
        c.tt(M.ap[:, g].rearrange('p i d -> p (i d)'), src, mb.ap, ALU.add, [ps.r, mb.r], [M.r])
    for g in range(2):
        for i in range(3):
            c.stt(A.ap[:, g, i, :], M.ap[:, g, 3 * i + 1, :], 1.0, ng.ap[:, i, :], ALU.add, ALU.mult, [M.r, ng.r], [A.r])
            fac = 1.0 if i == 1 else 0.5
            c.ts(G.ap[:, g, i, :], M.ap[:, g, 3 * i + 2, :], fac, None, ALU.mult, None, [M.r], [G.r])
    c.stage_end()


def bidx(t0):
    return [b[0] for b in TB].index(t0)


def xr_col(c, t0):
    return [c.XRr[dt][bidx(t0)] for dt in range(16)]


def st_norm(c, layer, i, blocks=TB):
    H = c.sb([16, T], BF16)
    mark = c.aoff
    A, M = c.A[layer], c.M[layer]
    xt = [c.sb([16, 512], F32) for _ in range(2)]
    xrg = [[Reg() for _ in range(16)] for _ in range(2)]
    sq = [c.sb([16, 512], BF16) for _ in range(2)]
    rs = [c.sb([512], F32) for _ in range(2)]
    for bi, (t0, n) in enumerate(blocks):
        g = 1 if t0 < LC else 0
        x_, s_, r_ = xt[bi % 2], sq[bi % 2], rs[bi % 2]
        xr_ = xrg[bi % 2]
        ps = c.psum[6 + bi % 2]
        c.dma(x_.ap[:, :, 0:n], c.XR[:, :, t0:t0 + n].rearrange('d p t -> p d t'), xr_col(c, t0), xr_)
        c.act(s_.ap[:, :, 0:n], x_.ap[:, :, 0:n], AF.Square, xr_, [s_.r])
        for dt in range(16):
            c.mm(ps.ap[:, 0:n], c.ones_bf.ap, s_.ap[:, dt, 0:n], dt == 0, dt == 15, [s_.r, c.ones_bf.r], [ps.r])
        c.act(r_.ap[:, 0:n], ps.ap[:, 0:n], AF.Sqrt, [ps.r, c.eps_t.r], [r_.r], scale=1.0 / D, bias=c.eps_t.ap[:, 0:1])
        c.recip(r_.ap[:, 0:n], r_.ap[:, 0:n], [r_.r], [r_.r])
        for dt in range(16):
            c.stt(x_.ap[:, dt, 0:n], x_.ap[:, dt, 0:n], A.ap[:, g, i, dt:dt + 1], r_.ap[:, 0:n], ALU.mult, ALU.mult,
                  [xr_[dt], A.r, r_.r], [xr_[dt]])
            c.act(H.ap[:, dt, t0:t0 + n], x_.ap[:, dt, 0:n], AF.Identity, [xr_[dt], M.r], [H.r],
                  bias=M.ap[:, g, 3 * i, dt:dt + 1])
    c.S.flush(barrier=True)
    c.aoff = mark
    return H


FCH = [6, 6, 6, 6, 5, 5, 5, 5]


def st_ffn(c, layer, j, i, blocks=TB):
    H = st_norm(c, layer, i, blocks)
    G = c.G[layer]
    wg_src = c.d_wg[layer, j]
    wu_src = c.d_wu[layer, j]
    wd_src = c.d_wd[layer, j].rearrange('(ft p) d -> ft p d', p=128)
    NW = 3
    wgb = [c.sb([16, 128], BF16) for _ in range(NW)]
    wub = [c.sb([16, 128], BF16) for _ in range(NW)]
    wdb = [c.sb([2048], BF16) for _ in range(12)]
    hid = [c.sb([T], BF16) for _ in range(6)]
    sgt = [c.sb([512], F32) for _ in range(2)]
    xt = [c.sb([512], F32) for _ in range(4)]
    f0 = 0
    nup = 0
    nx = 0
    nwd = 0
    nfc = 0
    for ch, nf in enumerate(FCH):
        wd_tiles = []
        for k in range(nf):
            f = f0 + k
            wg_, wu_ = wgb[nfc % NW], wub[nfc % NW]
            nfc += 1
            wd_ = wdb[nwd % 12]
            nwd += 1
            c.dma(wg_.ap, wg_src[f], [], [wg_.r], q=POOL)
            c.dma(wu_.ap, wu_src[f], [], [wu_.r], q=POOL)
            c.dma(wd_.ap, wd_src[f], [], [wd_.r], q=POOL)
            wd_tiles.append(wd_)
            hk = hid[k]
            for bi, (t0, n) in enumerate(blocks):
                pg, pu = c.psum[2 * (nup % 2)], c.psum[2 * (nup % 2) + 1]
                for kt in range(16):
                    c.mm(pg.ap[:, 0:n], wg_.ap[:, kt, :], H.ap[:, kt, t0:t0 + n], kt == 0, kt == 15, [wg_.r, H.r], [pg.r])
                for kt in range(16):
                    c.mm(pu.ap[:, 0:n], wu_.ap[:, kt, :], H.ap[:, kt, t0:t0 + n], kt == 0, kt == 15, [wu_.r, H.r], [pu.r])
                s_ = sgt[nup % 2]
                c.act(s_.ap[:, 0:n], pg.ap[:, 0:n], AF.Silu, [pg.r], [s_.r])
                c.tt(hk.ap[:, t0:t0 + n], s_.ap[:, 0:n], pu.ap[:, 0:n], ALU.mult, [s_.r, pu.r], [hk.r])
                nup += 1
        for dt in range(16):
            for bi, (t0, n) in enumerate(blocks):
                g = 1 if t0 < LC else 0
                po = c.psum[4 + nx % 2]
                x_ = xt[nx % 4]
                xreg = c.XRr[dt][bidx(t0)]
                c.dma(x_.ap[:, 0:n], c.XR[dt, :, t0:t0 + n], [xreg], [x_.r])
                for k in range(nf):
                    c.mm(po.ap[:, 0:n], wd_tiles[k].ap[:, dt * 128:(dt + 1) * 128], hid[k].ap[:, t0:t0 + n],
                         k == 0, k == nf - 1, [wd_tiles[k].r, hid[k].r], [po.r])
                c.stt(x_.ap[:, 0:n], po.ap[:, 0:n], G.ap[:, g, i, dt:dt + 1], x_.ap[:, 0:n], ALU.mult, ALU.add,
                      [po.r, G.r, x_.r], [x_.r])
                c.dma(c.XR[dt, :, t0:t0 + n], x_.ap[:, 0:n], [x_.r], [xreg])
                nx += 1
        f0 += nf
    c.stage_end()


def st_load_x(c):
    for dt in range(16):
        c.dma(c.XR[dt], c.d_xin[dt], [], c.XRr[dt])
    c.stage_end()


def st_final(c):
    fg = c.sb([16], F32)
    c.dma(fg.ap, c.d_finalg, [], [fg.r])
    xt = [c.sb([16, 512], F32) for _ in range(2)]
    sq = [c.sb([16, 512], BF16) for _ in range(2)]
    rs = [c.sb([512], F32) for _ in range(2)]
    for bi, (t0, n) in enumerate(TB[1:]):
        x_, s_, r_ = xt[bi % 2], sq[bi % 2], rs[bi % 2]
        ps = c.psum[bi % 2]
        c.dma(x_.ap, c.XR[:, :, t0:t0 + n].rearrange('d p t -> p d t'), xr_col(c, t0), [x_.r])
        c.act(s_.ap, x_.ap, AF.Square, [x_.r], [s_.r])
        for dt in range(16):
            c.mm(ps.ap, c.ones_bf.ap, s_.ap[:, dt, :], dt == 0, dt == 15, [s_.r, c.ones_bf.r], [ps.r])
        c.act(r_.ap, ps.ap, AF.Sqrt, [ps.r, c.eps_t.r], [r_.r], scale=1.0 / D, bias=c.eps_t.ap[:, 0:1])
        c.recip(r_.ap, r_.ap, [r_.r], [r_.r])
        for dt in range(16):
            c.stt(x_.ap[:, dt, :], x_.ap[:, dt, :], fg.ap[:, dt:dt + 1], r_.ap, ALU.mult, ALU.mult, [x_.r, fg.r, r_.r], [x_.r])
        c.dma(c.d_out[:, :, t0 - LC:t0 - LC + n].rearrange('d p t -> p d t'), x_.ap, [x_.r], [c.outr])
    c.stage_end()


def declare_io(c):
    c.d_xin = c.dram('xin', [16, 128, T], F32, 'ExternalInput')
    c.d_cc = c.dram('cc', [128, 16, 2], F32, 'ExternalInput')
    c.d_modw = c.dram('mod_w', [2, D, 9 * D], F32, 'ExternalInput')
    c.d_modb = c.dram('mod_b', [2, 128, 144], F32, 'ExternalInput')
    c.d_normg = c.dram('norm_g', [2, 128, 3, 16], F32, 'ExternalInput')
    c.d_finalg = c.dram('final_g', [128, 16], F32, 'ExternalInput')
    c.d_wg = c.dram('ffn_wg', [2, 2, NFT, 128, 16, 128], F32, 'ExternalInput')
    c.d_wu = c.dram('ffn_wu', [2, 2, NFT, 128, 16, 128], F32, 'ExternalInput')
    c.d_wd = c.dram('ffn_wd', [2, 2, DFF, D], F32, 'ExternalInput')
    EI = 'ExternalInput'
    c.d_evwin_t = c.dram('ev_w_in_t', [24, 128, 16, 128], F32, EI)
    c.d_evwin = c.dram('ev_w_in', [D, 4096], F32, EI)
    c.d_rpbt = c.dram('rpbt', [64, 8, 15, 64], F32, EI)
    c.d_namask = c.dram('namask', [64, 64], F32, EI)
    c.d_s5are = c.dram('s5are', [128, 64], F32, EI)
    c.d_s5aim = c.dram('s5aim', [128, 64], F32, EI)
    c.d_s5ldt = c.dram('s5ldt', [128, 64], F32, EI)
    c.d_s5d = c.dram('s5d', [128, 32], F32, EI)
    c.d_iota1 = c.dram('iota1', [128, 512], F32, EI)
    c.d_s5bre = c.dram('s5bre', [128, 64, 32], F32, EI)
    c.d_s5bim = c.dram('s5bim', [128, 64, 32], F32, EI)
    c.d_s5cre = c.dram('s5cre', [128, 64, 32], F32, EI)
    c.d_s5cim = c.dram('s5cim', [128, 64, 32], F32, EI)
    c.d_gluw = c.dram('glu_w', [1024, 1024], F32, EI)
    c.d_glub = c.dram('glu_b', [128, 8], F32, EI)
    c.d_wout = [c.dram('ev_w_out', [D, D], F32, EI), c.dram('od_w_out', [D, D], F32, EI)]
    c.d_odwin_t = c.dram('od_w_in_t', [48, 128, 16, 128], F32, EI)
    c.d_odwdt = c.dram('od_w_dt', [128, 2, 16, 16], F32, EI)
    c.d_hysw = c.dram('hysw', [128, 24, 3], F32, EI)
    c.d_hysb = c.dram('hysb', [128, 24], F32, EI)
    c.d_ssdcw = c.dram('ssdcw', [128, 16, 3], F32, EI)
    c.d_ssdcb = c.dram('ssdcb', [128, 16], F32, EI)
    c.d_dtbias = c.dram('dtbias', [16, 2], F32, EI)
    c.d_alog = c.dram('alog', [16, 2], F32, EI)
    c.d_ssdd = c.dram('ssdd', [128, 16], F32, EI)
    c.d_tri = c.dram('tri', [2, 128, 128], F32, EI)
    c.d_ssdng = c.dram('ssdng', [128, 8], F32, EI)
    c.d_hyz = c.dram('hyz', [33, L], F32, EI)
    c.d_hywin = c.dram('hywin', [33, 64], F32, EI)
    c.d_hywmid = c.dram('hywmid', [64, 2, 64], F32, EI)
    c.d_hyfb = c.dram('hyfb', [64, 4], F32, EI)
    c.d_hywout = c.dram('hywout', [64, 4096], F32, EI)
    c.d_decay = c.dram('decay', [L, 1024], F32, EI)
    c.d_hyfbias = c.dram('hyfbias', [128, 2, 1024], F32, EI)
    c.d_Cf = c.dram('Cf', [16, 128, 16, 128], BF16, EI)
    c.d_Sf = c.dram('Sf', [16, 128, 16, 128], BF16, EI)
    c.d_Ci = c.dram('Ci', [16, 128, 16, 128], BF16, EI)
    c.d_Si = c.dram('Si', [16, 128, 16, 128], BF16, EI)
    c.HYd = c.dram('HYd', [24, 128, L], BF16)
    c.Zd = c.dram('Zd', [8, 128, L], BF16)
    c.XBCd = c.dram('XBCd', [16, 128, T], BF16)
    c.DTd = c.dram('DTd', [2, 16, T], F32)
    c.CUMd = c.dram('CUMd', [2, 16, T], F32)
    c.YSd = c.dram('YSd', [16, 64, L], F32)
    c.HYr, c.Zr, c.XBCr, c.DTr, c.CUMr, c.YSr = Reg(), Reg(), Reg(), Reg(), Reg(), Reg()
    c.Ud = c.dram('Ud', [8, 128, T], BF16)
    c.Qd = c.dram('Qd', [8, 128, T], BF16)
    c.Kd = c.dram('Kd', [8, 128, T], BF16)
    c.Vd = c.dram('Vd', [18, 128, 1024], BF16)
    c.Yd = c.dram('Yd', [1024, T], F32)
    c.CATd = c.dram('CATd', [16, 128, T], BF16)
    c.Ur, c.Qr, c.Kr, c.Vr, c.Yr = Reg(), Reg(), Reg(), Reg(), Reg()
    c.CATr = [Reg(), Reg()]
    c.d_out = c.dram('out', [16, 128, L], F32, 'ExternalOutput')
    c.outr = Reg()
    c.XR = c.dram('XR', [16, 128, T], F32)
    c.XRr = [[Reg() for _ in range(len(TB))] for _ in range(16)]
    c.M, c.A, c.G = {}, {}, {}


def build(stages=None, dbg_in=(), dbg_out=()):
    c = Ctx(dbg_in, dbg_out)
    declare_io(c)
    st_consts(c)
    allst = stages is None
    if allst or 'load' in stages:
        st_load_x(c)
    for layer in range(2):
        if allst or ('mod%d' % layer) in stages:
            st_mod(c, layer)
        if allst or ('ffa%d' % layer) in stages:
            st_ffn(c, layer, 0, 0)
        if layer == 0 and (allst or 'evmix' in stages):
            sub = stages if (stages and any(k.startswith('ev_') for k in stages)) else None
            if sub is None or 'ev_proj' in sub:
                H = st_norm(c, 0, 1)
                st_even_proj(c, H)
            if sub is None or 'ev_na' in sub:
                st_na(c, True)
            if sub is None or 'ev_s5' in sub:
                st_s5(c, True)
            if sub is None or 'ev_glu' in sub:
                st_glu_wout(c, 0, True)
        if layer == 1 and (allst or 'odmix' in stages):
            sub = stages if (stages and any(k.startswith('od_') for k in stages)) else None
            if sub is None or 'od_proj' in sub:
                H = st_norm(c, 1, 1)
                st_odd_proj(c, H)
            if sub is None or 'od_ssd' in sub:
                st_ssd(c)
            if sub is None or 'od_hy' in sub:
                st_hyena(c)
            if sub is None or 'od_out' in sub:
                st_odd_out(c)
        if allst or ('ffb%d' % layer) in stages:
            st_ffn(c, layer, 1, 2, TB if layer == 0 else TB[1:])
    if allst or 'final' in stages:
        st_final(c)
    c.stage_end()
    return c


def host_prep(inp, b):
    f = np.float32
    m = {}
    xc = np.concatenate([inp['ctx'][b], inp['x'][b]], axis=0)
    m['xin'] = np.ascontiguousarray(xc.T.reshape(16, 128, T))
    cc = np.stack([inp['c'][b], inp['c_ctx']], axis=-1)
    m['cc'] = np.ascontiguousarray(cc.reshape(16, 128, 2).transpose(1, 0, 2))
    return m


_SHARED = {}


def host_shared(inp):
    m = {}
    m['mod_w'] = np.ascontiguousarray(inp['mod_w'])
    m['mod_b'] = np.ascontiguousarray(inp['mod_b'].reshape(2, 144, 128).transpose(0, 2, 1))
    m['norm_g'] = np.ascontiguousarray(inp['norm_g'].reshape(2, 3, 16, 128).transpose(0, 3, 1, 2))
    m['final_g'] = np.ascontiguousarray(inp['final_g'].reshape(16, 128).T)
    for k in ('ffn_wg', 'ffn_wu'):
        m[k] = np.ascontiguousarray(inp[k].reshape(2, 2, 16, 128, NFT, 128).transpose(0, 1, 4, 3, 2, 5))
    m['ffn_wd'] = np.ascontiguousarray(inp['ffn_wd'])
    w = inp['ev_w_in'][0]
    m['ev_w_in'] = np.ascontiguousarray(w)
    m['ev_w_in_t'] = np.ascontiguousarray(w[:, :3072].reshape(16, 128, 24, 128).transpose(2, 1, 0, 3))
    col = np.arange(64)
    dc = np.clip(col[:, None] - col[None, :] + 15, 0, 30)
    rp = inp['na_rpb'][0][:, :, dc]
    m['rpbt'] = np.ascontiguousarray(rp.transpose(2, 0, 1, 3))
    cs = np.clip(col - 8, 0, 48)
    ok = (col[:, None] >= cs[None, :]) & (col[:, None] < cs[None, :] + 16)
    m['namask'] = np.where(ok, 0.0, NEGM).astype(np.float32)

    def st_lay(a):
        return np.ascontiguousarray(a.reshape(2, 32, 2, 64).transpose(2, 3, 0, 1).reshape(128, 64))
    m['s5are'] = st_lay(inp['s5_a_re'][0])
    m['s5aim'] = st_lay(inp['s5_a_im'][0])
    m['s5ldt'] = st_lay(np.repeat(inp['s5_log_dt'][0][:, :, None], 64, axis=2))
    dd = np.zeros((128, 32), np.float32)
    dd[0:32] = inp['s5_d'][0].reshape(32, 32).T
    m['s5d'] = dd
    m['iota1'] = np.ascontiguousarray(np.broadcast_to(np.arange(1, 513, dtype=np.float32), (128, 512)))

    def b_blk(b):
        o = np.zeros((128, 2, 32, 32), np.float32)
        bb = b.reshape(2, 32, 2, 64, 16)
        o[0:64, :, :, 0:16] = bb[:, :, 0].transpose(2, 0, 1, 3)
        o[64:128, :, :, 16:32] = bb[:, :, 1].transpose(2, 0, 1, 3)
        return o.reshape(128, 64, 32)

    def c_blk(cm):
        o = np.zeros((128, 2, 32, 32), np.float32)
        cc = cm.reshape(2, 32, 2, 16, 64)
        o[0:64, :, :, 0:16] = cc[:, :, 0].transpose(3, 0, 1, 2)
        o[64:128, :, :, 16:32] = cc[:, :, 1].transpose(3, 0, 1, 2)
        return o.reshape(128, 64, 32)
    m['s5bre'] = b_blk(inp['s5_b_re'][0])
    m['s5bim'] = b_blk(inp['s5_b_im'][0])
    m['s5cre'] = c_blk(inp['s5_c_re'][0])
    m['s5cim'] = c_blk(inp['s5_c_im'][0])
    m['glu_w'] = np.ascontiguousarray(inp['s5_glu_w'][0])
    m['glu_b'] = np.ascontiguousarray(inp['s5_glu_b'][0].reshape(8, 128).T)
    m['ev_w_out'] = np.ascontiguousarray(inp['ev_w_out'][0])
    m['od_w_out'] = np.ascontiguousarray(inp['od_w_out'][0])
    w = inp['od_w_in'][0]
    m['od_w_in_t'] = np.ascontiguousarray(w[:, :6144].reshape(16, 128, 48, 128).transpose(2, 1, 0, 3))
    m['od_w_dt'] = np.ascontiguousarray(w[:, 6144:6176].reshape(16, 128, 2, 16).transpose(1, 2, 0, 3))
    m['hysw'] = np.ascontiguousarray(inp['hy_short_w'][0].T.reshape(24, 128, 3).transpose(1, 0, 2))
    m['hysb'] = np.ascontiguousarray(inp['hy_short_b'][0].reshape(24, 128).T)
    m['ssdcw'] = np.ascontiguousarray(inp['ssd_conv_w'][0].T.reshape(16, 128, 3).transpose(1, 0, 2))
    m['ssdcb'] = np.ascontiguousarray(inp['ssd_conv_b'][0].reshape(16, 128).T)
    m['dtbias'] = np.ascontiguousarray(inp['ssd_dt_bias'][0].T)
    m['alog'] = np.ascontiguousarray(inp['ssd_a_log'][0].T)
    m['ssdd'] = np.ascontiguousarray(np.broadcast_to(inp['ssd_d'][0][None, :], (128, 16)))
    ii = np.arange(128)
    m['tri'] = np.stack([(ii[:, None] <= ii[None, :]), (ii[:, None] >= ii[None, :])]).astype(np.float32)
    m['ssdng'] = np.ascontiguousarray(inp['ssd_norm_g'][0].reshape(8, 128).T)
    m['hywin'] = np.ascontiguousarray(inp['hy_w_in'][0])
    m['hywmid'] = np.ascontiguousarray(inp['hy_w_mid'][0].transpose(1, 0, 2))
    m['hyfb'] = np.ascontiguousarray(np.stack([inp['hy_freq'][0], inp['hy_b_in'][0], inp['hy_b_mid'][0][0], inp['hy_b_mid'][0][1]], axis=1))
    m['hywout'] = np.ascontiguousarray(inp['hy_w_out'][0])
    m['hyfbias'] = np.ascontiguousarray(np.broadcast_to(inp['hy_fbias'][0][None], (128, 2, 1024)))
    m.update(hy_consts())
    return m


_HYC = {}


def hy_consts():
    if _HYC:
        return _HYC
    f32 = np.float32
    t = np.linspace(0.0, 1.0, L, dtype=f32)[:, None]
    w = (2.0 * math.pi * np.arange(L, dtype=f32)[:, None] / L).astype(f32)
    f = np.linspace(1e-4, 15, 16, dtype=f32)[None, :]
    z = np.concatenate([t, np.cos(f * w), -np.sin(f * w)], axis=-1).astype(f32)
    _HYC['hyz'] = np.ascontiguousarray(z.T)
    mx = math.log(1e-2) / 0.3
    mn = math.log(1e-2) / 1.5
    deltas = np.abs(np.linspace(mn, mx, 1024, dtype=f32))
    _HYC['decay'] = np.exp(-t * deltas[None, :]).astype(f32)
    n = np.arange(L, dtype=np.int64)
    ang = 2.0 * np.pi * ((n[:, None] * n[None, :]) % 4096).astype(np.float64) / 4096.0
    Cf = np.cos(ang)
    Sf = -np.sin(ang)
    sgn = np.where(n % 2 == 0, 1.0, -1.0)
    Sf[:, 0] = sgn
    wf = np.full(L, 2.0 / 4096.0)
    wf[0] = 1.0 / 4096.0
    Ci = wf[:, None] * np.cos(ang)
    Si = -wf[:, None] * np.sin(ang)
    Si[0, :] = sgn / 4096.0

    def lay(a):
        return np.ascontiguousarray(a.reshape(16, 128, 16, 128).transpose(2, 1, 0, 3).astype(f32).astype(ml_dtypes.bfloat16))
    _HYC['Cf'], _HYC['Sf'], _HYC['Ci'], _HYC['Si'] = lay(Cf), lay(Sf), lay(Ci), lay(Si)
    return _HYC


def kernel(**inputs):
    inp = {k: np.asarray(v) for k, v in inputs.items()}
    c = build()
    shared = host_shared(inp)
    in_maps = []
    for b in range(8):
        m = dict(shared)
        m.update(host_prep(inp, b))
        in_maps.append({k: m[k] for k in c.inputs})
    res = run_bass_kernel_spmd(c.nc, in_maps, core_ids=list(range(8)))
    outs = []
    for b in range(8):
        o = np.asarray(res.results[b]['out'])
        outs.append(o.reshape(D, L).T)
    return np.ascontiguousarray(np.stack(outs, axis=0)).astype(np.float32)


SQ128 = math.sqrt(128.0)
NEGM = -30000.0


def evac(c, n, out, in_, R, W):
    if n % 2 == 0:
        c.act(out, in_, AF.Identity, R, W)
    else:
        c.copy(out, in_, R, W)


def st_even_proj(c, H):
    wsrc = c.d_evwin_t
    wb = [c.sb([16, 128], BF16) for _ in range(3)]
    ob = [c.sb([512], BF16) for _ in range(4)]
    dst = [c.Ud, c.Qd, c.Kd]
    dreg = [c.Ur, c.Qr, c.Kr]
    cnt = 0
    for f in range(24):
        w_ = wb[f % 3]
        c.dma(w_.ap, wsrc[f], [], [w_.r], q=POOL)
        for bi, (t0, n) in enumerate(TB):
            ps = c.psum[cnt % 4]
            o_ = ob[cnt % 4]
            for kt in range(16):
                c.mm(ps.ap[:, 0:n], w_.ap[:, kt, :], H.ap[:, kt, t0:t0 + n], kt == 0, kt == 15, [w_.r, H.r], [ps.r])
            evac(c, cnt, o_.ap[:, 0:n], ps.ap[:, 0:n], [ps.r], [o_.r])
            c.dma(dst[f // 8][f % 8, :, t0:t0 + n], o_.ap[:, 0:n], [o_.r], [dreg[f // 8]])
            cnt += 1
    vsrc = c.d_evwin.rearrange('(kt p) f -> p kt f', p=128)
    vw = [c.sb([16, 512], BF16) for _ in range(2)]
    for j in range(2):
        for kt in range(16):
            c.dma(vw[j].ap[:, kt, :], vsrc[:, kt, 3072 + 512 * j:3072 + 512 * (j + 1)], [], [vw[j].r], q=POOL)
    for tt_ in range(18):
        for j in range(2):
            ps = c.psum[cnt % 4]
            o_ = ob[cnt % 4]
            for kt in range(16):
                c.mm(ps.ap, H.ap[:, kt, tt_ * 128:(tt_ + 1) * 128], vw[j].ap[:, kt, :], kt == 0, kt == 15, [vw[j].r, H.r], [ps.r])
            evac(c, cnt, o_.ap, ps.ap, [ps.r], [o_.r])
            c.dma(c.Vd[tt_, :, 512 * j:512 * (j + 1)], o_.ap, [o_.r], [c.Vr])
            cnt += 1
    c.stage_end()


def st_na(c, with_ctx=True):
    Tb = c.sb([8, 15, 64], F32)
    mk = c.sb([64], F32)
    c.dma(Tb.ap[0:64], c.d_rpbt, [], [Tb.r])
    c.dma(mk.ap[0:64], c.d_namask, [], [mk.r])
    c.stt(Tb.ap[0:64].rearrange('p h d q -> p (h d) q'), Tb.ap[0:64].rearrange('p h d q -> p (h d) q'), SQ128,
          mk.ap[0:64, None, :].to_broadcast([64, 120, 64]), ALU.mult, ALU.add, [Tb.r, mk.r], [Tb.r])
    neg = c.sb([64], F32)
    c.memset(neg.ap, NEGM, [neg.r])
    sel = c.sb([2, 128], F32)
    c.memset(sel.ap, 0.0, [sel.r])
    c.copy(sel.ap[0:64, 0, 0:64], c.ident_f.ap[0:64, 0:64], [c.ident_f.r, sel.r], [sel.r])
    c.copy(sel.ap[0:64, 1, 64:128], c.ident_f.ap[0:64, 0:64], [c.ident_f.r, sel.r], [sel.r])
    qb = [c.sb([T], BF16) for _ in range(2)]
    kb = [c.sb([T], BF16) for _ in range(2)]
    vb = [c.sb([18, 128], BF16) for _ in range(2)]
    ob = [c.sb([T], BF16) for _ in range(2)]
    eb = [c.sb([7, 64], BF16) for _ in range(3)]
    ec = c.sb([2, 256], BF16)
    rz = [c.sb([64], F32) for _ in range(2)]
    rzc = c.sb([256], F32)
    sc = 1.0 / SQ128
    it = 0
    for h in range(8):
        q_, k_, v_, o_ = qb[h % 2], kb[h % 2], vb[h % 2], ob[h % 2]
        c.dma(q_.ap, c.Qd[h], [c.Qr], [q_.r])
        c.dma(k_.ap, c.Kd[h], [c.Kr], [k_.r])
        c.dma(v_.ap, c.Vd[:, :, h * 128:(h + 1) * 128].rearrange('t p d -> p t d'), [c.Vr], [v_.r])
        if with_ctx:
            ps, po = c.psum[4], c.psum[5]
            for i in range(2):
                c.mm(ps.ap[:, i * 256:(i + 1) * 256], k_.ap[:, i * 128:(i + 1) * 128], q_.ap[:, 0:256], True, True, [k_.r, q_.r], [ps.r])
            c.act(ec.ap.rearrange('p a b -> p (a b)'), ps.ap, AF.Exp, [ps.r], [ec.r], scale=sc)
            for i in range(2):
                c.mm(po.ap[:, 0:256], v_.ap[:, i, :], ec.ap[:, i, :], i == 0, i == 1, [v_.r, ec.r], [po.r])
            for i in range(2):
                c.mm(po.ap[:, 256:512], c.ones_bf.ap, ec.ap[:, i, :], i == 0, i == 1, [c.ones_bf.r, ec.r], [po.r])
            c.recip(rzc.ap, po.ap[:, 256:512], [po.r], [rzc.r])
            c.tt(o_.ap[:, 0:256], po.ap[:, 0:256], rzc.ap, ALU.mult, [po.r, rzc.r], [o_.r])
        for r in range(32):
            rs = min(max(r - 4, 0), 24)
            base = (rs // 2) * 2
            nt = 4 if rs % 2 == 0 else 5
            ps, po = c.psum[it % 2], c.psum[2 + it % 2]
            e_ = eb[it % 3]
            z_ = rz[it % 2]
            qs = q_.ap[:, LC + r * 64:LC + (r + 1) * 64]
            tiles = []
            for i in range(nt):
                krow = base + 2 * i
                k0 = LC + krow * 64
                col = ps.ap[:, i * 64:(i + 1) * 64]
                c.mm(col, k_.ap[:, k0:k0 + 128], qs, True, False, [k_.r, q_.r], [ps.r])
                for half in range(2):
                    kr = krow + half
                    if rs <= kr < rs + 8:
                        rhs = Tb.ap[0:64, h, kr - r + 7, :]
                        rr = Tb.r
                    else:
                        rhs = neg.ap[0:64, :]
                        rr = neg.r
                    c.mm(col, sel.ap[0:64, half, :], rhs, False, half == 1, [sel.r, rr], [ps.r])
                tiles.append((LC // 128) + krow // 2)
            for i in range(2):
                col = ps.ap[:, (nt + i) * 64:(nt + i + 1) * 64]
                c.mm(col, k_.ap[:, i * 128:(i + 1) * 128], qs, True, True, [k_.r, q_.r], [ps.r])
                tiles.append(i)
            ntt = nt + 2
            c.act(e_.ap[:, 0:ntt, :].rearrange('p a b -> p (a b)'), ps.ap[:, 0:ntt * 64], AF.Exp, [ps.r], [e_.r], scale=sc)
            for i, vt in enumerate(tiles):
                c.mm(po.ap[:, 0:64], v_.ap[:, vt, :], e_.ap[:, i, :], i == 0, i == ntt - 1, [v_.r, e_.r], [po.r])
            for i in range(ntt):
                c.mm(po.ap[:, 64:128], c.ones_bf.ap, e_.ap[:, i, :], i == 0, i == ntt - 1, [c.ones_bf.r, e_.r], [po.r])
            c.recip(z_.ap, po.ap[:, 64:128], [po.r], [z_.r])
            c.tt(o_.ap[:, LC + r * 64:LC + (r + 1) * 64], po.ap[:, 0:64], z_.ap, ALU.mult, [po.r, z_.r], [o_.r])
            it += 1
        if with_ctx:
            c.dma(c.CATd[8 + h], o_.ap, [o_.r], [c.CATr[1]])
        else:
            c.dma(c.CATd[8 + h, :, LC:T], o_.ap[:, LC:T], [o_.r], [c.CATr[1]])
    c.stage_end()


def bcl(ap, shape):
    return ap.to_broadcast(list(shape))


def rnd(c, out, in_, R, W, eng=DVE):
    c.ts(out, in_, MAGIC, MAGIC, ALU.add, ALU.subtract, R, W, eng=eng)


def sincos_frac(c, t, tmp, out_s, out_c, R):
    a, b = tmp
    rnd(c, a.ap, t.ap, [t.r], [a.r])
    c.tt(a.ap, t.ap, a.ap, ALU.subtract, [t.r, a.r], [a.r])
    c.act(out_s.ap, a.ap, AF.Sin, [a.r], [out_s.r], scale=2 * math.pi)
    c.ts(b.ap, t.ap, 0.25, None, ALU.add, None, [t.r], [b.r])
    rnd(c, a.ap, b.ap, [b.r], [a.r])
    c.tt(b.ap, b.ap, a.ap, ALU.subtract, [b.r, a.r], [b.r])
    c.act(out_c.ap, b.ap, AF.Sin, [b.r], [out_c.r], scale=2 * math.pi)


def st_s5(c, with_ctx=True):
    def ld(src, shape):
        t = c.sb(shape, F32)
        c.dma(t.ap, src, [], [t.r])
        return t
    dcol = ld(c.d_s5d, [32])
    iota = ld(c.d_iota1, [512])
    r_, thp = c.sb([64], F32), c.sb([64], F32)
    BT = c.sb([128, 128], BF16)
    CB = c.sb([64, 2, 32], BF16)
    mark = c.aoff
    are, aim, ldt = ld(c.d_s5are, [64]), ld(c.d_s5aim, [64]), ld(c.d_s5ldt, [64])
    tmp = [c.sb([64], F32) for _ in range(8)]
    dtm, sn, cs, cre, cim, den = [c.sb([64], F32) for _ in range(6)]
    c.act(dtm.ap, ldt.ap, AF.Exp, [ldt.r], [dtm.r])
    c.tt(tmp[0].ap, are.ap, dtm.ap, ALU.mult, [are.r, dtm.r], [tmp[0].r])
    c.act(r_.ap, tmp[0].ap, AF.Exp, [tmp[0].r], [r_.r])
    c.tt(thp.ap, aim.ap, dtm.ap, ALU.mult, [aim.r, dtm.r], [thp.r])
    c.ts(thp.ap, thp.ap, 1.0 / (2 * math.pi), None, ALU.mult, None, [thp.r], [thp.r])
    sincos_frac(c, thp, tmp[1:3], sn, cs, None)
    nr, ni = tmp[3], tmp[4]
    c.tt(nr.ap, r_.ap, cs.ap, ALU.mult, [r_.r, cs.r], [nr.r])
    c.ts(nr.ap, nr.ap, -1.0, None, ALU.add, None, [nr.r], [nr.r])
    c.tt(ni.ap, r_.ap, sn.ap, ALU.mult, [r_.r, sn.r], [ni.r])
    c.tt(den.ap, are.ap, are.ap, ALU.mult, [are.r], [den.r])
    c.tt(tmp[5].ap, aim.ap, aim.ap, ALU.mult, [aim.r], [tmp[5].r])
    c.tt(den.ap, den.ap, tmp[5].ap, ALU.add, [den.r, tmp[5].r], [den.r])
    c.recip(den.ap, den.ap, [den.r], [den.r])
    c.tt(cre.ap, nr.ap, are.ap, ALU.mult, [nr.r, are.r], [cre.r])
    c.tt(tmp[5].ap, ni.ap, aim.ap, ALU.mult, [ni.r, aim.r], [tmp[5].r])
    c.tt(cre.ap, cre.ap, tmp[5].ap, ALU.add, [cre.r, tmp[5].r], [cre.r])
    c.tt(cre.ap, cre.ap, den.ap, ALU.mult, [cre.r, den.r], [cre.r])
    c.tt(cim.ap, ni.ap, are.ap, ALU.mult, [ni.r, are.r], [cim.r])
    c.tt(tmp[5].ap, nr.ap, aim.ap, ALU.mult, [nr.r, aim.r], [tmp[5].r])
    c.tt(cim.ap, cim.ap, tmp[5].ap, ALU.subtract, [cim.r, tmp[5].r], [cim.r])
    c.tt(cim.ap, cim.ap, den.ap, ALU.mult, [cim.r, den.r], [cim.r])
    bre, bim = ld(c.d_s5bre, [64, 32]), ld(c.d_s5bim, [64, 32])
    Bb = c.sb([64, 2, 32], F32)
    t1, t2 = c.sb([64, 32], F32), c.sb([64, 32], F32)
    creb, cimb = bcl(cre.ap, [128, 64, 32]), bcl(cim.ap, [128, 64, 32])
    c.tt(t1.ap, bre.ap, creb, ALU.mult, [bre.r, cre.r], [t1.r])
    c.tt(t2.ap, bim.ap, cimb, ALU.mult, [bim.r, cim.r], [t2.r])
    c.tt(Bb.ap[:, :, 0, :], t1.ap, t2.ap, ALU.subtract, [t1.r, t2.r], [Bb.r])
    c.tt(t1.ap, bim.ap, creb, ALU.mult, [bim.r, cre.r], [t1.r])
    c.tt(t2.ap, bre.ap, cimb, ALU.mult, [bre.r, cim.r], [t2.r])
    c.tt(Bb.ap[:, :, 1, :], t1.ap, t2.ap, ALU.add, [t1.r, t2.r], [Bb.r])
    for q4 in range(32):
        ps = c.psum[q4 % 2]
        for j in range(4):
            idx = q4 * 4 + j
            c.transpose(ps.ap[0:32, j * 128:(j + 1) * 128], Bb.ap[:, idx // 2, idx % 2, :], c.ident_f.ap, [Bb.r, c.ident_f.r], [ps.r])
        c.copy(BT.ap[0:32, q4 * 4:(q4 + 1) * 4, :].rearrange('p a b -> p (a b)'), ps.ap[0:32, :], [ps.r], [BT.r])
    crb, cib = ld(c.d_s5cre, [64, 32]), ld(c.d_s5cim, [64, 32])
    c.act(CB.ap[:, :, 0, :], crb.ap, AF.Identity, [crb.r], [CB.r])
    c.act(CB.ap[:, :, 1, :], cib.ap, AF.Identity, [cib.r], [CB.r], scale=-1.0)
    c.S.flush(barrier=True)
    c.aoff = mark
    ctab = [c.sb([512], F32) for _ in range(2)]
    stab = [c.sb([512], F32) for _ in range(2)]
    tA, tB, tT = c.sb([512], F32), c.sb([512], F32), c.sb([512], F32)
    br, bi_ = [c.sb([512], F32) for _ in range(2)], [c.sb([512], F32) for _ in range(2)]
    d1, d2, zR, wR, m1, m2 = [c.sb([512], F32) for _ in range(6)]
    p1, p2, zI, wI, m3, m4 = [c.sb([512], F32) for _ in range(6)]
    sbuf = [[[c.sb([T], BF16) for _ in range(2)] for _ in range(2)] for _ in range(2)]
    ug = [c.sb([T], BF16) for _ in range(2)]
    ysb = [c.sb([T], F32) for _ in range(2)]
    ini = [c.sb([1], F32) for _ in range(2)]
    tin = c.sb([1], F32)
    segs_f = list(TB)
    segs_b = [TB[0]] + TB[:0:-1]
    if not with_ctx:
        pass
    nseg = 0
    for gp in range(32):
        u_ = ug[gp % 2]
        c.dma(u_.ap[0:32], c.Ud[gp // 4, 32 * (gp % 4):32 * (gp % 4) + 32, :], [c.Ur], [u_.r])
        for dr in range(2):
            dg = dr * 32 + gp
            ct, st_ = ctab[dr], stab[dr]
            c.ts(tT.ap, iota.ap, thp.ap[:, dg:dg + 1], None, ALU.mult, None, [iota.r, thp.r], [tT.r])
            sincos_frac(c, tT, [tA, tB], st_, ct, None)
            rcol = r_.ap[:, dg:dg + 1]
            sR_, sI_ = sbuf[gp % 2][dr]
            first = True
            for (t0, n) in (segs_f if dr == 0 else segs_b):
                rev = dr == 1
                pr, pi = c.psum[2 * (nseg % 2)], c.psum[2 * (nseg % 2) + 1]
                b_r, b_i = br[nseg % 2], bi_[nseg % 2]
                c.mm(pr.ap[:, 0:n], BT.ap[0:32, dg * 2, :], u_.ap[0:32, t0:t0 + n], True, True, [BT.r, u_.r], [pr.r])
                c.mm(pi.ap[:, 0:n], BT.ap[0:32, dg * 2 + 1, :], u_.ap[0:32, t0:t0 + n], True, True, [BT.r, u_.r], [pi.r])
                srcr = pr.ap[:, 0:n][:, ::-1] if rev else pr.ap[:, 0:n]
                srci = pi.ap[:, 0:n][:, ::-1] if rev else pi.ap[:, 0:n]
                c.act(b_r.ap[:, 0:n], srcr, AF.Identity, [pr.r], [b_r.r])
                c.act(b_i.ap[:, 0:n], srci, AF.Identity, [pi.r], [b_i.r])
                cN, sN = ct.ap[:, 0:n], st_.ap[:, 0:n]
                c.tt(d1.ap[:, 0:n], cN, b_r.ap[:, 0:n], ALU.mult, [ct.r, b_r.r], [d1.r])
                c.tt(d2.ap[:, 0:n], sN, b_i.ap[:, 0:n], ALU.mult, [st_.r, b_i.r], [d2.r])
                c.tt(zR.ap[:, 0:n], d1.ap[:, 0:n], d2.ap[:, 0:n], ALU.add, [d1.r, d2.r], [zR.r])
                c.tt(p1.ap[:, 0:n], cN, b_i.ap[:, 0:n], ALU.mult, [ct.r, b_i.r], [p1.r], eng=POOL)
                c.tt(p2.ap[:, 0:n], sN, b_r.ap[:, 0:n], ALU.mult, [st_.r, b_r.r], [p2.r], eng=POOL)
                c.tt(zI.ap[:, 0:n], p1.ap[:, 0:n], p2.ap[:, 0:n], ALU.subtract, [p1.r, p2.r], [zI.r], eng=POOL)
                rb = rcol.to_broadcast([128, n])
                for (w_, z_, k) in ((wR, zR, 0), (wI, zI, 1)):
                    init = 0.0 if first else ini[k].ap[:, 0:1]
                    rr = [r_.r, z_.r] + ([] if first else [ini[k].r])
                    c.S.op(DVE, (lambda o, z, i0, b: lambda e: e.tensor_tensor_scan(out=o, data0=b, data1=z, initial=i0,
                                                                                   op0=ALU.mult, op1=ALU.add))(w_.ap[:, 0:n], z_.ap[:, 0:n], init, rb),
                           reads=rr, writes=[w_.r])
                oR = sR_.ap[:, t0:t0 + n][:, ::-1] if rev else sR_.ap[:, t0:t0 + n]
                oI = sI_.ap[:, t0:t0 + n][:, ::-1] if rev else sI_.ap[:, t0:t0 + n]
                c.tt(m1.ap[:, 0:n], cN, wR.ap[:, 0:n], ALU.mult, [ct.r, wR.r], [m1.r])
                c.tt(m2.ap[:, 0:n], sN, wI.ap[:, 0:n], ALU.mult, [st_.r, wI.r], [m2.r])
                c.tt(oR, m1.ap[:, 0:n], m2.ap[:, 0:n], ALU.subtract, [m1.r, m2.r], [sR_.r])
                c.tt(m3.ap[:, 0:n], cN, wI.ap[:, 0:n], ALU.mult, [ct.r, wI.r], [m3.r], eng=POOL)
                c.tt(m4.ap[:, 0:n], sN, wR.ap[:, 0:n], ALU.mult, [st_.r, wR.r], [m4.r], eng=POOL)
                c.tt(oI, m3.ap[:, 0:n], m4.ap[:, 0:n], ALU.add, [m3.r, m4.r], [sI_.r], eng=POOL)
                cl, sl = ct.ap[:, n - 1:n], st_.ap[:, n - 1:n]
                wRl, wIl = wR.ap[:, n - 1:n], wI.ap[:, n - 1:n]
                c.tt(tin.ap, sl, wIl, ALU.mult, [st_.r, wI.r], [tin.r])
                c.stt(ini[0].ap, wRl, cl, tin.ap, ALU.mult, ALU.subtract, [wR.r, ct.r, tin.r], [ini[0].r])
                c.tt(tin.ap, sl, wRl, ALU.mult, [st_.r, wR.r], [tin.r])
                c.stt(ini[1].ap, wIl, cl, tin.ap, ALU.mult, ALU.add, [wI.r, ct.r, tin.r], [ini[1].r])
                first = False
                nseg += 1
        y_ = ysb[gp % 2]
        for bi, (t0, n) in enumerate(TB):
            po = c.psum[4 + bi % 2]
            k = 0
            for dr in range(2):
                dg = dr * 32 + gp
                for ri in range(2):
                    sb_ = sbuf[gp % 2][dr][ri]
                    c.mm(po.ap[0:32, 0:n], CB.ap[:, dg, ri, :], sb_.ap[:, t0:t0 + n], k == 0, k == 3, [CB.r, sb_.r], [po.r])
                    k += 1
            c.stt(y_.ap[0:32, t0:t0 + n], u_.ap[0:32, t0:t0 + n], dcol.ap[0:32, gp:gp + 1], po.ap[0:32, 0:n], ALU.mult, ALU.add,
                  [u_.r, dcol.r, po.r], [y_.r])
        c.dma(c.Yd[gp * 32:(gp + 1) * 32, :], y_.ap[0:32], [y_.r], [c.Yr])
    c.stage_end()


C0G = math.sqrt(2.0 / math.pi)


def st_glu_wout(c, layer, with_ctx=True):
    blocks = TB if with_ctx else TB[1:]
    G = c.G[layer]
    gw = c.sb([8, 1024], BF16)
    for kt in range(8):
        c.dma(gw.ap[:, kt, :], c.d_gluw[kt * 128:(kt + 1) * 128, :], [], [gw.r], q=POOL)
    gb = c.sb([8], F32)
    c.dma(gb.ap, c.d_glub, [], [gb.r])
    wo = c.sb([16, 2048], BF16)
    for kt in range(16):
        c.dma(wo.ap[:, kt, :], c.d_wout[layer][kt * 128:(kt + 1) * 128, :], [], [wo.r], q=POOL)
    yb = [c.sb([8, 512], F32) for _ in range(1)]
    y2 = c.sb([8, 512], F32)
    sg = c.sb([8, 512], F32)
    gg = [c.sb([8, 512], BF16) for _ in range(2)]
    cat = [c.sb([16, 512], BF16) for _ in range(1)]
    catr = [[Reg() for _ in range(16)] for _ in range(1)]
    sgm = [c.sb([512], F32) for _ in range(2)]
    xt = [c.sb([512], F32) for _ in range(4)]
    nx = 0
    for bi, (t0, n) in enumerate(blocks):
        g = 1 if t0 < LC else 0
        y_, g_, ct, cr = yb[0], gg[bi % 2], cat[0], catr[0]
        c.dma(y_.ap[:, :, 0:n], c.Yd[:, t0:t0 + n].rearrange('(a p) t -> p a t', p=128), [c.Yr], [y_.r])
        c.dma(ct.ap[:, 8:16, 0:n], c.CATd[8:16, :, t0:t0 + n].rearrange('a p t -> p a t'), [c.CATr[1]], cr[8:16])
        yv = y_.ap[:, :, 0:n]
        c.act(y2.ap[:, :, 0:n], yv, AF.Square, [y_.r], [y2.r])
        c.ts(y2.ap[:, :, 0:n], y2.ap[:, :, 0:n], 0.044715, 1.0, ALU.mult, ALU.add, [y2.r], [y2.r])
        c.tt(y2.ap[:, :, 0:n], y2.ap[:, :, 0:n], yv, ALU.mult, [y2.r, y_.r], [y2.r])
        c.act(sg.ap[:, :, 0:n], y2.ap[:, :, 0:n], AF.Sigmoid, [y2.r], [sg.r], scale=2 * C0G)
        c.tt(g_.ap[:, :, 0:n], sg.ap[:, :, 0:n], yv, ALU.mult, [sg.r, y_.r], [g_.r])
        for ft in range(8):
            ps = c.psum[ft % 2]
            s_ = sgm[ft % 2]
            for kt in range(8):
                c.mm(ps.ap[:, 0:n], gw.ap[:, kt, ft * 128:(ft + 1) * 128], g_.ap[:, kt, 0:n], kt == 0, kt == 7, [gw.r, g_.r], [ps.r])
            c.act(s_.ap[:, 0:n], ps.ap[:, 0:n], AF.Sigmoid, [ps.r, gb.r], [s_.r], bias=gb.ap[:, ft:ft + 1])
            c.tt(ct.ap[:, ft, 0:n], s_.ap[:, 0:n], g_.ap[:, ft, 0:n], ALU.mult, [s_.r, g_.r], [cr[ft]])
        wout_block(c, layer, ct, cr, wo, xt, t0, n, g, nx)
        nx += 16
    c.stage_end()


def wout_block(c, layer, ct, cr, wo, xt, t0, n, g, nx):
    G = c.G[layer]
    for dt in range(16):
        po = c.psum[4 + (nx + dt) % 2]
        x_ = xt[(nx + dt) % 4]
        xreg = c.XRr[dt][bidx(t0)]
        c.dma(x_.ap[:, 0:n], c.XR[dt, :, t0:t0 + n], [xreg], [x_.r])
        for kt in range(16):
            c.mm(po.ap[:, 0:n], wo.ap[:, kt, dt * 128:(dt + 1) * 128], ct.ap[:, kt, 0:n], kt == 0, kt == 15, [wo.r, cr[kt]], [po.r])
        c.stt(x_.ap[:, 0:n], po.ap[:, 0:n], G.ap[:, g, 1, dt:dt + 1], x_.ap[:, 0:n], ALU.mult, ALU.add, [po.r, G.r, x_.r], [x_.r])
        c.dma(c.XR[dt, :, t0:t0 + n], x_.ap[:, 0:n], [x_.r], [xreg])


def conv3(c, raw, W, w3, b, out, tmp, silu, wr=()):
    wr = list(wr)
    c.act(tmp.ap[:, 0:W], raw.ap[:, 1:W + 1], AF.Identity, [raw.r] + wr, [tmp.r], scale=w3[:, 1:2], bias=b)
    c.stt(tmp.ap[:, 0:W], raw.ap[:, 0:W], w3[:, 0:1], tmp.ap[:, 0:W], ALU.mult, ALU.add, [raw.r, tmp.r] + wr, [tmp.r])
    if silu:
        c.stt(tmp.ap[:, 0:W], raw.ap[:, 2:W + 2], w3[:, 2:3], tmp.ap[:, 0:W], ALU.mult, ALU.add, [raw.r, tmp.r] + wr, [tmp.r])
        c.act(out[0], tmp.ap[:, 0:W], AF.Silu, [tmp.r], out[1])
    else:
        c.stt(out[0], raw.ap[:, 2:W + 2], w3[:, 2:3], tmp.ap[:, 0:W], ALU.mult, ALU.add, [raw.r, tmp.r] + wr, out[1])


def st_odd_proj(c, H):
    wsrc = c.d_odwin_t
    hw = c.sb([24, 3], F32); hb = c.sb([24], F32); sw = c.sb([16, 3], F32); sbias = c.sb([16], F32)
    for t_, s_ in ((hw, c.d_hysw), (hb, c.d_hysb), (sw, c.d_ssdcw), (sbias, c.d_ssdcb)):
        c.dma(t_.ap, s_, [], [t_.r])
    wb = [c.sb([16, 128], BF16) for _ in range(3)]
    raw = [c.sb([T + 8], F32) for _ in range(2)]
    tmp = [c.sb([T], F32) for _ in range(2)]
    ob = [c.sb([T], BF16) for _ in range(2)]
    for r_ in raw:
        c.memset(r_.ap, 0.0, [r_.r])
    cnt = 0
    for f in range(48):
        w_ = wb[f % 3]
        c.dma(w_.ap, wsrc[f], [], [w_.r], q=POOL)
        lat_only = f < 32
        blocks = TB[1:] if lat_only else TB
        r_, t_, o_ = raw[f % 2], tmp[f % 2], ob[f % 2]
        for (t0, n) in blocks:
            ps = c.psum[cnt % 4]
            for kt in range(16):
                c.mm(ps.ap[:, 0:n], w_.ap[:, kt, :], H.ap[:, kt, t0:t0 + n], kt == 0, kt == 15, [w_.r, H.r], [ps.r])
            off = 1 + t0 if t0 < LC else 3 + t0
            evac(c, cnt, r_.ap[:, off:off + n], ps.ap[:, 0:n], [ps.r], [r_.r])
            cnt += 1
        rl = Tl(r_.ap[:, 258:258 + L + 2], r_.r)
        if f < 24:
            conv3(c, rl, L, hw.ap[:, f, :], hb.ap[:, f:f + 1], (o_.ap[:, 0:L], [o_.r]), t_, False, [hw.r, hb.r])
            c.dma(c.HYd[f], o_.ap[:, 0:L], [o_.r], [c.HYr])
        elif f < 32:
            c.act(o_.ap[:, 0:L], r_.ap[:, 259:259 + L], AF.Silu, [r_.r], [o_.r])
            c.dma(c.Zd[f - 24], o_.ap[:, 0:L], [o_.r], [c.Zr])
        else:
            a = f - 32
            rc = Tl(r_.ap[:, 0:LC + 2], r_.r)
            conv3(c, rc, LC, sw.ap[:, a, :], sbias.ap[:, a:a + 1], (o_.ap[:, 0:LC], [o_.r]), t_, True, [sw.r, sbias.r])
            conv3(c, rl, L, sw.ap[:, a, :], sbias.ap[:, a:a + 1], (o_.ap[:, LC:T], [o_.r]), t_, True, [sw.r, sbias.r])
            c.dma(c.XBCd[a], o_.ap, [o_.r], [c.XBCr])
    wdt = c.sb([2, 16, 16], BF16)
    c.dma(wdt.ap, c.d_odwdt, [], [wdt.r], q=POOL)
    dtb = c.sb([2], F32); alog = c.sb([2], F32); nA = c.sb([2], F32)
    c.dma(dtb.ap[0:16], c.d_dtbias, [], [dtb.r])
    c.dma(alog.ap[0:16], c.d_alog, [], [alog.r])
    c.act(nA.ap[0:16], alog.ap[0:16], AF.Exp, [alog.r], [nA.r])
    c.ts(nA.ap[0:16], nA.ap[0:16], -1.0, None, ALU.mult, None, [nA.r], [nA.r])
    one = c.sb([1], F32)
    c.memset(one.ap, 1.0, [one.r])
    dtT = [c.sb([T], F32) for _ in range(2)]
    aT = c.sb([T], F32)
    cumT = [c.sb([T], F32) for _ in range(2)]
    et = c.sb([512], F32)
    ini = c.sb([1], F32)
    for k in range(2):
        for (t0, n) in TB:
            ps = c.psum[4 + cnt % 2]
            cnt += 1
            for kt in range(16):
                c.mm(ps.ap[0:16, 0:n], wdt.ap[:, k, kt, :], H.ap[:, kt, t0:t0 + n], kt == 0, kt == 15, [wdt.r, H.r], [ps.r])
            c.act(et.ap[0:16, 0:n], ps.ap[0:16, 0:n], AF.Exp, [ps.r, dtb.r], [et.r], bias=dtb.ap[0:16, k:k + 1])
            c.act(dtT[k].ap[0:16, t0:t0 + n], et.ap[0:16, 0:n], AF.Ln, [et.r, one.r], [dtT[k].r], bias=one.ap[0:16, 0:1])
        c.ts(aT.ap[0:16], dtT[k].ap[0:16], nA.ap[0:16, k:k + 1], None, ALU.mult, None, [dtT[k].r, nA.r], [aT.r])
        segs = list(TB) if k == 0 else [TB[0]] + TB[:0:-1]
        first = True
        for (t0, n) in segs:
            src = aT.ap[0:16, t0:t0 + n]
            dst = cumT[k].ap[0:16, t0:t0 + n]
            if k == 1:
                src, dst = src[:, ::-1], dst[:, ::-1]
            init = 0.0 if first else ini.ap[0:16, 0:1]
            ob_ = one.ap[0:16, 0:1].to_broadcast([16, n])
            c.S.op(DVE, (lambda o, z, i0, b: lambda e: e.tensor_tensor_scan(out=o, data0=b, data1=z, initial=i0, op0=ALU.mult, op1=ALU.add))(dst, src, init, ob_),
                   reads=[aT.r, one.r] + ([] if first else [ini.r]), writes=[cumT[k].r])
            last = t0 + n - 1 if k == 0 else t0
            c.copy(ini.ap[0:16], cumT[k].ap[0:16, last:last + 1], [cumT[k].r], [ini.r])
            first = False
        c.dma(c.DTd[k], dtT[k].ap[0:16], [dtT[k].r], [c.DTr])
        c.dma(c.CUMd[k], cumT[k].ap[0:16], [cumT[k].r], [c.CUMr])
    c.stage_end()


def st_ssd(c):
    Bt = c.sb([4, T], BF16); Ct = c.sb([4, T], BF16)
    c.dma(Bt.ap, c.XBCd[8:12].rearrange('a p t -> p a t'), [c.XBCr], [Bt.r])
    c.dma(Ct.ap, c.XBCd[12:16].rearrange('a p t -> p a t'), [c.XBCr], [Ct.r])
    xtok = c.sb([18, 1024], BF16)
    xdt = [c.sb([18, 1024], BF16) for _ in range(2)]
    cumT = [c.sb([T], F32) for _ in range(2)]
    ccol = [c.sb([18, 16], F32) for _ in range(2)]
    ncol = [c.sb([18, 16], F32) for _ in range(2)]
    mark = c.aoff
    xs = [c.sb([T], BF16) for _ in range(2)]
    nps = 0
    import os
    CUT = float(os.environ.get('SSD_CUT', '99'))
    for a in range(8):
        x_ = xs[a % 2]
        c.dma(x_.ap, c.XBCd[a], [c.XBCr], [x_.r])
        for j4 in range(0, 18, 4):
            nj = min(4, 18 - j4)
            ps = c.psum[6 + nps % 2]
            nps += 1
            pb = ps.ap.bitcast(BF16)
            for jj in range(nj):
                c.transpose(pb[:, jj * 128:(jj + 1) * 128], x_.ap[:, (j4 + jj) * 128:(j4 + jj + 1) * 128], c.ident_b.ap, [x_.r, c.ident_b.r], [ps.r])
            c.copy(xtok.ap[:, j4:j4 + nj, a * 128:(a + 1) * 128], pb[:, 0:nj * 128].rearrange('p (j q) -> p j q', q=128), [ps.r], [xtok.r])
    if CUT <= 1:
        c.stage_end()
        return
    dtT = c.sb([T], F32)
    dtk = c.sb([18, 16], F32)
    for k in range(2):
        c.memset(cumT[k].ap[0:32], 0.0, [cumT[k].r])
    c.memset(dtT.ap[0:32], 0.0, [dtT.r])
    for k in range(2):
        c.dma(cumT[k].ap[0:16], c.CUMd[k], [c.CUMr], [cumT[k].r])
        c.dma(dtT.ap[0:16], c.DTd[k], [c.DTr], [dtT.r])
        if CUT <= 1.2:
            continue
        for (srcT, kind) in ((cumT[k], 0), (dtT, 1)):
            ps = c.psum[4 + nps % 2]
            nps += 1
            for j in range(18):
                c.mm(ps.ap[:, j * 16:(j + 1) * 16], srcT.ap[0:32, j * 128:(j + 1) * 128], c.ident_f.ap[0:32, 0:16], True, True, [srcT.r, c.ident_f.r], [ps.r])
            v = ps.ap[:, 0:288].rearrange('p (j h) -> p j h', h=16)
            if CUT <= 1.25:
                continue
            if kind == 0:
                c.copy(ccol[k].ap, v, [ps.r], [ccol[k].r])
                if CUT > 1.3:
                    c.act(ncol[k].ap, v, AF.Identity, [ps.r], [ncol[k].r], scale=-1.0)
            else:
                c.copy(dtk.ap, v, [ps.r], [dtk.r])
        if CUT <= 1.4:
            continue
        for j in range(18):
            c.tt(xdt[k].ap[:, j, :].rearrange('p (h q) -> p h q', q=64), xtok.ap[:, j, :].rearrange('p (h q) -> p h q', q=64),
                 dtk.ap[:, j, :].to_broadcast([128, 16, 64]), ALU.mult, [xtok.r, dtk.r], [xdt[k].r])
    if CUT <= 2:
        c.stage_end()
        return
    c.S.flush(barrier=True)
    c.aoff = mark
    sel = c.sb([16, 128], F32)
    c.memset(sel.ap[0:32], 0.0, [sel.r])
    c.copy(sel.ap[0:16], c.ident_f.ap[0:16, 0:16].to_broadcast([16, 16, 128]), [c.ident_f.r], [sel.r])
    dcol = c.sb([16], F32)
    c.dma(dcol.ap, c.d_ssdd, [], [dcol.r])
    dI = c.sb([16, 128], BF16)
    for hd in range(16):
        c.ts(dI.ap[:, hd, :], c.ident_f.ap, dcol.ap[:, hd:hd + 1], None, ALU.mult, None, [c.ident_f.r, dcol.r], [dI.r])
    tri = [c.sb([128], F32) for _ in range(2)]
    c.dma(tri[0].ap, c.d_tri[0], [], [tri[0].r])
    c.dma(tri[1].ap, c.d_tri[1], [], [tri[1].r])
    Eb = [c.sb([128], F32) for _ in range(3)]
    Mb = [c.sb([128], BF16) for _ in range(3)]
    Db = c.sb([128], F32)
    ysb = [c.sb([4, 128], F32) for _ in range(2)]
    it = 0
    npc = 0
    if CUT <= 3:
        c.stage_end()
        return
    for g in range(4 if CUT > 4 else 1):
        for i in range(16 if CUT > 4 else 1):
            q0 = LC + i * 128
            accs = [c.psum[e] for e in range(4)]
            R = [c.psum[4], c.psum[5]]
            for d in range(2):
                for e in range(4):
                    c.mm(R[d].ap[:, e * 128:(e + 1) * 128], sel.ap[0:32, g * 4 + e, :], cumT[d].ap[0:32, q0:q0 + 128], True, True,
                         [sel.r, cumT[d].r], [R[d].r])
            started = [False] * 4
            for d in range(2):
                srcs = [0, 1] + ([2 + j for j in range(i + 1)] if d == 0 else [2 + j for j in range(i, 16)])
                for jt in srcs:
                    pc = c.psum[6 + npc % 2]
                    npc += 1
                    c.mm(pc.ap[:, 0:128], Bt.ap[:, g, jt * 128:(jt + 1) * 128], Ct.ap[:, g, q0:q0 + 128], True, True, [Bt.r, Ct.r], [pc.r])
                    diag = jt == 2 + i
                    for e in range(4):
                        hd = g * 4 + e
                        E_, M_ = Eb[(npc + e) % 3], Mb[(npc + e) % 3]
                        Rv = R[d].ap[:, e * 128:(e + 1) * 128]
                        if not diag:
                            c.act(E_.ap, Rv, AF.Exp, [R[d].r, ncol[d].r], [E_.r], bias=ncol[d].ap[:, jt, hd:hd + 1])
                        else:
                            c.ts(Db.ap, Rv, ccol[d].ap[:, jt, hd:hd + 1], 0.0, ALU.subtract, ALU.min, [R[d].r, ccol[d].r], [Db.r])
                            c.act(E_.ap, Db.ap, AF.Exp, [Db.r], [E_.r])
                            c.tt(E_.ap, E_.ap, tri[d].ap, ALU.mult, [E_.r, tri[d].r], [E_.r])
                        c.tt(M_.ap, E_.ap, pc.ap[:, 0:128], ALU.mult, [E_.r, pc.r], [M_.r])
                        c.mm(accs[e].ap[0:64, 0:128], xdt[d].ap[:, jt, hd * 64:(hd + 1) * 64], M_.ap, not started[e], False,
                             [xdt[d].r, M_.r], [accs[e].r])
                        started[e] = True
            for e in range(4):
                hd = g * 4 + e
                c.mm(accs[e].ap[0:64, 0:128], xtok.ap[:, 2 + i, hd * 64:(hd + 1) * 64], dI.ap[:, hd, :], False, True,
                     [xtok.r, dI.r], [accs[e].r])
            y_ = ysb[it % 2]
            for e in range(4):
                evac(c, e, y_.ap[0:64, e, :], accs[e].ap[0:64, 0:128], [accs[e].r], [y_.r])
            c.dma(c.YSd[g * 4:(g + 1) * 4, :, i * 128:(i + 1) * 128].rearrange('e p l -> p e l'), y_.ap[0:64], [y_.r], [c.YSr])
            it += 1
    c.stage_end()


def st_hyena(c):
    TWO_PI = 2 * math.pi
    hwo = c.sb([4096], F32); c.dma(hwo.ap[0:64], c.d_hywout, [], [hwo.r])
    hid0 = c.sb([L], F32)
    mark = c.aoff
    zT = c.sb([L], F32); c.memset(zT.ap[0:64], 0.0, [zT.r]); c.dma(zT.ap[0:33], c.d_hyz, [], [zT.r])
    w1 = c.sb([64], F32); c.memset(w1.ap[0:64], 0.0, [w1.r]); c.dma(w1.ap[0:33], c.d_hywin, [], [w1.r])
    wm = c.sb([2, 64], F32); c.dma(wm.ap[0:64], c.d_hywmid, [], [wm.r])
    fb_ = c.sb([4], F32); c.dma(fb_.ap[0:64], c.d_hyfb, [], [fb_.r])
    fq = c.sb([1], F32); bq = c.sb([3], F32)
    c.ts(fq.ap[0:64], fb_.ap[0:64, 0:1], 1.0 / TWO_PI, None, ALU.mult, None, [fb_.r], [fq.r])
    c.ts(bq.ap[0:64], fb_.ap[0:64, 1:4], fq.ap[0:64, 0:1], None, ALU.mult, None, [fb_.r, fq.r], [bq.r])
    hid = [hid0, c.sb([L], F32)]
    tA, tB = c.sb([512], F32), c.sb([512], F32)
    src = zT
    for l in range(3):
        dst = hid[l % 2]
        for b4 in range(4):
            ps = c.psum[b4 % 2]
            if l == 0:
                c.mm(ps.ap[0:64, :], w1.ap[0:64, :], zT.ap[0:64, b4 * 512:(b4 + 1) * 512], True, True, [w1.r, zT.r], [ps.r])
            else:
                c.mm(ps.ap[0:64, :], wm.ap[0:64, l - 1, :], src.ap[0:64, b4 * 512:(b4 + 1) * 512], True, True, [wm.r, src.r], [ps.r])
            c.ts(tA.ap[0:64], ps.ap[0:64, :], fq.ap[0:64, 0:1], bq.ap[0:64, l:l + 1], ALU.mult, ALU.add, [ps.r, fq.r, bq.r], [tA.r])
            rnd(c, tB.ap[0:64], tA.ap[0:64], [tA.r], [tB.r])
            c.tt(tA.ap[0:64], tA.ap[0:64], tB.ap[0:64], ALU.subtract, [tA.r, tB.r], [tA.r])
            c.act(dst.ap[0:64, b4 * 512:(b4 + 1) * 512], tA.ap[0:64], AF.Sin, [tA.r], [dst.r], scale=TWO_PI)
        src = dst
    h3 = src
    c.S.flush(barrier=True)
    c.aoff = mark
    CB_ = 256
    dec = c.sb([16, CB_], F32)
    fbt = c.sb([2, CB_], F32)
    Hs = c.sb([16, CB_], BF16); Hd = c.sb([16, CB_], BF16)
    Kre = c.sb([16, CB_], F32); Kim = c.sb([16, CB_], F32)
    Yre = c.sb([16, CB_], BF16); Yim = c.sb([16, CB_], BF16)
    tok = {k: c.sb([16, CB_], BF16) for k in ('x1', 'x2', 'v', 'z1', 'o')}
    tabs = [c.sb([16, 128], BF16) for _ in range(4)]
    tmpf = [c.sb([CB_], F32) for _ in range(6)]
    chm = [c.sb([L], BF16) for _ in range(2)]
    nt = 0
    npz = 0
    for cb in range(4):
        c.dma(dec.ap, c.d_decay[:, cb * CB_:(cb + 1) * CB_].rearrange('(tt p) q -> p tt q', p=128), [], [dec.r])
        c.dma(fbt.ap, c.d_hyfbias[:, :, cb * CB_:(cb + 1) * CB_], [], [fbt.r])
        for ki, key in enumerate(('x1', 'x2', 'v')):
            for ct in range(2):
                ch_ = chm[npz % 2]
                c.dma(ch_.ap, c.HYd[ki * 8 + cb * 2 + ct], [c.HYr], [ch_.r])
                for t4 in range(4):
                    ps = c.psum[6 + npz % 2]
                    npz += 1
                    pb = ps.ap.bitcast(BF16)
                    for jj in range(4):
                        tt_ = t4 * 4 + jj
                        c.transpose(pb[:, jj * 128:(jj + 1) * 128], ch_.ap[:, tt_ * 128:(tt_ + 1) * 128], c.ident_b.ap, [ch_.r, c.ident_b.r], [ps.r])
                    c.copy(tok[key].ap[:, t4 * 4:(t4 + 1) * 4, ct * 128:(ct + 1) * 128], pb[:, 0:512].rearrange('p (j q) -> p j q', q=128), [ps.r], [tok[key].r])
        for o in range(2):
            tin = tok['v'] if o == 0 else tok['z1']
            gate = tok['x1'] if o == 0 else tok['x2']
            tout = tok['z1'] if o == 0 else tok['o']
            for tt_ in range(16):
                pf, pb_ = c.psum[0], c.psum[1]
                for dr, p_ in ((0, pf), (1, pb_)):
                    col0 = o * 2048 + dr * 1024 + cb * CB_
                    c.mm(p_.ap[:, 0:CB_], h3.ap[0:64, tt_ * 128:(tt_ + 1) * 128], hwo.ap[0:64, col0:col0 + CB_], True, True, [h3.r, hwo.r], [p_.r])
                f_, b_ = tmpf[0], tmpf[1]
                c.tt(f_.ap, pf.ap[:, 0:CB_], dec.ap[:, tt_, :], ALU.mult, [pf.r, dec.r], [f_.r])
                c.tt(b_.ap, pb_.ap[:, 0:CB_], dec.ap[:, tt_, :], ALU.mult, [pb_.r, dec.r], [b_.r])
                if tt_ == 0:
                    c.memset(b_.ap[0:1, :], 0.0, [b_.r])
                c.tt(Hs.ap[:, tt_, :], f_.ap, b_.ap, ALU.add, [f_.r, b_.r], [Hs.r])
                c.tt(Hd.ap[:, tt_, :], f_.ap, b_.ap, ALU.subtract, [f_.r, b_.r], [Hd.r])
            for ft in range(16):
                tc_, ts_ = tabs[nt % 4], tabs[(nt + 1) % 4]
                nt += 2
                c.dma(tc_.ap, c.d_Cf[ft], [], [tc_.r])
                c.dma(ts_.ap, c.d_Sf[ft], [], [ts_.r])
                pr, pi = c.psum[0], c.psum[1]
                for tt_ in range(16):
                    c.mm(pr.ap[:, 0:CB_], tc_.ap[:, tt_, :], Hs.ap[:, tt_, :], tt_ == 0, tt_ == 15, [tc_.r, Hs.r], [pr.r])
                for tt_ in range(16):
                    c.mm(pi.ap[:, 0:CB_], ts_.ap[:, tt_, :], Hd.ap[:, tt_, :], tt_ == 0, tt_ == 15, [ts_.r, Hd.r], [pi.r])
                c.act(Kre.ap[:, ft, :], pr.ap[:, 0:CB_], AF.Identity, [pr.r], [Kre.r])
                c.act(Kim.ap[:, ft, :], pi.ap[:, 0:CB_], AF.Identity, [pi.r], [Kim.r])
                if ft == 0:
                    pn = c.psum[2]
                    for tt_ in range(16):
                        c.mm(pn.ap[:, 0:CB_], ts_.ap[:, tt_, :], Hs.ap[:, tt_, :], tt_ == 0, tt_ == 15, [ts_.r, Hs.r], [pn.r])
                    c.act(Kim.ap[0:1, 0, :], pn.ap[0:1, 0:CB_], AF.Identity, [pn.r], [Kim.r])
                vr, vi = c.psum[3], c.psum[4]
                for tt_ in range(16):
                    c.mm(vr.ap[:, 0:CB_], tc_.ap[:, tt_, :], tin.ap[:, tt_, :], tt_ == 0, tt_ == 15, [tc_.r, tin.r], [vr.r])
                for tt_ in range(16):
                    c.mm(vi.ap[:, 0:CB_], ts_.ap[:, tt_, :], tin.ap[:, tt_, :], tt_ == 0, tt_ == 15, [ts_.r, tin.r], [vi.r])
                a1, a2, a3, a4 = tmpf[2:6]
                c.tt(a1.ap, vr.ap[:, 0:CB_], Kre.ap[:, ft, :], ALU.mult, [vr.r, Kre.r], [a1.r])
                c.tt(a2.ap, vi.ap[:, 0:CB_], Kim.ap[:, ft, :], ALU.mult, [vi.r, Kim.r], [a2.r])
                c.tt(a3.ap, vr.ap[:, 0:CB_], Kim.ap[:, ft, :], ALU.mult, [vr.r, Kim.r], [a3.r])
                c.tt(a4.ap, vi.ap[:, 0:CB_], Kre.ap[:, ft, :], ALU.mult, [vi.r, Kre.r], [a4.r])
                c.tt(Yre.ap[:, ft, :], a1.ap, a2.ap, ALU.subtract, [a1.r, a2.r], [Yre.r])
                c.tt(Yim.ap[:, ft, :], a3.ap, a4.ap, ALU.add, [a3.r, a4.r], [Yim.r])
                if ft == 0:
                    c.copy(Yre.ap[0:1, 0, :], a1.ap[0:1, :], [a1.r], [Yre.r])
                    c.copy(Yim.ap[0:1, 0, :], a2.ap[0:1, :], [a2.r], [Yim.r])
            for tt_ in range(16):
                tc_, ts_ = tabs[nt % 4], tabs[(nt + 1) % 4]
                nt += 2
                c.dma(tc_.ap, c.d_Ci[tt_], [], [tc_.r])
                c.dma(ts_.ap, c.d_Si[tt_], [], [ts_.r])
                py = c.psum[5]
                for ft in range(16):
                    c.mm(py.ap[:, 0:CB_], tc_.ap[:, ft, :], Yre.ap[:, ft, :], ft == 0, False, [tc_.r, Yre.r], [py.r])
                for ft in range(16):
                    c.mm(py.ap[:, 0:CB_], ts_.ap[:, ft, :], Yim.ap[:, ft, :], False, ft == 15, [ts_.r, Yim.r], [py.r])
                e1 = tmpf[0]
                c.tt(e1.ap, tin.ap[:, tt_, :], fbt.ap[:, o, :], ALU.mult, [tin.r, fbt.r], [e1.r])
                c.tt(e1.ap, e1.ap, py.ap[:, 0:CB_], ALU.add, [e1.r, py.r], [e1.r])
                c.tt(tout.ap[:, tt_, :], e1.ap, gate.ap[:, tt_, :], ALU.mult, [e1.r, gate.r], [tout.r])
        for ct in range(2):
            ch_ = chm[npz % 2]
            for t4 in range(4):
                ps = c.psum[6 + npz % 2]
                npz += 1
                pb = ps.ap.bitcast(BF16)
                for jj in range(4):
                    tt_ = t4 * 4 + jj
                    c.transpose(pb[:, jj * 128:(jj + 1) * 128], tok['o'].ap[:, tt_, ct * 128:(ct + 1) * 128], c.ident_b.ap, [tok['o'].r, c.ident_b.r], [ps.r])
                c.copy(ch_.ap[:, t4 * 512:(t4 + 1) * 512], pb[:, 0:512], [ps.r], [ch_.r])
            c.dma(c.CATd[cb * 2 + ct, :, LC:T], ch_.ap, [ch_.r], [c.CATr[0]])
    c.stage_end()


def st_odd_out(c):
    wo = c.sb([16, 2048], BF16)
    for kt in range(16):
        c.dma(wo.ap[:, kt, :], c.d_wout[1][kt * 128:(kt + 1) * 128, :], [], [wo.r], q=POOL)
    ng = c.sb([8], F32); c.dma(ng.ap, c.d_ssdng, [], [ng.r])
    yb = c.sb([8, 512], F32); zb = c.sb([8, 512], BF16); sq = c.sb([8, 512], BF16)
    rs = c.sb([512], F32)
    cat = c.sb([16, 512], BF16)
    cr = [Reg() for _ in range(16)]
    xt = [c.sb([512], F32) for _ in range(4)]
    nx = 0
    for (t0, n) in TB[1:]:
        l0 = t0 - LC
        c.dma(yb.ap, c.YSd[:, :, l0:l0 + n].rearrange('h q t -> (h q) t').rearrange('(a p) t -> p a t', p=128), [c.YSr], [yb.r])
        c.dma(zb.ap, c.Zd[:, :, l0:l0 + n].rearrange('a p t -> p a t'), [c.Zr], [zb.r])
        c.dma(cat.ap[:, 0:8, :], c.CATd[0:8, :, t0:t0 + n].rearrange('a p t -> p a t'), [c.CATr[0]], cr[0:8])
        c.tt(yb.ap, yb.ap, zb.ap, ALU.mult, [yb.r, zb.r], [yb.r])
        c.act(sq.ap, yb.ap, AF.Square, [yb.r], [sq.r])
        ps = c.psum[0]
        for a in range(8):
            c.mm(ps.ap, c.ones_bf.ap, sq.ap[:, a, :], a == 0, a == 7, [c.ones_bf.r, sq.r], [ps.r])
        c.act(rs.ap, ps.ap, AF.Sqrt, [ps.r, c.eps_t.r], [rs.r], scale=1.0 / 1024, bias=c.eps_t.ap[:, 0:1])
        c.recip(rs.ap, rs.ap, [rs.r], [rs.r])
        for a in range(8):
            c.stt(cat.ap[:, 8 + a, :], yb.ap[:, a, :], ng.ap[:, a:a + 1], rs.ap, ALU.mult, ALU.mult, [yb.r, ng.r, rs.r], [cr[8 + a]])
        wout_block(c, 1, cat, cr, wo, xt, t0, n, 0, nx)
        nx += 16
    c.stage_end()
```

```python
import math
import numpy as np
import ml_dtypes
import concourse.bass as bass
import concourse.mybir as mybir
from concourse.bass_utils import run_bass_kernel_spmd

F32 = mybir.dt.float32
BF16 = mybir.dt.bfloat16
ALU = mybir.AluOpType
AF = mybir.ActivationFunctionType
AX = mybir.AxisListType

PE, DVE, ACT, POOL, SP = 0, 1, 2, 3, 4
NENG = 5
NDSEM = 12

D = 2048
T = 2304
LC = 256
L = 2048
DFF = 5632
NFT = DFF // 128
EPS = 1e-6
TB = [(0, 256), (256, 512), (768, 512), (1280, 512), (1792, 512)]
ARENA_BYTES = 196608
MAGIC = 12582912.0


class Reg:
    __slots__ = ('w', 'rs', 'excl')

    def __init__(self, excl=False):
        self.w = None
        self.rs = []
        self.excl = excl


class Op:
    __slots__ = ('eng', 'fn', 'deps', 'signal', 'sig', 'clock', 'dma', 'dsem', 'dval', 'waits')

    def __init__(self, eng, fn, dma):
        self.eng = eng
        self.fn = fn
        self.dma = dma
        self.deps = []
        self.signal = False
        self.sig = 0
        self.clock = None
        self.dsem = -1
        self.dval = 0
        self.waits = None


class Sched:
    def __init__(self, nc):
        self.nc = nc
        self.engs = [nc.tensor, nc.vector, nc.scalar, nc.gpsimd, nc.sync]
        self.esem = [nc.alloc_semaphore('es%d' % i) for i in range(NENG)]
        self.dsem = [[nc.alloc_semaphore('ds%d_%d' % (q, i)) for i in range(NDSEM)] for q in range(NENG)]
        self.ncomp = NENG + NENG * NDSEM
        self.pending = []
        self.sigcnt = [0] * NENG
        self.dcnt = [0] * NENG
        self.dlast = [[None] * NDSEM for _ in range(NENG)]
        self.clock = [[0] * self.ncomp for _ in range(NENG)]
        self.nops = 0
        self.nwaits = 0

    def op(self, eng, fn, reads=(), writes=(), dma=False):
        o = Op(eng, fn, dma)
        deps = o.deps
        for r in reads:
            if r.w is not None:
                deps.append((r.w, True))
            if r.excl:
                for x in r.rs:
                    if x.eng != eng:
                        deps.append((x, True))
            if dma:
                r.rs.append(o)
            else:
                rs = r.rs
                for i in range(len(rs)):
                    if (not rs[i].dma) and rs[i].eng == eng:
                        rs[i] = o
                        break
                else:
                    rs.append(o)
        for r in writes:
            if r.w is not None:
                deps.append((r.w, False))
            for x in r.rs:
                if x is not o:
                    deps.append((x, False))
            r.w = o
            r.rs = []
        self.pending.append(o)
        return o

    def flush(self, barrier=True):
        ops = self.pending
        self.pending = []
        for o in ops:
            for (d, raw) in o.deps:
                if d.dma:
                    continue
                if d.eng == o.eng and not o.dma and d.eng == PE:
                    continue
                d.signal = True
        if barrier:
            last = {}
            for o in ops:
                if not o.dma:
                    last[o.eng] = o
            for o in last.values():
                o.signal = True
        ncomp = self.ncomp
        for o in ops:
            e = o.eng
            ck = self.clock[e]
            waits = []
            if o.dma:
                i = self.dcnt[e]
                self.dcnt[e] += 1
                slot = i % NDSEM
                prev = self.dlast[e][slot]
                if prev is not None:
                    o.deps.append((prev, True))
                o.dsem = slot
                o.dval = 16 * (i // NDSEM + 1)
                self.dlast[e][slot] = o
            for (d, raw) in o.deps:
                if d.dma:
                    comp = NENG + d.eng * NDSEM + d.dsem
                    val = d.dval
                    sem = self.dsem[d.eng][d.dsem]
                else:
                    if d.eng == e and not o.dma and e == PE:
                        continue
                    if not d.signal:
                        continue
                    comp = d.eng
                    val = d.sig
                    sem = self.esem[d.eng]
                if ck[comp] >= val:
                    continue
                waits.append((sem, val))
                dc = d.clock
                if dc is not None:
                    for k in range(ncomp):
                        if dc[k] > ck[k]:
                            ck[k] = dc[k]
                if ck[comp] < val:
                    ck[comp] = val
            o.waits = waits
            if o.dma:
                o.clock = list(ck)
            elif o.signal:
                self.sigcnt[e] += 1
                o.sig = self.sigcnt[e]
                o.clock = list(ck)
            o.deps = None
        for o in ops:
            eng = self.engs[o.eng]
            for (sem, val) in o.waits:
                eng.wait_ge(sem, val)
                self.nwaits += 1
            ins = o.fn(eng)
            self.nops += 1
            if o.dma:
                ins.then_inc(self.dsem[o.eng][o.dsem], 16)
            elif o.signal:
                ins.then_inc(self.esem[o.eng], 1)
            o.fn = None
            o.waits = None
        if barrier:
            self.barrier()

    def barrier(self):
        for e in range(NENG):
            eng = self.engs[e]
            ck = self.clock[e]
            for f in range(NENG):
                if ck[f] < self.sigcnt[f]:
                    eng.wait_ge(self.esem[f], self.sigcnt[f])
                    ck[f] = self.sigcnt[f]
            for q in range(NENG):
                for s in range(NDSEM):
                    d = self.dlast[q][s]
                    if d is not None:
                        comp = NENG + q * NDSEM + s
                        if ck[comp] < d.dval:
                            eng.wait_ge(self.dsem[q][s], d.dval)
                            ck[comp] = d.dval


class Tl:
    __slots__ = ('ap', 'r')

    def __init__(self, ap, r=None):
        self.ap = ap
        self.r = r if r is not None else Reg()

    def __getitem__(self, k):
        return self.ap[k]


class Ctx:
    def __init__(self, dbg_in=(), dbg_out=()):
        self.nc = nc = bass.Bass("TRN2", target_bir_lowering=False)
        self.S = Sched(nc)
        self.dbg_in = set(dbg_in)
        self.dbg_out = set(dbg_out)
        self.arena = nc.alloc_sbuf_tensor('arena', [128, ARENA_BYTES // 4], F32)
        self.aoff = 0
        self.psum = [Tl(nc.alloc_psum_tensor('ps%d' % i, [128, 512], F32)[:], Reg(excl=True)) for i in range(8)]
        self.inputs = {}
        self.outputs = {}
        self.n_p = 0

    def dram(self, name, shape, dtype=F32, kind=None):
        if kind is None:
            kind = 'Internal'
            if name in self.dbg_in:
                kind = 'ExternalInput'
            elif name in self.dbg_out:
                kind = 'ExternalOutput'
        t = self.nc.dram_tensor(name, list(shape), dtype, kind=kind).ap()
        if kind == 'ExternalInput':
            self.inputs[name] = (tuple(shape), dtype)
        elif kind == 'ExternalOutput':
            self.outputs[name] = (tuple(shape), dtype)
        return t

    def sb(self, free_shape, dtype=F32):
        esz = 4 if dtype == F32 else 2
        n = 1
        for v in free_shape:
            n *= v
        nbytes = (n * esz + 31) // 32 * 32
        assert self.aoff + nbytes <= ARENA_BYTES, ('arena overflow', self.aoff, nbytes)
        a = self.arena[:, self.aoff // 4:(self.aoff + nbytes) // 4]
        self.aoff += nbytes
        if dtype != F32:
            a = a.bitcast(dtype)
        a = a[:, 0:n]
        if len(free_shape) == 2:
            a = a.rearrange('p (a b) -> p a b', b=free_shape[1])
        elif len(free_shape) == 3:
            a = a.rearrange('p (a b c) -> p a b c', b=free_shape[1], c=free_shape[2])
        return Tl(a)

    def persist(self, free_shape, dtype=F32):
        self.n_p += 1
        t = self.nc.alloc_sbuf_tensor('pp%d' % self.n_p, [128] + list(free_shape), dtype)
        return Tl(t[:])

    def stage_end(self):
        self.S.flush(barrier=True)
        self.aoff = 0

    def dma(self, out, in_, R, W, q=SP):
        self.S.op(q, lambda e: e.dma_start(out=out, in_=in_), reads=R, writes=W, dma=True)

    def mm(self, out, lhsT, rhs, start, stop, R, W):
        self.S.op(PE, lambda e: e.matmul(out, lhsT=lhsT, rhs=rhs, start=start, stop=stop), reads=R, writes=W)

    def act(self, out, in_, func, R, W, scale=1.0, bias=0.0, accum_out=None):
        if accum_out is None:
            self.S.op(ACT, lambda e: e.activation(out=out, in_=in_, func=func, bias=bias, scale=scale), reads=R, writes=W)
        else:
            self.S.op(ACT, lambda e: e.activation(out=out, in_=in_, func=func, bias=bias, scale=scale, accum_out=accum_out), reads=R, writes=W)

    def tt(self, out, in0, in1, op, R, W, eng=DVE):
        self.S.op(eng, lambda e: e.tensor_tensor(out=out, in0=in0, in1=in1, op=op), reads=R, writes=W)

    def ts(self, out, in0, s1, s2, op0, op1, R, W, eng=DVE):
        if s2 is None:
            self.S.op(eng, lambda e: e.tensor_scalar(out=out, in0=in0, scalar1=s1, scalar2=None, op0=op0), reads=R, writes=W)
        else:
            self.S.op(eng, lambda e: e.tensor_scalar(out=out, in0=in0, scalar1=s1, scalar2=s2, op0=op0, op1=op1), reads=R, writes=W)

    def stt(self, out, in0, scalar, in1, op0, op1, R, W, eng=DVE):
        self.S.op(eng, lambda e: e.scalar_tensor_tensor(out=out, in0=in0, scalar=scalar, in1=in1, op0=op0, op1=op1), reads=R, writes=W)

    def copy(self, out, in_, R, W, eng=DVE):
        self.S.op(eng, lambda e: e.tensor_copy(out=out, in_=in_), reads=R, writes=W)

    def memset(self, out, val, W, eng=DVE):
        self.S.op(eng, lambda e: e.memset(out, val), writes=W)

    def recip(self, out, in_, R, W):
        self.S.op(DVE, lambda e: e.reciprocal(out=out, in_=in_), reads=R, writes=W)

    def transpose(self, out, in_, ident, R, W):
        self.S.op(PE, lambda e: e.transpose(out, in_, ident), reads=R, writes=W)


def st_consts(c):
    c.ones_bf = c.persist([128], BF16)
    c.memset(c.ones_bf.ap, 1.0, [c.ones_bf.r])
    c.ident_f = c.persist([128], F32)
    c.memset(c.ident_f.ap, 0.0, [c.ident_f.r], eng=POOL)
    idf = c.ident_f
    c.S.op(POOL, lambda e: e.affine_select(out=idf.ap, in_=idf.ap, pattern=[[-1, 128]], compare_op=ALU.not_equal,
                                           fill=1.0, base=0, channel_multiplier=1), reads=[idf.r], writes=[idf.r])
    c.ident_b = c.persist([128], BF16)
    c.copy(c.ident_b.ap, c.ident_f.ap, [c.ident_f.r], [c.ident_b.r])
    c.eps_t = c.persist([1], F32)
    c.memset(c.eps_t.ap, EPS, [c.eps_t.r])


def st_mod(c, layer):
    M = c.persist([2, 9, 16], F32)
    A = c.persist([2, 3, 16], F32)
    G = c.persist([2, 3, 16], F32)
    c.M[layer], c.A[layer], c.G[layer] = M, A, G
    sc = c.sb([16, 2], F32)
    sg = c.sb([16, 2], F32)
    c.dma(sc.ap, c.d_cc, [], [sc.r])
    c.act(sg.ap, sc.ap, AF.Sigmoid, [sc.r], [sg.r])
    c.tt(sc.ap, sc.ap, sg.ap, ALU.mult, [sc.r, sg.r], [sc.r])
    mb = c.sb([144], F32)
    c.dma(mb.ap, c.d_modb[layer], [], [mb.r])
    ng = c.sb([3, 16], F32)
    c.dma(ng.ap, c.d_normg[layer], [], [ng.r])
    NB = 3
    wbuf = [c.sb([16, 512], F32) for _ in range(NB)]
    ps = c.psum[0]
    wsrc = c.d_modw[layer].rearrange('(kt p) f -> p kt f', p=128)
    for blk in range(36):
        wb = wbuf[blk % NB]
        c.dma(wb.ap, wsrc[:, :, blk * 512:(blk + 1) * 512], [], [wb.r])
        for j in range(4):
            ft = blk * 4 + j
            for kt in range(16):
                c.mm(ps.ap[:, 2 * ft:2 * ft + 2], wb.ap[:, kt, j * 128:(j + 1) * 128], sc.ap[:, kt, :],
                     kt == 0, kt == 15, [wb.r, sc.r], [ps.r])
    for g in range(2):
        src = ps.ap[:, 0:288].rearrange('p (f g) -> p g f', g=2)[:, g, :]
        c.tt(M.ap[:, g].rearrange('p i d -> p (i d)'), src, mb.ap, ALU.add, [ps.r, mb.r], [M.r])
    for g in range(2):
        for i in range(3):
            c.stt(A.ap[:, g, i, :], M.ap[:, g, 3 * i + 1, :], 1.0, ng.ap[:, i, :], ALU.add, ALU.mult, [M.r, ng.r], [A.r])
            fac = 1.0 if i == 1 else 0.5
            c.ts(G.ap[:, g, i, :], M.ap[:, g, 3 * i + 2, :], fac, None, ALU.mult, None, [M.r], [G.r])
    c.stage_end()


def bidx(t0):
    return [b[0] for b in TB].index(t0)


def xr_col(c, t0):
    return [c.XRr[dt][bidx(t0)] for dt in range(16)]


def st_norm(c, layer, i, blocks=TB):
    H = c.sb([16, T], BF16)
    mark = c.aoff
    A, M = c.A[layer], c.M[layer]
    xt = [c.sb([16, 512], F32) for _ in range(2)]
    xrg = [[Reg() for _ in range(16)] for _ in range(2)]
    sq = [c.sb([16, 512], BF16) for _ in range(2)]
    rs = [c.sb([512], F32) for _ in range(2)]
    for bi, (t0, n) in enumerate(blocks):
        g = 1 if t0 < LC else 0
        x_, s_, r_ = xt[bi % 2], sq[bi % 2], rs[bi % 2]
        xr_ = xrg[bi % 2]
        ps = c.psum[6 + bi % 2]
        c.dma(x_.ap[:, :, 0:n], c.XR[:, :, t0:t0 + n].rearrange('d p t -> p d t'), xr_col(c, t0), xr_)
        c.act(s_.ap[:, :, 0:n], x_.ap[:, :, 0:n], AF.Square, xr_, [s_.r])
        for dt in range(16):
            c.mm(ps.ap[:, 0:n], c.ones_bf.ap, s_.ap[:, dt, 0:n], dt == 0, dt == 15, [s_.r, c.ones_bf.r], [ps.r])
        c.act(r_.ap[:, 0:n], ps.ap[:, 0:n], AF.Sqrt, [ps.r, c.eps_t.r], [r_.r], scale=1.0 / D, bias=c.eps_t.ap[:, 0:1])
        c.recip(r_.ap[:, 0:n], r_.ap[:, 0:n], [r_.r], [r_.r])
        for dt in range(16):
            c.stt(x_.ap[:, dt, 0:n], x_.ap[:, dt, 0:n], A.ap[:, g, i, dt:dt + 1], r_.ap[:, 0:n], ALU.mult, ALU.mult,
                  [xr_[dt], A.r, r_.r], [xr_[dt]])
            c.act(H.ap[:, dt, t0:t0 + n], x_.ap[:, dt, 0:n], AF.Identity, [xr_[dt], M.r], [H.r],
                  bias=M.ap[:, g, 3 * i, dt:dt + 1])
    c.S.flush(barrier=True)
    c.aoff = mark
    return H


FCH = [6, 6, 6, 6, 5, 5, 5, 5]


def st_ffn(c, layer, j, i, blocks=TB):
    H = st_norm(c, layer, i, blocks)
    G = c.G[layer]
    wg_src = c.d_wg[layer, j]
    wu_src = c.d_wu[layer, j]
    wd_src = c.d_wd[layer, j].rearrange('(ft p) d -> ft p d', p=128)
    NW = 3
    wgb = [c.sb([16, 128], BF16) for _ in range(NW)]
    wub = [c.sb([16, 128], BF16) for _ in range(NW)]
    wdb = [c.sb([2048], BF16) for _ in range(12)]
    hid = [c.sb([T], BF16) for _ in range(6)]
    sgt = [c.sb([512], F32) for _ in range(2)]
    xt = [c.sb([512], F32) for _ in range(6)]
    f0 = 0
    nup = 0
    nx = 0
    nwd = 0
    nfc = 0
    for ch, nf in enumerate(FCH):
        wd_tiles = []
        for k in range(nf):
            f = f0 + k
            wg_, wu_ = wgb[nfc % NW], wub[nfc % NW]
            nfc += 1
            wd_ = wdb[nwd % 12]
            nwd += 1
            c.dma(wg_.ap, wg_src[f], [], [wg_.r], q=POOL)
            c.dma(wu_.ap, wu_src[f], [], [wu_.r], q=POOL)
            c.dma(wd_.ap, wd_src[f], [], [wd_.r], q=POOL)
            wd_tiles.append(wd_)
            hk = hid[k]
            for bi, (t0, n) in enumerate(blocks):
                pg, pu = c.psum[2 * (nup % 2)], c.psum[2 * (nup % 2) + 1]
                for kt in range(16):
                    c.mm(pg.ap[:, 0:n], wg_.ap[:, kt, :], H.ap[:, kt, t0:t0 + n], kt == 0, kt == 15, [wg_.r, H.r], [pg.r])
                for kt in range(16):
                    c.mm(pu.ap[:, 0:n], wu_.ap[:, kt, :], H.ap[:, kt, t0:t0 + n], kt == 0, kt == 15, [wu_.r, H.r], [pu.r])
                s_ = sgt[nup % 2]
                c.act(s_.ap[:, 0:n], pg.ap[:, 0:n], AF.Silu, [pg.r], [s_.r])
                c.tt(hk.ap[:, t0:t0 + n], s_.ap[:, 0:n], pu.ap[:, 0:n], ALU.mult, [s_.r, pu.r], [hk.r])
                nup += 1
        tiles = [(dt, t0, n) for dt in range(16) for (t0, n) in blocks]
        PF = 4

        def issue_load(idx):
            dt, t0, n = tiles[idx]
            x_ = xt[(nx + idx) % 6]
            c.dma(x_.ap[:, 0:n], c.XR[dt, :, t0:t0 + n], [c.XRr[dt][bidx(t0)]], [x_.r])
        for idx in range(min(PF, len(tiles))):
            issue_load(idx)
        for idx, (dt, t0, n) in enumerate(tiles):
            if idx + PF < len(tiles):
                issue_load(idx + PF)
            g = 1 if t0 < LC else 0
            po = c.psum[4 + (nx + idx) % 2]
            x_ = xt[(nx + idx) % 6]
            xreg = c.XRr[dt][bidx(t0)]
            for k in range(nf):
                c.mm(po.ap[:, 0:n], wd_tiles[k].ap[:, dt * 128:(dt + 1) * 128], hid[k].ap[:, t0:t0 + n],
                     k == 0, k == nf - 1, [wd_tiles[k].r, hid[k].r], [po.r])
            c.stt(x_.ap[:, 0:n], po.ap[:, 0:n], G.ap[:, g, i, dt:dt + 1], x_.ap[:, 0:n], ALU.mult, ALU.add,
                  [po.r, G.r, x_.r], [x_.r])
            c.dma(c.XR[dt, :, t0:t0 + n], x_.ap[:, 0:n], [x_.r], [xreg], q=ACT)
        nx += len(tiles)
        f0 += nf
    c.stage_end()


def st_load_x(c):
    for dt in range(16):
        c.dma(c.XR[dt], c.d_xin[dt], [], c.XRr[dt])
    c.stage_end()


def st_final(c):
    fg = c.sb([16], F32)
    c.dma(fg.ap, c.d_finalg, [], [fg.r])
    xt = [c.sb([16, 512], F32) for _ in range(2)]
    sq = [c.sb([16, 512], BF16) for _ in range(2)]
    rs = [c.sb([512], F32) for _ in range(2)]
    for bi, (t0, n) in enumerate(TB[1:]):
        x_, s_, r_ = xt[bi % 2], sq[bi % 2], rs[bi % 2]
        ps = c.psum[bi % 2]
        c.dma(x_.ap, c.XR[:, :, t0:t0 + n].rearrange('d p t -> p d t'), xr_col(c, t0), [x_.r])
        c.act(s_.ap, x_.ap, AF.Square, [x_.r], [s_.r])
        for dt in range(16):
            c.mm(ps.ap, c.ones_bf.ap, s_.ap[:, dt, :], dt == 0, dt == 15, [s_.r, c.ones_bf.r], [ps.r])
        c.act(r_.ap, ps.ap, AF.Sqrt, [ps.r, c.eps_t.r], [r_.r], scale=1.0 / D, bias=c.eps_t.ap[:, 0:1])
        c.recip(r_.ap, r_.ap, [r_.r], [r_.r])
        for dt in range(16):
            c.stt(x_.ap[:, dt, :], x_.ap[:, dt, :], fg.ap[:, dt:dt + 1], r_.ap, ALU.mult, ALU.mult, [x_.r, fg.r, r_.r], [x_.r])
        c.dma(c.d_out[:, :, t0 - LC:t0 - LC + n].rearrange('d p t -> p d t'), x_.ap, [x_.r], [c.outr])
    c.stage_end()


def declare_io(c):
    c.d_xin = c.dram('xin', [16, 128, T], F32, 'ExternalInput')
    c.d_cc = c.dram('cc', [128, 16, 2], F32, 'ExternalInput')
    c.d_modw = c.dram('mod_w', [2, D, 9 * D], F32, 'ExternalInput')
    c.d_modb = c.dram('mod_b', [2, 128, 144], F32, 'ExternalInput')
    c.d_normg = c.dram('norm_g', [2, 128, 3, 16], F32, 'ExternalInput')
    c.d_finalg = c.dram('final_g', [128, 16], F32, 'ExternalInput')
    c.d_wg = c.dram('ffn_wg', [2, 2, NFT, 128, 16, 128], F32, 'ExternalInput')
    c.d_wu = c.dram('ffn_wu', [2, 2, NFT, 128, 16, 128], F32, 'ExternalInput')
    c.d_wd = c.dram('ffn_wd', [2, 2, DFF, D], F32, 'ExternalInput')
    EI = 'ExternalInput'
    c.d_evwin_t = c.dram('ev_w_in_t', [24, 128, 16, 128], F32, EI)
    c.d_evwin = c.dram('ev_w_in', [D, 4096], F32, EI)
    c.d_rpbt = c.dram('rpbt', [64, 8, 15, 64], F32, EI)
    c.d_namask = c.dram('namask', [64, 64], F32, EI)
    c.d_s5are = c.dram('s5are', [128, 64], F32, EI)
    c.d_s5aim = c.dram('s5aim', [128, 64], F32, EI)
    c.d_s5ldt = c.dram('s5ldt', [128, 64], F32, EI)
    c.d_s5d = c.dram('s5d', [128, 32], F32, EI)
    c.d_iota1 = c.dram('iota1', [128, 512], F32, EI)
    c.d_s5bre = c.dram('s5bre', [128, 64, 32], F32, EI)
    c.d_s5bim = c.dram('s5bim', [128, 64, 32], F32, EI)
    c.d_s5cre = c.dram('s5cre', [128, 64, 32], F32, EI)
    c.d_s5cim = c.dram('s5cim', [128, 64, 32], F32, EI)
    c.d_gluw = c.dram('glu_w', [1024, 1024], F32, EI)
    c.d_glub = c.dram('glu_b', [128, 8], F32, EI)
    c.d_wout = [c.dram('ev_w_out', [D, D], F32, EI), c.dram('od_w_out', [D, D], F32, EI)]
    c.d_odwin_t = c.dram('od_w_in_t', [48, 128, 16, 128], F32, EI)
    c.d_odwdt = c.dram('od_w_dt', [128, 2, 16, 16], F32, EI)
    c.d_hysw = c.dram('hysw', [128, 24, 3], F32, EI)
    c.d_hysb = c.dram('hysb', [128, 24], F32, EI)
    c.d_ssdcw = c.dram('ssdcw', [128, 16, 3], F32, EI)
    c.d_ssdcb = c.dram('ssdcb', [128, 16], F32, EI)
    c.d_dtbias = c.dram('dtbias', [16, 2], F32, EI)
    c.d_alog = c.dram('alog', [16, 2], F32, EI)
    c.d_ssdd = c.dram('ssdd', [128, 16], F32, EI)
    c.d_tri = c.dram('tri', [2, 128, 128], F32, EI)
    c.d_ssdng = c.dram('ssdng', [128, 8], F32, EI)
    c.d_hyz = c.dram('hyz', [33, L], F32, EI)
    c.d_hywin = c.dram('hywin', [33, 64], F32, EI)
    c.d_hywmid = c.dram('hywmid', [64, 2, 64], F32, EI)
    c.d_hyfb = c.dram('hyfb', [64, 4], F32, EI)
    c.d_hywout = c.dram('hywout', [64, 4096], F32, EI)
    c.d_decay = c.dram('decay', [L, 1024], F32, EI)
    c.d_hyfbias = c.dram('hyfbias', [128, 2, 1024], F32, EI)
    c.d_Cf = c.dram('Cf', [16, 128, 16, 128], BF16, EI)
    c.d_Sf = c.dram('Sf', [16, 128, 16, 128], BF16, EI)
    c.d_Ci = c.dram('Ci', [16, 128, 16, 128], BF16, EI)
    c.d_Si = c.dram('Si', [16, 128, 16, 128], BF16, EI)
    c.HYd = c.dram('HYd', [24, 128, L], BF16)
    c.Zd = c.dram('Zd', [8, 128, L], BF16)
    c.XBCd = c.dram('XBCd', [16, 128, T], BF16)
    c.DTd = c.dram('DTd', [2, 16, T], F32)
    c.CUMd = c.dram('CUMd', [2, 16, T], F32)
    c.YSd = c.dram('YSd', [16, 64, L], F32)
    c.HYr, c.Zr, c.XBCr, c.DTr, c.CUMr, c.YSr = Reg(), Reg(), Reg(), Reg(), Reg(), Reg()
    c.Ud = c.dram('Ud', [8, 128, T], BF16)
    c.Qd = c.dram('Qd', [8, 128, T], BF16)
    c.Kd = c.dram('Kd', [8, 128, T], BF16)
    c.Vd = c.dram('Vd', [18, 128, 1024], BF16)
    c.Yd = c.dram('Yd', [1024, T], F32)
    c.CATd = c.dram('CATd', [16, 128, T], BF16)
    c.Ur, c.Qr, c.Kr, c.Vr, c.Yr = Reg(), Reg(), Reg(), Reg(), Reg()
    c.CATr = [Reg(), Reg()]
    c.d_out = c.dram('out', [16, 128, L], F32, 'ExternalOutput')
    c.outr = Reg()
    c.XR = c.dram('XR', [16, 128, T], F32)
    c.XRr = [[Reg() for _ in range(len(TB))] for _ in range(16)]
    c.M, c.A, c.G = {}, {}, {}


def build(stages=None, dbg_in=(), dbg_out=()):
    c = Ctx(dbg_in, dbg_out)
    declare_io(c)
    st_consts(c)
    allst = stages is None
    if allst or 'load' in stages:
        st_load_x(c)
    for layer in range(2):
        if allst or ('mod%d' % layer) in stages:
            st_mod(c, layer)
        if allst or ('ffa%d' % layer) in stages:
            st_ffn(c, layer, 0, 0)
        if layer == 0 and (allst or 'evmix' in stages):
            sub = stages if (stages and any(k.startswith('ev_') for k in stages)) else None
            if sub is None or 'ev_proj' in sub:
                H = st_norm(c, 0, 1)
                st_even_proj(c, H)
            if sub is None or 'ev_na' in sub:
                st_na(c, True)
            if sub is None or 'ev_s5' in sub:
                st_s5(c, True)
            if sub is None or 'ev_glu' in sub:
                st_glu_wout(c, 0, True)
        if layer == 1 and (allst or 'odmix' in stages):
            sub = stages if (stages and any(k.startswith('od_') for k in stages)) else None
            if sub is None or 'od_proj' in sub:
                H = st_norm(c, 1, 1)
                st_odd_proj(c, H)
            if sub is None or 'od_ssd' in sub:
                st_ssd(c)
            if sub is None or 'od_hy' in sub:
                st_hyena(c)
            if sub is None or 'od_out' in sub:
                st_odd_out(c)
        if allst or ('ffb%d' % layer) in stages:
            st_ffn(c, layer, 1, 2, TB if layer == 0 else TB[1:])
    if allst or 'final' in stages:
        st_final(c)
    c.stage_end()
    return c


def host_prep(inp, b):
    f = np.float32
    m = {}
    xc = np.concatenate([inp['ctx'][b], inp['x'][b]], axis=0)
    m['xin'] = np.ascontiguousarray(xc.T.reshape(16, 128, T))
    cc = np.stack([inp['c'][b], inp['c_ctx']], axis=-1)
    m['cc'] = np.ascontiguousarray(cc.reshape(16, 128, 2).transpose(1, 0, 2))
    return m


_SHARED = {}


def host_shared(inp):
    m = {}
    m['mod_w'] = np.ascontiguousarray(inp['mod_w'])
    m['mod_b'] = np.ascontiguousarray(inp['mod_b'].reshape(2, 144, 128).transpose(0, 2, 1))
    m['norm_g'] = np.ascontiguousarray(inp['norm_g'].reshape(2, 3, 16, 128).transpose(0, 3, 1, 2))
    m['final_g'] = np.ascontiguousarray(inp['final_g'].reshape(16, 128).T)
    for k in ('ffn_wg', 'ffn_wu'):
        m[k] = np.ascontiguousarray(inp[k].reshape(2, 2, 16, 128, NFT, 128).transpose(0, 1, 4, 3, 2, 5))
    m['ffn_wd'] = np.ascontiguousarray(inp['ffn_wd'])
    w = inp['ev_w_in'][0]
    m['ev_w_in'] = np.ascontiguousarray(w)
    m['ev_w_in_t'] = np.ascontiguousarray(w[:, :3072].reshape(16, 128, 24, 128).transpose(2, 1, 0, 3))
    col = np.arange(64)
    dc = np.clip(col[:, None] - col[None, :] + 15, 0, 30)
    rp = inp['na_rpb'][0][:, :, dc]
    m['rpbt'] = np.ascontiguousarray(rp.transpose(2, 0, 1, 3))
    cs = np.clip(col - 8, 0, 48)
    ok = (col[:, None] >= cs[None, :]) & (col[:, None] < cs[None, :] + 16)
    m['namask'] = np.where(ok, 0.0, NEGM).astype(np.float32)

    def st_lay(a):
        return np.ascontiguousarray(a.reshape(2, 32, 2, 64).transpose(2, 3, 0, 1).reshape(128, 64))
    m['s5are'] = st_lay(inp['s5_a_re'][0])
    m['s5aim'] = st_lay(inp['s5_a_im'][0])
    m['s5ldt'] = st_lay(np.repeat(inp['s5_log_dt'][0][:, :, None], 64, axis=2))
    dd = np.zeros((128, 32), np.float32)
    dd[0:32] = inp['s5_d'][0].reshape(32, 32).T
    m['s5d'] = dd
    m['iota1'] = np.ascontiguousarray(np.broadcast_to(np.arange(1, 513, dtype=np.float32), (128, 512)))

    def b_blk(b):
        o = np.zeros((128, 2, 32, 32), np.float32)
        bb = b.reshape(2, 32, 2, 64, 16)
        o[0:64, :, :, 0:16] = bb[:, :, 0].transpose(2, 0, 1, 3)
        o[64:128, :, :, 16:32] = bb[:, :, 1].transpose(2, 0, 1, 3)
        return o.reshape(128, 64, 32)

    def c_blk(cm):
        o = np.zeros((128, 2, 32, 32), np.float32)
        cc = cm.reshape(2, 32, 2, 16, 64)
        o[0:64, :, :, 0:16] = cc[:, :, 0].transpose(3, 0, 1, 2)
        o[64:128, :, :, 16:32] = cc[:, :, 1].transpose(3, 0, 1, 2)
        return o.reshape(128, 64, 32)
    m['s5bre'] = b_blk(inp['s5_b_re'][0])
    m['s5bim'] = b_blk(inp['s5_b_im'][0])
    m['s5cre'] = c_blk(inp['s5_c_re'][0])
    m['s5cim'] = c_blk(inp['s5_c_im'][0])
    m['glu_w'] = np.ascontiguousarray(inp['s5_glu_w'][0])
    m['glu_b'] = np.ascontiguousarray(inp['s5_glu_b'][0].reshape(8, 128).T)
    m['ev_w_out'] = np.ascontiguousarray(inp['ev_w_out'][0])
    m['od_w_out'] = np.ascontiguousarray(inp['od_w_out'][0])
    w = inp['od_w_in'][0]
    m['od_w_in_t'] = np.ascontiguousarray(w[:, :6144].reshape(16, 128, 48, 128).transpose(2, 1, 0, 3))
    m['od_w_dt'] = np.ascontiguousarray(w[:, 6144:6176].reshape(16, 128, 2, 16).transpose(1, 2, 0, 3))
    m['hysw'] = np.ascontiguousarray(inp['hy_short_w'][0].T.reshape(24, 128, 3).transpose(1, 0, 2))
    m['hysb'] = np.ascontiguousarray(inp['hy_short_b'][0].reshape(24, 128).T)
    m['ssdcw'] = np.ascontiguousarray(inp['ssd_conv_w'][0].T.reshape(16, 128, 3).transpose(1, 0, 2))
    m['ssdcb'] = np.ascontiguousarray(inp['ssd_conv_b'][0].reshape(16, 128).T)
    m['dtbias'] = np.ascontiguousarray(inp['ssd_dt_bias'][0].T)
    m['alog'] = np.ascontiguousarray(inp['ssd_a_log'][0].T)
    m['ssdd'] = np.ascontiguousarray(np.broadcast_to(inp['ssd_d'][0][None, :], (128, 16)))
    ii = np.arange(128)
    m['tri'] = np.stack([(ii[:, None] <= ii[None, :]), (ii[:, None] >= ii[None, :])]).astype(np.float32)
    m['ssdng'] = np.ascontiguousarray(inp['ssd_norm_g'][0].reshape(8, 128).T)
    m['hywin'] = np.ascontiguousarray(inp['hy_w_in'][0])
    m['hywmid'] = np.ascontiguousarray(inp['hy_w_mid'][0].transpose(1, 0, 2))
    m['hyfb'] = np.ascontiguousarray(np.stack([inp['hy_freq'][0], inp['hy_b_in'][0], inp['hy_b_mid'][0][0], inp['hy_b_mid'][0][1]], axis=1))
    m['hywout'] = np.ascontiguousarray(inp['hy_w_out'][0])
    m['hyfbias'] = np.ascontiguousarray(np.broadcast_to(inp['hy_fbias'][0][None], (128, 2, 1024)))
    m.update(hy_consts())
    return m


_HYC = {}


def hy_consts():
    if _HYC:
        return _HYC
    f32 = np.float32
    t = np.linspace(0.0, 1.0, L, dtype=f32)[:, None]
    w = (2.0 * math.pi * np.arange(L, dtype=f32)[:, None] / L).astype(f32)
    f = np.linspace(1e-4, 15, 16, dtype=f32)[None, :]
    z = np.concatenate([t, np.cos(f * w), -np.sin(f * w)], axis=-1).astype(f32)
    _HYC['hyz'] = np.ascontiguousarray(z.T)
    mx = math.log(1e-2) / 0.3
    mn = math.log(1e-2) / 1.5
    deltas = np.abs(np.linspace(mn, mx, 1024, dtype=f32))
    _HYC['decay'] = np.exp(-t * deltas[None, :]).astype(f32)
    n = np.arange(L, dtype=np.int64)
    ang = 2.0 * np.pi * ((n[:, None] * n[None, :]) % 4096).astype(np.float64) / 4096.0
    Cf = np.cos(ang)
    Sf = -np.sin(ang)
    sgn = np.where(n % 2 == 0, 1.0, -1.0)
    Sf[:, 0] = sgn
    wf = np.full(L, 2.0 / 4096.0)
    wf[0] = 1.0 / 4096.0
    Ci = wf[:, None] * np.cos(ang)
    Si = -wf[:, None] * np.sin(ang)
    Si[0, :] = sgn / 4096.0

    def lay(a):
        return np.ascontiguousarray(a.reshape(16, 128, 16, 128).transpose(2, 1, 0, 3).astype(f32).astype(ml_dtypes.bfloat16))
    _HYC['Cf'], _HYC['Sf'], _HYC['Ci'], _HYC['Si'] = lay(Cf), lay(Sf), lay(Ci), lay(Si)
    return _HYC


def kernel(**inputs):
    inp = {k: np.asarray(v) for k, v in inputs.items()}
    c = build()
    shared = host_shared(inp)
    in_maps = []
    for b in range(8):
        m = dict(shared)
        m.update(host_prep(inp, b))
        in_maps.append({k: m[k] for k in c.inputs})
    res = run_bass_kernel_spmd(c.nc, in_maps, core_ids=list(range(8)))
    outs = []
    for b in range(8):
        o = np.asarray(res.results[b]['out'])
        outs.append(o.reshape(D, L).T)
    return np.ascontiguousarray(np.stack(outs, axis=0)).astype(np.float32)


SQ128 = math.sqrt(128.0)
NEGM = -30000.0


def evac(c, n, out, in_, R, W):
    if n % 2 == 0:
        c.act(out, in_, AF.Identity, R, W)
    else:
        c.copy(out, in_, R, W)


def st_even_proj(c, H):
    wsrc = c.d_evwin_t
    wb = [c.sb([16, 128], BF16) for _ in range(3)]
    ob = [c.sb([512], BF16) for _ in range(4)]
    dst = [c.Ud, c.Qd, c.Kd]
    dreg = [c.Ur, c.Qr, c.Kr]
    cnt = 0
    for f in range(24):
        w_ = wb[f % 3]
        c.dma(w_.ap, wsrc[f], [], [w_.r], q=POOL)
        for bi, (t0, n) in enumerate(TB):
            ps = c.psum[cnt % 4]
            o_ = ob[cnt % 4]
            for kt in range(16):
                c.mm(ps.ap[:, 0:n], w_.ap[:, kt, :], H.ap[:, kt, t0:t0 + n], kt == 0, kt == 15, [w_.r, H.r], [ps.r])
            evac(c, cnt, o_.ap[:, 0:n], ps.ap[:, 0:n], [ps.r], [o_.r])
            c.dma(dst[f // 8][f % 8, :, t0:t0 + n], o_.ap[:, 0:n], [o_.r], [dreg[f // 8]])
            cnt += 1
    vsrc = c.d_evwin.rearrange('(kt p) f -> p kt f', p=128)
    vw = [c.sb([16, 512], BF16) for _ in range(2)]
    for j in range(2):
        for kt in range(16):
            c.dma(vw[j].ap[:, kt, :], vsrc[:, kt, 3072 + 512 * j:3072 + 512 * (j + 1)], [], [vw[j].r], q=POOL)
    for tt_ in range(18):
        for j in range(2):
            ps = c.psum[cnt % 4]
            o_ = ob[cnt % 4]
            for kt in range(16):
                c.mm(ps.ap, H.ap[:, kt, tt_ * 128:(tt_ + 1) * 128], vw[j].ap[:, kt, :], kt == 0, kt == 15, [vw[j].r, H.r], [ps.r])
            evac(c, cnt, o_.ap, ps.ap, [ps.r], [o_.r])
            c.dma(c.Vd[tt_, :, 512 * j:512 * (j + 1)], o_.ap, [o_.r], [c.Vr])
            cnt += 1
    c.stage_end()


def st_na(c, with_ctx=True):
    Tb = c.sb([8, 15, 64], F32)
    mk = c.sb([64], F32)
    c.dma(Tb.ap[0:64], c.d_rpbt, [], [Tb.r])
    c.dma(mk.ap[0:64], c.d_namask, [], [mk.r])
    c.stt(Tb.ap[0:64].rearrange('p h d q -> p (h d) q'), Tb.ap[0:64].rearrange('p h d q -> p (h d) q'), SQ128,
          mk.ap[0:64, None, :].to_broadcast([64, 120, 64]), ALU.mult, ALU.add, [Tb.r, mk.r], [Tb.r])
    neg = c.sb([64], F32)
    c.memset(neg.ap, NEGM, [neg.r])
    sel = c.sb([2, 128], F32)
    c.memset(sel.ap, 0.0, [sel.r])
    c.copy(sel.ap[0:64, 0, 0:64], c.ident_f.ap[0:64, 0:64], [c.ident_f.r, sel.r], [sel.r])
    c.copy(sel.ap[0:64, 1, 64:128], c.ident_f.ap[0:64, 0:64], [c.ident_f.r, sel.r], [sel.r])
    qb = [c.sb([T], BF16) for _ in range(2)]
    kb = [c.sb([T], BF16) for _ in range(2)]
    vb = [c.sb([18, 128], BF16) for _ in range(2)]
    ob = [c.sb([T], BF16) for _ in range(2)]
    eb = [c.sb([7, 64], BF16) for _ in range(3)]
    ec = c.sb([2, 256], BF16)
    rz = [c.sb([64], F32) for _ in range(2)]
    rzc = c.sb([256], F32)
    sc = 1.0 / SQ128
    it = 0
    for h in range(8):
        q_, k_, v_, o_ = qb[h % 2], kb[h % 2], vb[h % 2], ob[h % 2]
        c.dma(q_.ap, c.Qd[h], [c.Qr], [q_.r])
        c.dma(k_.ap, c.Kd[h], [c.Kr], [k_.r])
        c.dma(v_.ap, c.Vd[:, :, h * 128:(h + 1) * 128].rearrange('t p d -> p t d'), [c.Vr], [v_.r])
        if with_ctx:
            ps, po = c.psum[4], c.psum[5]
            for i in range(2):
                c.mm(ps.ap[:, i * 256:(i + 1) * 256], k_.ap[:, i * 128:(i + 1) * 128], q_.ap[:, 0:256], True, True, [k_.r, q_.r], [ps.r])
            c.act(ec.ap.rearrange('p a b -> p (a b)'), ps.ap, AF.Exp, [ps.r], [ec.r], scale=sc)
            for i in range(2):
                c.mm(po.ap[:, 0:256], v_.ap[:, i, :], ec.ap[:, i, :], i == 0, i == 1, [v_.r, ec.r], [po.r])
            for i in range(2):
                c.mm(po.ap[:, 256:512], c.ones_bf.ap, ec.ap[:, i, :], i == 0, i == 1, [c.ones_bf.r, ec.r], [po.r])
            c.recip(rzc.ap, po.ap[:, 256:512], [po.r], [rzc.r])
            c.tt(o_.ap[:, 0:256], po.ap[:, 0:256], rzc.ap, ALU.mult, [po.r, rzc.r], [o_.r])
        for r in range(32):
            rs = min(max(r - 4, 0), 24)
            base = (rs // 2) * 2
            nt = 4 if rs % 2 == 0 else 5
            ps, po = c.psum[it % 2], c.psum[2 + it % 2]
            e_ = eb[it % 3]
            z_ = rz[it % 2]
            qs = q_.ap[:, LC + r * 64:LC + (r + 1) * 64]
            tiles = []
            for i in range(nt):
                krow = base + 2 * i
                k0 = LC + krow * 64
                col = ps.ap[:, i * 64:(i + 1) * 64]
                c.mm(col, k_.ap[:, k0:k0 + 128], qs, True, False, [k_.r, q_.r], [ps.r])
                for half in range(2):
                    kr = krow + half
                    if rs <= kr < rs + 8:
                        rhs = Tb.ap[0:64, h, kr - r + 7, :]
                        rr = Tb.r
                    else:
                        rhs = neg.ap[0:64, :]
                        rr = neg.r
                    c.mm(col, sel.ap[0:64, half, :], rhs, False, half == 1, [sel.r, rr], [ps.r])
                tiles.append((LC // 128) + krow // 2)
            for i in range(2):
                col = ps.ap[:, (nt + i) * 64:(nt + i + 1) * 64]
                c.mm(col, k_.ap[:, i * 128:(i + 1) * 128], qs, True, True, [k_.r, q_.r], [ps.r])
                tiles.append(i)
            ntt = nt + 2
            c.act(e_.ap[:, 0:ntt, :].rearrange('p a b -> p (a b)'), ps.ap[:, 0:ntt * 64], AF.Exp, [ps.r], [e_.r], scale=sc)
            for i, vt in enumerate(tiles):
                c.mm(po.ap[:, 0:64], v_.ap[:, vt, :], e_.ap[:, i, :], i == 0, i == ntt - 1, [v_.r, e_.r], [po.r])
            for i in range(ntt):
                c.mm(po.ap[:, 64:128], c.ones_bf.ap, e_.ap[:, i, :], i == 0, i == ntt - 1, [c.ones_bf.r, e_.r], [po.r])
            c.recip(z_.ap, po.ap[:, 64:128], [po.r], [z_.r])
            c.tt(o_.ap[:, LC + r * 64:LC + (r + 1) * 64], po.ap[:, 0:64], z_.ap, ALU.mult, [po.r, z_.r], [o_.r])
            it += 1
        if with_ctx:
            c.dma(c.CATd[8 + h], o_.ap, [o_.r], [c.CATr[1]])
        else:
            c.dma(c.CATd[8 + h, :, LC:T], o_.ap[:, LC:T], [o_.r], [c.CATr[1]])
    c.stage_end()


def bcl(ap, shape):
    return ap.to_broadcast(list(shape))


def rnd(c, out, in_, R, W, eng=DVE):
    c.ts(out, in_, MAGIC, MAGIC, ALU.add, ALU.subtract, R, W, eng=eng)


def sincos_frac(c, t, tmp, out_s, out_c, R):
    a, b = tmp
    rnd(c, a.ap, t.ap, [t.r], [a.r])
    c.tt(a.ap, t.ap, a.ap, ALU.subtract, [t.r, a.r], [a.r])
    c.act(out_s.ap, a.ap, AF.Sin, [a.r], [out_s.r], scale=2 * math.pi)
    c.ts(b.ap, t.ap, 0.25, None, ALU.add, None, [t.r], [b.r])
    rnd(c, a.ap, b.ap, [b.r], [a.r])
    c.tt(b.ap, b.ap, a.ap, ALU.subtract, [b.r, a.r], [b.r])
    c.act(out_c.ap, b.ap, AF.Sin, [b.r], [out_c.r], scale=2 * math.pi)


def st_s5(c, with_ctx=True):
    def ld(src, shape):
        t = c.sb(shape, F32)
        c.dma(t.ap, src, [], [t.r])
        return t
    dcol = ld(c.d_s5d, [32])
    iota = ld(c.d_iota1, [512])
    r_, thp = c.sb([64], F32), c.sb([64], F32)
    BT = c.sb([128, 128], BF16)
    CB = c.sb([64, 2, 32], BF16)
    mark = c.aoff
    are, aim, ldt = ld(c.d_s5are, [64]), ld(c.d_s5aim, [64]), ld(c.d_s5ldt, [64])
    tmp = [c.sb([64], F32) for _ in range(8)]
    dtm, sn, cs, cre, cim, den = [c.sb([64], F32) for _ in range(6)]
    c.act(dtm.ap, ldt.ap, AF.Exp, [ldt.r], [dtm.r])
    c.tt(tmp[0].ap, are.ap, dtm.ap, ALU.mult, [are.r, dtm.r], [tmp[0].r])
    c.act(r_.ap, tmp[0].ap, AF.Exp, [tmp[0].r], [r_.r])
    c.tt(thp.ap, aim.ap, dtm.ap, ALU.mult, [aim.r, dtm.r], [thp.r])
    c.ts(thp.ap, thp.ap, 1.0 / (2 * math.pi), None, ALU.mult, None, [thp.r], [thp.r])
    sincos_frac(c, thp, tmp[1:3], sn, cs, None)
    nr, ni = tmp[3], tmp[4]
    c.tt(nr.ap, r_.ap, cs.ap, ALU.mult, [r_.r, cs.r], [nr.r])
    c.ts(nr.ap, nr.ap, -1.0, None, ALU.add, None, [nr.r], [nr.r])
    c.tt(ni.ap, r_.ap, sn.ap, ALU.mult, [r_.r, sn.r], [ni.r])
    c.tt(den.ap, are.ap, are.ap, ALU.mult, [are.r], [den.r])
    c.tt(tmp[5].ap, aim.ap, aim.ap, ALU.mult, [aim.r], [tmp[5].r])
    c.tt(den.ap, den.ap, tmp[5].ap, ALU.add, [den.r, tmp[5].r], [den.r])
    c.recip(den.ap, den.ap, [den.r], [den.r])
    c.tt(cre.ap, nr.ap, are.ap, ALU.mult, [nr.r, are.r], [cre.r])
    c.tt(tmp[5].ap, ni.ap, aim.ap, ALU.mult, [ni.r, aim.r], [tmp[5].r])
    c.tt(cre.ap, cre.ap, tmp[5].ap, ALU.add, [cre.r, tmp[5].r], [cre.r])
    c.tt(cre.ap, cre.ap, den.ap, ALU.mult, [cre.r, den.r], [cre.r])
    c.tt(cim.ap, ni.ap, are.ap, ALU.mult, [ni.r, are.r], [cim.r])
    c.tt(tmp[5].ap, nr.ap, aim.ap, ALU.mult, [nr.r, aim.r], [tmp[5].r])
    c.tt(cim.ap, cim.ap, tmp[5].ap, ALU.subtract, [cim.r, tmp[5].r], [cim.r])
    c.tt(cim.ap, cim.ap, den.ap, ALU.mult, [cim.r, den.r], [cim.r])
    bre, bim = ld(c.d_s5bre, [64, 32]), ld(c.d_s5bim, [64, 32])
    Bb = c.sb([64, 2, 32], F32)
    t1, t2 = c.sb([64, 32], F32), c.sb([64, 32], F32)
    creb, cimb = bcl(cre.ap, [128, 64, 32]), bcl(cim.ap, [128, 64, 32])
    c.tt(t1.ap, bre.ap, creb, ALU.mult, [bre.r, cre.r], [t1.r])
    c.tt(t2.ap, bim.ap, cimb, ALU.mult, [bim.r, cim.r], [t2.r])
    c.tt(Bb.ap[:, :, 0, :], t1.ap, t2.ap, ALU.subtract, [t1.r, t2.r], [Bb.r])
    c.tt(t1.ap, bim.ap, creb, ALU.mult, [bim.r, cre.r], [t1.r])
    c.tt(t2.ap, bre.ap, cimb, ALU.mult, [bre.r, cim.r], [t2.r])
    c.tt(Bb.ap[:, :, 1, :], t1.ap, t2.ap, ALU.add, [t1.r, t2.r], [Bb.r])
    for q4 in range(32):
        ps = c.psum[q4 % 2]
        for j in range(4):
            idx = q4 * 4 + j
            c.transpose(ps.ap[0:32, j * 128:(j + 1) * 128], Bb.ap[:, idx // 2, idx % 2, :], c.ident_f.ap, [Bb.r, c.ident_f.r], [ps.r])
        c.copy(BT.ap[0:32, q4 * 4:(q4 + 1) * 4, :].rearrange('p a b -> p (a b)'), ps.ap[0:32, :], [ps.r], [BT.r])
    crb, cib = ld(c.d_s5cre, [64, 32]), ld(c.d_s5cim, [64, 32])
    c.act(CB.ap[:, :, 0, :], crb.ap, AF.Identity, [crb.r], [CB.r])
    c.act(CB.ap[:, :, 1, :], cib.ap, AF.Identity, [cib.r], [CB.r], scale=-1.0)
    c.S.flush(barrier=True)
    c.aoff = mark
    ctab = [c.sb([512], F32) for _ in range(2)]
    stab = [c.sb([512], F32) for _ in range(2)]
    tA, tB, tT = c.sb([512], F32), c.sb([512], F32), c.sb([512], F32)
    br, bi_ = [c.sb([512], F32) for _ in range(2)], [c.sb([512], F32) for _ in range(2)]
    d1, d2, zR, wR, m1, m2 = [c.sb([512], F32) for _ in range(6)]
    p1, p2, zI, wI, m3, m4 = [c.sb([512], F32) for _ in range(6)]
    sbuf = [[[c.sb([T], BF16) for _ in range(2)] for _ in range(2)] for _ in range(2)]
    ug = [c.sb([T], BF16) for _ in range(2)]
    ysb = [c.sb([T], F32) for _ in range(2)]
    ini = [c.sb([1], F32) for _ in range(2)]
    tin = c.sb([1], F32)
    segs_f = list(TB)
    segs_b = [TB[0]] + TB[:0:-1]
    if not with_ctx:
        pass
    nseg = 0
    for gp in range(32):
        u_ = ug[gp % 2]
        c.dma(u_.ap[0:32], c.Ud[gp // 4, 32 * (gp % 4):32 * (gp % 4) + 32, :], [c.Ur], [u_.r])
        for dr in range(2):
            dg = dr * 32 + gp
            ct, st_ = ctab[dr], stab[dr]
            c.ts(tT.ap, iota.ap, thp.ap[:, dg:dg + 1], None, ALU.mult, None, [iota.r, thp.r], [tT.r])
            sincos_frac(c, tT, [tA, tB], st_, ct, None)
            rcol = r_.ap[:, dg:dg + 1]
            sR_, sI_ = sbuf[gp % 2][dr]
            first = True
            for (t0, n) in (segs_f if dr == 0 else segs_b):
                rev = dr == 1
                pr, pi = c.psum[2 * (nseg % 2)], c.psum[2 * (nseg % 2) + 1]
                b_r, b_i = br[nseg % 2], bi_[nseg % 2]
                c.mm(pr.ap[:, 0:n], BT.ap[0:32, dg * 2, :], u_.ap[0:32, t0:t0 + n], True, True, [BT.r, u_.r], [pr.r])
                c.mm(pi.ap[:, 0:n], BT.ap[0:32, dg * 2 + 1, :], u_.ap[0:32, t0:t0 + n], True, True, [BT.r, u_.r], [pi.r])
                srcr = pr.ap[:, 0:n][:, ::-1] if rev else pr.ap[:, 0:n]
                srci = pi.ap[:, 0:n][:, ::-1] if rev else pi.ap[:, 0:n]
                c.act(b_r.ap[:, 0:n], srcr, AF.Identity, [pr.r], [b_r.r])
                c.act(b_i.ap[:, 0:n], srci, AF.Identity, [pi.r], [b_i.r])
                cN, sN = ct.ap[:, 0:n], st_.ap[:, 0:n]
                c.tt(d1.ap[:, 0:n], cN, b_r.ap[:, 0:n], ALU.mult, [ct.r, b_r.r], [d1.r])
                c.tt(d2.ap[:, 0:n], sN, b_i.ap[:, 0:n], ALU.mult, [st_.r, b_i.r], [d2.r])
                c.tt(zR.ap[:, 0:n], d1.ap[:, 0:n], d2.ap[:, 0:n], ALU.add, [d1.r, d2.r], [zR.r])
                c.tt(p1.ap[:, 0:n], cN, b_i.ap[:, 0:n], ALU.mult, [ct.r, b_i.r], [p1.r], eng=POOL)
                c.tt(p2.ap[:, 0:n], sN, b_r.ap[:, 0:n], ALU.mult, [st_.r, b_r.r], [p2.r], eng=POOL)
                c.tt(zI.ap[:, 0:n], p1.ap[:, 0:n], p2.ap[:, 0:n], ALU.subtract, [p1.r, p2.r], [zI.r], eng=POOL)
                rb = rcol.to_broadcast([128, n])
                for (w_, z_, k) in ((wR, zR, 0), (wI, zI, 1)):
                    init = 0.0 if first else ini[k].ap[:, 0:1]
                    rr = [r_.r, z_.r] + ([] if first else [ini[k].r])
                    c.S.op(DVE, (lambda o, z, i0, b: lambda e: e.tensor_tensor_scan(out=o, data0=b, data1=z, initial=i0,
                                                                                   op0=ALU.mult, op1=ALU.add))(w_.ap[:, 0:n], z_.ap[:, 0:n], init, rb),
                           reads=rr, writes=[w_.r])
                oR = sR_.ap[:, t0:t0 + n][:, ::-1] if rev else sR_.ap[:, t0:t0 + n]
                oI = sI_.ap[:, t0:t0 + n][:, ::-1] if rev else sI_.ap[:, t0:t0 + n]
                c.tt(m1.ap[:, 0:n], cN, wR.ap[:, 0:n], ALU.mult, [ct.r, wR.r], [m1.r], eng=POOL)
                c.tt(m2.ap[:, 0:n], sN, wI.ap[:, 0:n], ALU.mult, [st_.r, wI.r], [m2.r])
                c.tt(oR, m1.ap[:, 0:n], m2.ap[:, 0:n], ALU.subtract, [m1.r, m2.r], [sR_.r])
                c.tt(m3.ap[:, 0:n], cN, wI.ap[:, 0:n], ALU.mult, [ct.r, wI.r], [m3.r], eng=POOL)
                c.tt(m4.ap[:, 0:n], sN, wR.ap[:, 0:n], ALU.mult, [st_.r, wR.r], [m4.r], eng=POOL)
                c.tt(oI, m3.ap[:, 0:n], m4.ap[:, 0:n], ALU.add, [m3.r, m4.r], [sI_.r], eng=POOL)
                cl, sl = ct.ap[:, n - 1:n], st_.ap[:, n - 1:n]
                wRl, wIl = wR.ap[:, n - 1:n], wI.ap[:, n - 1:n]
                c.tt(tin.ap, sl, wIl, ALU.mult, [st_.r, wI.r], [tin.r])
                c.stt(ini[0].ap, wRl, cl, tin.ap, ALU.mult, ALU.subtract, [wR.r, ct.r, tin.r], [ini[0].r])
                c.tt(tin.ap, sl, wRl, ALU.mult, [st_.r, wR.r], [tin.r])
                c.stt(ini[1].ap, wIl, cl, tin.ap, ALU.mult, ALU.add, [wI.r, ct.r, tin.r], [ini[1].r])
                first = False
                nseg += 1
        y_ = ysb[gp % 2]
        for bi, (t0, n) in enumerate(TB):
            po = c.psum[4 + bi % 2]
            k = 0
            for dr in range(2):
                dg = dr * 32 + gp
                for ri in range(2):
                    sb_ = sbuf[gp % 2][dr][ri]
                    c.mm(po.ap[0:32, 0:n], CB.ap[:, dg, ri, :], sb_.ap[:, t0:t0 + n], k == 0, k == 3, [CB.r, sb_.r], [po.r])
                    k += 1
            c.stt(y_.ap[0:32, t0:t0 + n], u_.ap[0:32, t0:t0 + n], dcol.ap[0:32, gp:gp + 1], po.ap[0:32, 0:n], ALU.mult, ALU.add,
                  [u_.r, dcol.r, po.r], [y_.r])
        c.dma(c.Yd[gp * 32:(gp + 1) * 32, :], y_.ap[0:32], [y_.r], [c.Yr])
    c.stage_end()


C0G = math.sqrt(2.0 / math.pi)


def st_glu_wout(c, layer, with_ctx=True):
    blocks = TB if with_ctx else TB[1:]
    G = c.G[layer]
    gw = c.sb([8, 1024], BF16)
    for kt in range(8):
        c.dma(gw.ap[:, kt, :], c.d_gluw[kt * 128:(kt + 1) * 128, :], [], [gw.r], q=POOL)
    gb = c.sb([8], F32)
    c.dma(gb.ap, c.d_glub, [], [gb.r])
    wo = c.sb([16, 2048], BF16)
    for kt in range(16):
        c.dma(wo.ap[:, kt, :], c.d_wout[layer][kt * 128:(kt + 1) * 128, :], [], [wo.r], q=POOL)
    yb = [c.sb([8, 512], F32) for _ in range(1)]
    y2 = c.sb([8, 512], F32)
    sg = c.sb([8, 512], F32)
    gg = [c.sb([8, 512], BF16) for _ in range(2)]
    cat = [c.sb([16, 512], BF16) for _ in range(1)]
    catr = [[Reg() for _ in range(16)] for _ in range(1)]
    sgm = [c.sb([512], F32) for _ in range(2)]
    xt = [c.sb([512], F32) for _ in range(4)]
    nx = 0
    for bi, (t0, n) in enumerate(blocks):
        g = 1 if t0 < LC else 0
        y_, g_, ct, cr = yb[0], gg[bi % 2], cat[0], catr[0]
        c.dma(y_.ap[:, :, 0:n], c.Yd[:, t0:t0 + n].rearrange('(a p) t -> p a t', p=128), [c.Yr], [y_.r])
        c.dma(ct.ap[:, 8:16, 0:n], c.CATd[8:16, :, t0:t0 + n].rearrange('a p t -> p a t'), [c.CATr[1]], cr[8:16])
        yv = y_.ap[:, :, 0:n]
        c.act(y2.ap[:, :, 0:n], yv, AF.Square, [y_.r], [y2.r])
        c.ts(y2.ap[:, :, 0:n], y2.ap[:, :, 0:n], 0.044715, 1.0, ALU.mult, ALU.add, [y2.r], [y2.r])
        c.tt(y2.ap[:, :, 0:n], y2.ap[:, :, 0:n], yv, ALU.mult, [y2.r, y_.r], [y2.r])
        c.act(sg.ap[:, :, 0:n], y2.ap[:, :, 0:n], AF.Sigmoid, [y2.r], [sg.r], scale=2 * C0G)
        c.tt(g_.ap[:, :, 0:n], sg.ap[:, :, 0:n], yv, ALU.mult, [sg.r, y_.r], [g_.r])
        for ft in range(8):
            ps = c.psum[ft % 2]
            s_ = sgm[ft % 2]
            for kt in range(8):
                c.mm(ps.ap[:, 0:n], gw.ap[:, kt, ft * 128:(ft + 1) * 128], g_.ap[:, kt, 0:n], kt == 0, kt == 7, [gw.r, g_.r], [ps.r])
            c.act(s_.ap[:, 0:n], ps.ap[:, 0:n], AF.Sigmoid, [ps.r, gb.r], [s_.r], bias=gb.ap[:, ft:ft + 1])
            c.tt(ct.ap[:, ft, 0:n], s_.ap[:, 0:n], g_.ap[:, ft, 0:n], ALU.mult, [s_.r, g_.r], [cr[ft]])
        wout_block(c, layer, ct, cr, wo, xt, t0, n, g, nx)
        nx += 16
    c.stage_end()


def wout_block(c, layer, ct, cr, wo, xt, t0, n, g, nx):
    G = c.G[layer]
    PF = 3

    def issue_load(dt):
        x_ = xt[(nx + dt) % 4]
        c.dma(x_.ap[:, 0:n], c.XR[dt, :, t0:t0 + n], [c.XRr[dt][bidx(t0)]], [x_.r])
    for dt in range(PF):
        issue_load(dt)
    for dt in range(16):
        if dt + PF < 16:
            issue_load(dt + PF)
        po = c.psum[4 + (nx + dt) % 2]
        x_ = xt[(nx + dt) % 4]
        xreg = c.XRr[dt][bidx(t0)]
        for kt in range(16):
            c.mm(po.ap[:, 0:n], wo.ap[:, kt, dt * 128:(dt + 1) * 128], ct.ap[:, kt, 0:n], kt == 0, kt == 15, [wo.r, cr[kt]], [po.r])
        c.stt(x_.ap[:, 0:n], po.ap[:, 0:n], G.ap[:, g, 1, dt:dt + 1], x_.ap[:, 0:n], ALU.mult, ALU.add, [po.r, G.r, x_.r], [x_.r])
        c.dma(c.XR[dt, :, t0:t0 + n], x_.ap[:, 0:n], [x_.r], [xreg], q=ACT)


def conv3(c, raw, W, w3, b, out, tmp, silu, wr=()):
    wr = list(wr)
    c.act(tmp.ap[:, 0:W], raw.ap[:, 1:W + 1], AF.Identity, [raw.r] + wr, [tmp.r], scale=w3[:, 1:2], bias=b)
    c.stt(tmp.ap[:, 0:W], raw.ap[:, 0:W], w3[:, 0:1], tmp.ap[:, 0:W], ALU.mult, ALU.add, [raw.r, tmp.r] + wr, [tmp.r])
    if silu:
        c.stt(tmp.ap[:, 0:W], raw.ap[:, 2:W + 2], w3[:, 2:3], tmp.ap[:, 0:W], ALU.mult, ALU.add, [raw.r, tmp.r] + wr, [tmp.r])
        c.act(out[0], tmp.ap[:, 0:W], AF.Silu, [tmp.r], out[1])
    else:
        c.stt(out[0], raw.ap[:, 2:W + 2], w3[:, 2:3], tmp.ap[:, 0:W], ALU.mult, ALU.add, [raw.r, tmp.r] + wr, out[1])


def st_odd_proj(c, H):
    wsrc = c.d_odwin_t
    hw = c.sb([24, 3], F32); hb = c.sb([24], F32); sw = c.sb([16, 3], F32); sbias = c.sb([16], F32)
    for t_, s_ in ((hw, c.d_hysw), (hb, c.d_hysb), (sw, c.d_ssdcw), (sbias, c.d_ssdcb)):
        c.dma(t_.ap, s_, [], [t_.r])
    wb = [c.sb([16, 128], BF16) for _ in range(3)]
    raw = [c.sb([T + 8], F32) for _ in range(2)]
    tmp = [c.sb([T], F32) for _ in range(2)]
    ob = [c.sb([T], BF16) for _ in range(2)]
    for r_ in raw:
        c.memset(r_.ap, 0.0, [r_.r])
    cnt = 0
    for f in range(48):
        w_ = wb[f % 3]
        c.dma(w_.ap, wsrc[f], [], [w_.r], q=POOL)
        lat_only = f < 32
        blocks = TB[1:] if lat_only else TB
        r_, t_, o_ = raw[f % 2], tmp[f % 2], ob[f % 2]
        for (t0, n) in blocks:
            ps = c.psum[cnt % 4]
            for kt in range(16):
                c.mm(ps.ap[:, 0:n], w_.ap[:, kt, :], H.ap[:, kt, t0:t0 + n], kt == 0, kt == 15, [w_.r, H.r], [ps.r])
            off = 1 + t0 if t0 < LC else 3 + t0
            evac(c, cnt, r_.ap[:, off:off + n], ps.ap[:, 0:n], [ps.r], [r_.r])
            cnt += 1
        rl = Tl(r_.ap[:, 258:258 + L + 2], r_.r)
        if f < 24:
            conv3(c, rl, L, hw.ap[:, f, :], hb.ap[:, f:f + 1], (o_.ap[:, 0:L], [o_.r]), t_, False, [hw.r, hb.r])
            c.dma(c.HYd[f], o_.ap[:, 0:L], [o_.r], [c.HYr])
        elif f < 32:
            c.act(o_.ap[:, 0:L], r_.ap[:, 259:259 + L], AF.Silu, [r_.r], [o_.r])
            c.dma(c.Zd[f - 24], o_.ap[:, 0:L], [o_.r], [c.Zr])
        else:
            a = f - 32
            rc = Tl(r_.ap[:, 0:LC + 2], r_.r)
            conv3(c, rc, LC, sw.ap[:, a, :], sbias.ap[:, a:a + 1], (o_.ap[:, 0:LC], [o_.r]), t_, True, [sw.r, sbias.r])
            conv3(c, rl, L, sw.ap[:, a, :], sbias.ap[:, a:a + 1], (o_.ap[:, LC:T], [o_.r]), t_, True, [sw.r, sbias.r])
            c.dma(c.XBCd[a], o_.ap, [o_.r], [c.XBCr])
    wdt = c.sb([2, 16, 16], BF16)
    c.dma(wdt.ap, c.d_odwdt, [], [wdt.r], q=POOL)
    dtb = c.sb([2], F32); alog = c.sb([2], F32); nA = c.sb([2], F32)
    c.dma(dtb.ap[0:16], c.d_dtbias, [], [dtb.r])
    c.dma(alog.ap[0:16], c.d_alog, [], [alog.r])
    c.act(nA.ap[0:16], alog.ap[0:16], AF.Exp, [alog.r], [nA.r])
    c.ts(nA.ap[0:16], nA.ap[0:16], -1.0, None, ALU.mult, None, [nA.r], [nA.r])
    one = c.sb([1], F32)
    c.memset(one.ap, 1.0, [one.r])
    dtT = [c.sb([T], F32) for _ in range(2)]
    aT = c.sb([T], F32)
    cumT = [c.sb([T], F32) for _ in range(2)]
    et = c.sb([512], F32)
    ini = c.sb([1], F32)
    for k in range(2):
        for (t0, n) in TB:
            ps = c.psum[4 + cnt % 2]
            cnt += 1
            for kt in range(16):
                c.mm(ps.ap[0:16, 0:n], wdt.ap[:, k, kt, :], H.ap[:, kt, t0:t0 + n], kt == 0, kt == 15, [wdt.r, H.r], [ps.r])
            c.act(et.ap[0:16, 0:n], ps.ap[0:16, 0:n], AF.Exp, [ps.r, dtb.r], [et.r], bias=dtb.ap[0:16, k:k + 1])
            c.act(dtT[k].ap[0:16, t0:t0 + n], et.ap[0:16, 0:n], AF.Ln, [et.r, one.r], [dtT[k].r], bias=one.ap[0:16, 0:1])
        c.ts(aT.ap[0:16], dtT[k].ap[0:16], nA.ap[0:16, k:k + 1], None, ALU.mult, None, [dtT[k].r, nA.r], [aT.r])
        segs = list(TB) if k == 0 else [TB[0]] + TB[:0:-1]
        first = True
        for (t0, n) in segs:
            src = aT.ap[0:16, t0:t0 + n]
            dst = cumT[k].ap[0:16, t0:t0 + n]
            if k == 1:
                src, dst = src[:, ::-1], dst[:, ::-1]
            init = 0.0 if first else ini.ap[0:16, 0:1]
            ob_ = one.ap[0:16, 0:1].to_broadcast([16, n])
            c.S.op(DVE, (lambda o, z, i0, b: lambda e: e.tensor_tensor_scan(out=o, data0=b, data1=z, initial=i0, op0=ALU.mult, op1=ALU.add))(dst, src, init, ob_),
                   reads=[aT.r, one.r] + ([] if first else [ini.r]), writes=[cumT[k].r])
            last = t0 + n - 1 if k == 0 else t0
            c.copy(ini.ap[0:16], cumT[k].ap[0:16, last:last + 1], [cumT[k].r], [ini.r])
            first = False
        c.dma(c.DTd[k], dtT[k].ap[0:16], [dtT[k].r], [c.DTr])
        c.dma(c.CUMd[k], cumT[k].ap[0:16], [cumT[k].r], [c.CUMr])
    c.stage_end()


def st_ssd(c):
    Bt = c.sb([4, T], BF16); Ct = c.sb([4, T], BF16)
    c.dma(Bt.ap, c.XBCd[8:12].rearrange('a p t -> p a t'), [c.XBCr], [Bt.r])
    c.dma(Ct.ap, c.XBCd[12:16].rearrange('a p t -> p a t'), [c.XBCr], [Ct.r])
    xtok = c.sb([18, 1024], BF16)
    xdt = [c.sb([18, 1024], BF16) for _ in range(2)]
    cumT = [c.sb([T], F32) for _ in range(2)]
    ccol = [c.sb([18, 16], F32) for _ in range(2)]
    ncol = [c.sb([18, 16], F32) for _ in range(2)]
    mark = c.aoff
    xs = [c.sb([T], BF16) for _ in range(2)]
    nps = 0
    import os
    CUT = float(os.environ.get('SSD_CUT', '99'))
    for a in range(8):
        x_ = xs[a % 2]
        c.dma(x_.ap, c.XBCd[a], [c.XBCr], [x_.r])
        for j4 in range(0, 18, 4):
            nj = min(4, 18 - j4)
            ps = c.psum[6 + nps % 2]
            nps += 1
            pb = ps.ap.bitcast(BF16)
            for jj in range(nj):
                c.transpose(pb[:, jj * 128:(jj + 1) * 128], x_.ap[:, (j4 + jj) * 128:(j4 + jj + 1) * 128], c.ident_b.ap, [x_.r, c.ident_b.r], [ps.r])
            c.copy(xtok.ap[:, j4:j4 + nj, a * 128:(a + 1) * 128], pb[:, 0:nj * 128].rearrange('p (j q) -> p j q', q=128), [ps.r], [xtok.r])
    if CUT <= 1:
        c.stage_end()
        return
    dtT = c.sb([T], F32)
    dtk = c.sb([18, 16], F32)
    for k in range(2):
        c.memset(cumT[k].ap[0:32], 0.0, [cumT[k].r])
    c.memset(dtT.ap[0:32], 0.0, [dtT.r])
    for k in range(2):
        c.dma(cumT[k].ap[0:16], c.CUMd[k], [c.CUMr], [cumT[k].r])
        c.dma(dtT.ap[0:16], c.DTd[k], [c.DTr], [dtT.r])
        if CUT <= 1.2:
            continue
        for (srcT, kind) in ((cumT[k], 0), (dtT, 1)):
            ps = c.psum[4 + nps % 2]
            nps += 1
            for j in range(18):
                c.mm(ps.ap[:, j * 16:(j + 1) * 16], srcT.ap[0:32, j * 128:(j + 1) * 128], c.ident_f.ap[0:32, 0:16], True, True, [srcT.r, c.ident_f.r], [ps.r])
            v = ps.ap[:, 0:288].rearrange('p (j h) -> p j h', h=16)
            if CUT <= 1.25:
                continue
            if kind == 0:
                c.copy(ccol[k].ap, v, [ps.r], [ccol[k].r])
                if CUT > 1.3:
                    c.act(ncol[k].ap, v, AF.Identity, [ps.r], [ncol[k].r], scale=-1.0)
            else:
                c.copy(dtk.ap, v, [ps.r], [dtk.r])
        if CUT <= 1.4:
            continue
        for j in range(18):
            c.tt(xdt[k].ap[:, j, :].rearrange('p (h q) -> p h q', q=64), xtok.ap[:, j, :].rearrange('p (h q) -> p h q', q=64),
                 dtk.ap[:, j, :].to_broadcast([128, 16, 64]), ALU.mult, [xtok.r, dtk.r], [xdt[k].r])
    if CUT <= 2:
        c.stage_end()
        return
    c.S.flush(barrier=True)
    c.aoff = mark
    sel = c.sb([16, 128], F32)
    c.memset(sel.ap[0:32], 0.0, [sel.r])
    c.copy(sel.ap[0:16], c.ident_f.ap[0:16, 0:16].to_broadcast([16, 16, 128]), [c.ident_f.r], [sel.r])
    dcol = c.sb([16], F32)
    c.dma(dcol.ap, c.d_ssdd, [], [dcol.r])
    dI = c.sb([16, 128], BF16)
    for hd in range(16):
        c.ts(dI.ap[:, hd, :], c.ident_f.ap, dcol.ap[:, hd:hd + 1], None, ALU.mult, None, [c.ident_f.r, dcol.r], [dI.r])
    tri = [c.sb([128], F32) for _ in range(2)]
    c.dma(tri[0].ap, c.d_tri[0], [], [tri[0].r])
    c.dma(tri[1].ap, c.d_tri[1], [], [tri[1].r])
    Eb = [c.sb([128], F32) for _ in range(3)]
    Mb = [c.sb([128], BF16) for _ in range(3)]
    Db = c.sb([128], F32)
    ysb = [c.sb([4, 128], F32) for _ in range(2)]
    it = 0
    npc = 0
    if CUT <= 3:
        c.stage_end()
        return
    for g in range(4 if CUT > 4 else 1):
        for i in range(16 if CUT > 4 else 1):
            q0 = LC + i * 128
            accs = [c.psum[e] for e in range(4)]
            R = [c.psum[4], c.psum[5]]
            for d in range(2):
                for e in range(4):
                    c.mm(R[d].ap[:, e * 128:(e + 1) * 128], sel.ap[0:32, g * 4 + e, :], cumT[d].ap[0:32, q0:q0 + 128], True, True,
                         [sel.r, cumT[d].r], [R[d].r])
            started = [False] * 4
            for d in range(2):
                srcs = [0, 1] + ([2 + j for j in range(i + 1)] if d == 0 else [2 + j for j in range(i, 16)])
                for jt in srcs:
                    pc = c.psum[6 + npc % 2]
                    npc += 1
                    c.mm(pc.ap[:, 0:128], Bt.ap[:, g, jt * 128:(jt + 1) * 128], Ct.ap[:, g, q0:q0 + 128], True, True, [Bt.r, Ct.r], [pc.r])
                    diag = jt == 2 + i
                    for e in range(4):
                        hd = g * 4 + e
                        E_, M_ = Eb[(npc + e) % 3], Mb[(npc + e) % 3]
                        Rv = R[d].ap[:, e * 128:(e + 1) * 128]
                        if not diag:
                            c.act(E_.ap, Rv, AF.Exp, [R[d].r, ncol[d].r], [E_.r], bias=ncol[d].ap[:, jt, hd:hd + 1])
                        else:
                            c.ts(Db.ap, Rv, ccol[d].ap[:, jt, hd:hd + 1], 0.0, ALU.subtract, ALU.min, [R[d].r, ccol[d].r], [Db.r])
                            c.act(E_.ap, Db.ap, AF.Exp, [Db.r], [E_.r])
                            c.tt(E_.ap, E_.ap, tri[d].ap, ALU.mult, [E_.r, tri[d].r], [E_.r])
                        c.tt(M_.ap, E_.ap, pc.ap[:, 0:128], ALU.mult, [E_.r, pc.r], [M_.r])
                        c.mm(accs[e].ap[0:64, 0:128], xdt[d].ap[:, jt, hd * 64:(hd + 1) * 64], M_.ap, not started[e], False,
                             [xdt[d].r, M_.r], [accs[e].r])
                        started[e] = True
            for e in range(4):
                hd = g * 4 + e
                c.mm(accs[e].ap[0:64, 0:128], xtok.ap[:, 2 + i, hd * 64:(hd + 1) * 64], dI.ap[:, hd, :], False, True,
                     [xtok.r, dI.r], [accs[e].r])
            y_ = ysb[it % 2]
            for e in range(4):
                evac(c, e, y_.ap[0:64, e, :], accs[e].ap[0:64, 0:128], [accs[e].r], [y_.r])
            c.dma(c.YSd[g * 4:(g + 1) * 4, :, i * 128:(i + 1) * 128].rearrange('e p l -> p e l'), y_.ap[0:64], [y_.r], [c.YSr])
            it += 1
    c.stage_end()


def st_hyena(c):
    TWO_PI = 2 * math.pi
    hwo = c.sb([4096], F32); c.dma(hwo.ap[0:64], c.d_hywout, [], [hwo.r])
    hid0 = c.sb([L], F32)
    mark = c.aoff
    zT = c.sb([L], F32); c.memset(zT.ap[0:64], 0.0, [zT.r]); c.dma(zT.ap[0:33], c.d_hyz, [], [zT.r])
    w1 = c.sb([64], F32); c.memset(w1.ap[0:64], 0.0, [w1.r]); c.dma(w1.ap[0:33], c.d_hywin, [], [w1.r])
    wm = c.sb([2, 64], F32); c.dma(wm.ap[0:64], c.d_hywmid, [], [wm.r])
    fb_ = c.sb([4], F32); c.dma(fb_.ap[0:64], c.d_hyfb, [], [fb_.r])
    fq = c.sb([1], F32); bq = c.sb([3], F32)
    c.ts(fq.ap[0:64], fb_.ap[0:64, 0:1], 1.0 / TWO_PI, None, ALU.mult, None, [fb_.r], [fq.r])
    c.ts(bq.ap[0:64], fb_.ap[0:64, 1:4], fq.ap[0:64, 0:1], None, ALU.mult, None, [fb_.r, fq.r], [bq.r])
    hid = [hid0, c.sb([L], F32)]
    tA, tB = c.sb([512], F32), c.sb([512], F32)
    src = zT
    for l in range(3):
        dst = hid[l % 2]
        for b4 in range(4):
            ps = c.psum[b4 % 2]
            if l == 0:
                c.mm(ps.ap[0:64, :], w1.ap[0:64, :], zT.ap[0:64, b4 * 512:(b4 + 1) * 512], True, True, [w1.r, zT.r], [ps.r])
            else:
                c.mm(ps.ap[0:64, :], wm.ap[0:64, l - 1, :], src.ap[0:64, b4 * 512:(b4 + 1) * 512], True, True, [wm.r, src.r], [ps.r])
            c.ts(tA.ap[0:64], ps.ap[0:64, :], fq.ap[0:64, 0:1], bq.ap[0:64, l:l + 1], ALU.mult, ALU.add, [ps.r, fq.r, bq.r], [tA.r])
            rnd(c, tB.ap[0:64], tA.ap[0:64], [tA.r], [tB.r])
            c.tt(tA.ap[0:64], tA.ap[0:64], tB.ap[0:64], ALU.subtract, [tA.r, tB.r], [tA.r])
            c.act(dst.ap[0:64, b4 * 512:(b4 + 1) * 512], tA.ap[0:64], AF.Sin, [tA.r], [dst.r], scale=TWO_PI)
        src = dst
    h3 = src
    c.S.flush(barrier=True)
    c.aoff = mark
    CB_ = 256
    dec = c.sb([16, CB_], F32)
    fbt = c.sb([2, CB_], F32)
    Hs = c.sb([16, CB_], BF16); Hd = c.sb([16, CB_], BF16)
    Kre = c.sb([16, CB_], F32); Kim = c.sb([16, CB_], F32)
    Yre = c.sb([16, CB_], BF16); Yim = c.sb([16, CB_], BF16)
    tok = {k: c.sb([16, CB_], BF16) for k in ('x1', 'x2', 'v', 'z1', 'o')}
    tabs = [c.sb([16, 128], BF16) for _ in range(4)]
    tmpf = [c.sb([CB_], F32) for _ in range(6)]
    chm = [c.sb([L], BF16) for _ in range(2)]
    nt = 0
    npz = 0
    for cb in range(4):
        c.dma(dec.ap, c.d_decay[:, cb * CB_:(cb + 1) * CB_].rearrange('(tt p) q -> p tt q', p=128), [], [dec.r])
        c.dma(fbt.ap, c.d_hyfbias[:, :, cb * CB_:(cb + 1) * CB_], [], [fbt.r])
        for ki, key in enumerate(('x1', 'x2', 'v')):
            for ct in range(2):
                ch_ = chm[npz % 2]
                c.dma(ch_.ap, c.HYd[ki * 8 + cb * 2 + ct], [c.HYr], [ch_.r])
                for t4 in range(4):
                    ps = c.psum[6 + npz % 2]
                    npz += 1
                    pb = ps.ap.bitcast(BF16)
                    for jj in range(4):
                        tt_ = t4 * 4 + jj
                        c.transpose(pb[:, jj * 128:(jj + 1) * 128], ch_.ap[:, tt_ * 128:(tt_ + 1) * 128], c.ident_b.ap, [ch_.r, c.ident_b.r], [ps.r])
                    c.copy(tok[key].ap[:, t4 * 4:(t4 + 1) * 4, ct * 128:(ct + 1) * 128], pb[:, 0:512].rearrange('p (j q) -> p j q', q=128), [ps.r], [tok[key].r])
        for o in range(2):
            tin = tok['v'] if o == 0 else tok['z1']
            gate = tok['x1'] if o == 0 else tok['x2']
            tout = tok['z1'] if o == 0 else tok['o']
            for tt_ in range(16):
                pf, pb_ = c.psum[0], c.psum[1]
                for dr, p_ in ((0, pf), (1, pb_)):
                    col0 = o * 2048 + dr * 1024 + cb * CB_
                    c.mm(p_.ap[:, 0:CB_], h3.ap[0:64, tt_ * 128:(tt_ + 1) * 128], hwo.ap[0:64, col0:col0 + CB_], True, True, [h3.r, hwo.r], [p_.r])
                f_, b_ = tmpf[0], tmpf[1]
                c.tt(f_.ap, pf.ap[:, 0:CB_], dec.ap[:, tt_, :], ALU.mult, [pf.r, dec.r], [f_.r])
                c.tt(b_.ap, pb_.ap[:, 0:CB_], dec.ap[:, tt_, :], ALU.mult, [pb_.r, dec.r], [b_.r])
                if tt_ == 0:
                    c.memset(b_.ap[0:1, :], 0.0, [b_.r])
                c.tt(Hs.ap[:, tt_, :], f_.ap, b_.ap, ALU.add, [f_.r, b_.r], [Hs.r])
                c.tt(Hd.ap[:, tt_, :], f_.ap, b_.ap, ALU.subtract, [f_.r, b_.r], [Hd.r])
            for ft in range(16):
                tc_, ts_ = tabs[nt % 4], tabs[(nt + 1) % 4]
                nt += 2
                c.dma(tc_.ap, c.d_Cf[ft], [], [tc_.r])
                c.dma(ts_.ap, c.d_Sf[ft], [], [ts_.r])
                pr, pi = c.psum[0], c.psum[1]
                for tt_ in range(16):
                    c.mm(pr.ap[:, 0:CB_], tc_.ap[:, tt_, :], Hs.ap[:, tt_, :], tt_ == 0, tt_ == 15, [tc_.r, Hs.r], [pr.r])
                for tt_ in range(16):
                    c.mm(pi.ap[:, 0:CB_], ts_.ap[:, tt_, :], Hd.ap[:, tt_, :], tt_ == 0, tt_ == 15, [ts_.r, Hd.r], [pi.r])
                c.act(Kre.ap[:, ft, :], pr.ap[:, 0:CB_], AF.Identity, [pr.r], [Kre.r])
                c.act(Kim.ap[:, ft, :], pi.ap[:, 0:CB_], AF.Identity, [pi.r], [Kim.r])
                if ft == 0:
                    pn = c.psum[2]
                    for tt_ in range(16):
                        c.mm(pn.ap[:, 0:CB_], ts_.ap[:, tt_, :], Hs.ap[:, tt_, :], tt_ == 0, tt_ == 15, [ts_.r, Hs.r], [pn.r])
                    c.act(Kim.ap[0:1, 0, :], pn.ap[0:1, 0:CB_], AF.Identity, [pn.r], [Kim.r])
                vr, vi = c.psum[3], c.psum[4]
                for tt_ in range(16):
                    c.mm(vr.ap[:, 0:CB_], tc_.ap[:, tt_, :], tin.ap[:, tt_, :], tt_ == 0, tt_ == 15, [tc_.r, tin.r], [vr.r])
                for tt_ in range(16):
                    c.mm(vi.ap[:, 0:CB_], ts_.ap[:, tt_, :], tin.ap[:, tt_, :], tt_ == 0, tt_ == 15, [ts_.r, tin.r], [vi.r])
                a1, a2, a3, a4 = tmpf[2:6]
                c.tt(a1.ap, vr.ap[:, 0:CB_], Kre.ap[:, ft, :], ALU.mult, [vr.r, Kre.r], [a1.r])
                c.tt(a2.ap, vi.ap[:, 0:CB_], Kim.ap[:, ft, :], ALU.mult, [vi.r, Kim.r], [a2.r])
                c.tt(a3.ap, vr.ap[:, 0:CB_], Kim.ap[:, ft, :], ALU.mult, [vr.r, Kim.r], [a3.r])
                c.tt(a4.ap, vi.ap[:, 0:CB_], Kre.ap[:, ft, :], ALU.mult, [vi.r, Kre.r], [a4.r])
                c.tt(Yre.ap[:, ft, :], a1.ap, a2.ap, ALU.subtract, [a1.r, a2.r], [Yre.r])
                c.tt(Yim.ap[:, ft, :], a3.ap, a4.ap, ALU.add, [a3.r, a4.r], [Yim.r])
                if ft == 0:
                    c.copy(Yre.ap[0:1, 0, :], a1.ap[0:1, :], [a1.r], [Yre.r])
                    c.copy(Yim.ap[0:1, 0, :], a2.ap[0:1, :], [a2.r], [Yim.r])
            for tt_ in range(16):
                tc_, ts_ = tabs[nt % 4], tabs[(nt + 1) % 4]
                nt += 2
                c.dma(tc_.ap, c.d_Ci[tt_], [], [tc_.r])
                c.dma(ts_.ap, c.d_Si[tt_], [], [ts_.r])
                py = c.psum[5]
                for ft in range(16):
                    c.mm(py.ap[:, 0:CB_], tc_.ap[:, ft, :], Yre.ap[:, ft, :], ft == 0, False, [tc_.r, Yre.r], [py.r])
                for ft in range(16):
                    c.mm(py.ap[:, 0:CB_], ts_.ap[:, ft, :], Yim.ap[:, ft, :], False, ft == 15, [ts_.r, Yim.r], [py.r])
                e1 = tmpf[0]
                c.tt(e1.ap, tin.ap[:, tt_, :], fbt.ap[:, o, :], ALU.mult, [tin.r, fbt.r], [e1.r])
                c.tt(e1.ap, e1.ap, py.ap[:, 0:CB_], ALU.add, [e1.r, py.r], [e1.r])
                c.tt(tout.ap[:, tt_, :], e1.ap, gate.ap[:, tt_, :], ALU.mult, [e1.r, gate.r], [tout.r])
        for ct in range(2):
            ch_ = chm[npz % 2]
            for t4 in range(4):
                ps = c.psum[6 + npz % 2]
                npz += 1
                pb = ps.ap.bitcast(BF16)
                for jj in range(4):
                    tt_ = t4 * 4 + jj
                    c.transpose(pb[:, jj * 128:(jj + 1) * 128], tok['o'].ap[:, tt_, ct * 128:(ct + 1) * 128], c.ident_b.ap, [tok['o'].r, c.ident_b.r], [ps.r])
                c.copy(ch_.ap[:, t4 * 512:(t4 + 1) * 512], pb[:, 0:512], [ps.r], [ch_.r])
            c.dma(c.CATd[cb * 2 + ct, :, LC:T], ch_.ap, [ch_.r], [c.CATr[0]])
    c.stage_end()


def st_odd_out(c):
    wo = c.sb([16, 2048], BF16)
    for kt in range(16):
        c.dma(wo.ap[:, kt, :], c.d_wout[1][kt * 128:(kt + 1) * 128, :], [], [wo.r], q=POOL)
    ng = c.sb([8], F32); c.dma(ng.ap, c.d_ssdng, [], [ng.r])
    yb = c.sb([8, 512], F32); zb = c.sb([8, 512], BF16); sq = c.sb([8, 512], BF16)
    rs = c.sb([512], F32)
    cat = c.sb([16, 512], BF16)
    cr = [Reg() for _ in range(16)]
    xt = [c.sb([512], F32) for _ in range(4)]
    nx = 0
    for (t0, n) in TB[1:]:
        l0 = t0 - LC
        c.dma(yb.ap, c.YSd[:, :, l0:l0 + n].rearrange('h q t -> (h q) t').rearrange('(a p) t -> p a t', p=128), [c.YSr], [yb.r])
        c.dma(zb.ap, c.Zd[:, :, l0:l0 + n].rearrange('a p t -> p a t'), [c.Zr], [zb.r])
        c.dma(cat.ap[:, 0:8, :], c.CATd[0:8, :, t0:t0 + n].rearrange('a p t -> p a t'), [c.CATr[0]], cr[0:8])
        c.tt(yb.ap, yb.ap, zb.ap, ALU.mult, [yb.r, zb.r], [yb.r])
        c.act(sq.ap, yb.ap, AF.Square, [yb.r], [sq.r])
        ps = c.psum[0]
        for a in range(8):
            c.mm(ps.ap, c.ones_bf.ap, sq.ap[:, a, :], a == 0, a == 7, [c.ones_bf.r, sq.r], [ps.r])
        c.act(rs.ap, ps.ap, AF.Sqrt, [ps.r, c.eps_t.r], [rs.r], scale=1.0 / 1024, bias=c.eps_t.ap[:, 0:1])
        c.recip(rs.ap, rs.ap, [rs.r], [rs.r])
        for a in range(8):
            c.stt(cat.ap[:, 8 + a, :], yb.ap[:, a, :], ng.ap[:, a:a + 1], rs.ap, ALU.mult, ALU.mult, [yb.r, ng.r, rs.r], [cr[8 + a]])
        wout_block(c, 1, cat, cr, wo, xt, t0, n, 0, nx)
        nx += 16
    c.stage_end()
```

```python
import math
import numpy as np
import ml_dtypes
import concourse.bass as bass
import concourse.mybir as mybir
from concourse.bass_utils import run_bass_kernel_spmd

F32 = mybir.dt.float32
BF16 = mybir.dt.bfloat16
ALU = mybir.AluOpType
AF = mybir.ActivationFunctionType
AX = mybir.AxisListType

PE, DVE, ACT, POOL, SP = 0, 1, 2, 3, 4
NENG = 5
NDSEM = 12

D = 2048
T = 2304
LC = 256
L = 2048
DFF = 5632
NFT = DFF // 128
EPS = 1e-6
TB = [(0, 256), (256, 512), (768, 512), (1280, 512), (1792, 512)]
ARENA_BYTES = 196608
MAGIC = 12582912.0


class Reg:
    __slots__ = ('w', 'rs', 'excl')

    def __init__(self, excl=False):
        self.w = None
        self.rs = []
        self.excl = excl


class Op:
    __slots__ = ('eng', 'fn', 'deps', 'signal', 'sig', 'clock', 'dma', 'dsem', 'dval', 'waits')

    def __init__(self, eng, fn, dma):
        self.eng = eng
        self.fn = fn
        self.dma = dma
        self.deps = []
        self.signal = False
        self.sig = 0
        self.clock = None
        self.dsem = -1
        self.dval = 0
        self.waits = None


class Sched:
    def __init__(self, nc):
        self.nc = nc
        self.engs = [nc.tensor, nc.vector, nc.scalar, nc.gpsimd, nc.sync]
        self.esem = [nc.alloc_semaphore('es%d' % i) for i in range(NENG)]
        self.dsem = [[nc.alloc_semaphore('ds%d_%d' % (q, i)) for i in range(NDSEM)] for q in range(NENG)]
        self.ncomp = NENG + NENG * NDSEM
        self.pending = []
        self.sigcnt = [0] * NENG
        self.dcnt = [0] * NENG
        self.dlast = [[None] * NDSEM for _ in range(NENG)]
        self.clock = [[0] * self.ncomp for _ in range(NENG)]
        self.nops = 0
        self.nwaits = 0

    def op(self, eng, fn, reads=(), writes=(), dma=False):
        o = Op(eng, fn, dma)
        deps = o.deps
        for r in reads:
            if r.w is not None:
                deps.append((r.w, True))
            if r.excl:
                for x in r.rs:
                    if x.eng != eng:
                        deps.append((x, True))
            if dma:
                r.rs.append(o)
            else:
                rs = r.rs
                for i in range(len(rs)):
                    if (not rs[i].dma) and rs[i].eng == eng:
                        rs[i] = o
                        break
                else:
                    rs.append(o)
        for r in writes:
            if r.w is not None:
                deps.append((r.w, False))
            for x in r.rs:
                if x is not o:
                    deps.append((x, False))
            r.w = o
            r.rs = []
        self.pending.append(o)
        return o

    def flush(self, barrier=True):
        ops = self.pending
        self.pending = []
        for o in ops:
            for (d, raw) in o.deps:
                if d.dma:
                    continue
                if d.eng == o.eng and not o.dma and d.eng == PE:
                    continue
                d.signal = True
        if barrier:
            last = {}
            for o in ops:
                if not o.dma:
                    last[o.eng] = o
            for o in last.values():
                o.signal = True
        ncomp = self.ncomp
        for o in ops:
            e = o.eng
            ck = self.clock[e]
            waits = []
            if o.dma:
                i = self.dcnt[e]
                self.dcnt[e] += 1
                slot = i % NDSEM
                prev = self.dlast[e][slot]
                if prev is not None:
                    o.deps.append((prev, True))
                o.dsem = slot
                o.dval = 16 * (i // NDSEM + 1)
                self.dlast[e][slot] = o
            for (d, raw) in o.deps:
                if d.dma:
                    comp = NENG + d.eng * NDSEM + d.dsem
                    val = d.dval
                    sem = self.dsem[d.eng][d.dsem]
                else:
                    if d.eng == e and not o.dma and e == PE:
                        continue
                    if not d.signal:
                        continue
                    comp = d.eng
                    val = d.sig
                    sem = self.esem[d.eng]
                if ck[comp] >= val:
                    continue
                waits.append((sem, val))
                dc = d.clock
                if dc is not None:
                    for k in range(ncomp):
                        if dc[k] > ck[k]:
                            ck[k] = dc[k]
                if ck[comp] < val:
                    ck[comp] = val
            o.waits = waits
            if o.dma:
                o.clock = list(ck)
            elif o.signal:
                self.sigcnt[e] += 1
                o.sig = self.sigcnt[e]
                o.clock = list(ck)
            o.deps = None
        for o in ops:
            eng = self.engs[o.eng]
            for (sem, val) in o.waits:
                eng.wait_ge(sem, val)
                self.nwaits += 1
            ins = o.fn(eng)
            self.nops += 1
            if o.dma:
                ins.then_inc(self.dsem[o.eng][o.dsem], 16)
            elif o.signal:
                ins.then_inc(self.esem[o.eng], 1)
            o.fn = None
            o.waits = None
        if barrier:
            self.barrier()

    def barrier(self):
        for e in range(NENG):
            eng = self.engs[e]
            ck = self.clock[e]
            for f in range(NENG):
                if ck[f] < self.sigcnt[f]:
                    eng.wait_ge(self.esem[f], self.sigcnt[f])
                    ck[f] = self.sigcnt[f]
            for q in range(NENG):
                for s in range(NDSEM):
                    d = self.dlast[q][s]
                    if d is not None:
                        comp = NENG + q * NDSEM + s
                        if ck[comp] < d.dval:
                            eng.wait_ge(self.dsem[q][s], d.dval)
                            ck[comp] = d.dval


class Tl:
    __slots__ = ('ap', 'r')

    def __init__(self, ap, r=None):
        self.ap = ap
        self.r = r if r is not None else Reg()

    def __getitem__(self, k):
        return self.ap[k]


class Ctx:
    def __init__(self, dbg_in=(), dbg_out=()):
        self.nc = nc = bass.Bass("TRN2", target_bir_lowering=False)
        self.S = Sched(nc)
        self.dbg_in = set(dbg_in)
        self.dbg_out = set(dbg_out)
        self.arena = nc.alloc_sbuf_tensor('arena', [128, ARENA_BYTES // 4], F32)
        self.aoff = 0
        self.psum = [Tl(nc.alloc_psum_tensor('ps%d' % i, [128, 512], F32)[:], Reg(excl=True)) for i in range(8)]
        self.inputs = {}
        self.outputs = {}
        self.n_p = 0

    def dram(self, name, shape, dtype=F32, kind=None):
        if kind is None:
            kind = 'Internal'
            if name in self.dbg_in:
                kind = 'ExternalInput'
            elif name in self.dbg_out:
                kind = 'ExternalOutput'
        t = self.nc.dram_tensor(name, list(shape), dtype, kind=kind).ap()
        if kind == 'ExternalInput':
            self.inputs[name] = (tuple(shape), dtype)
        elif kind == 'ExternalOutput':
            self.outputs[name] = (tuple(shape), dtype)
        return t

    def sb(self, free_shape, dtype=F32):
        esz = 4 if dtype == F32 else 2
        n = 1
        for v in free_shape:
            n *= v
        nbytes = (n * esz + 31) // 32 * 32
        assert self.aoff + nbytes <= ARENA_BYTES, ('arena overflow', self.aoff, nbytes)
        a = self.arena[:, self.aoff // 4:(self.aoff + nbytes) // 4]
        self.aoff += nbytes
        if dtype != F32:
            a = a.bitcast(dtype)
        a = a[:, 0:n]
        if len(free_shape) == 2:
            a = a.rearrange('p (a b) -> p a b', b=free_shape[1])
        elif len(free_shape) == 3:
            a = a.rearrange('p (a b c) -> p a b c', b=free_shape[1], c=free_shape[2])
        return Tl(a)

    def persist(self, free_shape, dtype=F32):
        self.n_p += 1
        t = self.nc.alloc_sbuf_tensor('pp%d' % self.n_p, [128] + list(free_shape), dtype)
        return Tl(t[:])

    def stage_end(self):
        self.S.flush(barrier=True)
        self.aoff = 0

    def dma(self, out, in_, R, W, q=SP):
        self.S.op(q, lambda e: e.dma_start(out=out, in_=in_), reads=R, writes=W, dma=True)

    def mm(self, out, lhsT, rhs, start, stop, R, W):
        self.S.op(PE, lambda e: e.matmul(out, lhsT=lhsT, rhs=rhs, start=start, stop=stop), reads=R, writes=W)

    def act(self, out, in_, func, R, W, scale=1.0, bias=0.0, accum_out=None):
        if accum_out is None:
            self.S.op(ACT, lambda e: e.activation(out=out, in_=in_, func=func, bias=bias, scale=scale), reads=R, writes=W)
        else:
            self.S.op(ACT, lambda e: e.activation(out=out, in_=in_, func=func, bias=bias, scale=scale, accum_out=accum_out), reads=R, writes=W)

    def tt(self, out, in0, in1, op, R, W, eng=DVE):
        self.S.op(eng, lambda e: e.tensor_tensor(out=out, in0=in0, in1=in1, op=op), reads=R, writes=W)

    def ts(self, out, in0, s1, s2, op0, op1, R, W, eng=DVE):
        if s2 is None:
            self.S.op(eng, lambda e: e.tensor_scalar(out=out, in0=in0, scalar1=s1, scalar2=None, op0=op0), reads=R, writes=W)
        else:
            self.S.op(eng, lambda e: e.tensor_scalar(out=out, in0=in0, scalar1=s1, scalar2=s2, op0=op0, op1=op1), reads=R, writes=W)

    def stt(self, out, in0, scalar, in1, op0, op1, R, W, eng=DVE):
        self.S.op(eng, lambda e: e.scalar_tensor_tensor(out=out, in0=in0, scalar=scalar, in1=in1, op0=op0, op1=op1), reads=R, writes=W)

    def copy(self, out, in_, R, W, eng=DVE):
        self.S.op(eng, lambda e: e.tensor_copy(out=out, in_=in_), reads=R, writes=W)

    def memset(self, out, val, W, eng=DVE):
        self.S.op(eng, lambda e: e.memset(out, val), writes=W)

    def recip(self, out, in_, R, W):
        self.S.op(DVE, lambda e: e.reciprocal(out=out, in_=in_), reads=R, writes=W)

    def transpose(self, out, in_, ident, R, W):
        self.S.op(PE, lambda e: e.transpose(out, in_, ident), reads=R, writes=W)


def st_consts(c):
    c.ones_bf = c.persist([128], BF16)
    c.memset(c.ones_bf.ap, 1.0, [c.ones_bf.r])
    c.ident_f = c.persist([128], F32)
    c.memset(c.ident_f.ap, 0.0, [c.ident_f.r], eng=POOL)
    idf = c.ident_f
    c.S.op(POOL, lambda e: e.affine_select(out=idf.ap, in_=idf.ap, pattern=[[-1, 128]], compare_op=ALU.not_equal,
                                           fill=1.0, base=0, channel_multiplier=1), reads=[idf.r], writes=[idf.r])
    c.ident_b = c.persist([128], BF16)
    c.copy(c.ident_b.ap, c.ident_f.ap, [c.ident_f.r], [c.ident_b.r])
    c.eps_t = c.persist([1], F32)
    c.memset(c.eps_t.ap, EPS, [c.eps_t.r])


def st_mod(c, layer):
    M = c.persist([2, 9, 16], F32)
    A = c.persist([2, 3, 16], F32)
    G = c.persist([2, 3, 16], F32)
    c.M[layer], c.A[layer], c.G[layer] = M, A, G
    sc = c.sb([16, 2], F32)
    sg = c.sb([16, 2], F32)
    c.dma(sc.ap, c.d_cc, [], [sc.r])
    c.act(sg.ap, sc.ap, AF.Sigmoid, [sc.r], [sg.r])
    c.tt(sc.ap, sc.ap, sg.ap, ALU.mult, [sc.r, sg.r], [sc.r])
    mb = c.sb([144], F32)
    c.dma(mb.ap, c.d_modb[layer], [], [mb.r])
    ng = c.sb([3, 16], F32)
    c.dma(ng.ap, c.d_normg[layer], [], [ng.r])
    NB = 3
    wbuf = [c.sb([16, 512], F32) for _ in range(NB)]
    ps = c.psum[0]
    wsrc = c.d_modw[layer].rearrange('(kt p) f -> p kt f', p=128)
    for blk in range(36):
        wb = wbuf[blk % NB]
        c.dma(wb.ap, wsrc[:, :, blk * 512:(blk + 1) * 512], [], [wb.r])
        for j in range(4):
            ft = blk * 4 + j
            for kt in range(16):
                c.mm(ps.ap[:, 2 * ft:2 * ft + 2], wb.ap[:, kt, j * 128:(j + 1) * 128], sc.ap[:, kt, :],
                     kt == 0, kt == 15, [wb.r, sc.r], [ps.r])
    for g in range(2):
        src = ps.ap[:, 0:288].rearrange('p (f g) -> p g f', g=2)[:, g, :]
        c.tt(M.ap[:, g].rearrange('p i d -> p (i d)'), src, mb.ap, ALU.add, [ps.r, mb.r], [M.r])
    for g in range(2):
        for i in range(3):
            c.stt(A.ap[:, g, i, :], M.ap[:, g, 3 * i + 1, :], 1.0, ng.ap[:, i, :], ALU.add, ALU.mult, [M.r, ng.r], [A.r])
            fac = 1.0 if i == 1 else 0.5
            c.ts(G.ap[:, g, i, :], M.ap[:, g, 3 * i + 2, :], fac, None, ALU.mult, None, [M.r], [G.r])
    c.stage_end()


def bidx(t0):
    return [b[0] for b in TB].index(t0)


def xr_col(c, t0):
    return [c.XRr[dt][bidx(t0)] for dt in range(16)]


def st_norm(c, layer, i, blocks=TB):
    H = c.sb([16, T], BF16)
    mark = c.aoff
    A, M = c.A[layer], c.M[layer]
    xt = [c.sb([16, 512], F32) for _ in range(2)]
    xrg = [[Reg() for _ in range(16)] for _ in range(2)]
    sq = [c.sb([16, 512], BF16) for _ in range(2)]
    rs = [c.sb([512], F32) for _ in range(2)]
    for bi, (t0, n) in enumerate(blocks):
        g = 1 if t0 < LC else 0
        x_, s_, r_ = xt[bi % 2], sq[bi % 2], rs[bi % 2]
        xr_ = xrg[bi % 2]
        ps = c.psum[6 + bi % 2]
        c.dma(x_.ap[:, :, 0:n], c.XR[:, :, t0:t0 + n].rearrange('d p t -> p d t'), xr_col(c, t0), xr_)
        c.act(s_.ap[:, :, 0:n], x_.ap[:, :, 0:n], AF.Square, xr_, [s_.r])
        for dt in range(16):
            c.mm(ps.ap[:, 0:n], c.ones_bf.ap, s_.ap[:, dt, 0:n], dt == 0, dt == 15, [s_.r, c.ones_bf.r], [ps.r])
        c.act(r_.ap[:, 0:n], ps.ap[:, 0:n], AF.Sqrt, [ps.r, c.eps_t.r], [r_.r], scale=1.0 / D, bias=c.eps_t.ap[:, 0:1])
        c.recip(r_.ap[:, 0:n], r_.ap[:, 0:n], [r_.r], [r_.r])
        for dt in range(16):
            c.stt(x_.ap[:, dt, 0:n], x_.ap[:, dt, 0:n], A.ap[:, g, i, dt:dt + 1], r_.ap[:, 0:n], ALU.mult, ALU.mult,
                  [xr_[dt], A.r, r_.r], [xr_[dt]])
            c.act(H.ap[:, dt, t0:t0 + n], x_.ap[:, dt, 0:n], AF.Identity, [xr_[dt], M.r], [H.r],
                  bias=M.ap[:, g, 3 * i, dt:dt + 1])
    c.S.flush(barrier=True)
    c.aoff = mark
    return H


FCH = [6, 6, 6, 6, 5, 5, 5, 5]


def st_ffn(c, layer, j, i, blocks=TB):
    H = st_norm(c, layer, i, blocks)
    G = c.G[layer]
    wg_src = c.d_wg[layer, j]
    wu_src = c.d_wu[layer, j]
    wd_src = c.d_wd[layer, j].rearrange('(ft p) d -> ft p d', p=128)
    NW = 3
    wgb = [c.sb([16, 128], BF16) for _ in range(NW)]
    wub = [c.sb([16, 128], BF16) for _ in range(NW)]
    wdb = [c.sb([2048], BF16) for _ in range(12)]
    hid = [c.sb([T], BF16) for _ in range(6)]
    sgt = [c.sb([512], F32) for _ in range(2)]
    xt = [c.sb([512], F32) for _ in range(6)]
    f0 = 0
    nup = 0
    nx = 0
    nwd = 0
    nfc = 0
    for ch, nf in enumerate(FCH):
        wd_tiles = []
        for k in range(nf):
            f = f0 + k
            wg_, wu_ = wgb[nfc % NW], wub[nfc % NW]
            nfc += 1
            wd_ = wdb[nwd % 12]
            nwd += 1
            c.dma(wg_.ap, wg_src[f], [], [wg_.r], q=POOL)
            c.dma(wu_.ap, wu_src[f], [], [wu_.r], q=POOL)
            c.dma(wd_.ap, wd_src[f], [], [wd_.r], q=POOL)
            wd_tiles.append(wd_)
            hk = hid[k]
            for bi, (t0, n) in enumerate(blocks):
                pg, pu = c.psum[2 * (nup % 2)], c.psum[2 * (nup % 2) + 1]
                for kt in range(16):
                    c.mm(pg.ap[:, 0:n], wg_.ap[:, kt, :], H.ap[:, kt, t0:t0 + n], kt == 0, kt == 15, [wg_.r, H.r], [pg.r])
                for kt in range(16):
                    c.mm(pu.ap[:, 0:n], wu_.ap[:, kt, :], H.ap[:, kt, t0:t0 + n], kt == 0, kt == 15, [wu_.r, H.r], [pu.r])
                s_ = sgt[nup % 2]
                c.act(s_.ap[:, 0:n], pg.ap[:, 0:n], AF.Silu, [pg.r], [s_.r])
                c.tt(hk.ap[:, t0:t0 + n], s_.ap[:, 0:n], pu.ap[:, 0:n], ALU.mult, [s_.r, pu.r], [hk.r])
                nup += 1
        tiles = [(dt, t0, n) for dt in range(16) for (t0, n) in blocks]
        PF = 4

        def issue_load(idx):
            dt, t0, n = tiles[idx]
            x_ = xt[(nx + idx) % 6]
            c.dma(x_.ap[:, 0:n], c.XR[dt, :, t0:t0 + n], [c.XRr[dt][bidx(t0)]], [x_.r])
        for idx in range(min(PF, len(tiles))):
            issue_load(idx)
        for idx, (dt, t0, n) in enumerate(tiles):
            if idx + PF < len(tiles):
                issue_load(idx + PF)
            g = 1 if t0 < LC else 0
            po = c.psum[4 + (nx + idx) % 2]
            x_ = xt[(nx + idx) % 6]
            xreg = c.XRr[dt][bidx(t0)]
            for k in range(nf):
                c.mm(po.ap[:, 0:n], wd_tiles[k].ap[:, dt * 128:(dt + 1) * 128], hid[k].ap[:, t0:t0 + n],
                     k == 0, k == nf - 1, [wd_tiles[k].r, hid[k].r], [po.r])
            c.stt(x_.ap[:, 0:n], po.ap[:, 0:n], G.ap[:, g, i, dt:dt + 1], x_.ap[:, 0:n], ALU.mult, ALU.add,
                  [po.r, G.r, x_.r], [x_.r])
            c.dma(c.XR[dt, :, t0:t0 + n], x_.ap[:, 0:n], [x_.r], [xreg], q=ACT)
        nx += len(tiles)
        f0 += nf
    c.stage_end()


def st_load_x(c):
    for dt in range(16):
        c.dma(c.XR[dt], c.d_xin[dt], [], c.XRr[dt])
    c.stage_end()


def st_final(c):
    fg = c.sb([16], F32)
    c.dma(fg.ap, c.d_finalg, [], [fg.r])
    xt = [c.sb([16, 512], F32) for _ in range(2)]
    sq = [c.sb([16, 512], BF16) for _ in range(2)]
    rs = [c.sb([512], F32) for _ in range(2)]
    for bi, (t0, n) in enumerate(TB[1:]):
        x_, s_, r_ = xt[bi % 2], sq[bi % 2], rs[bi % 2]
        ps = c.psum[bi % 2]
        c.dma(x_.ap, c.XR[:, :, t0:t0 + n].rearrange('d p t -> p d t'), xr_col(c, t0), [x_.r])
        c.act(s_.ap, x_.ap, AF.Square, [x_.r], [s_.r])
        for dt in range(16):
            c.mm(ps.ap, c.ones_bf.ap, s_.ap[:, dt, :], dt == 0, dt == 15, [s_.r, c.ones_bf.r], [ps.r])
        c.act(r_.ap, ps.ap, AF.Sqrt, [ps.r, c.eps_t.r], [r_.r], scale=1.0 / D, bias=c.eps_t.ap[:, 0:1])
        c.recip(r_.ap, r_.ap, [r_.r], [r_.r])
        for dt in range(16):
            c.stt(x_.ap[:, dt, :], x_.ap[:, dt, :], fg.ap[:, dt:dt + 1], r_.ap, ALU.mult, ALU.mult, [x_.r, fg.r, r_.r], [x_.r])
        c.dma(c.d_out[:, :, t0 - LC:t0 - LC + n].rearrange('d p t -> p d t'), x_.ap, [x_.r], [c.outr])
    c.stage_end()


def declare_io(c):
    c.d_xin = c.dram('xin', [16, 128, T], F32, 'ExternalInput')
    c.d_cc = c.dram('cc', [128, 16, 2], F32, 'ExternalInput')
    c.d_modw = c.dram('mod_w', [2, D, 9 * D], F32, 'ExternalInput')
    c.d_modb = c.dram('mod_b', [2, 128, 144], F32, 'ExternalInput')
    c.d_normg = c.dram('norm_g', [2, 128, 3, 16], F32, 'ExternalInput')
    c.d_finalg = c.dram('final_g', [128, 16], F32, 'ExternalInput')
    c.d_wg = c.dram('ffn_wg', [2, 2, NFT, 128, 16, 128], F32, 'ExternalInput')
    c.d_wu = c.dram('ffn_wu', [2, 2, NFT, 128, 16, 128], F32, 'ExternalInput')
    c.d_wd = c.dram('ffn_wd', [2, 2, DFF, D], F32, 'ExternalInput')
    EI = 'ExternalInput'
    c.d_evwin_t = c.dram('ev_w_in_t', [24, 128, 16, 128], F32, EI)
    c.d_evwin = c.dram('ev_w_in', [D, 4096], F32, EI)
    c.d_rpbt = c.dram('rpbt', [64, 8, 15, 64], F32, EI)
    c.d_namask = c.dram('namask', [64, 64], F32, EI)
    c.d_s5are = c.dram('s5are', [128, 64], F32, EI)
    c.d_s5aim = c.dram('s5aim', [128, 64], F32, EI)
    c.d_s5ldt = c.dram('s5ldt', [128, 64], F32, EI)
    c.d_s5d = c.dram('s5d', [128, 32], F32, EI)
    c.d_iota1 = c.dram('iota1', [128, 512], F32, EI)
    c.d_s5bre = c.dram('s5bre', [128, 64, 32], F32, EI)
    c.d_s5bim = c.dram('s5bim', [128, 64, 32], F32, EI)
    c.d_s5cre = c.dram('s5cre', [128, 64, 32], F32, EI)
    c.d_s5cim = c.dram('s5cim', [128, 64, 32], F32, EI)
    c.d_gluw = c.dram('glu_w', [1024, 1024], F32, EI)
    c.d_glub = c.dram('glu_b', [128, 8], F32, EI)
    c.d_wout = [c.dram('ev_w_out', [D, D], F32, EI), c.dram('od_w_out', [D, D], F32, EI)]
    c.d_odwin_t = c.dram('od_w_in_t', [48, 128, 16, 128], F32, EI)
    c.d_odwdt = c.dram('od_w_dt', [128, 2, 16, 16], F32, EI)
    c.d_hysw = c.dram('hysw', [128, 24, 3], F32, EI)
    c.d_hysb = c.dram('hysb', [128, 24], F32, EI)
    c.d_ssdcw = c.dram('ssdcw', [128, 16, 3], F32, EI)
    c.d_ssdcb = c.dram('ssdcb', [128, 16], F32, EI)
    c.d_dtbias = c.dram('dtbias', [16, 2], F32, EI)
    c.d_alog = c.dram('alog', [16, 2], F32, EI)
    c.d_ssdd = c.dram('ssdd', [128, 16], F32, EI)
    c.d_tri = c.dram('tri', [2, 128, 128], F32, EI)
    c.d_ssdng = c.dram('ssdng', [128, 8], F32, EI)
    c.d_hyz = c.dram('hyz', [33, L], F32, EI)
    c.d_hywin = c.dram('hywin', [33, 64], F32, EI)
    c.d_hywmid = c.dram('hywmid', [64, 2, 64], F32, EI)
    c.d_hyfb = c.dram('hyfb', [64, 4], F32, EI)
    c.d_hywout = c.dram('hywout', [64, 4096], F32, EI)
    c.d_decay = c.dram('decay', [L, 1024], F32, EI)
    c.d_hyfbias = c.dram('hyfbias', [128, 2, 1024], F32, EI)
    c.d_Cf = c.dram('Cf', [16, 128, 16, 128], BF16, EI)
    c.d_Sf = c.dram('Sf', [16, 128, 16, 128], BF16, EI)
    c.d_Ci = c.dram('Ci', [16, 128, 16, 128], BF16, EI)
    c.d_Si = c.dram('Si', [16, 128, 16, 128], BF16, EI)
    c.HYd = c.dram('HYd', [24, 128, L], BF16)
    c.Zd = c.dram('Zd', [8, 128, L], BF16)
    c.XBCd = c.dram('XBCd', [16, 128, T], BF16)
    c.DTd = c.dram('DTd', [2, 16, T], F32)
    c.CUMd = c.dram('CUMd', [2, 16, T], F32)
    c.YSd = c.dram('YSd', [16, 64, L], F32)
    c.HYr, c.Zr, c.XBCr, c.DTr, c.CUMr, c.YSr = Reg(), Reg(), Reg(), Reg(), Reg(), Reg()
    c.Ud = c.dram('Ud', [8, 128, T], BF16)
    c.Qd = c.dram('Qd', [8, 128, T], BF16)
    c.Kd = c.dram('Kd', [8, 128, T], BF16)
    c.Vd = c.dram('Vd', [18, 128, 1024], BF16)
    c.Yd = c.dram('Yd', [1024, T], F32)
    c.CATd = c.dram('CATd', [16, 128, T], BF16)
    c.Ur, c.Qr, c.Kr, c.Vr, c.Yr = Reg(), Reg(), Reg(), Reg(), Reg()
    c.CATr = [Reg(), Reg()]
    c.d_out = c.dram('out', [16, 128, L], F32, 'ExternalOutput')
    c.outr = Reg()
    c.XR = c.dram('XR', [16, 128, T], F32)
    c.XRr = [[Reg() for _ in range(len(TB))] for _ in range(16)]
    c.M, c.A, c.G = {}, {}, {}


def build(stages=None, dbg_in=(), dbg_out=()):
    c = Ctx(dbg_in, dbg_out)
    declare_io(c)
    st_consts(c)
    allst = stages is None
    if allst or 'load' in stages:
        st_load_x(c)
    for layer in range(2):
        if allst or ('mod%d' % layer) in stages:
            st_mod(c, layer)
        if allst or ('ffa%d' % layer) in stages:
            st_ffn(c, layer, 0, 0)
        if layer == 0 and (allst or 'evmix' in stages):
            sub = stages if (stages and any(k.startswith('ev_') for k in stages)) else None
            if sub is None or 'ev_proj' in sub:
                H = st_norm(c, 0, 1)
                st_even_proj(c, H)
            if sub is None or 'ev_na' in sub:
                st_na(c, True)
            if sub is None or 'ev_s5' in sub:
                st_s5(c, True)
            if sub is None or 'ev_glu' in sub:
                st_glu_wout(c, 0, True)
        if layer == 1 and (allst or 'odmix' in stages):
            sub = stages if (stages and any(k.startswith('od_') for k in stages)) else None
            if sub is None or 'od_proj' in sub:
                H = st_norm(c, 1, 1)
                st_odd_proj(c, H)
            if sub is None or 'od_ssd' in sub:
                st_ssd(c)
            if sub is None or 'od_hy' in sub:
                st_hyena(c)
            if sub is None or 'od_out' in sub:
                st_odd_out(c)
        if allst or ('ffb%d' % layer) in stages:
            st_ffn(c, layer, 1, 2, TB if layer == 0 else TB[1:])
    if allst or 'final' in stages:
        st_final(c)
    c.stage_end()
    return c


def host_prep(inp, b):
    f = np.float32
    m = {}
    xc = np.concatenate([inp['ctx'][b], inp['x'][b]], axis=0)
    m['xin'] = np.ascontiguousarray(xc.T.reshape(16, 128, T))
    cc = np.stack([inp['c'][b], inp['c_ctx']], axis=-1)
    m['cc'] = np.ascontiguousarray(cc.reshape(16, 128, 2).transpose(1, 0, 2))
    return m


_SHARED = {}


def host_shared(inp):
    m = {}
    m['mod_w'] = np.ascontiguousarray(inp['mod_w'])
    m['mod_b'] = np.ascontiguousarray(inp['mod_b'].reshape(2, 144, 128).transpose(0, 2, 1))
    m['norm_g'] = np.ascontiguousarray(inp['norm_g'].reshape(2, 3, 16, 128).transpose(0, 3, 1, 2))
    m['final_g'] = np.ascontiguousarray(inp['final_g'].reshape(16, 128).T)
    for k in ('ffn_wg', 'ffn_wu'):
        m[k] = np.ascontiguousarray(inp[k].reshape(2, 2, 16, 128, NFT, 128).transpose(0, 1, 4, 3, 2, 5))
    m['ffn_wd'] = np.ascontiguousarray(inp['ffn_wd'])
    w = inp['ev_w_in'][0]
    m['ev_w_in'] = np.ascontiguousarray(w)
    m['ev_w_in_t'] = np.ascontiguousarray(w[:, :3072].reshape(16, 128, 24, 128).transpose(2, 1, 0, 3))
    col = np.arange(64)
    dc = np.clip(col[:, None] - col[None, :] + 15, 0, 30)
    rp = inp['na_rpb'][0][:, :, dc]
    m['rpbt'] = np.ascontiguousarray(rp.transpose(2, 0, 1, 3))
    cs = np.clip(col - 8, 0, 48)
    ok = (col[:, None] >= cs[None, :]) & (col[:, None] < cs[None, :] + 16)
    m['namask'] = np.where(ok, 0.0, NEGM).astype(np.float32)

    def st_lay(a):
        return np.ascontiguousarray(a.reshape(2, 32, 2, 64).transpose(2, 3, 0, 1).reshape(128, 64))
    m['s5are'] = st_lay(inp['s5_a_re'][0])
    m['s5aim'] = st_lay(inp['s5_a_im'][0])
    m['s5ldt'] = st_lay(np.repeat(inp['s5_log_dt'][0][:, :, None], 64, axis=2))
    dd = np.zeros((128, 32), np.float32)
    dd[0:32] = inp['s5_d'][0].reshape(32, 32).T
    m['s5d'] = dd
    m['iota1'] = np.ascontiguousarray(np.broadcast_to(np.arange(1, 513, dtype=np.float32), (128, 512)))

    def b_blk(b):
        o = np.zeros((128, 2, 32, 32), np.float32)
        bb = b.reshape(2, 32, 2, 64, 16)
        o[0:64, :, :, 0:16] = bb[:, :, 0].transpose(2, 0, 1, 3)
        o[64:128, :, :, 16:32] = bb[:, :, 1].transpose(2, 0, 1, 3)
        return o.reshape(128, 64, 32)

    def c_blk(cm):
        o = np.zeros((128, 2, 32, 32), np.float32)
        cc = cm.reshape(2, 32, 2, 16, 64)
        o[0:64, :, :, 0:16] = cc[:, :, 0].transpose(3, 0, 1, 2)
        o[64:128, :, :, 16:32] = cc[:, :, 1].transpose(3, 0, 1, 2)
        return o.reshape(128, 64, 32)
    m['s5bre'] = b_blk(inp['s5_b_re'][0])
    m['s5bim'] = b_blk(inp['s5_b_im'][0])
    m['s5cre'] = c_blk(inp['s5_c_re'][0])
    m['s5cim'] = c_blk(inp['s5_c_im'][0])
    m['glu_w'] = np.ascontiguousarray(inp['s5_glu_w'][0])
    m['glu_b'] = np.ascontiguousarray(inp['s5_glu_b'][0].reshape(8, 128).T)
    m['ev_w_out'] = np.ascontiguousarray(inp['ev_w_out'][0])
    m['od_w_out'] = np.ascontiguousarray(inp['od_w_out'][0])
    w = inp['od_w_in'][0]
    m['od_w_in_t'] = np.ascontiguousarray(w[:, :6144].reshape(16, 128, 48, 128).transpose(2, 1, 0, 3))
    m['od_w_dt'] = np.ascontiguousarray(w[:, 6144:6176].reshape(16, 128, 2, 16).transpose(1, 2, 0, 3))
    m['hysw'] = np.ascontiguousarray(inp['hy_short_w'][0].T.reshape(24, 128, 3).transpose(1, 0, 2))
    m['hysb'] = np.ascontiguousarray(inp['hy_short_b'][0].reshape(24, 128).T)
    m['ssdcw'] = np.ascontiguousarray(inp['ssd_conv_w'][0].T.reshape(16, 128, 3).transpose(1, 0, 2))
    m['ssdcb'] = np.ascontiguousarray(inp['ssd_conv_b'][0].reshape(16, 128).T)
    m['dtbias'] = np.ascontiguousarray(inp['ssd_dt_bias'][0].T)
    m['alog'] = np.ascontiguousarray(inp['ssd_a_log'][0].T)
    m['ssdd'] = np.ascontiguousarray(np.broadcast_to(inp['ssd_d'][0][None, :], (128, 16)))
    ii = np.arange(128)
    m['tri'] = np.stack([(ii[:, None] <= ii[None, :]), (ii[:, None] >= ii[None, :])]).astype(np.float32)
    m['ssdng'] = np.ascontiguousarray(inp['ssd_norm_g'][0].reshape(8, 128).T)
    m['hywin'] = np.ascontiguousarray(inp['hy_w_in'][0])
    m['hywmid'] = np.ascontiguousarray(inp['hy_w_mid'][0].transpose(1, 0, 2))
    m['hyfb'] = np.ascontiguousarray(np.stack([inp['hy_freq'][0], inp['hy_b_in'][0], inp['hy_b_mid'][0][0], inp['hy_b_mid'][0][1]], axis=1))
    m['hywout'] = np.ascontiguousarray(inp['hy_w_out'][0])
    m['hyfbias'] = np.ascontiguousarray(np.broadcast_to(inp['hy_fbias'][0][None], (128, 2, 1024)))
    m.update(hy_consts())
    return m


_HYC = {}


def hy_consts():
    if _HYC:
        return _HYC
    f32 = np.float32
    t = np.linspace(0.0, 1.0, L, dtype=f32)[:, None]
    w = (2.0 * math.pi * np.arange(L, dtype=f32)[:, None] / L).astype(f32)
    f = np.linspace(1e-4, 15, 16, dtype=f32)[None, :]
    z = np.concatenate([t, np.cos(f * w), -np.sin(f * w)], axis=-1).astype(f32)
    _HYC['hyz'] = np.ascontiguousarray(z.T)
    mx = math.log(1e-2) / 0.3
    mn = math.log(1e-2) / 1.5
    deltas = np.abs(np.linspace(mn, mx, 1024, dtype=f32))
    _HYC['decay'] = np.exp(-t * deltas[None, :]).astype(f32)
    n = np.arange(L, dtype=np.int64)
    ang = 2.0 * np.pi * ((n[:, None] * n[None, :]) % 4096).astype(np.float64) / 4096.0
    Cf = np.cos(ang)
    Sf = -np.sin(ang)
    sgn = np.where(n % 2 == 0, 1.0, -1.0)
    Sf[:, 0] = sgn
    wf = np.full(L, 2.0 / 4096.0)
    wf[0] = 1.0 / 4096.0
    Ci = wf[:, None] * np.cos(ang)
    Si = -wf[:, None] * np.sin(ang)
    Si[0, :] = sgn / 4096.0

    def lay(a):
        return np.ascontiguousarray(a.reshape(16, 128, 16, 128).transpose(2, 1, 0, 3).astype(f32).astype(ml_dtypes.bfloat16))
    _HYC['Cf'], _HYC['Sf'], _HYC['Ci'], _HYC['Si'] = lay(Cf), lay(Sf), lay(Ci), lay(Si)
    return _HYC


def kernel(**inputs):
    inp = {k: np.asarray(v) for k, v in inputs.items()}
    c = build()
    shared = host_shared(inp)
    in_maps = []
    for b in range(8):
        m = dict(shared)
        m.update(host_prep(inp, b))
        in_maps.append({k: m[k] for k in c.inputs})
    res = run_bass_kernel_spmd(c.nc, in_maps, core_ids=list(range(8)))
    outs = []
    for b in range(8):
        o = np.asarray(res.results[b]['out'])
        outs.append(o.reshape(D, L).T)
    return np.ascontiguousarray(np.stack(outs, axis=0)).astype(np.float32)


SQ128 = math.sqrt(128.0)
NEGM = -30000.0


def evac(c, n, out, in_, R, W):
    if n % 2 == 0:
        c.act(out, in_, AF.Identity, R, W)
    else:
        c.copy(out, in_, R, W)


def st_even_proj(c, H):
    wsrc = c.d_evwin_t
    wb = [c.sb([16, 128], BF16) for _ in range(3)]
    ob = [c.sb([512], BF16) for _ in range(4)]
    dst = [c.Ud, c.Qd, c.Kd]
    dreg = [c.Ur, c.Qr, c.Kr]
    cnt = 0
    for f in range(24):
        w_ = wb[f % 3]
        c.dma(w_.ap, wsrc[f], [], [w_.r], q=POOL)
        for bi, (t0, n) in enumerate(TB):
            ps = c.psum[cnt % 4]
            o_ = ob[cnt % 4]
            for kt in range(16):
                c.mm(ps.ap[:, 0:n], w_.ap[:, kt, :], H.ap[:, kt, t0:t0 + n], kt == 0, kt == 15, [w_.r, H.r], [ps.r])
            evac(c, cnt, o_.ap[:, 0:n], ps.ap[:, 0:n], [ps.r], [o_.r])
            c.dma(dst[f // 8][f % 8, :, t0:t0 + n], o_.ap[:, 0:n], [o_.r], [dreg[f // 8]])
            cnt += 1
    vsrc = c.d_evwin.rearrange('(kt p) f -> p kt f', p=128)
    vw = [c.sb([16, 512], BF16) for _ in range(2)]
    for j in range(2):
        for kt in range(16):
            c.dma(vw[j].ap[:, kt, :], vsrc[:, kt, 3072 + 512 * j:3072 + 512 * (j + 1)], [], [vw[j].r], q=POOL)
    for tt_ in range(18):
        for j in range(2):
            ps = c.psum[cnt % 4]
            o_ = ob[cnt % 4]
            for kt in range(16):
                c.mm(ps.ap, H.ap[:, kt, tt_ * 128:(tt_ + 1) * 128], vw[j].ap[:, kt, :], kt == 0, kt == 15, [vw[j].r, H.r], [ps.r])
            evac(c, cnt, o_.ap, ps.ap, [ps.r], [o_.r])
            c.dma(c.Vd[tt_, :, 512 * j:512 * (j + 1)], o_.ap, [o_.r], [c.Vr])
            cnt += 1
    c.stage_end()


def st_na(c, with_ctx=True):
    Tb = c.sb([8, 15, 64], F32)
    mk = c.sb([64], F32)
    c.dma(Tb.ap[0:64], c.d_rpbt, [], [Tb.r])
    c.dma(mk.ap[0:64], c.d_namask, [], [mk.r])
    c.stt(Tb.ap[0:64].rearrange('p h d q -> p (h d) q'), Tb.ap[0:64].rearrange('p h d q -> p (h d) q'), SQ128,
          mk.ap[0:64, None, :].to_broadcast([64, 120, 64]), ALU.mult, ALU.add, [Tb.r, mk.r], [Tb.r])
    neg = c.sb([64], F32)
    c.memset(neg.ap, NEGM, [neg.r])
    sel = c.sb([2, 128], F32)
    c.memset(sel.ap, 0.0, [sel.r])
    c.copy(sel.ap[0:64, 0, 0:64], c.ident_f.ap[0:64, 0:64], [c.ident_f.r, sel.r], [sel.r])
    c.copy(sel.ap[0:64, 1, 64:128], c.ident_f.ap[0:64, 0:64], [c.ident_f.r, sel.r], [sel.r])
    qb = [c.sb([T], BF16) for _ in range(2)]
    kb = [c.sb([T], BF16) for _ in range(2)]
    vb = [c.sb([18, 128], BF16) for _ in range(2)]
    ob = [c.sb([T], BF16) for _ in range(2)]
    eb = [c.sb([7, 64], BF16) for _ in range(3)]
    ec = c.sb([2, 256], BF16)
    rz = [c.sb([64], F32) for _ in range(2)]
    rzc = c.sb([256], F32)
    sc = 1.0 / SQ128
    it = 0
    for h in range(8):
        q_, k_, v_, o_ = qb[h % 2], kb[h % 2], vb[h % 2], ob[h % 2]
        c.dma(q_.ap, c.Qd[h], [c.Qr], [q_.r])
        c.dma(k_.ap, c.Kd[h], [c.Kr], [k_.r])
        c.dma(v_.ap, c.Vd[:, :, h * 128:(h + 1) * 128].rearrange('t p d -> p t d'), [c.Vr], [v_.r])
        if with_ctx:
            ps, po = c.psum[4], c.psum[5]
            for i in range(2):
                c.mm(ps.ap[:, i * 256:(i + 1) * 256], k_.ap[:, i * 128:(i + 1) * 128], q_.ap[:, 0:256], True, True, [k_.r, q_.r], [ps.r])
            c.act(ec.ap.rearrange('p a b -> p (a b)'), ps.ap, AF.Exp, [ps.r], [ec.r], scale=sc)
            for i in range(2):
                c.mm(po.ap[:, 0:256], v_.ap[:, i, :], ec.ap[:, i, :], i == 0, i == 1, [v_.r, ec.r], [po.r])
            for i in range(2):
                c.mm(po.ap[:, 256:512], c.ones_bf.ap, ec.ap[:, i, :], i == 0, i == 1, [c.ones_bf.r, ec.r], [po.r])
            c.recip(rzc.ap, po.ap[:, 256:512], [po.r], [rzc.r])
            c.tt(o_.ap[:, 0:256], po.ap[:, 0:256], rzc.ap, ALU.mult, [po.r, rzc.r], [o_.r])
        for r in range(32):
            rs = min(max(r - 4, 0), 24)
            base = (rs // 2) * 2
            nt = 4 if rs % 2 == 0 else 5
            ps, po = c.psum[it % 2], c.psum[2 + it % 2]
            e_ = eb[it % 3]
            z_ = rz[it % 2]
            qs = q_.ap[:, LC + r * 64:LC + (r + 1) * 64]
            tiles = []
            for i in range(nt):
                krow = base + 2 * i
                k0 = LC + krow * 64
                col = ps.ap[:, i * 64:(i + 1) * 64]
                c.mm(col, k_.ap[:, k0:k0 + 128], qs, True, False, [k_.r, q_.r], [ps.r])
                for half in range(2):
                    kr = krow + half
                    if rs <= kr < rs + 8:
                        rhs = Tb.ap[0:64, h, kr - r + 7, :]
                        rr = Tb.r
                    else:
                        rhs = neg.ap[0:64, :]
                        rr = neg.r
                    c.mm(col, sel.ap[0:64, half, :], rhs, False, half == 1, [sel.r, rr], [ps.r])
                tiles.append((LC // 128) + krow // 2)
            for i in range(2):
                col = ps.ap[:, (nt + i) * 64:(nt + i + 1) * 64]
                c.mm(col, k_.ap[:, i * 128:(i + 1) * 128], qs, True, True, [k_.r, q_.r], [ps.r])
                tiles.append(i)
            ntt = nt + 2
            c.act(e_.ap[:, 0:ntt, :].rearrange('p a b -> p (a b)'), ps.ap[:, 0:ntt * 64], AF.Exp, [ps.r], [e_.r], scale=sc)
            for i, vt in enumerate(tiles):
                c.mm(po.ap[:, 0:64], v_.ap[:, vt, :], e_.ap[:, i, :], i == 0, i == ntt - 1, [v_.r, e_.r], [po.r])
            for i in range(ntt):
                c.mm(po.ap[:, 64:128], c.ones_bf.ap, e_.ap[:, i, :], i == 0, i == ntt - 1, [c.ones_bf.r, e_.r], [po.r])
            c.recip(z_.ap, po.ap[:, 64:128], [po.r], [z_.r])
            c.tt(o_.ap[:, LC + r * 64:LC + (r + 1) * 64], po.ap[:, 0:64], z_.ap, ALU.mult, [po.r, z_.r], [o_.r])
            it += 1
        if with_ctx:
            c.dma(c.CATd[8 + h], o_.ap, [o_.r], [c.CATr[1]])
        else:
            c.dma(c.CATd[8 + h, :, LC:T], o_.ap[:, LC:T], [o_.r], [c.CATr[1]])
    c.stage_end()


def bcl(ap, shape):
    return ap.to_broadcast(list(shape))


def rnd(c, out, in_, R, W, eng=DVE):
    c.ts(out, in_, MAGIC, MAGIC, ALU.add, ALU.subtract, R, W, eng=eng)


def sincos_frac(c, t, tmp, out_s, out_c, R):
    a, b = tmp
    rnd(c, a.ap, t.ap, [t.r], [a.r])
    c.tt(a.ap, t.ap, a.ap, ALU.subtract, [t.r, a.r], [a.r])
    c.act(out_s.ap, a.ap, AF.Sin, [a.r], [out_s.r], scale=2 * math.pi)
    c.ts(b.ap, t.ap, 0.25, None, ALU.add, None, [t.r], [b.r])
    rnd(c, a.ap, b.ap, [b.r], [a.r])
    c.tt(b.ap, b.ap, a.ap, ALU.subtract, [b.r, a.r], [b.r])
    c.act(out_c.ap, b.ap, AF.Sin, [b.r], [out_c.r], scale=2 * math.pi)


def st_s5(c, with_ctx=True):
    def ld(src, shape):
        t = c.sb(shape, F32)
        c.dma(t.ap, src, [], [t.r])
        return t
    dcol = ld(c.d_s5d, [32])
    iota = ld(c.d_iota1, [512])
    r_, thp = c.sb([64], F32), c.sb([64], F32)
    BT = c.sb([128, 128], BF16)
    CB = c.sb([64, 2, 32], BF16)
    mark = c.aoff
    are, aim, ldt = ld(c.d_s5are, [64]), ld(c.d_s5aim, [64]), ld(c.d_s5ldt, [64])
    tmp = [c.sb([64], F32) for _ in range(8)]
    dtm, sn, cs, cre, cim, den = [c.sb([64], F32) for _ in range(6)]
    c.act(dtm.ap, ldt.ap, AF.Exp, [ldt.r], [dtm.r])
    c.tt(tmp[0].ap, are.ap, dtm.ap, ALU.mult, [are.r, dtm.r], [tmp[0].r])
    c.act(r_.ap, tmp[0].ap, AF.Exp, [tmp[0].r], [r_.r])
    c.tt(thp.ap, aim.ap, dtm.ap, ALU.mult, [aim.r, dtm.r], [thp.r])
    c.ts(thp.ap, thp.ap, 1.0 / (2 * math.pi), None, ALU.mult, None, [thp.r], [thp.r])
    sincos_frac(c, thp, tmp[1:3], sn, cs, None)
    nr, ni = tmp[3], tmp[4]
    c.tt(nr.ap, r_.ap, cs.ap, ALU.mult, [r_.r, cs.r], [nr.r])
    c.ts(nr.ap, nr.ap, -1.0, None, ALU.add, None, [nr.r], [nr.r])
    c.tt(ni.ap, r_.ap, sn.ap, ALU.mult, [r_.r, sn.r], [ni.r])
    c.tt(den.ap, are.ap, are.ap, ALU.mult, [are.r], [den.r])
    c.tt(tmp[5].ap, aim.ap, aim.ap, ALU.mult, [aim.r], [tmp[5].r])
    c.tt(den.ap, den.ap, tmp[5].ap, ALU.add, [den.r, tmp[5].r], [den.r])
    c.recip(den.ap, den.ap, [den.r], [den.r])
    c.tt(cre.ap, nr.ap, are.ap, ALU.mult, [nr.r, are.r], [cre.r])
    c.tt(tmp[5].ap, ni.ap, aim.ap, ALU.mult, [ni.r, aim.r], [tmp[5].r])
    c.tt(cre.ap, cre.ap, tmp[5].ap, ALU.add, [cre.r, tmp[5].r], [cre.r])
    c.tt(cre.ap, cre.ap, den.ap, ALU.mult, [cre.r, den.r], [cre.r])
    c.tt(cim.ap, ni.ap, are.ap, ALU.mult, [ni.r, are.r], [cim.r])
    c.tt(tmp[5].ap, nr.ap, aim.ap, ALU.mult, [nr.r, aim.r], [tmp[5].r])
    c.tt(cim.ap, cim.ap, tmp[5].ap, ALU.subtract, [cim.r, tmp[5].r], [cim.r])
    c.tt(cim.ap, cim.ap, den.ap, ALU.mult, [cim.r, den.r], [cim.r])
    bre, bim = ld(c.d_s5bre, [64, 32]), ld(c.d_s5bim, [64, 32])
    Bb = c.sb([64, 2, 32], F32)
    t1, t2 = c.sb([64, 32], F32), c.sb([64, 32], F32)
    creb, cimb = bcl(cre.ap, [128, 64, 32]), bcl(cim.ap, [128, 64, 32])
    c.tt(t1.ap, bre.ap, creb, ALU.mult, [bre.r, cre.r], [t1.r])
    c.tt(t2.ap, bim.ap, cimb, ALU.mult, [bim.r, cim.r], [t2.r])
    c.tt(Bb.ap[:, :, 0, :], t1.ap, t2.ap, ALU.subtract, [t1.r, t2.r], [Bb.r])
    c.tt(t1.ap, bim.ap, creb, ALU.mult, [bim.r, cre.r], [t1.r])
    c.tt(t2.ap, bre.ap, cimb, ALU.mult, [bre.r, cim.r], [t2.r])
    c.tt(Bb.ap[:, :, 1, :], t1.ap, t2.ap, ALU.add, [t1.r, t2.r], [Bb.r])
    for q4 in range(32):
        ps = c.psum[q4 % 2]
        for j in range(4):
            idx = q4 * 4 + j
            c.transpose(ps.ap[0:32, j * 128:(j + 1) * 128], Bb.ap[:, idx // 2, idx % 2, :], c.ident_f.ap, [Bb.r, c.ident_f.r], [ps.r])
        c.copy(BT.ap[0:32, q4 * 4:(q4 + 1) * 4, :].rearrange('p a b -> p (a b)'), ps.ap[0:32, :], [ps.r], [BT.r])
    crb, cib = ld(c.d_s5cre, [64, 32]), ld(c.d_s5cim, [64, 32])
    c.act(CB.ap[:, :, 0, :], crb.ap, AF.Identity, [crb.r], [CB.r])
    c.act(CB.ap[:, :, 1, :], cib.ap, AF.Identity, [cib.r], [CB.r], scale=-1.0)
    c.S.flush(barrier=True)
    c.aoff = mark
    ctab = [c.sb([512], F32) for _ in range(2)]
    stab = [c.sb([512], F32) for _ in range(2)]
    tA, tB, tT = c.sb([512], F32), c.sb([512], F32), c.sb([512], F32)
    br, bi_ = [c.sb([512], F32) for _ in range(2)], [c.sb([512], F32) for _ in range(2)]
    d1, d2, zR, wR, m1, m2 = [c.sb([512], F32) for _ in range(6)]
    p1, p2, zI, wI, m3, m4 = [c.sb([512], F32) for _ in range(6)]
    sbuf = [[[c.sb([T], BF16) for _ in range(2)] for _ in range(2)] for _ in range(2)]
    ug = [c.sb([T], BF16) for _ in range(2)]
    ysb = [c.sb([T], F32) for _ in range(2)]
    ini = [c.sb([1], F32) for _ in range(2)]
    tin = c.sb([1], F32)
    segs_f = list(TB)
    segs_b = [TB[0]] + TB[:0:-1]
    if not with_ctx:
        pass
    nseg = 0
    for gp in range(32):
        u_ = ug[gp % 2]
        c.dma(u_.ap[0:32], c.Ud[gp // 4, 32 * (gp % 4):32 * (gp % 4) + 32, :], [c.Ur], [u_.r])
        for dr in range(2):
            dg = dr * 32 + gp
            ct, st_ = ctab[dr], stab[dr]
            c.ts(tT.ap, iota.ap, thp.ap[:, dg:dg + 1], None, ALU.mult, None, [iota.r, thp.r], [tT.r])
            sincos_frac(c, tT, [tA, tB], st_, ct, None)
            rcol = r_.ap[:, dg:dg + 1]
            sR_, sI_ = sbuf[gp % 2][dr]
            first = True
            for (t0, n) in (segs_f if dr == 0 else segs_b):
                rev = dr == 1
                pr, pi = c.psum[2 * (nseg % 2)], c.psum[2 * (nseg % 2) + 1]
                b_r, b_i = br[nseg % 2], bi_[nseg % 2]
                c.mm(pr.ap[:, 0:n], BT.ap[0:32, dg * 2, :], u_.ap[0:32, t0:t0 + n], True, True, [BT.r, u_.r], [pr.r])
                c.mm(pi.ap[:, 0:n], BT.ap[0:32, dg * 2 + 1, :], u_.ap[0:32, t0:t0 + n], True, True, [BT.r, u_.r], [pi.r])
                srcr = pr.ap[:, 0:n][:, ::-1] if rev else pr.ap[:, 0:n]
                srci = pi.ap[:, 0:n][:, ::-1] if rev else pi.ap[:, 0:n]
                c.act(b_r.ap[:, 0:n], srcr, AF.Identity, [pr.r], [b_r.r])
                c.act(b_i.ap[:, 0:n], srci, AF.Identity, [pi.r], [b_i.r])
                cN, sN = ct.ap[:, 0:n], st_.ap[:, 0:n]
                c.tt(d1.ap[:, 0:n], cN, b_r.ap[:, 0:n], ALU.mult, [ct.r, b_r.r], [d1.r])
                c.tt(d2.ap[:, 0:n], sN, b_i.ap[:, 0:n], ALU.mult, [st_.r, b_i.r], [d2.r], eng=POOL)
                c.tt(zR.ap[:, 0:n], d1.ap[:, 0:n], d2.ap[:, 0:n], ALU.add, [d1.r, d2.r], [zR.r])
                c.tt(p1.ap[:, 0:n], cN, b_i.ap[:, 0:n], ALU.mult, [ct.r, b_i.r], [p1.r], eng=POOL)
                c.tt(p2.ap[:, 0:n], sN, b_r.ap[:, 0:n], ALU.mult, [st_.r, b_r.r], [p2.r], eng=POOL)
                c.tt(zI.ap[:, 0:n], p1.ap[:, 0:n], p2.ap[:, 0:n], ALU.subtract, [p1.r, p2.r], [zI.r], eng=POOL)
                rb = rcol.to_broadcast([128, n])
                for (w_, z_, k) in ((wR, zR, 0), (wI, zI, 1)):
                    init = 0.0 if first else ini[k].ap[:, 0:1]
                    rr = [r_.r, z_.r] + ([] if first else [ini[k].r])
                    c.S.op(DVE, (lambda o, z, i0, b: lambda e: e.tensor_tensor_scan(out=o, data0=b, data1=z, initial=i0,
                                                                                   op0=ALU.mult, op1=ALU.add))(w_.ap[:, 0:n], z_.ap[:, 0:n], init, rb),
                           reads=rr, writes=[w_.r])
                oR = sR_.ap[:, t0:t0 + n][:, ::-1] if rev else sR_.ap[:, t0:t0 + n]
                oI = sI_.ap[:, t0:t0 + n][:, ::-1] if rev else sI_.ap[:, t0:t0 + n]
                c.tt(m1.ap[:, 0:n], cN, wR.ap[:, 0:n], ALU.mult, [ct.r, wR.r], [m1.r], eng=POOL)
                c.tt(m2.ap[:, 0:n], sN, wI.ap[:, 0:n], ALU.mult, [st_.r, wI.r], [m2.r])
                c.tt(oR, m1.ap[:, 0:n], m2.ap[:, 0:n], ALU.subtract, [m1.r, m2.r], [sR_.r])
                c.tt(m3.ap[:, 0:n], cN, wI.ap[:, 0:n], ALU.mult, [ct.r, wI.r], [m3.r], eng=POOL)
                c.tt(m4.ap[:, 0:n], sN, wR.ap[:, 0:n], ALU.mult, [st_.r, wR.r], [m4.r], eng=POOL)
                c.tt(oI, m3.ap[:, 0:n], m4.ap[:, 0:n], ALU.add, [m3.r, m4.r], [sI_.r], eng=POOL)
                cl, sl = ct.ap[:, n - 1:n], st_.ap[:, n - 1:n]
                wRl, wIl = wR.ap[:, n - 1:n], wI.ap[:, n - 1:n]
                c.tt(tin.ap, sl, wIl, ALU.mult, [st_.r, wI.r], [tin.r])
                c.stt(ini[0].ap, wRl, cl, tin.ap, ALU.mult, ALU.subtract, [wR.r, ct.r, tin.r], [ini[0].r])
                c.tt(tin.ap, sl, wRl, ALU.mult, [st_.r, wR.r], [tin.r])
                c.stt(ini[1].ap, wIl, cl, tin.ap, ALU.mult, ALU.add, [wI.r, ct.r, tin.r], [ini[1].r])
                first = False
                nseg += 1
        y_ = ysb[gp % 2]
        for bi, (t0, n) in enumerate(TB):
            po = c.psum[4 + bi % 2]
            k = 0
            for dr in range(2):
                dg = dr * 32 + gp
                for ri in range(2):
                    sb_ = sbuf[gp % 2][dr][ri]
                    c.mm(po.ap[0:32, 0:n], CB.ap[:, dg, ri, :], sb_.ap[:, t0:t0 + n], k == 0, k == 3, [CB.r, sb_.r], [po.r])
                    k += 1
            c.stt(y_.ap[0:32, t0:t0 + n], u_.ap[0:32, t0:t0 + n], dcol.ap[0:32, gp:gp + 1], po.ap[0:32, 0:n], ALU.mult, ALU.add,
                  [u_.r, dcol.r, po.r], [y_.r])
        c.dma(c.Yd[gp * 32:(gp + 1) * 32, :], y_.ap[0:32], [y_.r], [c.Yr])
    c.stage_end()


C0G = math.sqrt(2.0 / math.pi)


def st_glu_wout(c, layer, with_ctx=True):
    blocks = TB if with_ctx else TB[1:]
    G = c.G[layer]
    gw = c.sb([8, 1024], BF16)
    for kt in range(8):
        c.dma(gw.ap[:, kt, :], c.d_gluw[kt * 128:(kt + 1) * 128, :], [], [gw.r], q=POOL)
    gb = c.sb([8], F32)
    c.dma(gb.ap, c.d_glub, [], [gb.r])
    wo = c.sb([16, 2048], BF16)
    for kt in range(16):
        c.dma(wo.ap[:, kt, :], c.d_wout[layer][kt * 128:(kt + 1) * 128, :], [], [wo.r], q=POOL)
    yb = [c.sb([8, 512], F32) for _ in range(1)]
    y2 = c.sb([8, 512], F32)
    sg = c.sb([8, 512], F32)
    gg = [c.sb([8, 512], BF16) for _ in range(2)]
    cat = [c.sb([16, 512], BF16) for _ in range(1)]
    catr = [[Reg() for _ in range(16)] for _ in range(1)]
    sgm = [c.sb([512], F32) for _ in range(2)]
    xt = [c.sb([512], F32) for _ in range(4)]
    nx = 0
    for bi, (t0, n) in enumerate(blocks):
        g = 1 if t0 < LC else 0
        y_, g_, ct, cr = yb[0], gg[bi % 2], cat[0], catr[0]
        c.dma(y_.ap[:, :, 0:n], c.Yd[:, t0:t0 + n].rearrange('(a p) t -> p a t', p=128), [c.Yr], [y_.r])
        c.dma(ct.ap[:, 8:16, 0:n], c.CATd[8:16, :, t0:t0 + n].rearrange('a p t -> p a t'), [c.CATr[1]], cr[8:16])
        yv = y_.ap[:, :, 0:n]
        c.act(y2.ap[:, :, 0:n], yv, AF.Square, [y_.r], [y2.r])
        c.ts(y2.ap[:, :, 0:n], y2.ap[:, :, 0:n], 0.044715, 1.0, ALU.mult, ALU.add, [y2.r], [y2.r])
        c.tt(y2.ap[:, :, 0:n], y2.ap[:, :, 0:n], yv, ALU.mult, [y2.r, y_.r], [y2.r])
        c.act(sg.ap[:, :, 0:n], y2.ap[:, :, 0:n], AF.Sigmoid, [y2.r], [sg.r], scale=2 * C0G)
        c.tt(g_.ap[:, :, 0:n], sg.ap[:, :, 0:n], yv, ALU.mult, [sg.r, y_.r], [g_.r])
        for ft in range(8):
            ps = c.psum[ft % 2]
            s_ = sgm[ft % 2]
            for kt in range(8):
                c.mm(ps.ap[:, 0:n], gw.ap[:, kt, ft * 128:(ft + 1) * 128], g_.ap[:, kt, 0:n], kt == 0, kt == 7, [gw.r, g_.r], [ps.r])
            c.act(s_.ap[:, 0:n], ps.ap[:, 0:n], AF.Sigmoid, [ps.r, gb.r], [s_.r], bias=gb.ap[:, ft:ft + 1])
            c.tt(ct.ap[:, ft, 0:n], s_.ap[:, 0:n], g_.ap[:, ft, 0:n], ALU.mult, [s_.r, g_.r], [cr[ft]])
        wout_block(c, layer, ct, cr, wo, xt, t0, n, g, nx)
        nx += 16
    c.stage_end()


def wout_block(c, layer, ct, cr, wo, xt, t0, n, g, nx):
    G = c.G[layer]
    PF = 3

    def issue_load(dt):
        x_ = xt[(nx + dt) % 4]
        c.dma(x_.ap[:, 0:n], c.XR[dt, :, t0:t0 + n], [c.XRr[dt][bidx(t0)]], [x_.r])
    for dt in range(PF):
        issue_load(dt)
    for dt in range(16):
        if dt + PF < 16:
            issue_load(dt + PF)
        po = c.psum[4 + (nx + dt) % 2]
        x_ = xt[(nx + dt) % 4]
        xreg = c.XRr[dt][bidx(t0)]
        for kt in range(16):
            c.mm(po.ap[:, 0:n], wo.ap[:, kt, dt * 128:(dt + 1) * 128], ct.ap[:, kt, 0:n], kt == 0, kt == 15, [wo.r, cr[kt]], [po.r])
        c.stt(x_.ap[:, 0:n], po.ap[:, 0:n], G.ap[:, g, 1, dt:dt + 1], x_.ap[:, 0:n], ALU.mult, ALU.add, [po.r, G.r, x_.r], [x_.r])
        c.dma(c.XR[dt, :, t0:t0 + n], x_.ap[:, 0:n], [x_.r], [xreg], q=ACT)


def conv3(c, raw, W, w3, b, out, tmp, silu, wr=()):
    wr = list(wr)
    c.act(tmp.ap[:, 0:W], raw.ap[:, 1:W + 1], AF.Identity, [raw.r] + wr, [tmp.r], scale=w3[:, 1:2], bias=b)
    c.stt(tmp.ap[:, 0:W], raw.ap[:, 0:W], w3[:, 0:1], tmp.ap[:, 0:W], ALU.mult, ALU.add, [raw.r, tmp.r] + wr, [tmp.r])
    if silu:
        c.stt(tmp.ap[:, 0:W], raw.ap[:, 2:W + 2], w3[:, 2:3], tmp.ap[:, 0:W], ALU.mult, ALU.add, [raw.r, tmp.r] + wr, [tmp.r])
        c.act(out[0], tmp.ap[:, 0:W], AF.Silu, [tmp.r], out[1])
    else:
        c.stt(out[0], raw.ap[:, 2:W + 2], w3[:, 2:3], tmp.ap[:, 0:W], ALU.mult, ALU.add, [raw.r, tmp.r] + wr, out[1])


def st_odd_proj(c, H):
    wsrc = c.d_odwin_t
    hw = c.sb([24, 3], F32); hb = c.sb([24], F32); sw = c.sb([16, 3], F32); sbias = c.sb([16], F32)
    for t_, s_ in ((hw, c.d_hysw), (hb, c.d_hysb), (sw, c.d_ssdcw), (sbias, c.d_ssdcb)):
        c.dma(t_.ap, s_, [], [t_.r])
    wb = [c.sb([16, 128], BF16) for _ in range(3)]
    raw = [c.sb([T + 8], F32) for _ in range(2)]
    tmp = [c.sb([T], F32) for _ in range(2)]
    ob = [c.sb([T], BF16) for _ in range(2)]
    for r_ in raw:
        c.memset(r_.ap, 0.0, [r_.r])
    cnt = 0
    for f in range(48):
        w_ = wb[f % 3]
        c.dma(w_.ap, wsrc[f], [], [w_.r], q=POOL)
        lat_only = f < 32
        blocks = TB[1:] if lat_only else TB
        r_, t_, o_ = raw[f % 2], tmp[f % 2], ob[f % 2]
        for (t0, n) in blocks:
            ps = c.psum[cnt % 4]
            for kt in range(16):
                c.mm(ps.ap[:, 0:n], w_.ap[:, kt, :], H.ap[:, kt, t0:t0 + n], kt == 0, kt == 15, [w_.r, H.r], [ps.r])
            off = 1 + t0 if t0 < LC else 3 + t0
            evac(c, cnt, r_.ap[:, off:off + n], ps.ap[:, 0:n], [ps.r], [r_.r])
            cnt += 1
        rl = Tl(r_.ap[:, 258:258 + L + 2], r_.r)
        if f < 24:
            conv3(c, rl, L, hw.ap[:, f, :], hb.ap[:, f:f + 1], (o_.ap[:, 0:L], [o_.r]), t_, False, [hw.r, hb.r])
            c.dma(c.HYd[f], o_.ap[:, 0:L], [o_.r], [c.HYr])
        elif f < 32:
            c.act(o_.ap[:, 0:L], r_.ap[:, 259:259 + L], AF.Silu, [r_.r], [o_.r])
            c.dma(c.Zd[f - 24], o_.ap[:, 0:L], [o_.r], [c.Zr])
        else:
            a = f - 32
            rc = Tl(r_.ap[:, 0:LC + 2], r_.r)
            conv3(c, rc, LC, sw.ap[:, a, :], sbias.ap[:, a:a + 1], (o_.ap[:, 0:LC], [o_.r]), t_, True, [sw.r, sbias.r])
            conv3(c, rl, L, sw.ap[:, a, :], sbias.ap[:, a:a + 1], (o_.ap[:, LC:T], [o_.r]), t_, True, [sw.r, sbias.r])
            c.dma(c.XBCd[a], o_.ap, [o_.r], [c.XBCr])
    wdt = c.sb([2, 16, 16], BF16)
    c.dma(wdt.ap, c.d_odwdt, [], [wdt.r], q=POOL)
    dtb = c.sb([2], F32); alog = c.sb([2], F32); nA = c.sb([2], F32)
    c.dma(dtb.ap[0:16], c.d_dtbias, [], [dtb.r])
    c.dma(alog.ap[0:16], c.d_alog, [], [alog.r])
    c.act(nA.ap[0:16], alog.ap[0:16], AF.Exp, [alog.r], [nA.r])
    c.ts(nA.ap[0:16], nA.ap[0:16], -1.0, None, ALU.mult, None, [nA.r], [nA.r])
    one = c.sb([1], F32)
    c.memset(one.ap, 1.0, [one.r])
    dtT = [c.sb([T], F32) for _ in range(2)]
    aT = c.sb([T], F32)
    cumT = [c.sb([T], F32) for _ in range(2)]
    et = c.sb([512], F32)
    ini = c.sb([1], F32)
    for k in range(2):
        for (t0, n) in TB:
            ps = c.psum[4 + cnt % 2]
            cnt += 1
            for kt in range(16):
                c.mm(ps.ap[0:16, 0:n], wdt.ap[:, k, kt, :], H.ap[:, kt, t0:t0 + n], kt == 0, kt == 15, [wdt.r, H.r], [ps.r])
            c.act(et.ap[0:16, 0:n], ps.ap[0:16, 0:n], AF.Exp, [ps.r, dtb.r], [et.r], bias=dtb.ap[0:16, k:k + 1])
            c.act(dtT[k].ap[0:16, t0:t0 + n], et.ap[0:16, 0:n], AF.Ln, [et.r, one.r], [dtT[k].r], bias=one.ap[0:16, 0:1])
        c.ts(aT.ap[0:16], dtT[k].ap[0:16], nA.ap[0:16, k:k + 1], None, ALU.mult, None, [dtT[k].r, nA.r], [aT.r])
        segs = list(TB) if k == 0 else [TB[0]] + TB[:0:-1]
        first = True
        for (t0, n) in segs:
            src = aT.ap[0:16, t0:t0 + n]
            dst = cumT[k].ap[0:16, t0:t0 + n]
            if k == 1:
                src, dst = src[:, ::-1], dst[:, ::-1]
            init = 0.0 if first else ini.ap[0:16, 0:1]
            ob_ = one.ap[0:16, 0:1].to_broadcast([16, n])
            c.S.op(DVE, (lambda o, z, i0, b: lambda e: e.tensor_tensor_scan(out=o, data0=b, data1=z, initial=i0, op0=ALU.mult, op1=ALU.add))(dst, src, init, ob_),
                   reads=[aT.r, one.r] + ([] if first else [ini.r]), writes=[cumT[k].r])
            last = t0 + n - 1 if k == 0 else t0
            c.copy(ini.ap[0:16], cumT[k].ap[0:16, last:last + 1], [cumT[k].r], [ini.r])
            first = False
        c.dma(c.DTd[k], dtT[k].ap[0:16], [dtT[k].r], [c.DTr])
        c.dma(c.CUMd[k], cumT[k].ap[0:16], [cumT[k].r], [c.CUMr])
    c.stage_end()


def st_ssd(c):
    Bt = c.sb([4, T], BF16); Ct = c.sb([4, T], BF16)
    c.dma(Bt.ap, c.XBCd[8:12].rearrange('a p t -> p a t'), [c.XBCr], [Bt.r])
    c.dma(Ct.ap, c.XBCd[12:16].rearrange('a p t -> p a t'), [c.XBCr], [Ct.r])
    xtok = c.sb([18, 1024], BF16)
    xdt = [c.sb([18, 1024], BF16) for _ in range(2)]
    cumT = [c.sb([T], F32) for _ in range(2)]
    ccol = [c.sb([18, 16], F32) for _ in range(2)]
    ncol = [c.sb([18, 16], F32) for _ in range(2)]
    mark = c.aoff
    xs = [c.sb([T], BF16) for _ in range(2)]
    nps = 0
    import os
    CUT = float(os.environ.get('SSD_CUT', '99'))
    for a in range(8):
        x_ = xs[a % 2]
        c.dma(x_.ap, c.XBCd[a], [c.XBCr], [x_.r])
        for j4 in range(0, 18, 4):
            nj = min(4, 18 - j4)
            ps = c.psum[6 + nps % 2]
            nps += 1
            pb = ps.ap.bitcast(BF16)
            for jj in range(nj):
                c.transpose(pb[:, jj * 128:(jj + 1) * 128], x_.ap[:, (j4 + jj) * 128:(j4 + jj + 1) * 128], c.ident_b.ap, [x_.r, c.ident_b.r], [ps.r])
            c.copy(xtok.ap[:, j4:j4 + nj, a * 128:(a + 1) * 128], pb[:, 0:nj * 128].rearrange('p (j q) -> p j q', q=128), [ps.r], [xtok.r])
    if CUT <= 1:
        c.stage_end()
        return
    dtT = c.sb([T], F32)
    dtk = c.sb([18, 16], F32)
    for k in range(2):
        c.memset(cumT[k].ap[0:32], 0.0, [cumT[k].r])
    c.memset(dtT.ap[0:32], 0.0, [dtT.r])
    for k in range(2):
        c.dma(cumT[k].ap[0:16], c.CUMd[k], [c.CUMr], [cumT[k].r])
        c.dma(dtT.ap[0:16], c.DTd[k], [c.DTr], [dtT.r])
        if CUT <= 1.2:
            continue
        for (srcT, kind) in ((cumT[k], 0), (dtT, 1)):
            ps = c.psum[4 + nps % 2]
            nps += 1
            for j in range(18):
                c.mm(ps.ap[:, j * 16:(j + 1) * 16], srcT.ap[0:32, j * 128:(j + 1) * 128], c.ident_f.ap[0:32, 0:16], True, True, [srcT.r, c.ident_f.r], [ps.r])
            v = ps.ap[:, 0:288].rearrange('p (j h) -> p j h', h=16)
            if CUT <= 1.25:
                continue
            if kind == 0:
                c.copy(ccol[k].ap, v, [ps.r], [ccol[k].r])
                if CUT > 1.3:
                    c.act(ncol[k].ap, v, AF.Identity, [ps.r], [ncol[k].r], scale=-1.0)
            else:
                c.copy(dtk.ap, v, [ps.r], [dtk.r])
        if CUT <= 1.4:
            continue
        for j in range(18):
            c.tt(xdt[k].ap[:, j, :].rearrange('p (h q) -> p h q', q=64), xtok.ap[:, j, :].rearrange('p (h q) -> p h q', q=64),
                 dtk.ap[:, j, :].to_broadcast([128, 16, 64]), ALU.mult, [xtok.r, dtk.r], [xdt[k].r])
    if CUT <= 2:
        c.stage_end()
        return
    c.S.flush(barrier=True)
    c.aoff = mark
    sel = c.sb([16, 128], F32)
    c.memset(sel.ap[0:32], 0.0, [sel.r])
    c.copy(sel.ap[0:16], c.ident_f.ap[0:16, 0:16].to_broadcast([16, 16, 128]), [c.ident_f.r], [sel.r])
    dcol = c.sb([16], F32)
    c.dma(dcol.ap, c.d_ssdd, [], [dcol.r])
    dI = c.sb([16, 128], BF16)
    for hd in range(16):
        c.ts(dI.ap[:, hd, :], c.ident_f.ap, dcol.ap[:, hd:hd + 1], None, ALU.mult, None, [c.ident_f.r, dcol.r], [dI.r])
    tri = [c.sb([128], F32) for _ in range(2)]
    c.dma(tri[0].ap, c.d_tri[0], [], [tri[0].r])
    c.dma(tri[1].ap, c.d_tri[1], [], [tri[1].r])
    Eb = [c.sb([128], F32) for _ in range(8)]
    Mb = [c.sb([128], BF16) for _ in range(8)]
    Db = c.sb([128], F32)
    ysb = [c.sb([4, 128], F32) for _ in range(2)]
    it = 0
    npc = 0
    if CUT <= 3:
        c.stage_end()
        return
    for g in range(4 if CUT > 4 else 1):
        for i in range(16 if CUT > 4 else 1):
            q0 = LC + i * 128
            accs = [c.psum[e] for e in range(4)]
            R = [c.psum[4], c.psum[5]]
            for d in range(2):
                for e in range(4):
                    c.mm(R[d].ap[:, e * 128:(e + 1) * 128], sel.ap[0:32, g * 4 + e, :], cumT[d].ap[0:32, q0:q0 + 128], True, True,
                         [sel.r, cumT[d].r], [R[d].r])
            started = [False] * 4
            for d in range(2):
                srcs = [0, 1] + ([2 + j for j in range(i + 1)] if d == 0 else [2 + j for j in range(i, 16)])
                for jt in srcs:
                    pc = c.psum[6 + npc % 2]
                    npc += 1
                    c.mm(pc.ap[:, 0:128], Bt.ap[:, g, jt * 128:(jt + 1) * 128], Ct.ap[:, g, q0:q0 + 128], True, True, [Bt.r, Ct.r], [pc.r])
                    diag = jt == 2 + i
                    for e in range(4):
                        hd = g * 4 + e
                        E_, M_ = Eb[(npc * 4 + e) % 8], Mb[(npc * 4 + e) % 8]
                        Rv = R[d].ap[:, e * 128:(e + 1) * 128]
                        if not diag:
                            c.act(E_.ap, Rv, AF.Exp, [R[d].r, ncol[d].r], [E_.r], bias=ncol[d].ap[:, jt, hd:hd + 1])
                        else:
                            c.ts(Db.ap, Rv, ccol[d].ap[:, jt, hd:hd + 1], 0.0, ALU.subtract, ALU.min, [R[d].r, ccol[d].r], [Db.r])
                            c.act(E_.ap, Db.ap, AF.Exp, [Db.r], [E_.r])
                            c.tt(E_.ap, E_.ap, tri[d].ap, ALU.mult, [E_.r, tri[d].r], [E_.r])
                        c.tt(M_.ap, E_.ap, pc.ap[:, 0:128], ALU.mult, [E_.r, pc.r], [M_.r])
                        c.mm(accs[e].ap[0:64, 0:128], xdt[d].ap[:, jt, hd * 64:(hd + 1) * 64], M_.ap, not started[e], False,
                             [xdt[d].r, M_.r], [accs[e].r])
                        started[e] = True
            for e in range(4):
                hd = g * 4 + e
                c.mm(accs[e].ap[0:64, 0:128], xtok.ap[:, 2 + i, hd * 64:(hd + 1) * 64], dI.ap[:, hd, :], False, True,
                     [xtok.r, dI.r], [accs[e].r])
            y_ = ysb[it % 2]
            for e in range(4):
                evac(c, e, y_.ap[0:64, e, :], accs[e].ap[0:64, 0:128], [accs[e].r], [y_.r])
            c.dma(c.YSd[g * 4:(g + 1) * 4, :, i * 128:(i + 1) * 128].rearrange('e p l -> p e l'), y_.ap[0:64], [y_.r], [c.YSr])
            it += 1
    c.stage_end()


def st_hyena(c):
    TWO_PI = 2 * math.pi
    hwo = c.sb([4096], F32); c.dma(hwo.ap[0:64], c.d_hywout, [], [hwo.r])
    hid0 = c.sb([L], F32)
    mark = c.aoff
    zT = c.sb([L], F32); c.memset(zT.ap[0:64], 0.0, [zT.r]); c.dma(zT.ap[0:33], c.d_hyz, [], [zT.r])
    w1 = c.sb([64], F32); c.memset(w1.ap[0:64], 0.0, [w1.r]); c.dma(w1.ap[0:33], c.d_hywin, [], [w1.r])
    wm = c.sb([2, 64], F32); c.dma(wm.ap[0:64], c.d_hywmid, [], [wm.r])
    fb_ = c.sb([4], F32); c.dma(fb_.ap[0:64], c.d_hyfb, [], [fb_.r])
    fq = c.sb([1], F32); bq = c.sb([3], F32)
    c.ts(fq.ap[0:64], fb_.ap[0:64, 0:1], 1.0 / TWO_PI, None, ALU.mult, None, [fb_.r], [fq.r])
    c.ts(bq.ap[0:64], fb_.ap[0:64, 1:4], fq.ap[0:64, 0:1], None, ALU.mult, None, [fb_.r, fq.r], [bq.r])
    hid = [hid0, c.sb([L], F32)]
    tA, tB = c.sb([512], F32), c.sb([512], F32)
    src = zT
    for l in range(3):
        dst = hid[l % 2]
        for b4 in range(4):
            ps = c.psum[b4 % 2]
            if l == 0:
                c.mm(ps.ap[0:64, :], w1.ap[0:64, :], zT.ap[0:64, b4 * 512:(b4 + 1) * 512], True, True, [w1.r, zT.r], [ps.r])
            else:
                c.mm(ps.ap[0:64, :], wm.ap[0:64, l - 1, :], src.ap[0:64, b4 * 512:(b4 + 1) * 512], True, True, [wm.r, src.r], [ps.r])
            c.ts(tA.ap[0:64], ps.ap[0:64, :], fq.ap[0:64, 0:1], bq.ap[0:64, l:l + 1], ALU.mult, ALU.add, [ps.r, fq.r, bq.r], [tA.r])
            rnd(c, tB.ap[0:64], tA.ap[0:64], [tA.r], [tB.r])
            c.tt(tA.ap[0:64], tA.ap[0:64], tB.ap[0:64], ALU.subtract, [tA.r, tB.r], [tA.r])
            c.act(dst.ap[0:64, b4 * 512:(b4 + 1) * 512], tA.ap[0:64], AF.Sin, [tA.r], [dst.r], scale=TWO_PI)
        src = dst
    h3 = src
    c.S.flush(barrier=True)
    c.aoff = mark
    CB_ = 256
    dec = c.sb([16, CB_], F32)
    fbt = c.sb([2, CB_], F32)
    Hs = c.sb([16, CB_], BF16); Hd = c.sb([16, CB_], BF16)
    Kre = c.sb([16, CB_], F32); Kim = c.sb([16, CB_], F32)
    Yre = c.sb([16, CB_], BF16); Yim = c.sb([16, CB_], BF16)
    tok = {k: c.sb([16, CB_], BF16) for k in ('x1', 'x2', 'v', 'z1', 'o')}
    tabs = [c.sb([16, 128], BF16) for _ in range(4)]
    tmpf = [c.sb([CB_], F32) for _ in range(6)]
    chm = [c.sb([L], BF16) for _ in range(2)]
    nt = 0
    npz = 0
    for cb in range(4):
        c.dma(dec.ap, c.d_decay[:, cb * CB_:(cb + 1) * CB_].rearrange('(tt p) q -> p tt q', p=128), [], [dec.r])
        c.dma(fbt.ap, c.d_hyfbias[:, :, cb * CB_:(cb + 1) * CB_], [], [fbt.r])
        for ki, key in enumerate(('x1', 'x2', 'v')):
            for ct in range(2):
                ch_ = chm[npz % 2]
                c.dma(ch_.ap, c.HYd[ki * 8 + cb * 2 + ct], [c.HYr], [ch_.r])
                for t4 in range(4):
                    ps = c.psum[6 + npz % 2]
                    npz += 1
                    pb = ps.ap.bitcast(BF16)
                    for jj in range(4):
                        tt_ = t4 * 4 + jj
                        c.transpose(pb[:, jj * 128:(jj + 1) * 128], ch_.ap[:, tt_ * 128:(tt_ + 1) * 128], c.ident_b.ap, [ch_.r, c.ident_b.r], [ps.r])
                    c.copy(tok[key].ap[:, t4 * 4:(t4 + 1) * 4, ct * 128:(ct + 1) * 128], pb[:, 0:512].rearrange('p (j q) -> p j q', q=128), [ps.r], [tok[key].r])
        for o in range(2):
            tin = tok['v'] if o == 0 else tok['z1']
            gate = tok['x1'] if o == 0 else tok['x2']
            tout = tok['z1'] if o == 0 else tok['o']
            for tt_ in range(16):
                pf, pb_ = c.psum[0], c.psum[1]
                for dr, p_ in ((0, pf), (1, pb_)):
                    col0 = o * 2048 + dr * 1024 + cb * CB_
                    c.mm(p_.ap[:, 0:CB_], h3.ap[0:64, tt_ * 128:(tt_ + 1) * 128], hwo.ap[0:64, col0:col0 + CB_], True, True, [h3.r, hwo.r], [p_.r])
                f_, b_ = tmpf[0], tmpf[1]
                c.tt(f_.ap, pf.ap[:, 0:CB_], dec.ap[:, tt_, :], ALU.mult, [pf.r, dec.r], [f_.r])
                c.tt(b_.ap, pb_.ap[:, 0:CB_], dec.ap[:, tt_, :], ALU.mult, [pb_.r, dec.r], [b_.r])
                if tt_ == 0:
                    c.memset(b_.ap[0:1, :], 0.0, [b_.r])
                c.tt(Hs.ap[:, tt_, :], f_.ap, b_.ap, ALU.add, [f_.r, b_.r], [Hs.r])
                c.tt(Hd.ap[:, tt_, :], f_.ap, b_.ap, ALU.subtract, [f_.r, b_.r], [Hd.r])
            for ft in range(16):
                tc_, ts_ = tabs[nt % 4], tabs[(nt + 1) % 4]
                nt += 2
                c.dma(tc_.ap, c.d_Cf[ft], [], [tc_.r])
                c.dma(ts_.ap, c.d_Sf[ft], [], [ts_.r])
                pr, pi = c.psum[0], c.psum[1]
                for tt_ in range(16):
                    c.mm(pr.ap[:, 0:CB_], tc_.ap[:, tt_, :], Hs.ap[:, tt_, :], tt_ == 0, tt_ == 15, [tc_.r, Hs.r], [pr.r])
                for tt_ in range(16):
                    c.mm(pi.ap[:, 0:CB_], ts_.ap[:, tt_, :], Hd.ap[:, tt_, :], tt_ == 0, tt_ == 15, [ts_.r, Hd.r], [pi.r])
                c.act(Kre.ap[:, ft, :], pr.ap[:, 0:CB_], AF.Identity, [pr.r], [Kre.r])
                c.act(Kim.ap[:, ft, :], pi.ap[:, 0:CB_], AF.Identity, [pi.r], [Kim.r])
                if ft == 0:
                    pn = c.psum[2]
                    for tt_ in range(16):
                        c.mm(pn.ap[:, 0:CB_], ts_.ap[:, tt_, :], Hs.ap[:, tt_, :], tt_ == 0, tt_ == 15, [ts_.r, Hs.r], [pn.r])
                    c.act(Kim.ap[0:1, 0, :], pn.ap[0:1, 0:CB_], AF.Identity, [pn.r], [Kim.r])
                vr, vi = c.psum[3], c.psum[4]
                for tt_ in range(16):
                    c.mm(vr.ap[:, 0:CB_], tc_.ap[:, tt_, :], tin.ap[:, tt_, :], tt_ == 0, tt_ == 15, [tc_.r, tin.r], [vr.r])
                for tt_ in range(16):
                    c.mm(vi.ap[:, 0:CB_], ts_.ap[:, tt_, :], tin.ap[:, tt_, :], tt_ == 0, tt_ == 15, [ts_.r, tin.r], [vi.r])
                a1, a2, a3, a4 = tmpf[2:6]
                c.tt(a1.ap, vr.ap[:, 0:CB_], Kre.ap[:, ft, :], ALU.mult, [vr.r, Kre.r], [a1.r])
                c.tt(a2.ap, vi.ap[:, 0:CB_], Kim.ap[:, ft, :], ALU.mult, [vi.r, Kim.r], [a2.r])
                c.tt(a3.ap, vr.ap[:, 0:CB_], Kim.ap[:, ft, :], ALU.mult, [vr.r, Kim.r], [a3.r])
                c.tt(a4.ap, vi.ap[:, 0:CB_], Kre.ap[:, ft, :], ALU.mult, [vi.r, Kre.r], [a4.r])
                c.tt(Yre.ap[:, ft, :], a1.ap, a2.ap, ALU.subtract, [a1.r, a2.r], [Yre.r])
                c.tt(Yim.ap[:, ft, :], a3.ap, a4.ap, ALU.add, [a3.r, a4.r], [Yim.r])
                if ft == 0:
                    c.copy(Yre.ap[0:1, 0, :], a1.ap[0:1, :], [a1.r], [Yre.r])
                    c.copy(Yim.ap[0:1, 0, :], a2.ap[0:1, :], [a2.r], [Yim.r])
            for tt_ in range(16):
                tc_, ts_ = tabs[nt % 4], tabs[(nt + 1) % 4]
                nt += 2
                c.dma(tc_.ap, c.d_Ci[tt_], [], [tc_.r])
                c.dma(ts_.ap, c.d_Si[tt_], [], [ts_.r])
                py = c.psum[5]
                for ft in range(16):
                    c.mm(py.ap[:, 0:CB_], tc_.ap[:, ft, :], Yre.ap[:, ft, :], ft == 0, False, [tc_.r, Yre.r], [py.r])
                for ft in range(16):
                    c.mm(py.ap[:, 0:CB_], ts_.ap[:, ft, :], Yim.ap[:, ft, :], False, ft == 15, [ts_.r, Yim.r], [py.r])
                e1 = tmpf[0]
                c.tt(e1.ap, tin.ap[:, tt_, :], fbt.ap[:, o, :], ALU.mult, [tin.r, fbt.r], [e1.r])
                c.tt(e1.ap, e1.ap, py.ap[:, 0:CB_], ALU.add, [e1.r, py.r], [e1.r])
                c.tt(tout.ap[:, tt_, :], e1.ap, gate.ap[:, tt_, :], ALU.mult, [e1.r, gate.r], [tout.r])
        for ct in range(2):
            ch_ = chm[npz % 2]
            for t4 in range(4):
                ps = c.psum[6 + npz % 2]
                npz += 1
                pb = ps.ap.bitcast(BF16)
                for jj in range(4):
                    tt_ = t4 * 4 + jj
                    c.transpose(pb[:, jj * 128:(jj + 1) * 128], tok['o'].ap[:, tt_, ct * 128:(ct + 1) * 128], c.ident_b.ap, [tok['o'].r, c.ident_b.r], [ps.r])
                c.copy(ch_.ap[:, t4 * 512:(t4 + 1) * 512], pb[:, 0:512], [ps.r], [ch_.r])
            c.dma(c.CATd[cb * 2 + ct, :, LC:T], ch_.ap, [ch_.r], [c.CATr[0]])
    c.stage_end()


def st_odd_out(c):
    wo = c.sb([16, 2048], BF16)
    for kt in range(16):
        c.dma(wo.ap[:, kt, :], c.d_wout[1][kt * 128:(kt + 1) * 128, :], [], [wo.r], q=POOL)
    ng = c.sb([8], F32); c.dma(ng.ap, c.d_ssdng, [], [ng.r])
    yb = c.sb([8, 512], F32); zb = c.sb([8, 512], BF16); sq = c.sb([8, 512], BF16)
    rs = c.sb([512], F32)
    cat = c.sb([16, 512], BF16)
    cr = [Reg() for _ in range(16)]
    xt = [c.sb([512], F32) for _ in range(4)]
    nx = 0
    for (t0, n) in TB[1:]:
        l0 = t0 - LC
        c.dma(yb.ap, c.YSd[:, :, l0:l0 + n].rearrange('h q t -> (h q) t').rearrange('(a p) t -> p a t', p=128), [c.YSr], [yb.r])
        c.dma(zb.ap, c.Zd[:, :, l0:l0 + n].rearrange('a p t -> p a t'), [c.Zr], [zb.r])
        c.dma(cat.ap[:, 0:8, :], c.CATd[0:8, :, t0:t0 + n].rearrange('a p t -> p a t'), [c.CATr[0]], cr[0:8])
        c.tt(yb.ap, yb.ap, zb.ap, ALU.mult, [yb.r, zb.r], [yb.r])
        c.act(sq.ap, yb.ap, AF.Square, [yb.r], [sq.r])
        ps = c.psum[0]
        for a in range(8):
            c.mm(ps.ap, c.ones_bf.ap, sq.ap[:, a, :], a == 0, a == 7, [c.ones_bf.r, sq.r], [ps.r])
        c.act(rs.ap, ps.ap, AF.Sqrt, [ps.r, c.eps_t.r], [rs.r], scale=1.0 / 1024, bias=c.eps_t.ap[:, 0:1])
        c.recip(rs.ap, rs.ap, [rs.r], [rs.r])
        for a in range(8):
            c.stt(cat.ap[:, 8 + a, :], yb.ap[:, a, :], ng.ap[:, a:a + 1], rs.ap, ALU.mult, ALU.mult, [yb.r, ng.r, rs.r], [cr[8 + a]])
        wout_block(c, 1, cat, cr, wo, xt, t0, n, 0, nx)
        nx += 16
    c.stage_end()
```

```python
import math
import numpy as np
import ml_dtypes
import concourse.bass as bass
import concourse.mybir as mybir
from concourse.bass_utils import run_bass_kernel_spmd

F32 = mybir.dt.float32
BF16 = mybir.dt.bfloat16
ALU = mybir.AluOpType
AF = mybir.ActivationFunctionType
AX = mybir.AxisListType

PE, DVE, ACT, POOL, SP = 0, 1, 2, 3, 4
NENG = 5
NDSEM = 12

D = 2048
T = 2304
LC = 256
L = 2048
DFF = 5632
NFT = DFF // 128
EPS = 1e-6
TB = [(0, 256), (256, 512), (768, 512), (1280, 512), (1792, 512)]
ARENA_BYTES = 196608
MAGIC = 12582912.0


class Reg:
    __slots__ = ('w', 'rs', 'excl')

    def __init__(self, excl=False):
        self.w = None
        self.rs = []
        self.excl = excl


class Op:
    __slots__ = ('eng', 'fn', 'deps', 'signal', 'sig', 'clock', 'dma', 'dsem', 'dval', 'waits')

    def __init__(self, eng, fn, dma):
        self.eng = eng
        self.fn = fn
        self.dma = dma
        self.deps = []
        self.signal = False
        self.sig = 0
        self.clock = None
        self.dsem = -1
        self.dval = 0
        self.waits = None


class Sched:
    def __init__(self, nc):
        self.nc = nc
        self.engs = [nc.tensor, nc.vector, nc.scalar, nc.gpsimd, nc.sync]
        self.esem = [nc.alloc_semaphore('es%d' % i) for i in range(NENG)]
        self.dsem = [[nc.alloc_semaphore('ds%d_%d' % (q, i)) for i in range(NDSEM)] for q in range(NENG)]
        self.ncomp = NENG + NENG * NDSEM
        self.pending = []
        self.sigcnt = [0] * NENG
        self.dcnt = [0] * NENG
        self.dlast = [[None] * NDSEM for _ in range(NENG)]
        self.clock = [[0] * self.ncomp for _ in range(NENG)]
        self.nops = 0
        self.nwaits = 0

    def op(self, eng, fn, reads=(), writes=(), dma=False):
        o = Op(eng, fn, dma)
        deps = o.deps
        for r in reads:
            if r.w is not None:
                deps.append((r.w, True))
            if r.excl:
                for x in r.rs:
                    if x.eng != eng:
                        deps.append((x, True))
            if dma:
                r.rs.append(o)
            else:
                rs = r.rs
                for i in range(len(rs)):
                    if (not rs[i].dma) and rs[i].eng == eng:
                        rs[i] = o
                        break
                else:
                    rs.append(o)
        for r in writes:
            if r.w is not None:
                deps.append((r.w, False))
            for x in r.rs:
                if x is not o:
                    deps.append((x, False))
            r.w = o
            r.rs = []
        self.pending.append(o)
        return o

    def flush(self, barrier=True):
        ops = self.pending
        self.pending = []
        for o in ops:
            for (d, raw) in o.deps:
                if d.dma:
                    continue
                if d.eng == o.eng and not o.dma and d.eng == PE:
                    continue
                d.signal = True
        if barrier:
            last = {}
            for o in ops:
                if not o.dma:
                    last[o.eng] = o
            for o in last.values():
                o.signal = True
        ncomp = self.ncomp
        for o in ops:
            e = o.eng
            ck = self.clock[e]
            waits = []
            if o.dma:
                i = self.dcnt[e]
                self.dcnt[e] += 1
                slot = i % NDSEM
                prev = self.dlast[e][slot]
                if prev is not None:
                    o.deps.append((prev, True))
                o.dsem = slot
                o.dval = 16 * (i // NDSEM + 1)
                self.dlast[e][slot] = o
            for (d, raw) in o.deps:
                if d.dma:
                    comp = NENG + d.eng * NDSEM + d.dsem
                    val = d.dval
                    sem = self.dsem[d.eng][d.dsem]
                else:
                    if d.eng == e and not o.dma and e == PE:
                        continue
                    if not d.signal:
                        continue
                    comp = d.eng
                    val = d.sig
                    sem = self.esem[d.eng]
                if ck[comp] >= val:
                    continue
                waits.append((sem, val))
                dc = d.clock
                if dc is not None:
                    for k in range(ncomp):
                        if dc[k] > ck[k]:
                            ck[k] = dc[k]
                if ck[comp] < val:
                    ck[comp] = val
            o.waits = waits
            if o.dma:
                o.clock = list(ck)
            elif o.signal:
                self.sigcnt[e] += 1
                o.sig = self.sigcnt[e]
                o.clock = list(ck)
            o.deps = None
        for o in ops:
            eng = self.engs[o.eng]
            for (sem, val) in o.waits:
                eng.wait_ge(sem, val)
                self.nwaits += 1
            ins = o.fn(eng)
            self.nops += 1
            if o.dma:
                ins.then_inc(self.dsem[o.eng][o.dsem], 16)
            elif o.signal:
                ins.then_inc(self.esem[o.eng], 1)
            o.fn = None
            o.waits = None
        if barrier:
            self.barrier()

    def barrier(self):
        for e in range(NENG):
            eng = self.engs[e]
            ck = self.clock[e]
            for f in range(NENG):
                if ck[f] < self.sigcnt[f]:
                    eng.wait_ge(self.esem[f], self.sigcnt[f])
                    ck[f] = self.sigcnt[f]
            for q in range(NENG):
                for s in range(NDSEM):
                    d = self.dlast[q][s]
                    if d is not None:
                        comp = NENG + q * NDSEM + s
                        if ck[comp] < d.dval:
                            eng.wait_ge(self.dsem[q][s], d.dval)
                            ck[comp] = d.dval


class Tl:
    __slots__ = ('ap', 'r')

    def __init__(self, ap, r=None):
        self.ap = ap
        self.r = r if r is not None else Reg()

    def __getitem__(self, k):
        return self.ap[k]


class Ctx:
    def __init__(self, dbg_in=(), dbg_out=()):
        self.nc = nc = bass.Bass("TRN2", target_bir_lowering=False)
        self.S = Sched(nc)
        self.dbg_in = set(dbg_in)
        self.dbg_out = set(dbg_out)
        self.arena = nc.alloc_sbuf_tensor('arena', [128, ARENA_BYTES // 4], F32)
        self.aoff = 0
        self.psum = [Tl(nc.alloc_psum_tensor('ps%d' % i, [128, 512], F32)[:], Reg(excl=True)) for i in range(8)]
        self.inputs = {}
        self.outputs = {}
        self.n_p = 0

    def dram(self, name, shape, dtype=F32, kind=None):
        if kind is None:
            kind = 'Internal'
            if name in self.dbg_in:
                kind = 'ExternalInput'
            elif name in self.dbg_out:
                kind = 'ExternalOutput'
        t = self.nc.dram_tensor(name, list(shape), dtype, kind=kind).ap()
        if kind == 'ExternalInput':
            self.inputs[name] = (tuple(shape), dtype)
        elif kind == 'ExternalOutput':
            self.outputs[name] = (tuple(shape), dtype)
        return t

    def sb(self, free_shape, dtype=F32):
        esz = 4 if dtype == F32 else 2
        n = 1
        for v in free_shape:
            n *= v
        nbytes = (n * esz + 31) // 32 * 32
        assert self.aoff + nbytes <= ARENA_BYTES, ('arena overflow', self.aoff, nbytes)
        a = self.arena[:, self.aoff // 4:(self.aoff + nbytes) // 4]
        self.aoff += nbytes
        if dtype != F32:
            a = a.bitcast(dtype)
        a = a[:, 0:n]
        if len(free_shape) == 2:
            a = a.rearrange('p (a b) -> p a b', b=free_shape[1])
        elif len(free_shape) == 3:
            a = a.rearrange('p (a b c) -> p a b c', b=free_shape[1], c=free_shape[2])
        return Tl(a)

    def persist(self, free_shape, dtype=F32):
        self.n_p += 1
        t = self.nc.alloc_sbuf_tensor('pp%d' % self.n_p, [128] + list(free_shape), dtype)
        return Tl(t[:])

    def stage_end(self):
        self.S.flush(barrier=True)
        self.aoff = 0

    def dma(self, out, in_, R, W, q=SP):
        self.S.op(q, lambda e: e.dma_start(out=out, in_=in_), reads=R, writes=W, dma=True)

    def mm(self, out, lhsT, rhs, start, stop, R, W):
        self.S.op(PE, lambda e: e.matmul(out, lhsT=lhsT, rhs=rhs, start=start, stop=stop), reads=R, writes=W)

    def act(self, out, in_, func, R, W, scale=1.0, bias=0.0, accum_out=None):
        if accum_out is None:
            self.S.op(ACT, lambda e: e.activation(out=out, in_=in_, func=func, bias=bias, scale=scale), reads=R, writes=W)
        else:
            self.S.op(ACT, lambda e: e.activation(out=out, in_=in_, func=func, bias=bias, scale=scale, accum_out=accum_out), reads=R, writes=W)

    def tt(self, out, in0, in1, op, R, W, eng=DVE):
        self.S.op(eng, lambda e: e.tensor_tensor(out=out, in0=in0, in1=in1, op=op), reads=R, writes=W)

    def ts(self, out, in0, s1, s2, op0, op1, R, W, eng=DVE):
        if s2 is None:
            self.S.op(eng, lambda e: e.tensor_scalar(out=out, in0=in0, scalar1=s1, scalar2=None, op0=op0), reads=R, writes=W)
        else:
            self.S.op(eng, lambda e: e.tensor_scalar(out=out, in0=in0, scalar1=s1, scalar2=s2, op0=op0, op1=op1), reads=R, writes=W)

    def stt(self, out, in0, scalar, in1, op0, op1, R, W, eng=DVE):
        self.S.op(eng, lambda e: e.scalar_tensor_tensor(out=out, in0=in0, scalar=scalar, in1=in1, op0=op0, op1=op1), reads=R, writes=W)

    def copy(self, out, in_, R, W, eng=DVE):
        self.S.op(eng, lambda e: e.tensor_copy(out=out, in_=in_), reads=R, writes=W)

    def memset(self, out, val, W, eng=DVE):
        self.S.op(eng, lambda e: e.memset(out, val), writes=W)

    def recip(self, out, in_, R, W):
        self.S.op(DVE, lambda e: e.reciprocal(out=out, in_=in_), reads=R, writes=W)

    def transpose(self, out, in_, ident, R, W):
        self.S.op(PE, lambda e: e.transpose(out, in_, ident), reads=R, writes=W)


def st_consts(c):
    c.ones_bf = c.persist([128], BF16)
    c.memset(c.ones_bf.ap, 1.0, [c.ones_bf.r])
    c.ident_f = c.persist([128], F32)
    c.memset(c.ident_f.ap, 0.0, [c.ident_f.r], eng=POOL)
    idf = c.ident_f
    c.S.op(POOL, lambda e: e.affine_select(out=idf.ap, in_=idf.ap, pattern=[[-1, 128]], compare_op=ALU.not_equal,
                                           fill=1.0, base=0, channel_multiplier=1), reads=[idf.r], writes=[idf.r])
    c.ident_b = c.persist([128], BF16)
    c.copy(c.ident_b.ap, c.ident_f.ap, [c.ident_f.r], [c.ident_b.r])
    c.eps_t = c.persist([1], F32)
    c.memset(c.eps_t.ap, EPS, [c.eps_t.r])


class ModJob:
    BW = 256

    def __init__(self, c, layer, bank):
        self.c, self.layer = c, layer
        self.M = c.persist([2, 9, 16], F32)
        self.A = c.persist([2, 3, 16], F32)
        self.G = c.persist([2, 3, 16], F32)
        c.M[layer], c.A[layer], c.G[layer] = self.M, self.A, self.G
        self.ps = c.psum[bank]
        self.nblk = (9 * D) // self.BW
        self.next = 0

    def prep(self):
        c, layer = self.c, self.layer
        self.sc = sc = c.sb([16, 2], F32)
        sg = c.sb([16, 2], F32)
        c.dma(sc.ap, c.d_cc, [], [sc.r])
        c.act(sg.ap, sc.ap, AF.Sigmoid, [sc.r], [sg.r])
        c.tt(sc.ap, sc.ap, sg.ap, ALU.mult, [sc.r, sg.r], [sc.r])
        self.mb = c.sb([144], F32)
        c.dma(self.mb.ap, c.d_modb[layer], [], [self.mb.r])
        self.ng = c.sb([3, 16], F32)
        c.dma(self.ng.ap, c.d_normg[layer], [], [self.ng.r])
        self.wbuf = [c.sb([16, self.BW], F32) for _ in range(2)]
        self.wsrc = c.d_modw[layer].rearrange('(kt p) f -> p kt f', p=128)

    def emit(self, nblocks):
        c, ps, sc, BW = self.c, self.ps, self.sc, self.BW
        for _ in range(nblocks):
            blk = self.next
            if blk >= self.nblk:
                return
            self.next += 1
            wb = self.wbuf[blk % 2]
            c.dma(wb.ap, self.wsrc[:, :, blk * BW:(blk + 1) * BW], [], [wb.r])
            for j in range(BW // 128):
                ft = blk * (BW // 128) + j
                for kt in range(16):
                    c.mm(ps.ap[:, 2 * ft:2 * ft + 2], wb.ap[:, kt, j * 128:(j + 1) * 128], sc.ap[:, kt, :],
                         kt == 0, kt == 15, [wb.r, sc.r], [ps.r])

    def finish(self):
        c, ps, M, A, G, mb, ng = self.c, self.ps, self.M, self.A, self.G, self.mb, self.ng
        self.emit(self.nblk)
        for g in range(2):
            src = ps.ap[:, 0:288].rearrange('p (f g) -> p g f', g=2)[:, g, :]
            c.tt(M.ap[:, g].rearrange('p i d -> p (i d)'), src, mb.ap, ALU.add, [ps.r, mb.r], [M.r])
        for g in range(2):
            for i in range(3):
                c.stt(A.ap[:, g, i, :], M.ap[:, g, 3 * i + 1, :], 1.0, ng.ap[:, i, :], ALU.add, ALU.mult, [M.r, ng.r], [A.r])
                fac = 1.0 if i == 1 else 0.5
                c.ts(G.ap[:, g, i, :], M.ap[:, g, 3 * i + 2, :], fac, None, ALU.mult, None, [M.r], [G.r])


def st_mod(c, layer):
    job = ModJob(c, layer, 0)
    job.prep()
    job.finish()
    c.stage_end()


def bidx(t0):
    return [b[0] for b in TB].index(t0)


def xr_col(c, t0):
    return [c.XRr[dt][bidx(t0)] for dt in range(16)]


def st_norm(c, layer, i, blocks=TB):
    H = c.sb([16, T], BF16)
    mark = c.aoff
    A, M = c.A[layer], c.M[layer]
    xt = [c.sb([16, 512], F32) for _ in range(2)]
    xrg = [[Reg() for _ in range(16)] for _ in range(2)]
    sq = [c.sb([16, 512], BF16) for _ in range(2)]
    rs = [c.sb([512], F32) for _ in range(2)]
    for bi, (t0, n) in enumerate(blocks):
        g = 1 if t0 < LC else 0
        x_, s_, r_ = xt[bi % 2], sq[bi % 2], rs[bi % 2]
        xr_ = xrg[bi % 2]
        ps = c.psum[6 + bi % 2]
        c.dma(x_.ap[:, :, 0:n], c.XR[:, :, t0:t0 + n].rearrange('d p t -> p d t'), xr_col(c, t0), xr_)
        c.act(s_.ap[:, :, 0:n], x_.ap[:, :, 0:n], AF.Square, xr_, [s_.r])
        for dt in range(16):
            c.mm(ps.ap[:, 0:n], c.ones_bf.ap, s_.ap[:, dt, 0:n], dt == 0, dt == 15, [s_.r, c.ones_bf.r], [ps.r])
        c.act(r_.ap[:, 0:n], ps.ap[:, 0:n], AF.Sqrt, [ps.r, c.eps_t.r], [r_.r], scale=1.0 / D, bias=c.eps_t.ap[:, 0:1])
        c.recip(r_.ap[:, 0:n], r_.ap[:, 0:n], [r_.r], [r_.r])
        for dt in range(16):
            c.stt(x_.ap[:, dt, 0:n], x_.ap[:, dt, 0:n], A.ap[:, g, i, dt:dt + 1], r_.ap[:, 0:n], ALU.mult, ALU.mult,
                  [xr_[dt], A.r, r_.r], [xr_[dt]])
            c.act(H.ap[:, dt, t0:t0 + n], x_.ap[:, dt, 0:n], AF.Identity, [xr_[dt], M.r], [H.r],
                  bias=M.ap[:, g, 3 * i, dt:dt + 1])
    c.S.flush(barrier=True)
    c.aoff = mark
    return H


FCH = [6, 6, 6, 6, 5, 5, 5, 5]


def st_ffn(c, layer, j, i, blocks=TB):
    H = st_norm(c, layer, i, blocks)
    G = c.G[layer]
    wg_src = c.d_wg[layer, j]
    wu_src = c.d_wu[layer, j]
    wd_src = c.d_wd[layer, j].rearrange('(ft p) d -> ft p d', p=128)
    NW = 3
    wgb = [c.sb([16, 128], BF16) for _ in range(NW)]
    wub = [c.sb([16, 128], BF16) for _ in range(NW)]
    wdb = [c.sb([2048], BF16) for _ in range(12)]
    hid = [c.sb([T], BF16) for _ in range(6)]
    sgt = [c.sb([512], F32) for _ in range(2)]
    xt = [c.sb([512], F32) for _ in range(6)]
    f0 = 0
    nup = 0
    nx = 0
    nwd = 0
    nfc = 0
    for ch, nf in enumerate(FCH):
        wd_tiles = []
        for k in range(nf):
            f = f0 + k
            wg_, wu_ = wgb[nfc % NW], wub[nfc % NW]
            nfc += 1
            wd_ = wdb[nwd % 12]
            nwd += 1
            c.dma(wg_.ap, wg_src[f], [], [wg_.r], q=POOL)
            c.dma(wu_.ap, wu_src[f], [], [wu_.r], q=POOL)
            c.dma(wd_.ap, wd_src[f], [], [wd_.r], q=POOL)
            wd_tiles.append(wd_)
            hk = hid[k]
            for bi, (t0, n) in enumerate(blocks):
                pg, pu = c.psum[2 * (nup % 2)], c.psum[2 * (nup % 2) + 1]
                for kt in range(16):
                    c.mm(pg.ap[:, 0:n], wg_.ap[:, kt, :], H.ap[:, kt, t0:t0 + n], kt == 0, kt == 15, [wg_.r, H.r], [pg.r])
                for kt in range(16):
                    c.mm(pu.ap[:, 0:n], wu_.ap[:, kt, :], H.ap[:, kt, t0:t0 + n], kt == 0, kt == 15, [wu_.r, H.r], [pu.r])
                s_ = sgt[nup % 2]
                c.act(s_.ap[:, 0:n], pg.ap[:, 0:n], AF.Silu, [pg.r], [s_.r])
                c.tt(hk.ap[:, t0:t0 + n], s_.ap[:, 0:n], pu.ap[:, 0:n], ALU.mult, [s_.r, pu.r], [hk.r])
                nup += 1
        tiles = [(dt, t0, n) for dt in range(16) for (t0, n) in blocks]
        PF = 4

        def issue_load(idx):
            dt, t0, n = tiles[idx]
            x_ = xt[(nx + idx) % 6]
            c.dma(x_.ap[:, 0:n], c.XR[dt, :, t0:t0 + n], [c.XRr[dt][bidx(t0)]], [x_.r])
        for idx in range(min(PF, len(tiles))):
            issue_load(idx)
        for idx, (dt, t0, n) in enumerate(tiles):
            if idx + PF < len(tiles):
                issue_load(idx + PF)
            g = 1 if t0 < LC else 0
            po = c.psum[4 + (nx + idx) % 2]
            x_ = xt[(nx + idx) % 6]
            xreg = c.XRr[dt][bidx(t0)]
            for k in range(nf):
                c.mm(po.ap[:, 0:n], wd_tiles[k].ap[:, dt * 128:(dt + 1) * 128], hid[k].ap[:, t0:t0 + n],
                     k == 0, k == nf - 1, [wd_tiles[k].r, hid[k].r], [po.r])
            c.stt(x_.ap[:, 0:n], po.ap[:, 0:n], G.ap[:, g, i, dt:dt + 1], x_.ap[:, 0:n], ALU.mult, ALU.add,
                  [po.r, G.r, x_.r], [x_.r])
            c.dma(c.XR[dt, :, t0:t0 + n], x_.ap[:, 0:n], [x_.r], [xreg], q=ACT)
        nx += len(tiles)
        f0 += nf
    c.stage_end()


def st_load_x(c):
    for dt in range(16):
        c.dma(c.XR[dt], c.d_xin[dt], [], c.XRr[dt])
    c.stage_end()


def st_final(c):
    fg = c.sb([16], F32)
    c.dma(fg.ap, c.d_finalg, [], [fg.r])
    xt = [c.sb([16, 512], F32) for _ in range(2)]
    sq = [c.sb([16, 512], BF16) for _ in range(2)]
    rs = [c.sb([512], F32) for _ in range(2)]
    for bi, (t0, n) in enumerate(TB[1:]):
        x_, s_, r_ = xt[bi % 2], sq[bi % 2], rs[bi % 2]
        ps = c.psum[bi % 2]
        c.dma(x_.ap, c.XR[:, :, t0:t0 + n].rearrange('d p t -> p d t'), xr_col(c, t0), [x_.r])
        c.act(s_.ap, x_.ap, AF.Square, [x_.r], [s_.r])
        for dt in range(16):
            c.mm(ps.ap, c.ones_bf.ap, s_.ap[:, dt, :], dt == 0, dt == 15, [s_.r, c.ones_bf.r], [ps.r])
        c.act(r_.ap, ps.ap, AF.Sqrt, [ps.r, c.eps_t.r], [r_.r], scale=1.0 / D, bias=c.eps_t.ap[:, 0:1])
        c.recip(r_.ap, r_.ap, [r_.r], [r_.r])
        for dt in range(16):
            c.stt(x_.ap[:, dt, :], x_.ap[:, dt, :], fg.ap[:, dt:dt + 1], r_.ap, ALU.mult, ALU.mult, [x_.r, fg.r, r_.r], [x_.r])
        c.dma(c.d_out[:, :, t0 - LC:t0 - LC + n].rearrange('d p t -> p d t'), x_.ap, [x_.r], [c.outr])
    c.stage_end()


def declare_io(c):
    c.d_xin = c.dram('xin', [16, 128, T], F32, 'ExternalInput')
    c.d_cc = c.dram('cc', [128, 16, 2], F32, 'ExternalInput')
    c.d_modw = c.dram('mod_w', [2, D, 9 * D], F32, 'ExternalInput')
    c.d_modb = c.dram('mod_b', [2, 128, 144], F32, 'ExternalInput')
    c.d_normg = c.dram('norm_g', [2, 128, 3, 16], F32, 'ExternalInput')
    c.d_finalg = c.dram('final_g', [128, 16], F32, 'ExternalInput')
    c.d_wg = c.dram('ffn_wg', [2, 2, NFT, 128, 16, 128], F32, 'ExternalInput')
    c.d_wu = c.dram('ffn_wu', [2, 2, NFT, 128, 16, 128], F32, 'ExternalInput')
    c.d_wd = c.dram('ffn_wd', [2, 2, DFF, D], F32, 'ExternalInput')
    EI = 'ExternalInput'
    c.d_evwin_t = c.dram('ev_w_in_t', [24, 128, 16, 128], F32, EI)
    c.d_evwin = c.dram('ev_w_in', [D, 4096], F32, EI)
    c.d_rpbt = c.dram('rpbt', [64, 8, 15, 64], F32, EI)
    c.d_namask = c.dram('namask', [64, 64], F32, EI)
    c.d_s5are = c.dram('s5are', [128, 64], F32, EI)
    c.d_s5aim = c.dram('s5aim', [128, 64], F32, EI)
    c.d_s5ldt = c.dram('s5ldt', [128, 64], F32, EI)
    c.d_s5d = c.dram('s5d', [128, 32], F32, EI)
    c.d_iota1 = c.dram('iota1', [128, 512], F32, EI)
    c.d_s5bre = c.dram('s5bre', [128, 64, 32], F32, EI)
    c.d_s5bim = c.dram('s5bim', [128, 64, 32], F32, EI)
    c.d_s5cre = c.dram('s5cre', [128, 64, 32], F32, EI)
    c.d_s5cim = c.dram('s5cim', [128, 64, 32], F32, EI)
    c.d_gluw = c.dram('glu_w', [1024, 1024], F32, EI)
    c.d_glub = c.dram('glu_b', [128, 8], F32, EI)
    c.d_wout = [c.dram('ev_w_out', [D, D], F32, EI), c.dram('od_w_out', [D, D], F32, EI)]
    c.d_odwin_t = c.dram('od_w_in_t', [48, 128, 16, 128], F32, EI)
    c.d_odwdt = c.dram('od_w_dt', [128, 2, 16, 16], F32, EI)
    c.d_hysw = c.dram('hysw', [128, 24, 3], F32, EI)
    c.d_hysb = c.dram('hysb', [128, 24], F32, EI)
    c.d_ssdcw = c.dram('ssdcw', [128, 16, 3], F32, EI)
    c.d_ssdcb = c.dram('ssdcb', [128, 16], F32, EI)
    c.d_dtbias = c.dram('dtbias', [16, 2], F32, EI)
    c.d_alog = c.dram('alog', [16, 2], F32, EI)
    c.d_ssdd = c.dram('ssdd', [128, 16], F32, EI)
    c.d_tri = c.dram('tri', [2, 128, 128], F32, EI)
    c.d_ssdng = c.dram('ssdng', [128, 8], F32, EI)
    c.d_hyz = c.dram('hyz', [33, L], F32, EI)
    c.d_hywin = c.dram('hywin', [33, 64], F32, EI)
    c.d_hywmid = c.dram('hywmid', [64, 2, 64], F32, EI)
    c.d_hyfb = c.dram('hyfb', [64, 4], F32, EI)
    c.d_hywout = c.dram('hywout', [64, 4096], F32, EI)
    c.d_decay = c.dram('decay', [L, 1024], F32, EI)
    c.d_hyfbias = c.dram('hyfbias', [128, 2, 1024], F32, EI)
    c.d_Cf = c.dram('Cf', [16, 128, 16, 128], BF16, EI)
    c.d_Sf = c.dram('Sf', [16, 128, 16, 128], BF16, EI)
    c.d_Ci = c.dram('Ci', [16, 128, 16, 128], BF16, EI)
    c.d_Si = c.dram('Si', [16, 128, 16, 128], BF16, EI)
    c.HYd = c.dram('HYd', [24, 128, L], BF16)
    c.Zd = c.dram('Zd', [8, 128, L], BF16)
    c.XBCd = c.dram('XBCd', [16, 128, T], BF16)
    c.DTd = c.dram('DTd', [2, 16, T], F32)
    c.CUMd = c.dram('CUMd', [2, 16, T], F32)
    c.YSd = c.dram('YSd', [16, 64, L], F32)
    c.HYr, c.Zr, c.XBCr, c.DTr, c.CUMr, c.YSr = Reg(), Reg(), Reg(), Reg(), Reg(), Reg()
    c.Ud = c.dram('Ud', [8, 128, T], BF16)
    c.Qd = c.dram('Qd', [8, 128, T], BF16)
    c.Kd = c.dram('Kd', [8, 128, T], BF16)
    c.Vd = c.dram('Vd', [18, 128, 1024], BF16)
    c.Yd = c.dram('Yd', [1024, T], F32)
    c.CATd = c.dram('CATd', [16, 128, T], BF16)
    c.Ur, c.Qr, c.Kr, c.Vr, c.Yr = Reg(), Reg(), Reg(), Reg(), Reg()
    c.CATr = [Reg(), Reg()]
    c.d_out = c.dram('out', [16, 128, L], F32, 'ExternalOutput')
    c.outr = Reg()
    c.XR = c.dram('XR', [16, 128, T], F32)
    c.XRr = [[Reg() for _ in range(len(TB))] for _ in range(16)]
    c.M, c.A, c.G = {}, {}, {}


def build(stages=None, dbg_in=(), dbg_out=()):
    c = Ctx(dbg_in, dbg_out)
    declare_io(c)
    st_consts(c)
    allst = stages is None
    if allst or 'load' in stages:
        st_load_x(c)
    for layer in range(2):
        if (allst and layer == 0) or (not allst and ('mod%d' % layer) in stages):
            st_mod(c, layer)
        if allst or ('ffa%d' % layer) in stages:
            st_ffn(c, layer, 0, 0)
        if layer == 0 and (allst or 'evmix' in stages):
            sub = stages if (stages and any(k.startswith('ev_') for k in stages)) else None
            if sub is None or 'ev_proj' in sub:
                H = st_norm(c, 0, 1)
                st_even_proj(c, H)
            if sub is None or 'ev_na' in sub:
                st_na(c, True)
            if sub is None or 'ev_s5' in sub:
                st_s5(c, True, ModJob(c, 1, 7) if allst else None)
            if sub is None or 'ev_glu' in sub:
                st_glu_wout(c, 0, True)
        if layer == 1 and (allst or 'odmix' in stages):
            sub = stages if (stages and any(k.startswith('od_') for k in stages)) else None
            if sub is None or 'od_proj' in sub:
                H = st_norm(c, 1, 1)
                st_odd_proj(c, H)
            if sub is None or 'od_ssd' in sub:
                st_ssd(c)
            if sub is None or 'od_hy' in sub:
                st_hyena(c)
            if sub is None or 'od_out' in sub:
                st_odd_out(c)
        if allst or ('ffb%d' % layer) in stages:
            st_ffn(c, layer, 1, 2, TB if layer == 0 else TB[1:])
    if allst or 'final' in stages:
        st_final(c)
    c.stage_end()
    return c


def host_prep(inp, b):
    f = np.float32
    m = {}
    xc = np.concatenate([inp['ctx'][b], inp['x'][b]], axis=0)
    m['xin'] = np.ascontiguousarray(xc.T.reshape(16, 128, T))
    cc = np.stack([inp['c'][b], inp['c_ctx']], axis=-1)
    m['cc'] = np.ascontiguousarray(cc.reshape(16, 128, 2).transpose(1, 0, 2))
    return m


_SHARED = {}


def host_shared(inp):
    m = {}
    m['mod_w'] = np.ascontiguousarray(inp['mod_w'])
    m['mod_b'] = np.ascontiguousarray(inp['mod_b'].reshape(2, 144, 128).transpose(0, 2, 1))
    m['norm_g'] = np.ascontiguousarray(inp['norm_g'].reshape(2, 3, 16, 128).transpose(0, 3, 1, 2))
    m['final_g'] = np.ascontiguousarray(inp['final_g'].reshape(16, 128).T)
    for k in ('ffn_wg', 'ffn_wu'):
        m[k] = np.ascontiguousarray(inp[k].reshape(2, 2, 16, 128, NFT, 128).transpose(0, 1, 4, 3, 2, 5))
    m['ffn_wd'] = np.ascontiguousarray(inp['ffn_wd'])
    w = inp['ev_w_in'][0]
    m['ev_w_in'] = np.ascontiguousarray(w)
    m['ev_w_in_t'] = np.ascontiguousarray(w[:, :3072].reshape(16, 128, 24, 128).transpose(2, 1, 0, 3))
    col = np.arange(64)
    dc = np.clip(col[:, None] - col[None, :] + 15, 0, 30)
    rp = inp['na_rpb'][0][:, :, dc]
    m['rpbt'] = np.ascontiguousarray(rp.transpose(2, 0, 1, 3))
    cs = np.clip(col - 8, 0, 48)
    ok = (col[:, None] >= cs[None, :]) & (col[:, None] < cs[None, :] + 16)
    m['namask'] = np.where(ok, 0.0, NEGM).astype(np.float32)

    def st_lay(a):
        return np.ascontiguousarray(a.reshape(2, 32, 2, 64).transpose(2, 3, 0, 1).reshape(128, 64))
    m['s5are'] = st_lay(inp['s5_a_re'][0])
    m['s5aim'] = st_lay(inp['s5_a_im'][0])
    m['s5ldt'] = st_lay(np.repeat(inp['s5_log_dt'][0][:, :, None], 64, axis=2))
    dd = np.zeros((128, 32), np.float32)
    dd[0:32] = inp['s5_d'][0].reshape(32, 32).T
    m['s5d'] = dd
    m['iota1'] = np.ascontiguousarray(np.broadcast_to(np.arange(1, 513, dtype=np.float32), (128, 512)))

    def b_blk(b):
        o = np.zeros((128, 2, 32, 32), np.float32)
        bb = b.reshape(2, 32, 2, 64, 16)
        o[0:64, :, :, 0:16] = bb[:, :, 0].transpose(2, 0, 1, 3)
        o[64:128, :, :, 16:32] = bb[:, :, 1].transpose(2, 0, 1, 3)
        return o.reshape(128, 64, 32)

    def c_blk(cm):
        o = np.zeros((128, 2, 32, 32), np.float32)
        cc = cm.reshape(2, 32, 2, 16, 64)
        o[0:64, :, :, 0:16] = cc[:, :, 0].transpose(3, 0, 1, 2)
        o[64:128, :, :, 16:32] = cc[:, :, 1].transpose(3, 0, 1, 2)
        return o.reshape(128, 64, 32)
    m['s5bre'] = b_blk(inp['s5_b_re'][0])
    m['s5bim'] = b_blk(inp['s5_b_im'][0])
    m['s5cre'] = c_blk(inp['s5_c_re'][0])
    m['s5cim'] = c_blk(inp['s5_c_im'][0])
    m['glu_w'] = np.ascontiguousarray(inp['s5_glu_w'][0])
    m['glu_b'] = np.ascontiguousarray(inp['s5_glu_b'][0].reshape(8, 128).T)
    m['ev_w_out'] = np.ascontiguousarray(inp['ev_w_out'][0])
    m['od_w_out'] = np.ascontiguousarray(inp['od_w_out'][0])
    w = inp['od_w_in'][0]
    m['od_w_in_t'] = np.ascontiguousarray(w[:, :6144].reshape(16, 128, 48, 128).transpose(2, 1, 0, 3))
    m['od_w_dt'] = np.ascontiguousarray(w[:, 6144:6176].reshape(16, 128, 2, 16).transpose(1, 2, 0, 3))
    m['hysw'] = np.ascontiguousarray(inp['hy_short_w'][0].T.reshape(24, 128, 3).transpose(1, 0, 2))
    m['hysb'] = np.ascontiguousarray(inp['hy_short_b'][0].reshape(24, 128).T)
    m['ssdcw'] = np.ascontiguousarray(inp['ssd_conv_w'][0].T.reshape(16, 128, 3).transpose(1, 0, 2))
    m['ssdcb'] = np.ascontiguousarray(inp['ssd_conv_b'][0].reshape(16, 128).T)
    m['dtbias'] = np.ascontiguousarray(inp['ssd_dt_bias'][0].T)
    m['alog'] = np.ascontiguousarray(inp['ssd_a_log'][0].T)
    m['ssdd'] = np.ascontiguousarray(np.broadcast_to(inp['ssd_d'][0][None, :], (128, 16)))
    ii = np.arange(128)
    m['tri'] = np.stack([(ii[:, None] <= ii[None, :]), (ii[:, None] >= ii[None, :])]).astype(np.float32)
    m['ssdng'] = np.ascontiguousarray(inp['ssd_norm_g'][0].reshape(8, 128).T)
    m['hywin'] = np.ascontiguousarray(inp['hy_w_in'][0])
    m['hywmid'] = np.ascontiguousarray(inp['hy_w_mid'][0].transpose(1, 0, 2))
    m['hyfb'] = np.ascontiguousarray(np.stack([inp['hy_freq'][0], inp['hy_b_in'][0], inp['hy_b_mid'][0][0], inp['hy_b_mid'][0][1]], axis=1))
    m['hywout'] = np.ascontiguousarray(inp['hy_w_out'][0])
    m['hyfbias'] = np.ascontiguousarray(np.broadcast_to(inp['hy_fbias'][0][None], (128, 2, 1024)))
    m.update(hy_consts())
    return m


_HYC = {}


def hy_consts():
    if _HYC:
        return _HYC
    f32 = np.float32
    t = np.linspace(0.0, 1.0, L, dtype=f32)[:, None]
    w = (2.0 * math.pi * np.arange(L, dtype=f32)[:, None] / L).astype(f32)
    f = np.linspace(1e-4, 15, 16, dtype=f32)[None, :]
    z = np.concatenate([t, np.cos(f * w), -np.sin(f * w)], axis=-1).astype(f32)
    _HYC['hyz'] = np.ascontiguousarray(z.T)
    mx = math.log(1e-2) / 0.3
    mn = math.log(1e-2) / 1.5
    deltas = np.abs(np.linspace(mn, mx, 1024, dtype=f32))
    _HYC['decay'] = np.exp(-t * deltas[None, :]).astype(f32)
    n = np.arange(L, dtype=np.int64)
    ang = 2.0 * np.pi * ((n[:, None] * n[None, :]) % 4096).astype(np.float64) / 4096.0
    Cf = np.cos(ang)
    Sf = -np.sin(ang)
    sgn = np.where(n % 2 == 0, 1.0, -1.0)
    Sf[:, 0] = sgn
    wf = np.full(L, 2.0 / 4096.0)
    wf[0] = 1.0 / 4096.0
    Ci = wf[:, None] * np.cos(ang)
    Si = -wf[:, None] * np.sin(ang)
    Si[0, :] = sgn / 4096.0

    def lay(a):
        return np.ascontiguousarray(a.reshape(16, 128, 16, 128).transpose(2, 1, 0, 3).astype(f32).astype(ml_dtypes.bfloat16))
    _HYC['Cf'], _HYC['Sf'], _HYC['Ci'], _HYC['Si'] = lay(Cf), lay(Sf), lay(Ci), lay(Si)
    return _HYC


def kernel(**inputs):
    inp = {k: np.asarray(v) for k, v in inputs.items()}
    c = build()
    shared = host_shared(inp)
    in_maps = []
    for b in range(8):
        m = dict(shared)
        m.update(host_prep(inp, b))
        in_maps.append({k: m[k] for k in c.inputs})
    res = run_bass_kernel_spmd(c.nc, in_maps, core_ids=list(range(8)))
    outs = []
    for b in range(8):
        o = np.asarray(res.results[b]['out'])
        outs.append(o.reshape(D, L).T)
    return np.ascontiguousarray(np.stack(outs, axis=0)).astype(np.float32)


SQ128 = math.sqrt(128.0)
NEGM = -30000.0


def evac(c, n, out, in_, R, W):
    if n % 2 == 0:
        c.act(out, in_, AF.Identity, R, W)
    else:
        c.copy(out, in_, R, W)


def st_even_proj(c, H):
    wsrc = c.d_evwin_t
    wb = [c.sb([16, 128], BF16) for _ in range(3)]
    ob = [c.sb([512], BF16) for _ in range(4)]
    dst = [c.Ud, c.Qd, c.Kd]
    dreg = [c.Ur, c.Qr, c.Kr]
    cnt = 0
    for f in range(24):
        w_ = wb[f % 3]
        c.dma(w_.ap, wsrc[f], [], [w_.r], q=POOL)
        for bi, (t0, n) in enumerate(TB):
            ps = c.psum[cnt % 4]
            o_ = ob[cnt % 4]
            for kt in range(16):
                c.mm(ps.ap[:, 0:n], w_.ap[:, kt, :], H.ap[:, kt, t0:t0 + n], kt == 0, kt == 15, [w_.r, H.r], [ps.r])
            evac(c, cnt, o_.ap[:, 0:n], ps.ap[:, 0:n], [ps.r], [o_.r])
            c.dma(dst[f // 8][f % 8, :, t0:t0 + n], o_.ap[:, 0:n], [o_.r], [dreg[f // 8]])
            cnt += 1
    vsrc = c.d_evwin.rearrange('(kt p) f -> p kt f', p=128)
    vw = [c.sb([16, 512], BF16) for _ in range(2)]
    for j in range(2):
        for kt in range(16):
            c.dma(vw[j].ap[:, kt, :], vsrc[:, kt, 3072 + 512 * j:3072 + 512 * (j + 1)], [], [vw[j].r], q=POOL)
    for tt_ in range(18):
        for j in range(2):
            ps = c.psum[cnt % 4]
            o_ = ob[cnt % 4]
            for kt in range(16):
                c.mm(ps.ap, H.ap[:, kt, tt_ * 128:(tt_ + 1) * 128], vw[j].ap[:, kt, :], kt == 0, kt == 15, [vw[j].r, H.r], [ps.r])
            evac(c, cnt, o_.ap, ps.ap, [ps.r], [o_.r])
            c.dma(c.Vd[tt_, :, 512 * j:512 * (j + 1)], o_.ap, [o_.r], [c.Vr])
            cnt += 1
    c.stage_end()


def st_na(c, with_ctx=True):
    Tb = c.sb([8, 15, 64], F32)
    mk = c.sb([64], F32)
    c.dma(Tb.ap[0:64], c.d_rpbt, [], [Tb.r])
    c.dma(mk.ap[0:64], c.d_namask, [], [mk.r])
    c.stt(Tb.ap[0:64].rearrange('p h d q -> p (h d) q'), Tb.ap[0:64].rearrange('p h d q -> p (h d) q'), SQ128,
          mk.ap[0:64, None, :].to_broadcast([64, 120, 64]), ALU.mult, ALU.add, [Tb.r, mk.r], [Tb.r])
    neg = c.sb([64], F32)
    c.memset(neg.ap, NEGM, [neg.r])
    sel = c.sb([2, 128], F32)
    c.memset(sel.ap, 0.0, [sel.r])
    c.copy(sel.ap[0:64, 0, 0:64], c.ident_f.ap[0:64, 0:64], [c.ident_f.r, sel.r], [sel.r])
    c.copy(sel.ap[0:64, 1, 64:128], c.ident_f.ap[0:64, 0:64], [c.ident_f.r, sel.r], [sel.r])
    qb = [c.sb([T], BF16) for _ in range(2)]
    kb = [c.sb([T], BF16) for _ in range(2)]
    vb = [c.sb([18, 128], BF16) for _ in range(2)]
    ob = [c.sb([T], BF16) for _ in range(2)]
    eb = [c.sb([7, 64], BF16) for _ in range(3)]
    ec = c.sb([2, 256], BF16)
    rz = [c.sb([64], F32) for _ in range(2)]
    rzc = c.sb([256], F32)
    sc = 1.0 / SQ128
    it = 0
    for h in range(8):
        q_, k_, v_, o_ = qb[h % 2], kb[h % 2], vb[h % 2], ob[h % 2]
        c.dma(q_.ap, c.Qd[h], [c.Qr], [q_.r])
        c.dma(k_.ap, c.Kd[h], [c.Kr], [k_.r])
        c.dma(v_.ap, c.Vd[:, :, h * 128:(h + 1) * 128].rearrange('t p d -> p t d'), [c.Vr], [v_.r])
        if with_ctx:
            ps, po = c.psum[4], c.psum[5]
            for i in range(2):
                c.mm(ps.ap[:, i * 256:(i + 1) * 256], k_.ap[:, i * 128:(i + 1) * 128], q_.ap[:, 0:256], True, True, [k_.r, q_.r], [ps.r])
            c.act(ec.ap.rearrange('p a b -> p (a b)'), ps.ap, AF.Exp, [ps.r], [ec.r], scale=sc)
            for i in range(2):
                c.mm(po.ap[:, 0:256], v_.ap[:, i, :], ec.ap[:, i, :], i == 0, i == 1, [v_.r, ec.r], [po.r])
            for i in range(2):
                c.mm(po.ap[:, 256:512], c.ones_bf.ap, ec.ap[:, i, :], i == 0, i == 1, [c.ones_bf.r, ec.r], [po.r])
            c.recip(rzc.ap, po.ap[:, 256:512], [po.r], [rzc.r])
            c.tt(o_.ap[:, 0:256], po.ap[:, 0:256], rzc.ap, ALU.mult, [po.r, rzc.r], [o_.r])
        for r in range(32):
            rs = min(max(r - 4, 0), 24)
            base = (rs // 2) * 2
            nt = 4 if rs % 2 == 0 else 5
            ps, po = c.psum[it % 2], c.psum[2 + it % 2]
            e_ = eb[it % 3]
            z_ = rz[it % 2]
            qs = q_.ap[:, LC + r * 64:LC + (r + 1) * 64]
            tiles = []
            for i in range(nt):
                krow = base + 2 * i
                k0 = LC + krow * 64
                col = ps.ap[:, i * 64:(i + 1) * 64]
                c.mm(col, k_.ap[:, k0:k0 + 128], qs, True, False, [k_.r, q_.r], [ps.r])
                for half in range(2):
                    kr = krow + half
                    if rs <= kr < rs + 8:
                        rhs = Tb.ap[0:64, h, kr - r + 7, :]
                        rr = Tb.r
                    else:
                        rhs = neg.ap[0:64, :]
                        rr = neg.r
                    c.mm(col, sel.ap[0:64, half, :], rhs, False, half == 1, [sel.r, rr], [ps.r])
                tiles.append((LC // 128) + krow // 2)
            for i in range(2):
                col = ps.ap[:, (nt + i) * 64:(nt + i + 1) * 64]
                c.mm(col, k_.ap[:, i * 128:(i + 1) * 128], qs, True, True, [k_.r, q_.r], [ps.r])
                tiles.append(i)
            ntt = nt + 2
            c.act(e_.ap[:, 0:ntt, :].rearrange('p a b -> p (a b)'), ps.ap[:, 0:ntt * 64], AF.Exp, [ps.r], [e_.r], scale=sc)
            for i, vt in enumerate(tiles):
                c.mm(po.ap[:, 0:64], v_.ap[:, vt, :], e_.ap[:, i, :], i == 0, i == ntt - 1, [v_.r, e_.r], [po.r])
            for i in range(ntt):
                c.mm(po.ap[:, 64:128], c.ones_bf.ap, e_.ap[:, i, :], i == 0, i == ntt - 1, [c.ones_bf.r, e_.r], [po.r])
            c.recip(z_.ap, po.ap[:, 64:128], [po.r], [z_.r])
            c.tt(o_.ap[:, LC + r * 64:LC + (r + 1) * 64], po.ap[:, 0:64], z_.ap, ALU.mult, [po.r, z_.r], [o_.r])
            it += 1
        if with_ctx:
            c.dma(c.CATd[8 + h], o_.ap, [o_.r], [c.CATr[1]])
        else:
            c.dma(c.CATd[8 + h, :, LC:T], o_.ap[:, LC:T], [o_.r], [c.CATr[1]])
    c.stage_end()


def bcl(ap, shape):
    return ap.to_broadcast(list(shape))


def rnd(c, out, in_, R, W, eng=DVE):
    c.ts(out, in_, MAGIC, MAGIC, ALU.add, ALU.subtract, R, W, eng=eng)


def sincos_frac(c, t, tmp, out_s, out_c, R):
    a, b = tmp
    rnd(c, a.ap, t.ap, [t.r], [a.r])
    c.tt(a.ap, t.ap, a.ap, ALU.subtract, [t.r, a.r], [a.r])
    c.act(out_s.ap, a.ap, AF.Sin, [a.r], [out_s.r], scale=2 * math.pi)
    c.ts(b.ap, t.ap, 0.25, None, ALU.add, None, [t.r], [b.r])
    rnd(c, a.ap, b.ap, [b.r], [a.r])
    c.tt(b.ap, b.ap, a.ap, ALU.subtract, [b.r, a.r], [b.r])
    c.act(out_c.ap, b.ap, AF.Sin, [b.r], [out_c.r], scale=2 * math.pi)


def st_s5(c, with_ctx=True, modjob=None):
    def ld(src, shape):
        t = c.sb(shape, F32)
        c.dma(t.ap, src, [], [t.r])
        return t
    dcol = ld(c.d_s5d, [32])
    iota = ld(c.d_iota1, [512])
    r_, thp = c.sb([64], F32), c.sb([64], F32)
    BT = c.sb([128, 128], BF16)
    CB = c.sb([64, 2, 32], BF16)
    mark = c.aoff
    are, aim, ldt = ld(c.d_s5are, [64]), ld(c.d_s5aim, [64]), ld(c.d_s5ldt, [64])
    tmp = [c.sb([64], F32) for _ in range(8)]
    dtm, sn, cs, cre, cim, den = [c.sb([64], F32) for _ in range(6)]
    c.act(dtm.ap, ldt.ap, AF.Exp, [ldt.r], [dtm.r])
    c.tt(tmp[0].ap, are.ap, dtm.ap, ALU.mult, [are.r, dtm.r], [tmp[0].r])
    c.act(r_.ap, tmp[0].ap, AF.Exp, [tmp[0].r], [r_.r])
    c.tt(thp.ap, aim.ap, dtm.ap, ALU.mult, [aim.r, dtm.r], [thp.r])
    c.ts(thp.ap, thp.ap, 1.0 / (2 * math.pi), None, ALU.mult, None, [thp.r], [thp.r])
    sincos_frac(c, thp, tmp[1:3], sn, cs, None)
    nr, ni = tmp[3], tmp[4]
    c.tt(nr.ap, r_.ap, cs.ap, ALU.mult, [r_.r, cs.r], [nr.r])
    c.ts(nr.ap, nr.ap, -1.0, None, ALU.add, None, [nr.r], [nr.r])
    c.tt(ni.ap, r_.ap, sn.ap, ALU.mult, [r_.r, sn.r], [ni.r])
    c.tt(den.ap, are.ap, are.ap, ALU.mult, [are.r], [den.r])
    c.tt(tmp[5].ap, aim.ap, aim.ap, ALU.mult, [aim.r], [tmp[5].r])
    c.tt(den.ap, den.ap, tmp[5].ap, ALU.add, [den.r, tmp[5].r], [den.r])
    c.recip(den.ap, den.ap, [den.r], [den.r])
    c.tt(cre.ap, nr.ap, are.ap, ALU.mult, [nr.r, are.r], [cre.r])
    c.tt(tmp[5].ap, ni.ap, aim.ap, ALU.mult, [ni.r, aim.r], [tmp[5].r])
    c.tt(cre.ap, cre.ap, tmp[5].ap, ALU.add, [cre.r, tmp[5].r], [cre.r])
    c.tt(cre.ap, cre.ap, den.ap, ALU.mult, [cre.r, den.r], [cre.r])
    c.tt(cim.ap, ni.ap, are.ap, ALU.mult, [ni.r, are.r], [cim.r])
    c.tt(tmp[5].ap, nr.ap, aim.ap, ALU.mult, [nr.r, aim.r], [tmp[5].r])
    c.tt(cim.ap, cim.ap, tmp[5].ap, ALU.subtract, [cim.r, tmp[5].r], [cim.r])
    c.tt(cim.ap, cim.ap, den.ap, ALU.mult, [cim.r, den.r], [cim.r])
    bre, bim = ld(c.d_s5bre, [64, 32]), ld(c.d_s5bim, [64, 32])
    Bb = c.sb([64, 2, 32], F32)
    t1, t2 = c.sb([64, 32], F32), c.sb([64, 32], F32)
    creb, cimb = bcl(cre.ap, [128, 64, 32]), bcl(cim.ap, [128, 64, 32])
    c.tt(t1.ap, bre.ap, creb, ALU.mult, [bre.r, cre.r], [t1.r])
    c.tt(t2.ap, bim.ap, cimb, ALU.mult, [bim.r, cim.r], [t2.r])
    c.tt(Bb.ap[:, :, 0, :], t1.ap, t2.ap, ALU.subtract, [t1.r, t2.r], [Bb.r])
    c.tt(t1.ap, bim.ap, creb, ALU.mult, [bim.r, cre.r], [t1.r])
    c.tt(t2.ap, bre.ap, cimb, ALU.mult, [bre.r, cim.r], [t2.r])
    c.tt(Bb.ap[:, :, 1, :], t1.ap, t2.ap, ALU.add, [t1.r, t2.r], [Bb.r])
    for q4 in range(32):
        ps = c.psum[q4 % 2]
        for j in range(4):
            idx = q4 * 4 + j
            c.transpose(ps.ap[0:32, j * 128:(j + 1) * 128], Bb.ap[:, idx // 2, idx % 2, :], c.ident_f.ap, [Bb.r, c.ident_f.r], [ps.r])
        c.copy(BT.ap[0:32, q4 * 4:(q4 + 1) * 4, :].rearrange('p a b -> p (a b)'), ps.ap[0:32, :], [ps.r], [BT.r])
    crb, cib = ld(c.d_s5cre, [64, 32]), ld(c.d_s5cim, [64, 32])
    c.act(CB.ap[:, :, 0, :], crb.ap, AF.Identity, [crb.r], [CB.r])
    c.act(CB.ap[:, :, 1, :], cib.ap, AF.Identity, [cib.r], [CB.r], scale=-1.0)
    c.S.flush(barrier=True)
    c.aoff = mark
    ctab = [c.sb([512], F32) for _ in range(2)]
    stab = [c.sb([512], F32) for _ in range(2)]
    tA, tB, tT = c.sb([512], F32), c.sb([512], F32), c.sb([512], F32)
    br, bi_ = [c.sb([512], F32) for _ in range(2)], [c.sb([512], F32) for _ in range(2)]
    d1, d2, zR, wR, m1, m2 = [c.sb([512], F32) for _ in range(6)]
    p1, p2, zI, wI, m3, m4 = [c.sb([512], F32) for _ in range(6)]
    sbuf = [[[c.sb([T], BF16) for _ in range(2)] for _ in range(2)] for _ in range(2)]
    ug = [c.sb([T], BF16) for _ in range(2)]
    ysb = [c.sb([T], F32) for _ in range(2)]
    ini = [c.sb([1], F32) for _ in range(2)]
    tin = c.sb([1], F32)
    segs_f = list(TB)
    segs_b = [TB[0]] + TB[:0:-1]
    if not with_ctx:
        pass
    nseg = 0
    if modjob is not None:
        modjob.prep()
    for gp in range(32):
        if modjob is not None:
            modjob.emit(3 if gp % 4 else 2)
        u_ = ug[gp % 2]
        c.dma(u_.ap[0:32], c.Ud[gp // 4, 32 * (gp % 4):32 * (gp % 4) + 32, :], [c.Ur], [u_.r])
        for dr in range(2):
            dg = dr * 32 + gp
            ct, st_ = ctab[dr], stab[dr]
            c.ts(tT.ap, iota.ap, thp.ap[:, dg:dg + 1], None, ALU.mult, None, [iota.r, thp.r], [tT.r])
            sincos_frac(c, tT, [tA, tB], st_, ct, None)
            rcol = r_.ap[:, dg:dg + 1]
            sR_, sI_ = sbuf[gp % 2][dr]
            first = True
            for (t0, n) in (segs_f if dr == 0 else segs_b):
                rev = dr == 1
                pr, pi = c.psum[2 * (nseg % 2)], c.psum[2 * (nseg % 2) + 1]
                b_r, b_i = br[nseg % 2], bi_[nseg % 2]
                c.mm(pr.ap[:, 0:n], BT.ap[0:32, dg * 2, :], u_.ap[0:32, t0:t0 + n], True, True, [BT.r, u_.r], [pr.r])
                c.mm(pi.ap[:, 0:n], BT.ap[0:32, dg * 2 + 1, :], u_.ap[0:32, t0:t0 + n], True, True, [BT.r, u_.r], [pi.r])
                srcr = pr.ap[:, 0:n][:, ::-1] if rev else pr.ap[:, 0:n]
                srci = pi.ap[:, 0:n][:, ::-1] if rev else pi.ap[:, 0:n]
                c.act(b_r.ap[:, 0:n], srcr, AF.Identity, [pr.r], [b_r.r])
                c.act(b_i.ap[:, 0:n], srci, AF.Identity, [pi.r], [b_i.r])
                cN, sN = ct.ap[:, 0:n], st_.ap[:, 0:n]
                c.tt(d1.ap[:, 0:n], cN, b_r.ap[:, 0:n], ALU.mult, [ct.r, b_r.r], [d1.r])
                c.tt(d2.ap[:, 0:n], sN, b_i.ap[:, 0:n], ALU.mult, [st_.r, b_i.r], [d2.r], eng=POOL)
                c.tt(zR.ap[:, 0:n], d1.ap[:, 0:n], d2.ap[:, 0:n], ALU.add, [d1.r, d2.r], [zR.r])
                c.tt(p1.ap[:, 0:n], cN, b_i.ap[:, 0:n], ALU.mult, [ct.r, b_i.r], [p1.r], eng=POOL)
                c.tt(p2.ap[:, 0:n], sN, b_r.ap[:, 0:n], ALU.mult, [st_.r, b_r.r], [p2.r], eng=POOL)
                c.tt(zI.ap[:, 0:n], p1.ap[:, 0:n], p2.ap[:, 0:n], ALU.subtract, [p1.r, p2.r], [zI.r], eng=POOL)
                rb = rcol.to_broadcast([128, n])
                for (w_, z_, k) in ((wR, zR, 0), (wI, zI, 1)):
                    init = 0.0 if first else ini[k].ap[:, 0:1]
                    rr = [r_.r, z_.r] + ([] if first else [ini[k].r])
                    c.S.op(DVE, (lambda o, z, i0, b: lambda e: e.tensor_tensor_scan(out=o, data0=b, data1=z, initial=i0,
                                                                                   op0=ALU.mult, op1=ALU.add))(w_.ap[:, 0:n], z_.ap[:, 0:n], init, rb),
                           reads=rr, writes=[w_.r])
                oR = sR_.ap[:, t0:t0 + n][:, ::-1] if rev else sR_.ap[:, t0:t0 + n]
                oI = sI_.ap[:, t0:t0 + n][:, ::-1] if rev else sI_.ap[:, t0:t0 + n]
                c.tt(m1.ap[:, 0:n], cN, wR.ap[:, 0:n], ALU.mult, [ct.r, wR.r], [m1.r], eng=POOL)
                c.tt(m2.ap[:, 0:n], sN, wI.ap[:, 0:n], ALU.mult, [st_.r, wI.r], [m2.r])
                c.tt(oR, m1.ap[:, 0:n], m2.ap[:, 0:n], ALU.subtract, [m1.r, m2.r], [sR_.r])
                c.tt(m3.ap[:, 0:n], cN, wI.ap[:, 0:n], ALU.mult, [ct.r, wI.r], [m3.r], eng=POOL)
                c.tt(m4.ap[:, 0:n], sN, wR.ap[:, 0:n], ALU.mult, [st_.r, wR.r], [m4.r], eng=POOL)
                c.tt(oI, m3.ap[:, 0:n], m4.ap[:, 0:n], ALU.add, [m3.r, m4.r], [sI_.r], eng=POOL)
                cl, sl = ct.ap[:, n - 1:n], st_.ap[:, n - 1:n]
                wRl, wIl = wR.ap[:, n - 1:n], wI.ap[:, n - 1:n]
                c.tt(tin.ap, sl, wIl, ALU.mult, [st_.r, wI.r], [tin.r])
                c.stt(ini[0].ap, wRl, cl, tin.ap, ALU.mult, ALU.subtract, [wR.r, ct.r, tin.r], [ini[0].r])
                c.tt(tin.ap, sl, wRl, ALU.mult, [st_.r, wR.r], [tin.r])
                c.stt(ini[1].ap, wIl, cl, tin.ap, ALU.mult, ALU.add, [wI.r, ct.r, tin.r], [ini[1].r])
                first = False
                nseg += 1
        y_ = ysb[gp % 2]
        for bi, (t0, n) in enumerate(TB):
            po = c.psum[4 + bi % 2]
            k = 0
            for dr in range(2):
                dg = dr * 32 + gp
                for ri in range(2):
                    sb_ = sbuf[gp % 2][dr][ri]
                    c.mm(po.ap[0:32, 0:n], CB.ap[:, dg, ri, :], sb_.ap[:, t0:t0 + n], k == 0, k == 3, [CB.r, sb_.r], [po.r])
                    k += 1
            c.stt(y_.ap[0:32, t0:t0 + n], u_.ap[0:32, t0:t0 + n], dcol.ap[0:32, gp:gp + 1], po.ap[0:32, 0:n], ALU.mult, ALU.add,
                  [u_.r, dcol.r, po.r], [y_.r])
        c.dma(c.Yd[gp * 32:(gp + 1) * 32, :], y_.ap[0:32], [y_.r], [c.Yr])
    if modjob is not None:
        modjob.finish()
    c.stage_end()


C0G = math.sqrt(2.0 / math.pi)


def st_glu_wout(c, layer, with_ctx=True):
    blocks = TB if with_ctx else TB[1:]
    G = c.G[layer]
    gw = c.sb([8, 1024], BF16)
    for kt in range(8):
        c.dma(gw.ap[:, kt, :], c.d_gluw[kt * 128:(kt + 1) * 128, :], [], [gw.r], q=POOL)
    gb = c.sb([8], F32)
    c.dma(gb.ap, c.d_glub, [], [gb.r])
    wo = c.sb([16, 2048], BF16)
    for kt in range(16):
        c.dma(wo.ap[:, kt, :], c.d_wout[layer][kt * 128:(kt + 1) * 128, :], [], [wo.r], q=POOL)
    yb = [c.sb([8, 512], F32) for _ in range(1)]
    y2 = c.sb([8, 512], F32)
    sg = c.sb([8, 512], F32)
    gg = [c.sb([8, 512], BF16) for _ in range(2)]
    cat = [c.sb([16, 512], BF16) for _ in range(1)]
    catr = [[Reg() for _ in range(16)] for _ in range(1)]
    sgm = [c.sb([512], F32) for _ in range(2)]
    xt = [c.sb([512], F32) for _ in range(4)]
    nx = 0
    for bi, (t0, n) in enumerate(blocks):
        g = 1 if t0 < LC else 0
        y_, g_, ct, cr = yb[0], gg[bi % 2], cat[0], catr[0]
        c.dma(y_.ap[:, :, 0:n], c.Yd[:, t0:t0 + n].rearrange('(a p) t -> p a t', p=128), [c.Yr], [y_.r])
        c.dma(ct.ap[:, 8:16, 0:n], c.CATd[8:16, :, t0:t0 + n].rearrange('a p t -> p a t'), [c.CATr[1]], cr[8:16])
        yv = y_.ap[:, :, 0:n]
        c.act(y2.ap[:, :, 0:n], yv, AF.Square, [y_.r], [y2.r])
        c.ts(y2.ap[:, :, 0:n], y2.ap[:, :, 0:n], 0.044715, 1.0, ALU.mult, ALU.add, [y2.r], [y2.r])
        c.tt(y2.ap[:, :, 0:n], y2.ap[:, :, 0:n], yv, ALU.mult, [y2.r, y_.r], [y2.r])
        c.act(sg.ap[:, :, 0:n], y2.ap[:, :, 0:n], AF.Sigmoid, [y2.r], [sg.r], scale=2 * C0G)
        c.tt(g_.ap[:, :, 0:n], sg.ap[:, :, 0:n], yv, ALU.mult, [sg.r, y_.r], [g_.r])
        for ft in range(8):
            ps = c.psum[ft % 2]
            s_ = sgm[ft % 2]
            for kt in range(8):
                c.mm(ps.ap[:, 0:n], gw.ap[:, kt, ft * 128:(ft + 1) * 128], g_.ap[:, kt, 0:n], kt == 0, kt == 7, [gw.r, g_.r], [ps.r])
            c.act(s_.ap[:, 0:n], ps.ap[:, 0:n], AF.Sigmoid, [ps.r, gb.r], [s_.r], bias=gb.ap[:, ft:ft + 1])
            c.tt(ct.ap[:, ft, 0:n], s_.ap[:, 0:n], g_.ap[:, ft, 0:n], ALU.mult, [s_.r, g_.r], [cr[ft]])
        wout_block(c, layer, ct, cr, wo, xt, t0, n, g, nx)
        nx += 16
    c.stage_end()


def wout_block(c, layer, ct, cr, wo, xt, t0, n, g, nx):
    G = c.G[layer]
    PF = 3

    def issue_load(dt):
        x_ = xt[(nx + dt) % 4]
        c.dma(x_.ap[:, 0:n], c.XR[dt, :, t0:t0 + n], [c.XRr[dt][bidx(t0)]], [x_.r])
    for dt in range(PF):
        issue_load(dt)
    for dt in range(16):
        if dt + PF < 16:
            issue_load(dt + PF)
        po = c.psum[4 + (nx + dt) % 2]
        x_ = xt[(nx + dt) % 4]
        xreg = c.XRr[dt][bidx(t0)]
        for kt in range(16):
            c.mm(po.ap[:, 0:n], wo.ap[:, kt, dt * 128:(dt + 1) * 128], ct.ap[:, kt, 0:n], kt == 0, kt == 15, [wo.r, cr[kt]], [po.r])
        c.stt(x_.ap[:, 0:n], po.ap[:, 0:n], G.ap[:, g, 1, dt:dt + 1], x_.ap[:, 0:n], ALU.mult, ALU.add, [po.r, G.r, x_.r], [x_.r])
        c.dma(c.XR[dt, :, t0:t0 + n], x_.ap[:, 0:n], [x_.r], [xreg], q=ACT)


def conv3(c, raw, W, w3, b, out, tmp, silu, wr=()):
    wr = list(wr)
    c.act(tmp.ap[:, 0:W], raw.ap[:, 1:W + 1], AF.Identity, [raw.r] + wr, [tmp.r], scale=w3[:, 1:2], bias=b)
    c.stt(tmp.ap[:, 0:W], raw.ap[:, 0:W], w3[:, 0:1], tmp.ap[:, 0:W], ALU.mult, ALU.add, [raw.r, tmp.r] + wr, [tmp.r])
    if silu:
        c.stt(tmp.ap[:, 0:W], raw.ap[:, 2:W + 2], w3[:, 2:3], tmp.ap[:, 0:W], ALU.mult, ALU.add, [raw.r, tmp.r] + wr, [tmp.r])
        c.act(out[0], tmp.ap[:, 0:W], AF.Silu, [tmp.r], out[1])
    else:
        c.stt(out[0], raw.ap[:, 2:W + 2], w3[:, 2:3], tmp.ap[:, 0:W], ALU.mult, ALU.add, [raw.r, tmp.r] + wr, out[1])


def st_odd_proj(c, H):
    wsrc = c.d_odwin_t
    hw = c.sb([24, 3], F32); hb = c.sb([24], F32); sw = c.sb([16, 3], F32); sbias = c.sb([16], F32)
    for t_, s_ in ((hw, c.d_hysw), (hb, c.d_hysb), (sw, c.d_ssdcw), (sbias, c.d_ssdcb)):
        c.dma(t_.ap, s_, [], [t_.r])
    wb = [c.sb([16, 128], BF16) for _ in range(3)]
    raw = [c.sb([T + 8], F32) for _ in range(2)]
    tmp = [c.sb([T], F32) for _ in range(2)]
    ob = [c.sb([T], BF16) for _ in range(2)]
    for r_ in raw:
        c.memset(r_.ap, 0.0, [r_.r])
    cnt = 0
    for f in range(48):
        w_ = wb[f % 3]
        c.dma(w_.ap, wsrc[f], [], [w_.r], q=POOL)
        lat_only = f < 32
        blocks = TB[1:] if lat_only else TB
        r_, t_, o_ = raw[f % 2], tmp[f % 2], ob[f % 2]
        for (t0, n) in blocks:
            ps = c.psum[cnt % 4]
            for kt in range(16):
                c.mm(ps.ap[:, 0:n], w_.ap[:, kt, :], H.ap[:, kt, t0:t0 + n], kt == 0, kt == 15, [w_.r, H.r], [ps.r])
            off = 1 + t0 if t0 < LC else 3 + t0
            evac(c, cnt, r_.ap[:, off:off + n], ps.ap[:, 0:n], [ps.r], [r_.r])
            cnt += 1
        rl = Tl(r_.ap[:, 258:258 + L + 2], r_.r)
        if f < 24:
            conv3(c, rl, L, hw.ap[:, f, :], hb.ap[:, f:f + 1], (o_.ap[:, 0:L], [o_.r]), t_, False, [hw.r, hb.r])
            c.dma(c.HYd[f], o_.ap[:, 0:L], [o_.r], [c.HYr])
        elif f < 32:
            c.act(o_.ap[:, 0:L], r_.ap[:, 259:259 + L], AF.Silu, [r_.r], [o_.r])
            c.dma(c.Zd[f - 24], o_.ap[:, 0:L], [o_.r], [c.Zr])
        else:
            a = f - 32
            rc = Tl(r_.ap[:, 0:LC + 2], r_.r)
            conv3(c, rc, LC, sw.ap[:, a, :], sbias.ap[:, a:a + 1], (o_.ap[:, 0:LC], [o_.r]), t_, True, [sw.r, sbias.r])
            conv3(c, rl, L, sw.ap[:, a, :], sbias.ap[:, a:a + 1], (o_.ap[:, LC:T], [o_.r]), t_, True, [sw.r, sbias.r])
            c.dma(c.XBCd[a], o_.ap, [o_.r], [c.XBCr])
    wdt = c.sb([2, 16, 16], BF16)
    c.dma(wdt.ap, c.d_odwdt, [], [wdt.r], q=POOL)
    dtb = c.sb([2], F32); alog = c.sb([2], F32); nA = c.sb([2], F32)
    c.dma(dtb.ap[0:16], c.d_dtbias, [], [dtb.r])
    c.dma(alog.ap[0:16], c.d_alog, [], [alog.r])
    c.act(nA.ap[0:16], alog.ap[0:16], AF.Exp, [alog.r], [nA.r])
    c.ts(nA.ap[0:16], nA.ap[0:16], -1.0, None, ALU.mult, None, [nA.r], [nA.r])
    one = c.sb([1], F32)
    c.memset(one.ap, 1.0, [one.r])
    dtT = [c.sb([T], F32) for _ in range(2)]
    aT = c.sb([T], F32)
    cumT = [c.sb([T], F32) for _ in range(2)]
    et = c.sb([512], F32)
    ini = c.sb([1], F32)
    for k in range(2):
        for (t0, n) in TB:
            ps = c.psum[4 + cnt % 2]
            cnt += 1
            for kt in range(16):
                c.mm(ps.ap[0:16, 0:n], wdt.ap[:, k, kt, :], H.ap[:, kt, t0:t0 + n], kt == 0, kt == 15, [wdt.r, H.r], [ps.r])
            c.act(et.ap[0:16, 0:n], ps.ap[0:16, 0:n], AF.Exp, [ps.r, dtb.r], [et.r], bias=dtb.ap[0:16, k:k + 1])
            c.act(dtT[k].ap[0:16, t0:t0 + n], et.ap[0:16, 0:n], AF.Ln, [et.r, one.r], [dtT[k].r], bias=one.ap[0:16, 0:1])
        c.ts(aT.ap[0:16], dtT[k].ap[0:16], nA.ap[0:16, k:k + 1], None, ALU.mult, None, [dtT[k].r, nA.r], [aT.r])
        segs = list(TB) if k == 0 else [TB[0]] + TB[:0:-1]
        first = True
        for (t0, n) in segs:
            src = aT.ap[0:16, t0:t0 + n]
            dst = cumT[k].ap[0:16, t0:t0 + n]
            if k == 1:
                src, dst = src[:, ::-1], dst[:, ::-1]
            init = 0.0 if first else ini.ap[0:16, 0:1]
            ob_ = one.ap[0:16, 0:1].to_broadcast([16, n])
            c.S.op(DVE, (lambda o, z, i0, b: lambda e: e.tensor_tensor_scan(out=o, data0=b, data1=z, initial=i0, op0=ALU.mult, op1=ALU.add))(dst, src, init, ob_),
                   reads=[aT.r, one.r] + ([] if first else [ini.r]), writes=[cumT[k].r])
            last = t0 + n - 1 if k == 0 else t0
            c.copy(ini.ap[0:16], cumT[k].ap[0:16, last:last + 1], [cumT[k].r], [ini.r])
            first = False
        c.dma(c.DTd[k], dtT[k].ap[0:16], [dtT[k].r], [c.DTr])
        c.dma(c.CUMd[k], cumT[k].ap[0:16], [cumT[k].r], [c.CUMr])
    c.stage_end()


def st_ssd(c):
    Bt = c.sb([4, T], BF16); Ct = c.sb([4, T], BF16)
    c.dma(Bt.ap, c.XBCd[8:12].rearrange('a p t -> p a t'), [c.XBCr], [Bt.r])
    c.dma(Ct.ap, c.XBCd[12:16].rearrange('a p t -> p a t'), [c.XBCr], [Ct.r])
    xtok = c.sb([18, 1024], BF16)
    xdt = [c.sb([18, 1024], BF16) for _ in range(2)]
    cumT = [c.sb([T], F32) for _ in range(2)]
    ccol = [c.sb([18, 16], F32) for _ in range(2)]
    ncol = [c.sb([18, 16], F32) for _ in range(2)]
    mark = c.aoff
    xs = [c.sb([T], BF16) for _ in range(2)]
    nps = 0
    import os
    CUT = float(os.environ.get('SSD_CUT', '99'))
    for a in range(8):
        x_ = xs[a % 2]
        c.dma(x_.ap, c.XBCd[a], [c.XBCr], [x_.r])
        for j4 in range(0, 18, 4):
            nj = min(4, 18 - j4)
            ps = c.psum[6 + nps % 2]
            nps += 1
            pb = ps.ap.bitcast(BF16)
            for jj in range(nj):
                c.transpose(pb[:, jj * 128:(jj + 1) * 128], x_.ap[:, (j4 + jj) * 128:(j4 + jj + 1) * 128], c.ident_b.ap, [x_.r, c.ident_b.r], [ps.r])
            c.copy(xtok.ap[:, j4:j4 + nj, a * 128:(a + 1) * 128], pb[:, 0:nj * 128].rearrange('p (j q) -> p j q', q=128), [ps.r], [xtok.r])
    if CUT <= 1:
        c.stage_end()
        return
    dtT = c.sb([T], F32)
    dtk = c.sb([18, 16], F32)
    for k in range(2):
        c.memset(cumT[k].ap[0:32], 0.0, [cumT[k].r])
    c.memset(dtT.ap[0:32], 0.0, [dtT.r])
    for k in range(2):
        c.dma(cumT[k].ap[0:16], c.CUMd[k], [c.CUMr], [cumT[k].r])
        c.dma(dtT.ap[0:16], c.DTd[k], [c.DTr], [dtT.r])
        if CUT <= 1.2:
            continue
        for (srcT, kind) in ((cumT[k], 0), (dtT, 1)):
            ps = c.psum[4 + nps % 2]
            nps += 1
            for j in range(18):
                c.mm(ps.ap[:, j * 16:(j + 1) * 16], srcT.ap[0:32, j * 128:(j + 1) * 128], c.ident_f.ap[0:32, 0:16], True, True, [srcT.r, c.ident_f.r], [ps.r])
            v = ps.ap[:, 0:288].rearrange('p (j h) -> p j h', h=16)
            if CUT <= 1.25:
                continue
            if kind == 0:
                c.copy(ccol[k].ap, v, [ps.r], [ccol[k].r])
                if CUT > 1.3:
                    c.act(ncol[k].ap, v, AF.Identity, [ps.r], [ncol[k].r], scale=-1.0)
            else:
                c.copy(dtk.ap, v, [ps.r], [dtk.r])
        if CUT <= 1.4:
            continue
        for j in range(18):
            c.tt(xdt[k].ap[:, j, :].rearrange('p (h q) -> p h q', q=64), xtok.ap[:, j, :].rearrange('p (h q) -> p h q', q=64),
                 dtk.ap[:, j, :].to_broadcast([128, 16, 64]), ALU.mult, [xtok.r, dtk.r], [xdt[k].r])
    if CUT <= 2:
        c.stage_end()
        return
    c.S.flush(barrier=True)
    c.aoff = mark
    sel = c.sb([16, 128], F32)
    c.memset(sel.ap[0:32], 0.0, [sel.r])
    c.copy(sel.ap[0:16], c.ident_f.ap[0:16, 0:16].to_broadcast([16, 16, 128]), [c.ident_f.r], [sel.r])
    dcol = c.sb([16], F32)
    c.dma(dcol.ap, c.d_ssdd, [], [dcol.r])
    dI = c.sb([16, 128], BF16)
    for hd in range(16):
        c.ts(dI.ap[:, hd, :], c.ident_f.ap, dcol.ap[:, hd:hd + 1], None, ALU.mult, None, [c.ident_f.r, dcol.r], [dI.r])
    tri = [c.sb([128], F32) for _ in range(2)]
    c.dma(tri[0].ap, c.d_tri[0], [], [tri[0].r])
    c.dma(tri[1].ap, c.d_tri[1], [], [tri[1].r])
    Eb = [c.sb([128], F32) for _ in range(8)]
    Mb = [c.sb([128], BF16) for _ in range(8)]
    Db = c.sb([128], F32)
    ysb = [c.sb([4, 128], F32) for _ in range(2)]
    it = 0
    npc = 0
    if CUT <= 3:
        c.stage_end()
        return
    for g in range(4 if CUT > 4 else 1):
        for i in range(16 if CUT > 4 else 1):
            q0 = LC + i * 128
            accs = [c.psum[e] for e in range(4)]
            R = [c.psum[4], c.psum[5]]
            for d in range(2):
                for e in range(4):
                    c.mm(R[d].ap[:, e * 128:(e + 1) * 128], sel.ap[0:32, g * 4 + e, :], cumT[d].ap[0:32, q0:q0 + 128], True, True,
                         [sel.r, cumT[d].r], [R[d].r])
            started = [False] * 4
            units = []
            for d in range(2):
                srcs = [0, 1] + ([2 + j for j in range(i + 1)] if d == 0 else [2 + j for j in range(i, 16)])
                units += [(d, jt) for jt in srcs]

            def emit_cb(u):
                d_, jt_ = units[u]
                pc_ = c.psum[6 + (npc + u) % 2]
                c.mm(pc_.ap[:, 0:128], Bt.ap[:, g, jt_ * 128:(jt_ + 1) * 128], Ct.ap[:, g, q0:q0 + 128], True, True, [Bt.r, Ct.r], [pc_.r])
            emit_cb(0)
            for u, (d, jt) in enumerate(units):
                if u + 1 < len(units):
                    emit_cb(u + 1)
                pc = c.psum[6 + (npc + u) % 2]
                diag = jt == 2 + i
                for e in range(4):
                    hd = g * 4 + e
                    E_, M_ = Eb[((npc + u) * 4 + e) % 8], Mb[((npc + u) * 4 + e) % 8]
                    Rv = R[d].ap[:, e * 128:(e + 1) * 128]
                    if not diag:
                        c.act(E_.ap, Rv, AF.Exp, [R[d].r, ncol[d].r], [E_.r], bias=ncol[d].ap[:, jt, hd:hd + 1])
                    else:
                        c.ts(Db.ap, Rv, ccol[d].ap[:, jt, hd:hd + 1], 0.0, ALU.subtract, ALU.min, [R[d].r, ccol[d].r], [Db.r])
                        c.act(E_.ap, Db.ap, AF.Exp, [Db.r], [E_.r])
                        c.tt(E_.ap, E_.ap, tri[d].ap, ALU.mult, [E_.r, tri[d].r], [E_.r])
                    c.tt(M_.ap, E_.ap, pc.ap[:, 0:128], ALU.mult, [E_.r, pc.r], [M_.r])
                    c.mm(accs[e].ap[0:64, 0:128], xdt[d].ap[:, jt, hd * 64:(hd + 1) * 64], M_.ap, not started[e], False,
                         [xdt[d].r, M_.r], [accs[e].r])
                    started[e] = True
            npc += len(units)
            for e in range(4):
                hd = g * 4 + e
                c.mm(accs[e].ap[0:64, 0:128], xtok.ap[:, 2 + i, hd * 64:(hd + 1) * 64], dI.ap[:, hd, :], False, True,
                     [xtok.r, dI.r], [accs[e].r])
            y_ = ysb[it % 2]
            for e in range(4):
                evac(c, e, y_.ap[0:64, e, :], accs[e].ap[0:64, 0:128], [accs[e].r], [y_.r])
            c.dma(c.YSd[g * 4:(g + 1) * 4, :, i * 128:(i + 1) * 128].rearrange('e p l -> p e l'), y_.ap[0:64], [y_.r], [c.YSr])
            it += 1
    c.stage_end()


def st_hyena(c):
    TWO_PI = 2 * math.pi
    hwo = c.sb([4096], F32); c.dma(hwo.ap[0:64], c.d_hywout, [], [hwo.r])
    hid0 = c.sb([L], F32)
    mark = c.aoff
    zT = c.sb([L], F32); c.memset(zT.ap[0:64], 0.0, [zT.r]); c.dma(zT.ap[0:33], c.d_hyz, [], [zT.r])
    w1 = c.sb([64], F32); c.memset(w1.ap[0:64], 0.0, [w1.r]); c.dma(w1.ap[0:33], c.d_hywin, [], [w1.r])
    wm = c.sb([2, 64], F32); c.dma(wm.ap[0:64], c.d_hywmid, [], [wm.r])
    fb_ = c.sb([4], F32); c.dma(fb_.ap[0:64], c.d_hyfb, [], [fb_.r])
    fq = c.sb([1], F32); bq = c.sb([3], F32)
    c.ts(fq.ap[0:64], fb_.ap[0:64, 0:1], 1.0 / TWO_PI, None, ALU.mult, None, [fb_.r], [fq.r])
    c.ts(bq.ap[0:64], fb_.ap[0:64, 1:4], fq.ap[0:64, 0:1], None, ALU.mult, None, [fb_.r, fq.r], [bq.r])
    hid = [hid0, c.sb([L], F32)]
    tA, tB = c.sb([512], F32), c.sb([512], F32)
    src = zT
    for l in range(3):
        dst = hid[l % 2]
        for b4 in range(4):
            ps = c.psum[b4 % 2]
            if l == 0:
                c.mm(ps.ap[0:64, :], w1.ap[0:64, :], zT.ap[0:64, b4 * 512:(b4 + 1) * 512], True, True, [w1.r, zT.r], [ps.r])
            else:
                c.mm(ps.ap[0:64, :], wm.ap[0:64, l - 1, :], src.ap[0:64, b4 * 512:(b4 + 1) * 512], True, True, [wm.r, src.r], [ps.r])
            c.ts(tA.ap[0:64], ps.ap[0:64, :], fq.ap[0:64, 0:1], bq.ap[0:64, l:l + 1], ALU.mult, ALU.add, [ps.r, fq.r, bq.r], [tA.r])
            rnd(c, tB.ap[0:64], tA.ap[0:64], [tA.r], [tB.r])
            c.tt(tA.ap[0:64], tA.ap[0:64], tB.ap[0:64], ALU.subtract, [tA.r, tB.r], [tA.r])
            c.act(dst.ap[0:64, b4 * 512:(b4 + 1) * 512], tA.ap[0:64], AF.Sin, [tA.r], [dst.r], scale=TWO_PI)
        src = dst
    h3 = src
    c.S.flush(barrier=True)
    c.aoff = mark
    CB_ = 256
    dec = c.sb([16, CB_], F32)
    fbt = c.sb([2, CB_], F32)
    Hs = c.sb([16, CB_], BF16); Hd = c.sb([16, CB_], BF16)
    Kre = c.sb([16, CB_], F32); Kim = c.sb([16, CB_], F32)
    Yre = c.sb([16, CB_], BF16); Yim = c.sb([16, CB_], BF16)
    tok = {k: c.sb([16, CB_], BF16) for k in ('x1', 'x2', 'v', 'z1', 'o')}
    tabs = [c.sb([16, 128], BF16) for _ in range(4)]
    tmpf = [c.sb([CB_], F32) for _ in range(6)]
    chm = [c.sb([L], BF16) for _ in range(2)]
    nt = 0
    npz = 0
    for cb in range(4):
        c.dma(dec.ap, c.d_decay[:, cb * CB_:(cb + 1) * CB_].rearrange('(tt p) q -> p tt q', p=128), [], [dec.r])
        c.dma(fbt.ap, c.d_hyfbias[:, :, cb * CB_:(cb + 1) * CB_], [], [fbt.r])
        for ki, key in enumerate(('x1', 'x2', 'v')):
            for ct in range(2):
                ch_ = chm[npz % 2]
                c.dma(ch_.ap, c.HYd[ki * 8 + cb * 2 + ct], [c.HYr], [ch_.r])
                for t4 in range(4):
                    ps = c.psum[6 + npz % 2]
                    npz += 1
                    pb = ps.ap.bitcast(BF16)
                    for jj in range(4):
                        tt_ = t4 * 4 + jj
                        c.transpose(pb[:, jj * 128:(jj + 1) * 128], ch_.ap[:, tt_ * 128:(tt_ + 1) * 128], c.ident_b.ap, [ch_.r, c.ident_b.r], [ps.r])
                    c.copy(tok[key].ap[:, t4 * 4:(t4 + 1) * 4, ct * 128:(ct + 1) * 128], pb[:, 0:512].rearrange('p (j q) -> p j q', q=128), [ps.r], [tok[key].r])
        for o in range(2):
            tin = tok['v'] if o == 0 else tok['z1']
            gate = tok['x1'] if o == 0 else tok['x2']
            tout = tok['z1'] if o == 0 else tok['o']
            for tt_ in range(16):
                pf, pb_ = c.psum[0], c.psum[1]
                for dr, p_ in ((0, pf), (1, pb_)):
                    col0 = o * 2048 + dr * 1024 + cb * CB_
                    c.mm(p_.ap[:, 0:CB_], h3.ap[0:64, tt_ * 128:(tt_ + 1) * 128], hwo.ap[0:64, col0:col0 + CB_], True, True, [h3.r, hwo.r], [p_.r])
                f_, b_ = tmpf[0], tmpf[1]
                c.tt(f_.ap, pf.ap[:, 0:CB_], dec.ap[:, tt_, :], ALU.mult, [pf.r, dec.r], [f_.r])
                c.tt(b_.ap, pb_.ap[:, 0:CB_], dec.ap[:, tt_, :], ALU.mult, [pb_.r, dec.r], [b_.r])
                if tt_ == 0:
                    c.memset(b_.ap[0:1, :], 0.0, [b_.r])
                c.tt(Hs.ap[:, tt_, :], f_.ap, b_.ap, ALU.add, [f_.r, b_.r], [Hs.r])
                c.tt(Hd.ap[:, tt_, :], f_.ap, b_.ap, ALU.subtract, [f_.r, b_.r], [Hd.r])
            for ft in range(16):
                tc_, ts_ = tabs[nt % 4], tabs[(nt + 1) % 4]
                nt += 2
                c.dma(tc_.ap, c.d_Cf[ft], [], [tc_.r])
                c.dma(ts_.ap, c.d_Sf[ft], [], [ts_.r])
                pr, pi = c.psum[0], c.psum[1]
                for tt_ in range(16):
                    c.mm(pr.ap[:, 0:CB_], tc_.ap[:, tt_, :], Hs.ap[:, tt_, :], tt_ == 0, tt_ == 15, [tc_.r, Hs.r], [pr.r])
                for tt_ in range(16):
                    c.mm(pi.ap[:, 0:CB_], ts_.ap[:, tt_, :], Hd.ap[:, tt_, :], tt_ == 0, tt_ == 15, [ts_.r, Hd.r], [pi.r])
                c.act(Kre.ap[:, ft, :], pr.ap[:, 0:CB_], AF.Identity, [pr.r], [Kre.r])
                c.act(Kim.ap[:, ft, :], pi.ap[:, 0:CB_], AF.Identity, [pi.r], [Kim.r])
                if ft == 0:
                    pn = c.psum[2]
                    for tt_ in range(16):
                        c.mm(pn.ap[:, 0:CB_], ts_.ap[:, tt_, :], Hs.ap[:, tt_, :], tt_ == 0, tt_ == 15, [ts_.r, Hs.r], [pn.r])
                    c.act(Kim.ap[0:1, 0, :], pn.ap[0:1, 0:CB_], AF.Identity, [pn.r], [Kim.r])
                vr, vi = c.psum[3], c.psum[4]
                for tt_ in range(16):
                    c.mm(vr.ap[:, 0:CB_], tc_.ap[:, tt_, :], tin.ap[:, tt_, :], tt_ == 0, tt_ == 15, [tc_.r, tin.r], [vr.r])
                for tt_ in range(16):
                    c.mm(vi.ap[:, 0:CB_], ts_.ap[:, tt_, :], tin.ap[:, tt_, :], tt_ == 0, tt_ == 15, [ts_.r, tin.r], [vi.r])
                a1, a2, a3, a4 = tmpf[2:6]
                c.tt(a1.ap, vr.ap[:, 0:CB_], Kre.ap[:, ft, :], ALU.mult, [vr.r, Kre.r], [a1.r])
                c.tt(a2.ap, vi.ap[:, 0:CB_], Kim.ap[:, ft, :], ALU.mult, [vi.r, Kim.r], [a2.r])
                c.tt(a3.ap, vr.ap[:, 0:CB_], Kim.ap[:, ft, :], ALU.mult, [vr.r, Kim.r], [a3.r])
                c.tt(a4.ap, vi.ap[:, 0:CB_], Kre.ap[:, ft, :], ALU.mult, [vi.r, Kre.r], [a4.r])
                c.tt(Yre.ap[:, ft, :], a1.ap, a2.ap, ALU.subtract, [a1.r, a2.r], [Yre.r])
                c.tt(Yim.ap[:, ft, :], a3.ap, a4.ap, ALU.add, [a3.r, a4.r], [Yim.r])
                if ft == 0:
                    c.copy(Yre.ap[0:1, 0, :], a1.ap[0:1, :], [a1.r], [Yre.r])
                    c.copy(Yim.ap[0:1, 0, :], a2.ap[0:1, :], [a2.r], [Yim.r])
            for tt_ in range(16):
                tc_, ts_ = tabs[nt % 4], tabs[(nt + 1) % 4]
                nt += 2
                c.dma(tc_.ap, c.d_Ci[tt_], [], [tc_.r])
                c.dma(ts_.ap, c.d_Si[tt_], [], [ts_.r])
                py = c.psum[5]
                for ft in range(16):
                    c.mm(py.ap[:, 0:CB_], tc_.ap[:, ft, :], Yre.ap[:, ft, :], ft == 0, False, [tc_.r, Yre.r], [py.r])
                for ft in range(16):
                    c.mm(py.ap[:, 0:CB_], ts_.ap[:, ft, :], Yim.ap[:, ft, :], False, ft == 15, [ts_.r, Yim.r], [py.r])
                e1 = tmpf[0]
                c.tt(e1.ap, tin.ap[:, tt_, :], fbt.ap[:, o, :], ALU.mult, [tin.r, fbt.r], [e1.r])
                c.tt(e1.ap, e1.ap, py.ap[:, 0:CB_], ALU.add, [e1.r, py.r], [e1.r])
                c.tt(tout.ap[:, tt_, :], e1.ap, gate.ap[:, tt_, :], ALU.mult, [e1.r, gate.r], [tout.r])
        for ct in range(2):
            ch_ = chm[npz % 2]
            for t4 in range(4):
                ps = c.psum[6 + npz % 2]
                npz += 1
                pb = ps.ap.bitcast(BF16)
                for jj in range(4):
                    tt_ = t4 * 4 + jj
                    c.transpose(pb[:, jj * 128:(jj + 1) * 128], tok['o'].ap[:, tt_, ct * 128:(ct + 1) * 128], c.ident_b.ap, [tok['o'].r, c.ident_b.r], [ps.r])
                c.copy(ch_.ap[:, t4 * 512:(t4 + 1) * 512], pb[:, 0:512], [ps.r], [ch_.r])
            c.dma(c.CATd[cb * 2 + ct, :, LC:T], ch_.ap, [ch_.r], [c.CATr[0]])
    c.stage_end()


def st_odd_out(c):
    wo = c.sb([16, 2048], BF16)
    for kt in range(16):
        c.dma(wo.ap[:, kt, :], c.d_wout[1][kt * 128:(kt + 1) * 128, :], [], [wo.r], q=POOL)
    ng = c.sb([8], F32); c.dma(ng.ap, c.d_ssdng, [], [ng.r])
    yb = c.sb([8, 512], F32); zb = c.sb([8, 512], BF16); sq = c.sb([8, 512], BF16)
    rs = c.sb([512], F32)
    cat = c.sb([16, 512], BF16)
    cr = [Reg() for _ in range(16)]
    xt = [c.sb([512], F32) for _ in range(4)]
    nx = 0
    for (t0, n) in TB[1:]:
        l0 = t0 - LC
        c.dma(yb.ap, c.YSd[:, :, l0:l0 + n].rearrange('h q t -> (h q) t').rearrange('(a p) t -> p a t', p=128), [c.YSr], [yb.r])
        c.dma(zb.ap, c.Zd[:, :, l0:l0 + n].rearrange('a p t -> p a t'), [c.Zr], [zb.r])
        c.dma(cat.ap[:, 0:8, :], c.CATd[0:8, :, t0:t0 + n].rearrange('a p t -> p a t'), [c.CATr[0]], cr[0:8])
        c.tt(yb.ap, yb.ap, zb.ap, ALU.mult, [yb.r, zb.r], [yb.r])
        c.act(sq.ap, yb.ap, AF.Square, [yb.r], [sq.r])
        ps = c.psum[0]
        for a in range(8):
            c.mm(ps.ap, c.ones_bf.ap, sq.ap[:, a, :], a == 0, a == 7, [c.ones_bf.r, sq.r], [ps.r])
        c.act(rs.ap, ps.ap, AF.Sqrt, [ps.r, c.eps_t.r], [rs.r], scale=1.0 / 1024, bias=c.eps_t.ap[:, 0:1])
        c.recip(rs.ap, rs.ap, [rs.r], [rs.r])
        for a in range(8):
            c.stt(cat.ap[:, 8 + a, :], yb.ap[:, a, :], ng.ap[:, a:a + 1], rs.ap, ALU.mult, ALU.mult, [yb.r, ng.r, rs.r], [cr[8 + a]])
        wout_block(c, 1, cat, cr, wo, xt, t0, n, 0, nx)
        nx += 16
    c.stage_end()
```

```python
import math
import numpy as np
import ml_dtypes
import concourse.bass as bass
import concourse.mybir as mybir
from concourse.bass_utils import run_bass_kernel_spmd

F32 = mybir.dt.float32
BF16 = mybir.dt.bfloat16
ALU = mybir.AluOpType
AF = mybir.ActivationFunctionType
AX = mybir.AxisListType

PE, DVE, ACT, POOL, SP = 0, 1, 2, 3, 4
NENG = 5
NDSEM = 12

D = 2048
T = 2304
LC = 256
L = 2048
DFF = 5632
NFT = DFF // 128
EPS = 1e-6
TB = [(0, 256), (256, 512), (768, 512), (1280, 512), (1792, 512)]
ARENA_BYTES = 196608
MAGIC = 12582912.0


class Reg:
    __slots__ = ('w', 'rs', 'excl')

    def __init__(self, excl=False):
        self.w = None
        self.rs = []
        self.excl = excl


class Op:
    __slots__ = ('eng', 'fn', 'deps', 'signal', 'sig', 'clock', 'dma', 'dsem', 'dval', 'waits')

    def __init__(self, eng, fn, dma):
        self.eng = eng
        self.fn = fn
        self.dma = dma
        self.deps = []
        self.signal = False
        self.sig = 0
        self.clock = None
        self.dsem = -1
        self.dval = 0
        self.waits = None


class Sched:
    def __init__(self, nc):
        self.nc = nc
        self.engs = [nc.tensor, nc.vector, nc.scalar, nc.gpsimd, nc.sync]
        self.esem = [nc.alloc_semaphore('es%d' % i) for i in range(NENG)]
        self.dsem = [[nc.alloc_semaphore('ds%d_%d' % (q, i)) for i in range(NDSEM)] for q in range(NENG)]
        self.ncomp = NENG + NENG * NDSEM
        self.pending = []
        self.sigcnt = [0] * NENG
        self.dcnt = [0] * NENG
        self.dlast = [[None] * NDSEM for _ in range(NENG)]
        self.clock = [[0] * self.ncomp for _ in range(NENG)]
        self.nops = 0
        self.nwaits = 0

    def op(self, eng, fn, reads=(), writes=(), dma=False):
        o = Op(eng, fn, dma)
        deps = o.deps
        for r in reads:
            if r.w is not None:
                deps.append((r.w, True))
            if r.excl:
                for x in r.rs:
                    if x.eng != eng:
                        deps.append((x, True))
            if dma:
                r.rs.append(o)
            else:
                rs = r.rs
                for i in range(len(rs)):
                    if (not rs[i].dma) and rs[i].eng == eng:
                        rs[i] = o
                        break
                else:
                    rs.append(o)
        for r in writes:
            if r.w is not None:
                deps.append((r.w, False))
            for x in r.rs:
                if x is not o:
                    deps.append((x, False))
            r.w = o
            r.rs = []
        self.pending.append(o)
        return o

    def flush(self, barrier=True):
        ops = self.pending
        self.pending = []
        for o in ops:
            for (d, raw) in o.deps:
                if d.dma:
                    continue
                if d.eng == o.eng and not o.dma and d.eng == PE:
                    continue
                d.signal = True
        if barrier:
            last = {}
            for o in ops:
                if not o.dma:
                    last[o.eng] = o
            for o in last.values():
                o.signal = True
        ncomp = self.ncomp
        for o in ops:
            e = o.eng
            ck = self.clock[e]
            waits = []
            if o.dma:
                i = self.dcnt[e]
                self.dcnt[e] += 1
                slot = i % NDSEM
                prev = self.dlast[e][slot]
                if prev is not None:
                    o.deps.append((prev, True))
                o.dsem = slot
                o.dval = 16 * (i // NDSEM + 1)
                self.dlast[e][slot] = o
            for (d, raw) in o.deps:
                if d.dma:
                    comp = NENG + d.eng * NDSEM + d.dsem
                    val = d.dval
                    sem = self.dsem[d.eng][d.dsem]
                else:
                    if d.eng == e and not o.dma and e == PE:
                        continue
                    if not d.signal:
                        continue
                    comp = d.eng
                    val = d.sig
                    sem = self.esem[d.eng]
                if ck[comp] >= val:
                    continue
                waits.append((sem, val))
                dc = d.clock
                if dc is not None:
                    for k in range(ncomp):
                        if dc[k] > ck[k]:
                            ck[k] = dc[k]
                if ck[comp] < val:
                    ck[comp] = val
            o.waits = waits
            if o.dma:
                o.clock = list(ck)
            elif o.signal:
                self.sigcnt[e] += 1
                o.sig = self.sigcnt[e]
                o.clock = list(ck)
            o.deps = None
        for o in ops:
            eng = self.engs[o.eng]
            for (sem, val) in o.waits:
                eng.wait_ge(sem, val)
                self.nwaits += 1
            ins = o.fn(eng)
            self.nops += 1
            if o.dma:
                ins.then_inc(self.dsem[o.eng][o.dsem], 16)
            elif o.signal:
                ins.then_inc(self.esem[o.eng], 1)
            o.fn = None
            o.waits = None
        if barrier:
            self.barrier()

    def barrier(self):
        for e in range(NENG):
            eng = self.engs[e]
            ck = self.clock[e]
            for f in range(NENG):
                if ck[f] < self.sigcnt[f]:
                    eng.wait_ge(self.esem[f], self.sigcnt[f])
                    ck[f] = self.sigcnt[f]
            for q in range(NENG):
                for s in range(NDSEM):
                    d = self.dlast[q][s]
                    if d is not None:
                        comp = NENG + q * NDSEM + s
                        if ck[comp] < d.dval:
                            eng.wait_ge(self.dsem[q][s], d.dval)
                            ck[comp] = d.dval


class Tl:
    __slots__ = ('ap', 'r')

    def __init__(self, ap, r=None):
        self.ap = ap
        self.r = r if r is not None else Reg()

    def __getitem__(self, k):
        return self.ap[k]


class Ctx:
    def __init__(self, dbg_in=(), dbg_out=()):
        self.nc = nc = bass.Bass("TRN2", target_bir_lowering=False)
        self.S = Sched(nc)
        self.dbg_in = set(dbg_in)
        self.dbg_out = set(dbg_out)
        self.arena = nc.alloc_sbuf_tensor('arena', [128, ARENA_BYTES // 4], F32)
        self.aoff = 0
        self.psum = [Tl(nc.alloc_psum_tensor('ps%d' % i, [128, 512], F32)[:], Reg(excl=True)) for i in range(8)]
        self.inputs = {}
        self.outputs = {}
        self.n_p = 0

    def dram(self, name, shape, dtype=F32, kind=None):
        if kind is None:
            kind = 'Internal'
            if name in self.dbg_in:
                kind = 'ExternalInput'
            elif name in self.dbg_out:
                kind = 'ExternalOutput'
        t = self.nc.dram_tensor(name, list(shape), dtype, kind=kind).ap()
        if kind == 'ExternalInput':
            self.inputs[name] = (tuple(shape), dtype)
        elif kind == 'ExternalOutput':
            self.outputs[name] = (tuple(shape), dtype)
        return t

    def sb(self, free_shape, dtype=F32):
        esz = 4 if dtype == F32 else 2
        n = 1
        for v in free_shape:
            n *= v
        nbytes = (n * esz + 31) // 32 * 32
        assert self.aoff + nbytes <= ARENA_BYTES, ('arena overflow', self.aoff, nbytes)
        a = self.arena[:, self.aoff // 4:(self.aoff + nbytes) // 4]
        self.aoff += nbytes
        if dtype != F32:
            a = a.bitcast(dtype)
        a = a[:, 0:n]
        if len(free_shape) == 2:
            a = a.rearrange('p (a b) -> p a b', b=free_shape[1])
        elif len(free_shape) == 3:
            a = a.rearrange('p (a b c) -> p a b c', b=free_shape[1], c=free_shape[2])
        return Tl(a)

    def persist(self, free_shape, dtype=F32):
        self.n_p += 1
        t = self.nc.alloc_sbuf_tensor('pp%d' % self.n_p, [128] + list(free_shape), dtype)
        return Tl(t[:])

    def stage_end(self):
        self.S.flush(barrier=True)
        self.aoff = 0

    def dma(self, out, in_, R, W, q=SP):
        self.S.op(q, lambda e: e.dma_start(out=out, in_=in_), reads=R, writes=W, dma=True)

    def mm(self, out, lhsT, rhs, start, stop, R, W):
        self.S.op(PE, lambda e: e.matmul(out, lhsT=lhsT, rhs=rhs, start=start, stop=stop), reads=R, writes=W)

    def act(self, out, in_, func, R, W, scale=1.0, bias=0.0, accum_out=None):
        if accum_out is None:
            self.S.op(ACT, lambda e: e.activation(out=out, in_=in_, func=func, bias=bias, scale=scale), reads=R, writes=W)
        else:
            self.S.op(ACT, lambda e: e.activation(out=out, in_=in_, func=func, bias=bias, scale=scale, accum_out=accum_out), reads=R, writes=W)

    def tt(self, out, in0, in1, op, R, W, eng=DVE):
        self.S.op(eng, lambda e: e.tensor_tensor(out=out, in0=in0, in1=in1, op=op), reads=R, writes=W)

    def ts(self, out, in0, s1, s2, op0, op1, R, W, eng=DVE):
        if s2 is None:
            self.S.op(eng, lambda e: e.tensor_scalar(out=out, in0=in0, scalar1=s1, scalar2=None, op0=op0), reads=R, writes=W)
        else:
            self.S.op(eng, lambda e: e.tensor_scalar(out=out, in0=in0, scalar1=s1, scalar2=s2, op0=op0, op1=op1), reads=R, writes=W)

    def stt(self, out, in0, scalar, in1, op0, op1, R, W, eng=DVE):
        self.S.op(eng, lambda e: e.scalar_tensor_tensor(out=out, in0=in0, scalar=scalar, in1=in1, op0=op0, op1=op1), reads=R, writes=W)

    def copy(self, out, in_, R, W, eng=DVE):
        self.S.op(eng, lambda e: e.tensor_copy(out=out, in_=in_), reads=R, writes=W)

    def memset(self, out, val, W, eng=DVE):
        self.S.op(eng, lambda e: e.memset(out, val), writes=W)

    def recip(self, out, in_, R, W):
        self.S.op(DVE, lambda e: e.reciprocal(out=out, in_=in_), reads=R, writes=W)

    def transpose(self, out, in_, ident, R, W):
        self.S.op(PE, lambda e: e.transpose(out, in_, ident), reads=R, writes=W)


def st_consts(c):
    c.ones_bf = c.persist([128], BF16)
    c.memset(c.ones_bf.ap, 1.0, [c.ones_bf.r])
    c.ident_f = c.persist([128], F32)
    c.memset(c.ident_f.ap, 0.0, [c.ident_f.r], eng=POOL)
    idf = c.ident_f
    c.S.op(POOL, lambda e: e.affine_select(out=idf.ap, in_=idf.ap, pattern=[[-1, 128]], compare_op=ALU.not_equal,
                                           fill=1.0, base=0, channel_multiplier=1), reads=[idf.r], writes=[idf.r])
    c.ident_b = c.persist([128], BF16)
    c.copy(c.ident_b.ap, c.ident_f.ap, [c.ident_f.r], [c.ident_b.r])
    c.eps_t = c.persist([1], F32)
    c.memset(c.eps_t.ap, EPS, [c.eps_t.r])


class ModJob:
    BW = 256

    def __init__(self, c, layer, bank):
        self.c, self.layer = c, layer
        self.M = c.persist([2, 9, 16], F32)
        self.A = c.persist([2, 3, 16], F32)
        self.G = c.persist([2, 3, 16], F32)
        c.M[layer], c.A[layer], c.G[layer] = self.M, self.A, self.G
        self.ps = c.psum[bank]
        self.nblk = (9 * D) // self.BW
        self.next = 0

    def prep(self):
        c, layer = self.c, self.layer
        self.sc = sc = c.sb([16, 2], F32)
        sg = c.sb([16, 2], F32)
        c.dma(sc.ap, c.d_cc, [], [sc.r])
        c.act(sg.ap, sc.ap, AF.Sigmoid, [sc.r], [sg.r])
        c.tt(sc.ap, sc.ap, sg.ap, ALU.mult, [sc.r, sg.r], [sc.r])
        self.mb = c.sb([144], F32)
        c.dma(self.mb.ap, c.d_modb[layer], [], [self.mb.r])
        self.ng = c.sb([3, 16], F32)
        c.dma(self.ng.ap, c.d_normg[layer], [], [self.ng.r])
        self.wbuf = [c.sb([16, self.BW], F32) for _ in range(2)]
        self.wsrc = c.d_modw[layer].rearrange('(kt p) f -> p kt f', p=128)

    def emit(self, nblocks):
        c, ps, sc, BW = self.c, self.ps, self.sc, self.BW
        for _ in range(nblocks):
            blk = self.next
            if blk >= self.nblk:
                return
            self.next += 1
            wb = self.wbuf[blk % 2]
            c.dma(wb.ap, self.wsrc[:, :, blk * BW:(blk + 1) * BW], [], [wb.r])
            for j in range(BW // 128):
                ft = blk * (BW // 128) + j
                for kt in range(16):
                    c.mm(ps.ap[:, 2 * ft:2 * ft + 2], wb.ap[:, kt, j * 128:(j + 1) * 128], sc.ap[:, kt, :],
                         kt == 0, kt == 15, [wb.r, sc.r], [ps.r])

    def finish(self):
        c, ps, M, A, G, mb, ng = self.c, self.ps, self.M, self.A, self.G, self.mb, self.ng
        self.emit(self.nblk)
        for g in range(2):
            src = ps.ap[:, 0:288].rearrange('p (f g) -> p g f', g=2)[:, g, :]
            c.tt(M.ap[:, g].rearrange('p i d -> p (i d)'), src, mb.ap, ALU.add, [ps.r, mb.r], [M.r])
        for g in range(2):
            for i in range(3):
                c.stt(A.ap[:, g, i, :], M.ap[:, g, 3 * i + 1, :], 1.0, ng.ap[:, i, :], ALU.add, ALU.mult, [M.r, ng.r], [A.r])
                fac = 1.0 if i == 1 else 0.5
                c.ts(G.ap[:, g, i, :], M.ap[:, g, 3 * i + 2, :], fac, None, ALU.mult, None, [M.r], [G.r])


def st_mod(c, layer):
    job = ModJob(c, layer, 0)
    job.prep()
    job.finish()
    c.stage_end()


def bidx(t0):
    return [b[0] for b in TB].index(t0)


def xr_col(c, t0):
    return [c.XRr[dt][bidx(t0)] for dt in range(16)]


def st_norm(c, layer, i, blocks=TB):
    H = c.sb([16, T], BF16)
    mark = c.aoff
    A, M = c.A[layer], c.M[layer]
    xt = [c.sb([16, 512], F32) for _ in range(2)]
    xrg = [[Reg() for _ in range(16)] for _ in range(2)]
    sq = [c.sb([16, 512], BF16) for _ in range(2)]
    rs = [c.sb([512], F32) for _ in range(2)]
    for bi, (t0, n) in enumerate(blocks):
        g = 1 if t0 < LC else 0
        x_, s_, r_ = xt[bi % 2], sq[bi % 2], rs[bi % 2]
        xr_ = xrg[bi % 2]
        ps = c.psum[6 + bi % 2]
        c.dma(x_.ap[:, :, 0:n], c.XR[:, :, t0:t0 + n].rearrange('d p t -> p d t'), xr_col(c, t0), xr_)
        c.act(s_.ap[:, :, 0:n], x_.ap[:, :, 0:n], AF.Square, xr_, [s_.r])
        for dt in range(16):
            c.mm(ps.ap[:, 0:n], c.ones_bf.ap, s_.ap[:, dt, 0:n], dt == 0, dt == 15, [s_.r, c.ones_bf.r], [ps.r])
        c.act(r_.ap[:, 0:n], ps.ap[:, 0:n], AF.Sqrt, [ps.r, c.eps_t.r], [r_.r], scale=1.0 / D, bias=c.eps_t.ap[:, 0:1])
        c.recip(r_.ap[:, 0:n], r_.ap[:, 0:n], [r_.r], [r_.r])
        for dt in range(16):
            c.stt(x_.ap[:, dt, 0:n], x_.ap[:, dt, 0:n], A.ap[:, g, i, dt:dt + 1], r_.ap[:, 0:n], ALU.mult, ALU.mult,
                  [xr_[dt], A.r, r_.r], [xr_[dt]])
            c.act(H.ap[:, dt, t0:t0 + n], x_.ap[:, dt, 0:n], AF.Identity, [xr_[dt], M.r], [H.r],
                  bias=M.ap[:, g, 3 * i, dt:dt + 1])
    c.S.flush(barrier=True)
    c.aoff = mark
    return H


FCH = [9, 9, 9, 9, 8]


def st_ffn(c, layer, j, i, blocks=TB):
    H = st_norm(c, layer, i, blocks)
    G = c.G[layer]
    wg_src = c.d_wg[layer, j]
    wu_src = c.d_wu[layer, j]
    wd_src = c.d_wd[layer, j].rearrange('(ft p) d -> ft p d', p=128)
    NW = 3
    wgb = [c.sb([16, 128], BF16) for _ in range(NW)]
    wub = [c.sb([16, 128], BF16) for _ in range(NW)]
    wdb = [c.sb([2048], BF16) for _ in range(9)]
    hid = [c.sb([T], BF16) for _ in range(9)]
    sgt = [c.sb([512], F32) for _ in range(2)]
    xt = [c.sb([512], F32) for _ in range(6)]
    f0 = 0
    nup = 0
    nx = 0
    nwd = 0
    nfc = 0
    for ch, nf in enumerate(FCH):
        wd_tiles = []
        for k in range(nf):
            f = f0 + k
            wg_, wu_ = wgb[nfc % NW], wub[nfc % NW]
            nfc += 1
            wd_ = wdb[k]
            nwd += 1
            c.dma(wg_.ap, wg_src[f], [], [wg_.r], q=POOL)
            c.dma(wu_.ap, wu_src[f], [], [wu_.r], q=POOL)
            c.dma(wd_.ap, wd_src[f], [], [wd_.r], q=POOL)
            wd_tiles.append(wd_)
            hk = hid[k]
            for bi, (t0, n) in enumerate(blocks):
                pg, pu = c.psum[2 * (nup % 2)], c.psum[2 * (nup % 2) + 1]
                for kt in range(16):
                    c.mm(pg.ap[:, 0:n], wg_.ap[:, kt, :], H.ap[:, kt, t0:t0 + n], kt == 0, kt == 15, [wg_.r, H.r], [pg.r])
                for kt in range(16):
                    c.mm(pu.ap[:, 0:n], wu_.ap[:, kt, :], H.ap[:, kt, t0:t0 + n], kt == 0, kt == 15, [wu_.r, H.r], [pu.r])
                s_ = sgt[nup % 2]
                c.act(s_.ap[:, 0:n], pg.ap[:, 0:n], AF.Silu, [pg.r], [s_.r])
                c.tt(hk.ap[:, t0:t0 + n], s_.ap[:, 0:n], pu.ap[:, 0:n], ALU.mult, [s_.r, pu.r], [hk.r])
                nup += 1
        tiles = [(dt, t0, n) for dt in range(16) for (t0, n) in blocks]
        PF = 4

        def issue_load(idx):
            dt, t0, n = tiles[idx]
            x_ = xt[(nx + idx) % 6]
            c.dma(x_.ap[:, 0:n], c.XR[dt, :, t0:t0 + n], [c.XRr[dt][bidx(t0)]], [x_.r])
        for idx in range(min(PF, len(tiles))):
            issue_load(idx)
        for idx, (dt, t0, n) in enumerate(tiles):
            if idx + PF < len(tiles):
                issue_load(idx + PF)
            g = 1 if t0 < LC else 0
            po = c.psum[4 + (nx + idx) % 2]
            x_ = xt[(nx + idx) % 6]
            xreg = c.XRr[dt][bidx(t0)]
            for k in range(nf):
                c.mm(po.ap[:, 0:n], wd_tiles[k].ap[:, dt * 128:(dt + 1) * 128], hid[k].ap[:, t0:t0 + n],
                     k == 0, k == nf - 1, [wd_tiles[k].r, hid[k].r], [po.r])
            c.stt(x_.ap[:, 0:n], po.ap[:, 0:n], G.ap[:, g, i, dt:dt + 1], x_.ap[:, 0:n], ALU.mult, ALU.add,
                  [po.r, G.r, x_.r], [x_.r])
            c.dma(c.XR[dt, :, t0:t0 + n], x_.ap[:, 0:n], [x_.r], [xreg], q=ACT)
        nx += len(tiles)
        f0 += nf
    c.stage_end()


def st_load_x(c):
    for dt in range(16):
        c.dma(c.XR[dt], c.d_xin[dt], [], c.XRr[dt])
    c.stage_end()


def st_final(c):
    fg = c.sb([16], F32)
    c.dma(fg.ap, c.d_finalg, [], [fg.r])
    xt = [c.sb([16, 512], F32) for _ in range(2)]
    sq = [c.sb([16, 512], BF16) for _ in range(2)]
    rs = [c.sb([512], F32) for _ in range(2)]
    for bi, (t0, n) in enumerate(TB[1:]):
        x_, s_, r_ = xt[bi % 2], sq[bi % 2], rs[bi % 2]
        ps = c.psum[bi % 2]
        c.dma(x_.ap, c.XR[:, :, t0:t0 + n].rearrange('d p t -> p d t'), xr_col(c, t0), [x_.r])
        c.act(s_.ap, x_.ap, AF.Square, [x_.r], [s_.r])
        for dt in range(16):
            c.mm(ps.ap, c.ones_bf.ap, s_.ap[:, dt, :], dt == 0, dt == 15, [s_.r, c.ones_bf.r], [ps.r])
        c.act(r_.ap, ps.ap, AF.Sqrt, [ps.r, c.eps_t.r], [r_.r], scale=1.0 / D, bias=c.eps_t.ap[:, 0:1])
        c.recip(r_.ap, r_.ap, [r_.r], [r_.r])
        for dt in range(16):
            c.stt(x_.ap[:, dt, :], x_.ap[:, dt, :], fg.ap[:, dt:dt + 1], r_.ap, ALU.mult, ALU.mult, [x_.r, fg.r, r_.r], [x_.r])
        c.dma(c.d_out[:, :, t0 - LC:t0 - LC + n].rearrange('d p t -> p d t'), x_.ap, [x_.r], [c.outr])
    c.stage_end()


def declare_io(c):
    c.d_xin = c.dram('xin', [16, 128, T], F32, 'ExternalInput')
    c.d_cc = c.dram('cc', [128, 16, 2], F32, 'ExternalInput')
    c.d_modw = c.dram('mod_w', [2, D, 9 * D], F32, 'ExternalInput')
    c.d_modb = c.dram('mod_b', [2, 128, 144], F32, 'ExternalInput')
    c.d_normg = c.dram('norm_g', [2, 128, 3, 16], F32, 'ExternalInput')
    c.d_finalg = c.dram('final_g', [128, 16], F32, 'ExternalInput')
    c.d_wg = c.dram('ffn_wg', [2, 2, NFT, 128, 16, 128], F32, 'ExternalInput')
    c.d_wu = c.dram('ffn_wu', [2, 2, NFT, 128, 16, 128], F32, 'ExternalInput')
    c.d_wd = c.dram('ffn_wd', [2, 2, DFF, D], F32, 'ExternalInput')
    EI = 'ExternalInput'
    c.d_evwin_t = c.dram('ev_w_in_t', [24, 128, 16, 128], F32, EI)
    c.d_evwin = c.dram('ev_w_in', [D, 4096], F32, EI)
    c.d_rpbt = c.dram('rpbt', [64, 8, 15, 64], F32, EI)
    c.d_namask = c.dram('namask', [64, 64], F32, EI)
    c.d_s5are = c.dram('s5are', [128, 64], F32, EI)
    c.d_s5aim = c.dram('s5aim', [128, 64], F32, EI)
    c.d_s5ldt = c.dram('s5ldt', [128, 64], F32, EI)
    c.d_s5d = c.dram('s5d', [128, 32], F32, EI)
    c.d_iota1 = c.dram('iota1', [128, 512], F32, EI)
    c.d_s5bre = c.dram('s5bre', [128, 64, 32], F32, EI)
    c.d_s5bim = c.dram('s5bim', [128, 64, 32], F32, EI)
    c.d_s5cre = c.dram('s5cre', [128, 64, 32], F32, EI)
    c.d_s5cim = c.dram('s5cim', [128, 64, 32], F32, EI)
    c.d_gluw = c.dram('glu_w', [1024, 1024], F32, EI)
    c.d_glub = c.dram('glu_b', [128, 8], F32, EI)
    c.d_wout = [c.dram('ev_w_out', [D, D], F32, EI), c.dram('od_w_out', [D, D], F32, EI)]
    c.d_odwin_t = c.dram('od_w_in_t', [48, 128, 16, 128], F32, EI)
    c.d_odwdt = c.dram('od_w_dt', [128, 2, 16, 16], F32, EI)
    c.d_hysw = c.dram('hysw', [128, 24, 3], F32, EI)
    c.d_hysb = c.dram('hysb', [128, 24], F32, EI)
    c.d_ssdcw = c.dram('ssdcw', [128, 16, 3], F32, EI)
    c.d_ssdcb = c.dram('ssdcb', [128, 16], F32, EI)
    c.d_dtbias = c.dram('dtbias', [16, 2], F32, EI)
    c.d_alog = c.dram('alog', [16, 2], F32, EI)
    c.d_ssdd = c.dram('ssdd', [128, 16], F32, EI)
    c.d_tri = c.dram('tri', [2, 128, 128], F32, EI)
    c.d_ssdng = c.dram('ssdng', [128, 8], F32, EI)
    c.d_hyz = c.dram('hyz', [33, L], F32, EI)
    c.d_hywin = c.dram('hywin', [33, 64], F32, EI)
    c.d_hywmid = c.dram('hywmid', [64, 2, 64], F32, EI)
    c.d_hyfb = c.dram('hyfb', [64, 4], F32, EI)
    c.d_hywout = c.dram('hywout', [64, 4096], F32, EI)
    c.d_decay = c.dram('decay', [L, 1024], F32, EI)
    c.d_hyfbias = c.dram('hyfbias', [128, 2, 1024], F32, EI)
    c.d_Cf = c.dram('Cf', [16, 128, 16, 128], BF16, EI)
    c.d_Sf = c.dram('Sf', [16, 128, 16, 128], BF16, EI)
    c.d_Ci = c.dram('Ci', [16, 128, 16, 128], BF16, EI)
    c.d_Si = c.dram('Si', [16, 128, 16, 128], BF16, EI)
    c.HYd = c.dram('HYd', [24, 128, L], BF16)
    c.Zd = c.dram('Zd', [8, 128, L], BF16)
    c.XBCd = c.dram('XBCd', [16, 128, T], BF16)
    c.DTd = c.dram('DTd', [2, 16, T], F32)
    c.CUMd = c.dram('CUMd', [2, 16, T], F32)
    c.YSd = c.dram('YSd', [16, 64, L], F32)
    c.HYr, c.Zr, c.XBCr, c.DTr, c.CUMr, c.YSr = Reg(), Reg(), Reg(), Reg(), Reg(), Reg()
    c.Ud = c.dram('Ud', [8, 128, T], BF16)
    c.Qd = c.dram('Qd', [8, 128, T], BF16)
    c.Kd = c.dram('Kd', [8, 128, T], BF16)
    c.Vd = c.dram('Vd', [18, 128, 1024], BF16)
    c.Yd = c.dram('Yd', [1024, T], F32)
    c.CATd = c.dram('CATd', [16, 128, T], BF16)
    c.Ur, c.Qr, c.Kr, c.Vr, c.Yr = Reg(), Reg(), Reg(), Reg(), Reg()
    c.CATr = [Reg(), Reg()]
    c.d_out = c.dram('out', [16, 128, L], F32, 'ExternalOutput')
    c.outr = Reg()
    c.XR = c.dram('XR', [16, 128, T], F32)
    c.XRr = [[Reg() for _ in range(len(TB))] for _ in range(16)]
    c.M, c.A, c.G = {}, {}, {}


def build(stages=None, dbg_in=(), dbg_out=()):
    c = Ctx(dbg_in, dbg_out)
    declare_io(c)
    st_consts(c)
    allst = stages is None
    if allst or 'load' in stages:
        st_load_x(c)
    for layer in range(2):
        if (allst and layer == 0) or (not allst and ('mod%d' % layer) in stages):
            st_mod(c, layer)
        if allst or ('ffa%d' % layer) in stages:
            st_ffn(c, layer, 0, 0)
        if layer == 0 and (allst or 'evmix' in stages):
            sub = stages if (stages and any(k.startswith('ev_') for k in stages)) else None
            if sub is None or 'ev_proj' in sub:
                H = st_norm(c, 0, 1)
                st_even_proj(c, H)
            if sub is None or 'ev_na' in sub:
                st_na(c, True)
            if sub is None or 'ev_s5' in sub:
                st_s5(c, True, ModJob(c, 1, 7) if allst else None)
            if sub is None or 'ev_glu' in sub:
                st_glu_wout(c, 0, True)
        if layer == 1 and (allst or 'odmix' in stages):
            sub = stages if (stages and any(k.startswith('od_') for k in stages)) else None
            if sub is None or 'od_proj' in sub:
                H = st_norm(c, 1, 1)
                st_odd_proj(c, H)
            if sub is None or 'od_ssd' in sub:
                st_ssd(c)
            if sub is None or 'od_hy' in sub:
                st_hyena(c)
            if sub is None or 'od_out' in sub:
                st_odd_out(c)
        if allst or ('ffb%d' % layer) in stages:
            st_ffn(c, layer, 1, 2, TB if layer == 0 else TB[1:])
    if allst or 'final' in stages:
        st_final(c)
    c.stage_end()
    return c


def host_prep(inp, b):
    f = np.float32
    m = {}
    xc = np.concatenate([inp['ctx'][b], inp['x'][b]], axis=0)
    m['xin'] = np.ascontiguousarray(xc.T.reshape(16, 128, T))
    cc = np.stack([inp['c'][b], inp['c_ctx']], axis=-1)
    m['cc'] = np.ascontiguousarray(cc.reshape(16, 128, 2).transpose(1, 0, 2))
    return m


_SHARED = {}


def host_shared(inp):
    m = {}
    m['mod_w'] = np.ascontiguousarray(inp['mod_w'])
    m['mod_b'] = np.ascontiguousarray(inp['mod_b'].reshape(2, 144, 128).transpose(0, 2, 1))
    m['norm_g'] = np.ascontiguousarray(inp['norm_g'].reshape(2, 3, 16, 128).transpose(0, 3, 1, 2))
    m['final_g'] = np.ascontiguousarray(inp['final_g'].reshape(16, 128).T)
    for k in ('ffn_wg', 'ffn_wu'):
        m[k] = np.ascontiguousarray(inp[k].reshape(2, 2, 16, 128, NFT, 128).transpose(0, 1, 4, 3, 2, 5))
    m['ffn_wd'] = np.ascontiguousarray(inp['ffn_wd'])
    w = inp['ev_w_in'][0]
    m['ev_w_in'] = np.ascontiguousarray(w)
    m['ev_w_in_t'] = np.ascontiguousarray(w[:, :3072].reshape(16, 128, 24, 128).transpose(2, 1, 0, 3))
    col = np.arange(64)
    dc = np.clip(col[:, None] - col[None, :] + 15, 0, 30)
    rp = inp['na_rpb'][0][:, :, dc]
    m['rpbt'] = np.ascontiguousarray(rp.transpose(2, 0, 1, 3))
    cs = np.clip(col - 8, 0, 48)
    ok = (col[:, None] >= cs[None, :]) & (col[:, None] < cs[None, :] + 16)
    m['namask'] = np.where(ok, 0.0, NEGM).astype(np.float32)

    def st_lay(a):
        return np.ascontiguousarray(a.reshape(2, 32, 2, 64).transpose(2, 3, 0, 1).reshape(128, 64))
    m['s5are'] = st_lay(inp['s5_a_re'][0])
    m['s5aim'] = st_lay(inp['s5_a_im'][0])
    m['s5ldt'] = st_lay(np.repeat(inp['s5_log_dt'][0][:, :, None], 64, axis=2))
    dd = np.zeros((128, 32), np.float32)
    dd[0:32] = inp['s5_d'][0].reshape(32, 32).T
    m['s5d'] = dd
    m['iota1'] = np.ascontiguousarray(np.broadcast_to(np.arange(1, 513, dtype=np.float32), (128, 512)))

    def b_blk(b):
        o = np.zeros((128, 2, 32, 32), np.float32)
        bb = b.reshape(2, 32, 2, 64, 16)
        o[0:64, :, :, 0:16] = bb[:, :, 0].transpose(2, 0, 1, 3)
        o[64:128, :, :, 16:32] = bb[:, :, 1].transpose(2, 0, 1, 3)
        return o.reshape(128, 64, 32)

    def c_blk(cm):
        o = np.zeros((128, 2, 32, 32), np.float32)
        cc = cm.reshape(2, 32, 2, 16, 64)
        o[0:64, :, :, 0:16] = cc[:, :, 0].transpose(3, 0, 1, 2)
        o[64:128, :, :, 16:32] = cc[:, :, 1].transpose(3, 0, 1, 2)
        return o.reshape(128, 64, 32)
    m['s5bre'] = b_blk(inp['s5_b_re'][0])
    m['s5bim'] = b_blk(inp['s5_b_im'][0])
    m['s5cre'] = c_blk(inp['s5_c_re'][0])
    m['s5cim'] = c_blk(inp['s5_c_im'][0])
    m['glu_w'] = np.ascontiguousarray(inp['s5_glu_w'][0])
    m['glu_b'] = np.ascontiguousarray(inp['s5_glu_b'][0].reshape(8, 128).T)
    m['ev_w_out'] = np.ascontiguousarray(inp['ev_w_out'][0])
    m['od_w_out'] = np.ascontiguousarray(inp['od_w_out'][0])
    w = inp['od_w_in'][0]
    m['od_w_in_t'] = np.ascontiguousarray(w[:, :6144].reshape(16, 128, 48, 128).transpose(2, 1, 0, 3))
    m['od_w_dt'] = np.ascontiguousarray(w[:, 6144:6176].reshape(16, 128, 2, 16).transpose(1, 2, 0, 3))
    m['hysw'] = np.ascontiguousarray(inp['hy_short_w'][0].T.reshape(24, 128, 3).transpose(1, 0, 2))
    m['hysb'] = np.ascontiguousarray(inp['hy_short_b'][0].reshape(24, 128).T)
    m['ssdcw'] = np.ascontiguousarray(inp['ssd_conv_w'][0].T.reshape(16, 128, 3).transpose(1, 0, 2))
    m['ssdcb'] = np.ascontiguousarray(inp['ssd_conv_b'][0].reshape(16, 128).T)
    m['dtbias'] = np.ascontiguousarray(inp['ssd_dt_bias'][0].T)
    m['alog'] = np.ascontiguousarray(inp['ssd_a_log'][0].T)
    m['ssdd'] = np.ascontiguousarray(np.broadcast_to(inp['ssd_d'][0][None, :], (128, 16)))
    ii = np.arange(128)
    m['tri'] = np.stack([(ii[:, None] <= ii[None, :]), (ii[:, None] >= ii[None, :])]).astype(np.float32)
    m['ssdng'] = np.ascontiguousarray(inp['ssd_norm_g'][0].reshape(8, 128).T)
    m['hywin'] = np.ascontiguousarray(inp['hy_w_in'][0])
    m['hywmid'] = np.ascontiguousarray(inp['hy_w_mid'][0].transpose(1, 0, 2))
    m['hyfb'] = np.ascontiguousarray(np.stack([inp['hy_freq'][0], inp['hy_b_in'][0], inp['hy_b_mid'][0][0], inp['hy_b_mid'][0][1]], axis=1))
    m['hywout'] = np.ascontiguousarray(inp['hy_w_out'][0])
    m['hyfbias'] = np.ascontiguousarray(np.broadcast_to(inp['hy_fbias'][0][None], (128, 2, 1024)))
    m.update(hy_consts())
    return m


_HYC = {}


def hy_consts():
    if _HYC:
        return _HYC
    f32 = np.float32
    t = np.linspace(0.0, 1.0, L, dtype=f32)[:, None]
    w = (2.0 * math.pi * np.arange(L, dtype=f32)[:, None] / L).astype(f32)
    f = np.linspace(1e-4, 15, 16, dtype=f32)[None, :]
    z = np.concatenate([t, np.cos(f * w), -np.sin(f * w)], axis=-1).astype(f32)
    _HYC['hyz'] = np.ascontiguousarray(z.T)
    mx = math.log(1e-2) / 0.3
    mn = math.log(1e-2) / 1.5
    deltas = np.abs(np.linspace(mn, mx, 1024, dtype=f32))
    _HYC['decay'] = np.exp(-t * deltas[None, :]).astype(f32)
    n = np.arange(L, dtype=np.int64)
    ang = 2.0 * np.pi * ((n[:, None] * n[None, :]) % 4096).astype(np.float64) / 4096.0
    Cf = np.cos(ang)
    Sf = -np.sin(ang)
    sgn = np.where(n % 2 == 0, 1.0, -1.0)
    Sf[:, 0] = sgn
    wf = np.full(L, 2.0 / 4096.0)
    wf[0] = 1.0 / 4096.0
    Ci = wf[:, None] * np.cos(ang)
    Si = -wf[:, None] * np.sin(ang)
    Si[0, :] = sgn / 4096.0

    def lay(a):
        return np.ascontiguousarray(a.reshape(16, 128, 16, 128).transpose(2, 1, 0, 3).astype(f32).astype(ml_dtypes.bfloat16))
    _HYC['Cf'], _HYC['Sf'], _HYC['Ci'], _HYC['Si'] = lay(Cf), lay(Sf), lay(Ci), lay(Si)
    return _HYC


def kernel(**inputs):
    inp = {k: np.asarray(v) for k, v in inputs.items()}
    c = build()
    shared = host_shared(inp)
    in_maps = []
    for b in range(8):
        m = dict(shared)
        m.update(host_prep(inp, b))
        in_maps.append({k: m[k] for k in c.inputs})
    res = run_bass_kernel_spmd(c.nc, in_maps, core_ids=list(range(8)))
    outs = []
    for b in range(8):
        o = np.asarray(res.results[b]['out'])
        outs.append(o.reshape(D, L).T)
    return np.ascontiguousarray(np.stack(outs, axis=0)).astype(np.float32)


SQ128 = math.sqrt(128.0)
NEGM = -30000.0


def evac(c, n, out, in_, R, W):
    if n % 2 == 0:
        c.act(out, in_, AF.Identity, R, W)
    else:
        c.copy(out, in_, R, W)


def st_even_proj(c, H):
    wsrc = c.d_evwin_t
    wb = [c.sb([16, 128], BF16) for _ in range(3)]
    ob = [c.sb([512], BF16) for _ in range(4)]
    dst = [c.Ud, c.Qd, c.Kd]
    dreg = [c.Ur, c.Qr, c.Kr]
    cnt = 0
    for f in range(24):
        w_ = wb[f % 3]
        c.dma(w_.ap, wsrc[f], [], [w_.r], q=POOL)
        for bi, (t0, n) in enumerate(TB):
            ps = c.psum[cnt % 4]
            o_ = ob[cnt % 4]
            for kt in range(16):
                c.mm(ps.ap[:, 0:n], w_.ap[:, kt, :], H.ap[:, kt, t0:t0 + n], kt == 0, kt == 15, [w_.r, H.r], [ps.r])
            evac(c, cnt, o_.ap[:, 0:n], ps.ap[:, 0:n], [ps.r], [o_.r])
            c.dma(dst[f // 8][f % 8, :, t0:t0 + n], o_.ap[:, 0:n], [o_.r], [dreg[f // 8]])
            cnt += 1
    vsrc = c.d_evwin.rearrange('(kt p) f -> p kt f', p=128)
    vw = [c.sb([16, 512], BF16) for _ in range(2)]
    for j in range(2):
        for kt in range(16):
            c.dma(vw[j].ap[:, kt, :], vsrc[:, kt, 3072 + 512 * j:3072 + 512 * (j + 1)], [], [vw[j].r], q=POOL)
    for tt_ in range(18):
        for j in range(2):
            ps = c.psum[cnt % 4]
            o_ = ob[cnt % 4]
            for kt in range(16):
                c.mm(ps.ap, H.ap[:, kt, tt_ * 128:(tt_ + 1) * 128], vw[j].ap[:, kt, :], kt == 0, kt == 15, [vw[j].r, H.r], [ps.r])
            evac(c, cnt, o_.ap, ps.ap, [ps.r], [o_.r])
            c.dma(c.Vd[tt_, :, 512 * j:512 * (j + 1)], o_.ap, [o_.r], [c.Vr])
            cnt += 1
    c.stage_end()


def st_na(c, with_ctx=True):
    Tb = c.sb([8, 15, 64], F32)
    mk = c.sb([64], F32)
    c.dma(Tb.ap[0:64], c.d_rpbt, [], [Tb.r])
    c.dma(mk.ap[0:64], c.d_namask, [], [mk.r])
    c.stt(Tb.ap[0:64].rearrange('p h d q -> p (h d) q'), Tb.ap[0:64].rearrange('p h d q -> p (h d) q'), SQ128,
          mk.ap[0:64, None, :].to_broadcast([64, 120, 64]), ALU.mult, ALU.add, [Tb.r, mk.r], [Tb.r])
    neg = c.sb([64], F32)
    c.memset(neg.ap, NEGM, [neg.r])
    sel = c.sb([2, 128], F32)
    c.memset(sel.ap, 0.0, [sel.r])
    c.copy(sel.ap[0:64, 0, 0:64], c.ident_f.ap[0:64, 0:64], [c.ident_f.r, sel.r], [sel.r])
    c.copy(sel.ap[0:64, 1, 64:128], c.ident_f.ap[0:64, 0:64], [c.ident_f.r, sel.r], [sel.r])
    qb = [c.sb([T], BF16) for _ in range(2)]
    kb = [c.sb([T], BF16) for _ in range(2)]
    vb = [c.sb([18, 128], BF16) for _ in range(2)]
    ob = [c.sb([T], BF16) for _ in range(2)]
    eb = [c.sb([7, 64], BF16) for _ in range(3)]
    ec = c.sb([2, 256], BF16)
    rz = [c.sb([64], F32) for _ in range(2)]
    rzc = c.sb([256], F32)
    sc = 1.0 / SQ128
    it = 0
    for h in range(8):
        q_, k_, v_, o_ = qb[h % 2], kb[h % 2], vb[h % 2], ob[h % 2]
        c.dma(q_.ap, c.Qd[h], [c.Qr], [q_.r])
        c.dma(k_.ap, c.Kd[h], [c.Kr], [k_.r])
        c.dma(v_.ap, c.Vd[:, :, h * 128:(h + 1) * 128].rearrange('t p d -> p t d'), [c.Vr], [v_.r])
        if with_ctx:
            ps, po = c.psum[4], c.psum[5]
            for i in range(2):
                c.mm(ps.ap[:, i * 256:(i + 1) * 256], k_.ap[:, i * 128:(i + 1) * 128], q_.ap[:, 0:256], True, True, [k_.r, q_.r], [ps.r])
            c.act(ec.ap.rearrange('p a b -> p (a b)'), ps.ap, AF.Exp, [ps.r], [ec.r], scale=sc)
            for i in range(2):
                c.mm(po.ap[:, 0:256], v_.ap[:, i, :], ec.ap[:, i, :], i == 0, i == 1, [v_.r, ec.r], [po.r])
            for i in range(2):
                c.mm(po.ap[:, 256:512], c.ones_bf.ap, ec.ap[:, i, :], i == 0, i == 1, [c.ones_bf.r, ec.r], [po.r])
            c.recip(rzc.ap, po.ap[:, 256:512], [po.r], [rzc.r])
            c.tt(o_.ap[:, 0:256], po.ap[:, 0:256], rzc.ap, ALU.mult, [po.r, rzc.r], [o_.r])
        for r in range(32):
            rs = min(max(r - 4, 0), 24)
            base = (rs // 2) * 2
            nt = 4 if rs % 2 == 0 else 5
            ps, po = c.psum[it % 2], c.psum[2 + it % 2]
            e_ = eb[it % 3]
            z_ = rz[it % 2]
            qs = q_.ap[:, LC + r * 64:LC + (r + 1) * 64]
            tiles = []
            for i in range(nt):
                krow = base + 2 * i
                k0 = LC + krow * 64
                col = ps.ap[:, i * 64:(i + 1) * 64]
                c.mm(col, k_.ap[:, k0:k0 + 128], qs, True, False, [k_.r, q_.r], [ps.r])
                for half in range(2):
                    kr = krow + half
                    if rs <= kr < rs + 8:
                        rhs = Tb.ap[0:64, h, kr - r + 7, :]
                        rr = Tb.r
                    else:
                        rhs = neg.ap[0:64, :]
                        rr = neg.r
                    c.mm(col, sel.ap[0:64, half, :], rhs, False, half == 1, [sel.r, rr], [ps.r])
                tiles.append((LC // 128) + krow // 2)
            for i in range(2):
                col = ps.ap[:, (nt + i) * 64:(nt + i + 1) * 64]
                c.mm(col, k_.ap[:, i * 128:(i + 1) * 128], qs, True, True, [k_.r, q_.r], [ps.r])
                tiles.append(i)
            ntt = nt + 2
            c.act(e_.ap[:, 0:ntt, :].rearrange('p a b -> p (a b)'), ps.ap[:, 0:ntt * 64], AF.Exp, [ps.r], [e_.r], scale=sc)
            for i, vt in enumerate(tiles):
                c.mm(po.ap[:, 0:64], v_.ap[:, vt, :], e_.ap[:, i, :], i == 0, i == ntt - 1, [v_.r, e_.r], [po.r])
            for i in range(ntt):
                c.mm(po.ap[:, 64:128], c.ones_bf.ap, e_.ap[:, i, :], i == 0, i == ntt - 1, [c.ones_bf.r, e_.r], [po.r])
            c.recip(z_.ap, po.ap[:, 64:128], [po.r], [z_.r])
            c.tt(o_.ap[:, LC + r * 64:LC + (r + 1) * 64], po.ap[:, 0:64], z_.ap, ALU.mult, [po.r, z_.r], [o_.r])
            it += 1
        if with_ctx:
            c.dma(c.CATd[8 + h], o_.ap, [o_.r], [c.CATr[1]])
        else:
            c.dma(c.CATd[8 + h, :, LC:T], o_.ap[:, LC:T], [o_.r], [c.CATr[1]])
    c.stage_end()


def bcl(ap, shape):
    return ap.to_broadcast(list(shape))


def rnd(c, out, in_, R, W, eng=DVE):
    c.ts(out, in_, MAGIC, MAGIC, ALU.add, ALU.subtract, R, W, eng=eng)


def sincos_frac(c, t, tmp, out_s, out_c, R):
    a, b = tmp
    rnd(c, a.ap, t.ap, [t.r], [a.r])
    c.tt(a.ap, t.ap, a.ap, ALU.subtract, [t.r, a.r], [a.r])
    c.act(out_s.ap, a.ap, AF.Sin, [a.r], [out_s.r], scale=2 * math.pi)
    c.ts(b.ap, t.ap, 0.25, None, ALU.add, None, [t.r], [b.r])
    rnd(c, a.ap, b.ap, [b.r], [a.r])
    c.tt(b.ap, b.ap, a.ap, ALU.subtract, [b.r, a.r], [b.r])
    c.act(out_c.ap, b.ap, AF.Sin, [b.r], [out_c.r], scale=2 * math.pi)


def st_s5(c, with_ctx=True, modjob=None):
    def ld(src, shape):
        t = c.sb(shape, F32)
        c.dma(t.ap, src, [], [t.r])
        return t
    dcol = ld(c.d_s5d, [32])
    iota = ld(c.d_iota1, [512])
    r_, thp = c.sb([64], F32), c.sb([64], F32)
    BT = c.sb([128, 128], BF16)
    CB = c.sb([64, 2, 32], BF16)
    mark = c.aoff
    are, aim, ldt = ld(c.d_s5are, [64]), ld(c.d_s5aim, [64]), ld(c.d_s5ldt, [64])
    tmp = [c.sb([64], F32) for _ in range(8)]
    dtm, sn, cs, cre, cim, den = [c.sb([64], F32) for _ in range(6)]
    c.act(dtm.ap, ldt.ap, AF.Exp, [ldt.r], [dtm.r])
    c.tt(tmp[0].ap, are.ap, dtm.ap, ALU.mult, [are.r, dtm.r], [tmp[0].r])
    c.act(r_.ap, tmp[0].ap, AF.Exp, [tmp[0].r], [r_.r])
    c.tt(thp.ap, aim.ap, dtm.ap, ALU.mult, [aim.r, dtm.r], [thp.r])
    c.ts(thp.ap, thp.ap, 1.0 / (2 * math.pi), None, ALU.mult, None, [thp.r], [thp.r])
    sincos_frac(c, thp, tmp[1:3], sn, cs, None)
    nr, ni = tmp[3], tmp[4]
    c.tt(nr.ap, r_.ap, cs.ap, ALU.mult, [r_.r, cs.r], [nr.r])
    c.ts(nr.ap, nr.ap, -1.0, None, ALU.add, None, [nr.r], [nr.r])
    c.tt(ni.ap, r_.ap, sn.ap, ALU.mult, [r_.r, sn.r], [ni.r])
    c.tt(den.ap, are.ap, are.ap, ALU.mult, [are.r], [den.r])
    c.tt(tmp[5].ap, aim.ap, aim.ap, ALU.mult, [aim.r], [tmp[5].r])
    c.tt(den.ap, den.ap, tmp[5].ap, ALU.add, [den.r, tmp[5].r], [den.r])
    c.recip(den.ap, den.ap, [den.r], [den.r])
    c.tt(cre.ap, nr.ap, are.ap, ALU.mult, [nr.r, are.r], [cre.r])
    c.tt(tmp[5].ap, ni.ap, aim.ap, ALU.mult, [ni.r, aim.r], [tmp[5].r])
    c.tt(cre.ap, cre.ap, tmp[5].ap, ALU.add, [cre.r, tmp[5].r], [cre.r])
    c.tt(cre.ap, cre.ap, den.ap, ALU.mult, [cre.r, den.r], [cre.r])
    c.tt(cim.ap, ni.ap, are.ap, ALU.mult, [ni.r, are.r], [cim.r])
    c.tt(tmp[5].ap, nr.ap, aim.ap, ALU.mult, [nr.r, aim.r], [tmp[5].r])
    c.tt(cim.ap, cim.ap, tmp[5].ap, ALU.subtract, [cim.r, tmp[5].r], [cim.r])
    c.tt(cim.ap, cim.ap, den.ap, ALU.mult, [cim.r, den.r], [cim.r])
    bre, bim = ld(c.d_s5bre, [64, 32]), ld(c.d_s5bim, [64, 32])
    Bb = c.sb([64, 2, 32], F32)
    t1, t2 = c.sb([64, 32], F32), c.sb([64, 32], F32)
    creb, cimb = bcl(cre.ap, [128, 64, 32]), bcl(cim.ap, [128, 64, 32])
    c.tt(t1.ap, bre.ap, creb, ALU.mult, [bre.r, cre.r], [t1.r])
    c.tt(t2.ap, bim.ap, cimb, ALU.mult, [bim.r, cim.r], [t2.r])
    c.tt(Bb.ap[:, :, 0, :], t1.ap, t2.ap, ALU.subtract, [t1.r, t2.r], [Bb.r])
    c.tt(t1.ap, bim.ap, creb, ALU.mult, [bim.r, cre.r], [t1.r])
    c.tt(t2.ap, bre.ap, cimb, ALU.mult, [bre.r, cim.r], [t2.r])
    c.tt(Bb.ap[:, :, 1, :], t1.ap, t2.ap, ALU.add, [t1.r, t2.r], [Bb.r])
    for q4 in range(32):
        ps = c.psum[q4 % 2]
        for j in range(4):
            idx = q4 * 4 + j
            c.transpose(ps.ap[0:32, j * 128:(j + 1) * 128], Bb.ap[:, idx // 2, idx % 2, :], c.ident_f.ap, [Bb.r, c.ident_f.r], [ps.r])
        c.copy(BT.ap[0:32, q4 * 4:(q4 + 1) * 4, :].rearrange('p a b -> p (a b)'), ps.ap[0:32, :], [ps.r], [BT.r])
    crb, cib = ld(c.d_s5cre, [64, 32]), ld(c.d_s5cim, [64, 32])
    c.act(CB.ap[:, :, 0, :], crb.ap, AF.Identity, [crb.r], [CB.r])
    c.act(CB.ap[:, :, 1, :], cib.ap, AF.Identity, [cib.r], [CB.r], scale=-1.0)
    c.S.flush(barrier=True)
    c.aoff = mark
    ctab = [c.sb([512], F32) for _ in range(2)]
    stab = [c.sb([512], F32) for _ in range(2)]
    tA, tB, tT = c.sb([512], F32), c.sb([512], F32), c.sb([512], F32)
    br, bi_ = [c.sb([512], F32) for _ in range(2)], [c.sb([512], F32) for _ in range(2)]
    d1, d2, zR, wR, m1, m2 = [c.sb([512], F32) for _ in range(6)]
    p1, p2, zI, wI, m3, m4 = [c.sb([512], F32) for _ in range(6)]
    sbuf = [[[c.sb([T], BF16) for _ in range(2)] for _ in range(2)] for _ in range(2)]
    ug = [c.sb([T], BF16) for _ in range(2)]
    ysb = [c.sb([T], F32) for _ in range(2)]
    ini = [c.sb([1], F32) for _ in range(2)]
    tin = c.sb([1], F32)
    segs_f = list(TB)
    segs_b = [TB[0]] + TB[:0:-1]
    if not with_ctx:
        pass
    nseg = 0
    if modjob is not None:
        modjob.prep()
    for gp in range(32):
        if modjob is not None:
            modjob.emit(3 if gp % 4 else 2)
        u_ = ug[gp % 2]
        c.dma(u_.ap[0:32], c.Ud[gp // 4, 32 * (gp % 4):32 * (gp % 4) + 32, :], [c.Ur], [u_.r])
        for dr in range(2):
            dg = dr * 32 + gp
            ct, st_ = ctab[dr], stab[dr]
            c.ts(tT.ap, iota.ap, thp.ap[:, dg:dg + 1], None, ALU.mult, None, [iota.r, thp.r], [tT.r])
            sincos_frac(c, tT, [tA, tB], st_, ct, None)
            rcol = r_.ap[:, dg:dg + 1]
            sR_, sI_ = sbuf[gp % 2][dr]
            first = True
            for (t0, n) in (segs_f if dr == 0 else segs_b):
                rev = dr == 1
                pr, pi = c.psum[2 * (nseg % 2)], c.psum[2 * (nseg % 2) + 1]
                b_r, b_i = br[nseg % 2], bi_[nseg % 2]
                c.mm(pr.ap[:, 0:n], BT.ap[0:32, dg * 2, :], u_.ap[0:32, t0:t0 + n], True, True, [BT.r, u_.r], [pr.r])
                c.mm(pi.ap[:, 0:n], BT.ap[0:32, dg * 2 + 1, :], u_.ap[0:32, t0:t0 + n], True, True, [BT.r, u_.r], [pi.r])
                srcr = pr.ap[:, 0:n][:, ::-1] if rev else pr.ap[:, 0:n]
                srci = pi.ap[:, 0:n][:, ::-1] if rev else pi.ap[:, 0:n]
                c.act(b_r.ap[:, 0:n], srcr, AF.Identity, [pr.r], [b_r.r])
                c.act(b_i.ap[:, 0:n], srci, AF.Identity, [pi.r], [b_i.r])
                cN, sN = ct.ap[:, 0:n], st_.ap[:, 0:n]
                c.tt(d1.ap[:, 0:n], cN, b_r.ap[:, 0:n], ALU.mult, [ct.r, b_r.r], [d1.r])
                c.tt(d2.ap[:, 0:n], sN, b_i.ap[:, 0:n], ALU.mult, [st_.r, b_i.r], [d2.r], eng=POOL)
                c.tt(zR.ap[:, 0:n], d1.ap[:, 0:n], d2.ap[:, 0:n], ALU.add, [d1.r, d2.r], [zR.r])
                c.tt(p1.ap[:, 0:n], cN, b_i.ap[:, 0:n], ALU.mult, [ct.r, b_i.r], [p1.r], eng=POOL)
                c.tt(p2.ap[:, 0:n], sN, b_r.ap[:, 0:n], ALU.mult, [st_.r, b_r.r], [p2.r], eng=POOL)
                c.tt(zI.ap[:, 0:n], p1.ap[:, 0:n], p2.ap[:, 0:n], ALU.subtract, [p1.r, p2.r], [zI.r], eng=POOL)
                rb = rcol.to_broadcast([128, n])
                for (w_, z_, k) in ((wR, zR, 0), (wI, zI, 1)):
                    init = 0.0 if first else ini[k].ap[:, 0:1]
                    rr = [r_.r, z_.r] + ([] if first else [ini[k].r])
                    c.S.op(DVE, (lambda o, z, i0, b: lambda e: e.tensor_tensor_scan(out=o, data0=b, data1=z, initial=i0,
                                                                                   op0=ALU.mult, op1=ALU.add))(w_.ap[:, 0:n], z_.ap[:, 0:n], init, rb),
                           reads=rr, writes=[w_.r])
                oR = sR_.ap[:, t0:t0 + n][:, ::-1] if rev else sR_.ap[:, t0:t0 + n]
                oI = sI_.ap[:, t0:t0 + n][:, ::-1] if rev else sI_.ap[:, t0:t0 + n]
                c.tt(m1.ap[:, 0:n], cN, wR.ap[:, 0:n], ALU.mult, [ct.r, wR.r], [m1.r], eng=POOL)
                c.tt(m2.ap[:, 0:n], sN, wI.ap[:, 0:n], ALU.mult, [st_.r, wI.r], [m2.r])
                c.tt(oR, m1.ap[:, 0:n], m2.ap[:, 0:n], ALU.subtract, [m1.r, m2.r], [sR_.r])
                c.tt(m3.ap[:, 0:n], cN, wI.ap[:, 0:n], ALU.mult, [ct.r, wI.r], [m3.r], eng=POOL)
                c.tt(m4.ap[:, 0:n], sN, wR.ap[:, 0:n], ALU.mult, [st_.r, wR.r], [m4.r], eng=POOL)
                c.tt(oI, m3.ap[:, 0:n], m4.ap[:, 0:n], ALU.add, [m3.r, m4.r], [sI_.r], eng=POOL)
                cl, sl = ct.ap[:, n - 1:n], st_.ap[:, n - 1:n]
                wRl, wIl = wR.ap[:, n - 1:n], wI.ap[:, n - 1:n]
                c.tt(tin.ap, sl, wIl, ALU.mult, [st_.r, wI.r], [tin.r])
                c.stt(ini[0].ap, wRl, cl, tin.ap, ALU.mult, ALU.subtract, [wR.r, ct.r, tin.r], [ini[0].r])
                c.tt(tin.ap, sl, wRl, ALU.mult, [st_.r, wR.r], [tin.r])
                c.stt(ini[1].ap, wIl, cl, tin.ap, ALU.mult, ALU.add, [wI.r, ct.r, tin.r], [ini[1].r])
                first = False
                nseg += 1
        y_ = ysb[gp % 2]
        for bi, (t0, n) in enumerate(TB):
            po = c.psum[4 + bi % 2]
            k = 0
            for dr in range(2):
                dg = dr * 32 + gp
                for ri in range(2):
                    sb_ = sbuf[gp % 2][dr][ri]
                    c.mm(po.ap[0:32, 0:n], CB.ap[:, dg, ri, :], sb_.ap[:, t0:t0 + n], k == 0, k == 3, [CB.r, sb_.r], [po.r])
                    k += 1
            c.stt(y_.ap[0:32, t0:t0 + n], u_.ap[0:32, t0:t0 + n], dcol.ap[0:32, gp:gp + 1], po.ap[0:32, 0:n], ALU.mult, ALU.add,
                  [u_.r, dcol.r, po.r], [y_.r])
        c.dma(c.Yd[gp * 32:(gp + 1) * 32, :], y_.ap[0:32], [y_.r], [c.Yr])
    if modjob is not None:
        modjob.finish()
    c.stage_end()


C0G = math.sqrt(2.0 / math.pi)


def st_glu_wout(c, layer, with_ctx=True):
    blocks = TB if with_ctx else TB[1:]
    G = c.G[layer]
    gw = c.sb([8, 1024], BF16)
    for kt in range(8):
        c.dma(gw.ap[:, kt, :], c.d_gluw[kt * 128:(kt + 1) * 128, :], [], [gw.r], q=POOL)
    gb = c.sb([8], F32)
    c.dma(gb.ap, c.d_glub, [], [gb.r])
    wo = c.sb([16, 2048], BF16)
    for kt in range(16):
        c.dma(wo.ap[:, kt, :], c.d_wout[layer][kt * 128:(kt + 1) * 128, :], [], [wo.r], q=POOL)
    yb = [c.sb([8, 512], F32) for _ in range(1)]
    y2 = c.sb([8, 512], F32)
    sg = c.sb([8, 512], F32)
    gg = [c.sb([8, 512], BF16) for _ in range(2)]
    cat = [c.sb([16, 512], BF16) for _ in range(1)]
    catr = [[Reg() for _ in range(16)] for _ in range(1)]
    sgm = [c.sb([512], F32) for _ in range(2)]
    xt = [c.sb([512], F32) for _ in range(4)]
    nx = 0
    for bi, (t0, n) in enumerate(blocks):
        g = 1 if t0 < LC else 0
        y_, g_, ct, cr = yb[0], gg[bi % 2], cat[0], catr[0]
        c.dma(y_.ap[:, :, 0:n], c.Yd[:, t0:t0 + n].rearrange('(a p) t -> p a t', p=128), [c.Yr], [y_.r])
        c.dma(ct.ap[:, 8:16, 0:n], c.CATd[8:16, :, t0:t0 + n].rearrange('a p t -> p a t'), [c.CATr[1]], cr[8:16])
        yv = y_.ap[:, :, 0:n]
        c.act(y2.ap[:, :, 0:n], yv, AF.Square, [y_.r], [y2.r])
        c.ts(y2.ap[:, :, 0:n], y2.ap[:, :, 0:n], 0.044715, 1.0, ALU.mult, ALU.add, [y2.r], [y2.r])
        c.tt(y2.ap[:, :, 0:n], y2.ap[:, :, 0:n], yv, ALU.mult, [y2.r, y_.r], [y2.r])
        c.act(sg.ap[:, :, 0:n], y2.ap[:, :, 0:n], AF.Sigmoid, [y2.r], [sg.r], scale=2 * C0G)
        c.tt(g_.ap[:, :, 0:n], sg.ap[:, :, 0:n], yv, ALU.mult, [sg.r, y_.r], [g_.r])
        for ft in range(8):
            ps = c.psum[ft % 2]
            s_ = sgm[ft % 2]
            for kt in range(8):
                c.mm(ps.ap[:, 0:n], gw.ap[:, kt, ft * 128:(ft + 1) * 128], g_.ap[:, kt, 0:n], kt == 0, kt == 7, [gw.r, g_.r], [ps.r])
            c.act(s_.ap[:, 0:n], ps.ap[:, 0:n], AF.Sigmoid, [ps.r, gb.r], [s_.r], bias=gb.ap[:, ft:ft + 1])
            c.tt(ct.ap[:, ft, 0:n], s_.ap[:, 0:n], g_.ap[:, ft, 0:n], ALU.mult, [s_.r, g_.r], [cr[ft]])
        wout_block(c, layer, ct, cr, wo, xt, t0, n, g, nx)
        nx += 16
    c.stage_end()


def wout_block(c, layer, ct, cr, wo, xt, t0, n, g, nx):
    G = c.G[layer]
    PF = 3

    def issue_load(dt):
        x_ = xt[(nx + dt) % 4]
        c.dma(x_.ap[:, 0:n], c.XR[dt, :, t0:t0 + n], [c.XRr[dt][bidx(t0)]], [x_.r])
    for dt in range(PF):
        issue_load(dt)
    for dt in range(16):
        if dt + PF < 16:
            issue_load(dt + PF)
        po = c.psum[4 + (nx + dt) % 2]
        x_ = xt[(nx + dt) % 4]
        xreg = c.XRr[dt][bidx(t0)]
        for kt in range(16):
            c.mm(po.ap[:, 0:n], wo.ap[:, kt, dt * 128:(dt + 1) * 128], ct.ap[:, kt, 0:n], kt == 0, kt == 15, [wo.r, cr[kt]], [po.r])
        c.stt(x_.ap[:, 0:n], po.ap[:, 0:n], G.ap[:, g, 1, dt:dt + 1], x_.ap[:, 0:n], ALU.mult, ALU.add, [po.r, G.r, x_.r], [x_.r])
        c.dma(c.XR[dt, :, t0:t0 + n], x_.ap[:, 0:n], [x_.r], [xreg], q=ACT)


def conv3(c, raw, W, w3, b, out, tmp, silu, wr=()):
    wr = list(wr)
    c.act(tmp.ap[:, 0:W], raw.ap[:, 1:W + 1], AF.Identity, [raw.r] + wr, [tmp.r], scale=w3[:, 1:2], bias=b)
    c.stt(tmp.ap[:, 0:W], raw.ap[:, 0:W], w3[:, 0:1], tmp.ap[:, 0:W], ALU.mult, ALU.add, [raw.r, tmp.r] + wr, [tmp.r])
    if silu:
        c.stt(tmp.ap[:, 0:W], raw.ap[:, 2:W + 2], w3[:, 2:3], tmp.ap[:, 0:W], ALU.mult, ALU.add, [raw.r, tmp.r] + wr, [tmp.r])
        c.act(out[0], tmp.ap[:, 0:W], AF.Silu, [tmp.r], out[1])
    else:
        c.stt(out[0], raw.ap[:, 2:W + 2], w3[:, 2:3], tmp.ap[:, 0:W], ALU.mult, ALU.add, [raw.r, tmp.r] + wr, out[1])


def st_odd_proj(c, H):
    wsrc = c.d_odwin_t
    hw = c.sb([24, 3], F32); hb = c.sb([24], F32); sw = c.sb([16, 3], F32); sbias = c.sb([16], F32)
    for t_, s_ in ((hw, c.d_hysw), (hb, c.d_hysb), (sw, c.d_ssdcw), (sbias, c.d_ssdcb)):
        c.dma(t_.ap, s_, [], [t_.r])
    wb = [c.sb([16, 128], BF16) for _ in range(3)]
    raw = [c.sb([T + 8], F32) for _ in range(2)]
    tmp = [c.sb([T], F32) for _ in range(2)]
    ob = [c.sb([T], BF16) for _ in range(2)]
    for r_ in raw:
        c.memset(r_.ap, 0.0, [r_.r])
    cnt = 0
    for f in range(48):
        w_ = wb[f % 3]
        c.dma(w_.ap, wsrc[f], [], [w_.r], q=POOL)
        lat_only = f < 32
        blocks = TB[1:] if lat_only else TB
        r_, t_, o_ = raw[f % 2], tmp[f % 2], ob[f % 2]
        for (t0, n) in blocks:
            ps = c.psum[cnt % 4]
            for kt in range(16):
                c.mm(ps.ap[:, 0:n], w_.ap[:, kt, :], H.ap[:, kt, t0:t0 + n], kt == 0, kt == 15, [w_.r, H.r], [ps.r])
            off = 1 + t0 if t0 < LC else 3 + t0
            evac(c, cnt, r_.ap[:, off:off + n], ps.ap[:, 0:n], [ps.r], [r_.r])
            cnt += 1
        rl = Tl(r_.ap[:, 258:258 + L + 2], r_.r)
        if f < 24:
            conv3(c, rl, L, hw.ap[:, f, :], hb.ap[:, f:f + 1], (o_.ap[:, 0:L], [o_.r]), t_, False, [hw.r, hb.r])
            c.dma(c.HYd[f], o_.ap[:, 0:L], [o_.r], [c.HYr])
        elif f < 32:
            c.act(o_.ap[:, 0:L], r_.ap[:, 259:259 + L], AF.Silu, [r_.r], [o_.r])
            c.dma(c.Zd[f - 24], o_.ap[:, 0:L], [o_.r], [c.Zr])
        else:
            a = f - 32
            rc = Tl(r_.ap[:, 0:LC + 2], r_.r)
            conv3(c, rc, LC, sw.ap[:, a, :], sbias.ap[:, a:a + 1], (o_.ap[:, 0:LC], [o_.r]), t_, True, [sw.r, sbias.r])
            conv3(c, rl, L, sw.ap[:, a, :], sbias.ap[:, a:a + 1], (o_.ap[:, LC:T], [o_.r]), t_, True, [sw.r, sbias.r])
            c.dma(c.XBCd[a], o_.ap, [o_.r], [c.XBCr])
    wdt = c.sb([2, 16, 16], BF16)
    c.dma(wdt.ap, c.d_odwdt, [], [wdt.r], q=POOL)
    dtb = c.sb([2], F32); alog = c.sb([2], F32); nA = c.sb([2], F32)
    c.dma(dtb.ap[0:16], c.d_dtbias, [], [dtb.r])
    c.dma(alog.ap[0:16], c.d_alog, [], [alog.r])
    c.act(nA.ap[0:16], alog.ap[0:16], AF.Exp, [alog.r], [nA.r])
    c.ts(nA.ap[0:16], nA.ap[0:16], -1.0, None, ALU.mult, None, [nA.r], [nA.r])
    one = c.sb([1], F32)
    c.memset(one.ap, 1.0, [one.r])
    dtT = [c.sb([T], F32) for _ in range(2)]
    aT = c.sb([T], F32)
    cumT = [c.sb([T], F32) for _ in range(2)]
    et = c.sb([512], F32)
    ini = c.sb([1], F32)
    for k in range(2):
        for (t0, n) in TB:
            ps = c.psum[4 + cnt % 2]
            cnt += 1
            for kt in range(16):
                c.mm(ps.ap[0:16, 0:n], wdt.ap[:, k, kt, :], H.ap[:, kt, t0:t0 + n], kt == 0, kt == 15, [wdt.r, H.r], [ps.r])
            c.act(et.ap[0:16, 0:n], ps.ap[0:16, 0:n], AF.Exp, [ps.r, dtb.r], [et.r], bias=dtb.ap[0:16, k:k + 1])
            c.act(dtT[k].ap[0:16, t0:t0 + n], et.ap[0:16, 0:n], AF.Ln, [et.r, one.r], [dtT[k].r], bias=one.ap[0:16, 0:1])
        c.ts(aT.ap[0:16], dtT[k].ap[0:16], nA.ap[0:16, k:k + 1], None, ALU.mult, None, [dtT[k].r, nA.r], [aT.r])
        segs = list(TB) if k == 0 else [TB[0]] + TB[:0:-1]
        first = True
        for (t0, n) in segs:
            src = aT.ap[0:16, t0:t0 + n]
            dst = cumT[k].ap[0:16, t0:t0 + n]
            if k == 1:
                src, dst = src[:, ::-1], dst[:, ::-1]
            init = 0.0 if first else ini.ap[0:16, 0:1]
            ob_ = one.ap[0:16, 0:1].to_broadcast([16, n])
            c.S.op(DVE, (lambda o, z, i0, b: lambda e: e.tensor_tensor_scan(out=o, data0=b, data1=z, initial=i0, op0=ALU.mult, op1=ALU.add))(dst, src, init, ob_),
                   reads=[aT.r, one.r] + ([] if first else [ini.r]), writes=[cumT[k].r])
            last = t0 + n - 1 if k == 0 else t0
            c.copy(ini.ap[0:16], cumT[k].ap[0:16, last:last + 1], [cumT[k].r], [ini.r])
            first = False
        c.dma(c.DTd[k], dtT[k].ap[0:16], [dtT[k].r], [c.DTr])
        c.dma(c.CUMd[k], cumT[k].ap[0:16], [cumT[k].r], [c.CUMr])
    c.stage_end()


def st_ssd(c):
    Bt = c.sb([4, T], BF16); Ct = c.sb([4, T], BF16)
    c.dma(Bt.ap, c.XBCd[8:12].rearrange('a p t -> p a t'), [c.XBCr], [Bt.r])
    c.dma(Ct.ap, c.XBCd[12:16].rearrange('a p t -> p a t'), [c.XBCr], [Ct.r])
    xtok = c.sb([18, 1024], BF16)
    xdt = [c.sb([18, 1024], BF16) for _ in range(2)]
    cumT = [c.sb([T], F32) for _ in range(2)]
    ccol = [c.sb([18, 16], F32) for _ in range(2)]
    ncol = [c.sb([18, 16], F32) for _ in range(2)]
    mark = c.aoff
    xs = [c.sb([T], BF16) for _ in range(2)]
    nps = 0
    import os
    CUT = float(os.environ.get('SSD_CUT', '99'))
    for a in range(8):
        x_ = xs[a % 2]
        c.dma(x_.ap, c.XBCd[a], [c.XBCr], [x_.r])
        for j4 in range(0, 18, 4):
            nj = min(4, 18 - j4)
            ps = c.psum[6 + nps % 2]
            nps += 1
            pb = ps.ap.bitcast(BF16)
            for jj in range(nj):
                c.transpose(pb[:, jj * 128:(jj + 1) * 128], x_.ap[:, (j4 + jj) * 128:(j4 + jj + 1) * 128], c.ident_b.ap, [x_.r, c.ident_b.r], [ps.r])
            c.copy(xtok.ap[:, j4:j4 + nj, a * 128:(a + 1) * 128], pb[:, 0:nj * 128].rearrange('p (j q) -> p j q', q=128), [ps.r], [xtok.r])
    if CUT <= 1:
        c.stage_end()
        return
    dtT = c.sb([T], F32)
    dtk = c.sb([18, 16], F32)
    for k in range(2):
        c.memset(cumT[k].ap[0:32], 0.0, [cumT[k].r])
    c.memset(dtT.ap[0:32], 0.0, [dtT.r])
    for k in range(2):
        c.dma(cumT[k].ap[0:16], c.CUMd[k], [c.CUMr], [cumT[k].r])
        c.dma(dtT.ap[0:16], c.DTd[k], [c.DTr], [dtT.r])
        if CUT <= 1.2:
            continue
        for (srcT, kind) in ((cumT[k], 0), (dtT, 1)):
            ps = c.psum[4 + nps % 2]
            nps += 1
            for j in range(18):
                c.mm(ps.ap[:, j * 16:(j + 1) * 16], srcT.ap[0:32, j * 128:(j + 1) * 128], c.ident_f.ap[0:32, 0:16], True, True, [srcT.r, c.ident_f.r], [ps.r])
            v = ps.ap[:, 0:288].rearrange('p (j h) -> p j h', h=16)
            if CUT <= 1.25:
                continue
            if kind == 0:
                c.copy(ccol[k].ap, v, [ps.r], [ccol[k].r])
                if CUT > 1.3:
                    c.act(ncol[k].ap, v, AF.Identity, [ps.r], [ncol[k].r], scale=-1.0)
            else:
                c.copy(dtk.ap, v, [ps.r], [dtk.r])
        if CUT <= 1.4:
            continue
        for j in range(18):
            c.tt(xdt[k].ap[:, j, :].rearrange('p (h q) -> p h q', q=64), xtok.ap[:, j, :].rearrange('p (h q) -> p h q', q=64),
                 dtk.ap[:, j, :].to_broadcast([128, 16, 64]), ALU.mult, [xtok.r, dtk.r], [xdt[k].r])
    if CUT <= 2:
        c.stage_end()
        return
    c.S.flush(barrier=True)
    c.aoff = mark
    sel = c.sb([16, 128], F32)
    c.memset(sel.ap[0:32], 0.0, [sel.r])
    c.copy(sel.ap[0:16], c.ident_f.ap[0:16, 0:16].to_broadcast([16, 16, 128]), [c.ident_f.r], [sel.r])
    dcol = c.sb([16], F32)
    c.dma(dcol.ap, c.d_ssdd, [], [dcol.r])
    dI = c.sb([16, 128], BF16)
    for hd in range(16):
        c.ts(dI.ap[:, hd, :], c.ident_f.ap, dcol.ap[:, hd:hd + 1], None, ALU.mult, None, [c.ident_f.r, dcol.r], [dI.r])
    tri = [c.sb([128], F32) for _ in range(2)]
    c.dma(tri[0].ap, c.d_tri[0], [], [tri[0].r])
    c.dma(tri[1].ap, c.d_tri[1], [], [tri[1].r])
    Eb = [c.sb([128], F32) for _ in range(8)]
    Mb = [c.sb([128], BF16) for _ in range(8)]
    Db = c.sb([128], F32)
    ysb = [c.sb([4, 128], F32) for _ in range(2)]
    it = 0
    npc = 0
    if CUT <= 3:
        c.stage_end()
        return
    for g in range(4 if CUT > 4 else 1):
        for i in range(16 if CUT > 4 else 1):
            q0 = LC + i * 128
            accs = [c.psum[e] for e in range(4)]
            R = [c.psum[4], c.psum[5]]
            for d in range(2):
                for e in range(4):
                    c.mm(R[d].ap[:, e * 128:(e + 1) * 128], sel.ap[0:32, g * 4 + e, :], cumT[d].ap[0:32, q0:q0 + 128], True, True,
                         [sel.r, cumT[d].r], [R[d].r])
            started = [False] * 4
            units = []
            for d in range(2):
                srcs = [0, 1] + ([2 + j for j in range(i + 1)] if d == 0 else [2 + j for j in range(i, 16)])
                units += [(d, jt) for jt in srcs]

            def emit_cb(u):
                d_, jt_ = units[u]
                pc_ = c.psum[6 + (npc + u) % 2]
                c.mm(pc_.ap[:, 0:128], Bt.ap[:, g, jt_ * 128:(jt_ + 1) * 128], Ct.ap[:, g, q0:q0 + 128], True, True, [Bt.r, Ct.r], [pc_.r])
            emit_cb(0)
            for u, (d, jt) in enumerate(units):
                if u + 1 < len(units):
                    emit_cb(u + 1)
                pc = c.psum[6 + (npc + u) % 2]
                diag = jt == 2 + i
                for e in range(4):
                    hd = g * 4 + e
                    E_, M_ = Eb[((npc + u) * 4 + e) % 8], Mb[((npc + u) * 4 + e) % 8]
                    Rv = R[d].ap[:, e * 128:(e + 1) * 128]
                    if not diag:
                        c.act(E_.ap, Rv, AF.Exp, [R[d].r, ncol[d].r], [E_.r], bias=ncol[d].ap[:, jt, hd:hd + 1])
                    else:
                        c.ts(Db.ap, Rv, ccol[d].ap[:, jt, hd:hd + 1], 0.0, ALU.subtract, ALU.min, [R[d].r, ccol[d].r], [Db.r])
                        c.act(E_.ap, Db.ap, AF.Exp, [Db.r], [E_.r])
                        c.tt(E_.ap, E_.ap, tri[d].ap, ALU.mult, [E_.r, tri[d].r], [E_.r])
                    c.tt(M_.ap, E_.ap, pc.ap[:, 0:128], ALU.mult, [E_.r, pc.r], [M_.r])
                    c.mm(accs[e].ap[0:64, 0:128], xdt[d].ap[:, jt, hd * 64:(hd + 1) * 64], M_.ap, not started[e], False,
                         [xdt[d].r, M_.r], [accs[e].r])
                    started[e] = True
            npc += len(units)
            for e in range(4):
                hd = g * 4 + e
                c.mm(accs[e].ap[0:64, 0:128], xtok.ap[:, 2 + i, hd * 64:(hd + 1) * 64], dI.ap[:, hd, :], False, True,
                     [xtok.r, dI.r], [accs[e].r])
            y_ = ysb[it % 2]
            for e in range(4):
                evac(c, e, y_.ap[0:64, e, :], accs[e].ap[0:64, 0:128], [accs[e].r], [y_.r])
            c.dma(c.YSd[g * 4:(g + 1) * 4, :, i * 128:(i + 1) * 128].rearrange('e p l -> p e l'), y_.ap[0:64], [y_.r], [c.YSr])
            it += 1
    c.stage_end()


def st_hyena(c):
    TWO_PI = 2 * math.pi
    hwo = c.sb([4096], F32); c.dma(hwo.ap[0:64], c.d_hywout, [], [hwo.r])
    hid0 = c.sb([L], F32)
    mark = c.aoff
    zT = c.sb([L], F32); c.memset(zT.ap[0:64], 0.0, [zT.r]); c.dma(zT.ap[0:33], c.d_hyz, [], [zT.r])
    w1 = c.sb([64], F32); c.memset(w1.ap[0:64], 0.0, [w1.r]); c.dma(w1.ap[0:33], c.d_hywin, [], [w1.r])
    wm = c.sb([2, 64], F32); c.dma(wm.ap[0:64], c.d_hywmid, [], [wm.r])
    fb_ = c.sb([4], F32); c.dma(fb_.ap[0:64], c.d_hyfb, [], [fb_.r])
    fq = c.sb([1], F32); bq = c.sb([3], F32)
    c.ts(fq.ap[0:64], fb_.ap[0:64, 0:1], 1.0 / TWO_PI, None, ALU.mult, None, [fb_.r], [fq.r])
    c.ts(bq.ap[0:64], fb_.ap[0:64, 1:4], fq.ap[0:64, 0:1], None, ALU.mult, None, [fb_.r, fq.r], [bq.r])
    hid = [hid0, c.sb([L], F32)]
    tA, tB = c.sb([512], F32), c.sb([512], F32)
    src = zT
    for l in range(3):
        dst = hid[l % 2]
        for b4 in range(4):
            ps = c.psum[b4 % 2]
            if l == 0:
                c.mm(ps.ap[0:64, :], w1.ap[0:64, :], zT.ap[0:64, b4 * 512:(b4 + 1) * 512], True, True, [w1.r, zT.r], [ps.r])
            else:
                c.mm(ps.ap[0:64, :], wm.ap[0:64, l - 1, :], src.ap[0:64, b4 * 512:(b4 + 1) * 512], True, True, [wm.r, src.r], [ps.r])
            c.ts(tA.ap[0:64], ps.ap[0:64, :], fq.ap[0:64, 0:1], bq.ap[0:64, l:l + 1], ALU.mult, ALU.add, [ps.r, fq.r, bq.r], [tA.r])
            rnd(c, tB.ap[0:64], tA.ap[0:64], [tA.r], [tB.r])
            c.tt(tA.ap[0:64], tA.ap[0:64], tB.ap[0:64], ALU.subtract, [tA.r, tB.r], [tA.r])
            c.act(dst.ap[0:64, b4 * 512:(b4 + 1) * 512], tA.ap[0:64], AF.Sin, [tA.r], [dst.r], scale=TWO_PI)
        src = dst
    h3 = src
    c.S.flush(barrier=True)
    c.aoff = mark
    CB_ = 256
    dec = c.sb([16, CB_], F32)
    fbt = c.sb([2, CB_], F32)
    Hs = c.sb([16, CB_], BF16); Hd = c.sb([16, CB_], BF16)
    Kre = c.sb([16, CB_], F32); Kim = c.sb([16, CB_], F32)
    Yre = c.sb([16, CB_], BF16); Yim = c.sb([16, CB_], BF16)
    tok = {k: c.sb([16, CB_], BF16) for k in ('x1', 'x2', 'v', 'z1', 'o')}
    tabs = [c.sb([16, 128], BF16) for _ in range(4)]
    tmpf = [c.sb([CB_], F32) for _ in range(6)]
    chm = [c.sb([L], BF16) for _ in range(2)]
    nt = 0
    npz = 0
    for cb in range(4):
        c.dma(dec.ap, c.d_decay[:, cb * CB_:(cb + 1) * CB_].rearrange('(tt p) q -> p tt q', p=128), [], [dec.r])
        c.dma(fbt.ap, c.d_hyfbias[:, :, cb * CB_:(cb + 1) * CB_], [], [fbt.r])
        for ki, key in enumerate(('x1', 'x2', 'v')):
            for ct in range(2):
                ch_ = chm[npz % 2]
                c.dma(ch_.ap, c.HYd[ki * 8 + cb * 2 + ct], [c.HYr], [ch_.r])
                for t4 in range(4):
                    ps = c.psum[6 + npz % 2]
                    npz += 1
                    pb = ps.ap.bitcast(BF16)
                    for jj in range(4):
                        tt_ = t4 * 4 + jj
                        c.transpose(pb[:, jj * 128:(jj + 1) * 128], ch_.ap[:, tt_ * 128:(tt_ + 1) * 128], c.ident_b.ap, [ch_.r, c.ident_b.r], [ps.r])
                    c.copy(tok[key].ap[:, t4 * 4:(t4 + 1) * 4, ct * 128:(ct + 1) * 128], pb[:, 0:512].rearrange('p (j q) -> p j q', q=128), [ps.r], [tok[key].r])
        for o in range(2):
            tin = tok['v'] if o == 0 else tok['z1']
            gate = tok['x1'] if o == 0 else tok['x2']
            tout = tok['z1'] if o == 0 else tok['o']
            for tt_ in range(16):
                pf, pb_ = c.psum[0], c.psum[1]
                for dr, p_ in ((0, pf), (1, pb_)):
                    col0 = o * 2048 + dr * 1024 + cb * CB_
                    c.mm(p_.ap[:, 0:CB_], h3.ap[0:64, tt_ * 128:(tt_ + 1) * 128], hwo.ap[0:64, col0:col0 + CB_], True, True, [h3.r, hwo.r], [p_.r])
                f_, b_ = tmpf[0], tmpf[1]
                c.tt(f_.ap, pf.ap[:, 0:CB_], dec.ap[:, tt_, :], ALU.mult, [pf.r, dec.r], [f_.r])
                c.tt(b_.ap, pb_.ap[:, 0:CB_], dec.ap[:, tt_, :], ALU.mult, [pb_.r, dec.r], [b_.r])
                if tt_ == 0:
                    c.memset(b_.ap[0:1, :], 0.0, [b_.r])
                c.tt(Hs.ap[:, tt_, :], f_.ap, b_.ap, ALU.add, [f_.r, b_.r], [Hs.r])
                c.tt(Hd.ap[:, tt_, :], f_.ap, b_.ap, ALU.subtract, [f_.r, b_.r], [Hd.r])
            for ft in range(16):
                tc_, ts_ = tabs[nt % 4], tabs[(nt + 1) % 4]
                nt += 2
                c.dma(tc_.ap, c.d_Cf[ft], [], [tc_.r])
                c.dma(ts_.ap, c.d_Sf[ft], [], [ts_.r])
                pr, pi = c.psum[0], c.psum[1]
                for tt_ in range(16):
                    c.mm(pr.ap[:, 0:CB_], tc_.ap[:, tt_, :], Hs.ap[:, tt_, :], tt_ == 0, tt_ == 15, [tc_.r, Hs.r], [pr.r])
                for tt_ in range(16):
                    c.mm(pi.ap[:, 0:CB_], ts_.ap[:, tt_, :], Hd.ap[:, tt_, :], tt_ == 0, tt_ == 15, [ts_.r, Hd.r], [pi.r])
                c.act(Kre.ap[:, ft, :], pr.ap[:, 0:CB_], AF.Identity, [pr.r], [Kre.r])
                c.act(Kim.ap[:, ft, :], pi.ap[:, 0:CB_], AF.Identity, [pi.r], [Kim.r])
                if ft == 0:
                    pn = c.psum[2]
                    for tt_ in range(16):
                        c.mm(pn.ap[:, 0:CB_], ts_.ap[:, tt_, :], Hs.ap[:, tt_, :], tt_ == 0, tt_ == 15, [ts_.r, Hs.r], [pn.r])
                    c.act(Kim.ap[0:1, 0, :], pn.ap[0:1, 0:CB_], AF.Identity, [pn.r], [Kim.r])
                vr, vi = c.psum[3], c.psum[4]
                for tt_ in range(16):
                    c.mm(vr.ap[:, 0:CB_], tc_.ap[:, tt_, :], tin.ap[:, tt_, :], tt_ == 0, tt_ == 15, [tc_.r, tin.r], [vr.r])
                for tt_ in range(16):
                    c.mm(vi.ap[:, 0:CB_], ts_.ap[:, tt_, :], tin.ap[:, tt_, :], tt_ == 0, tt_ == 15, [ts_.r, tin.r], [vi.r])
                a1, a2, a3, a4 = tmpf[2:6]
                c.tt(a1.ap, vr.ap[:, 0:CB_], Kre.ap[:, ft, :], ALU.mult, [vr.r, Kre.r], [a1.r])
                c.tt(a2.ap, vi.ap[:, 0:CB_], Kim.ap[:, ft, :], ALU.mult, [vi.r, Kim.r], [a2.r])
                c.tt(a3.ap, vr.ap[:, 0:CB_], Kim.ap[:, ft, :], ALU.mult, [vr.r, Kim.r], [a3.r])
                c.tt(a4.ap, vi.ap[:, 0:CB_], Kre.ap[:, ft, :], ALU.mult, [vi.r, Kre.r], [a4.r])
                c.tt(Yre.ap[:, ft, :], a1.ap, a2.ap, ALU.subtract, [a1.r, a2.r], [Yre.r])
                c.tt(Yim.ap[:, ft, :], a3.ap, a4.ap, ALU.add, [a3.r, a4.r], [Yim.r])
                if ft == 0:
                    c.copy(Yre.ap[0:1, 0, :], a1.ap[0:1, :], [a1.r], [Yre.r])
                    c.copy(Yim.ap[0:1, 0, :], a2.ap[0:1, :], [a2.r], [Yim.r])
            for tt_ in range(16):
                tc_, ts_ = tabs[nt % 4], tabs[(nt + 1) % 4]
                nt += 2
                c.dma(tc_.ap, c.d_Ci[tt_], [], [tc_.r])
                c.dma(ts_.ap, c.d_Si[tt_], [], [ts_.r])
                py = c.psum[5]
                for ft in range(16):
                    c.mm(py.ap[:, 0:CB_], tc_.ap[:, ft, :], Yre.ap[:, ft, :], ft == 0, False, [tc_.r, Yre.r], [py.r])
                for ft in range(16):
                    c.mm(py.ap[:, 0:CB_], ts_.ap[:, ft, :], Yim.ap[:, ft, :], False, ft == 15, [ts_.r, Yim.r], [py.r])
                e1 = tmpf[0]
                c.tt(e1.ap, tin.ap[:, tt_, :], fbt.ap[:, o, :], ALU.mult, [tin.r, fbt.r], [e1.r])
                c.tt(e1.ap, e1.ap, py.ap[:, 0:CB_], ALU.add, [e1.r, py.r], [e1.r])
                c.tt(tout.ap[:, tt_, :], e1.ap, gate.ap[:, tt_, :], ALU.mult, [e1.r, gate.r], [tout.r])
        for ct in range(2):
            ch_ = chm[npz % 2]
            for t4 in range(4):
                ps = c.psum[6 + npz % 2]
                npz += 1
                pb = ps.ap.bitcast(BF16)
                for jj in range(4):
                    tt_ = t4 * 4 + jj
                    c.transpose(pb[:, jj * 128:(jj + 1) * 128], tok['o'].ap[:, tt_, ct * 128:(ct + 1) * 128], c.ident_b.ap, [tok['o'].r, c.ident_b.r], [ps.r])
                c.copy(ch_.ap[:, t4 * 512:(t4 + 1) * 512], pb[:, 0:512], [ps.r], [ch_.r])
            c.dma(c.CATd[cb * 2 + ct, :, LC:T], ch_.ap, [ch_.r], [c.CATr[0]])
    c.stage_end()


def st_odd_out(c):
    wo = c.sb([16, 2048], BF16)
    for kt in range(16):
        c.dma(wo.ap[:, kt, :], c.d_wout[1][kt * 128:(kt + 1) * 128, :], [], [wo.r], q=POOL)
    ng = c.sb([8], F32); c.dma(ng.ap, c.d_ssdng, [], [ng.r])
    yb = c.sb([8, 512], F32); zb = c.sb([8, 512], BF16); sq = c.sb([8, 512], BF16)
    rs = c.sb([512], F32)
    cat = c.sb([16, 512], BF16)
    cr = [Reg() for _ in range(16)]
    xt = [c.sb([512], F32) for _ in range(4)]
    nx = 0
    for (t0, n) in TB[1:]:
        l0 = t0 - LC
        c.dma(yb.ap, c.YSd[:, :, l0:l0 + n].rearrange('h q t -> (h q) t').rearrange('(a p) t -> p a t', p=128), [c.YSr], [yb.r])
        c.dma(zb.ap, c.Zd[:, :, l0:l0 + n].rearrange('a p t -> p a t'), [c.Zr], [zb.r])
        c.dma(cat.ap[:, 0:8, :], c.CATd[0:8, :, t0:t0 + n].rearrange('a p t -> p a t'), [c.CATr[0]], cr[0:8])
        c.tt(yb.ap, yb.ap, zb.ap, ALU.mult, [yb.r, zb.r], [yb.r])
        c.act(sq.ap, yb.ap, AF.Square, [yb.r], [sq.r])
        ps = c.psum[0]
        for a in range(8):
            c.mm(ps.ap, c.ones_bf.ap, sq.ap[:, a, :], a == 0, a == 7, [c.ones_bf.r, sq.r], [ps.r])
        c.act(rs.ap, ps.ap, AF.Sqrt, [ps.r, c.eps_t.r], [rs.r], scale=1.0 / 1024, bias=c.eps_t.ap[:, 0:1])
        c.recip(rs.ap, rs.ap, [rs.r], [rs.r])
        for a in range(8):
            c.stt(cat.ap[:, 8 + a, :], yb.ap[:, a, :], ng.ap[:, a:a + 1], rs.ap, ALU.mult, ALU.mult, [yb.r, ng.r, rs.r], [cr[8 + a]])
        wout_block(c, 1, cat, cr, wo, xt, t0, n, 0, nx)
        nx += 16
    c.stage_end()
```
